# Optimizing a Trainium2 kernel written in Bass

```python
import math
import jax, jax.numpy as jnp
from jax import lax
import numpy as np

D_MODEL = 2048
BATCH = 32
SEQ = 256
DEPTH = 4
DEC_BATCH = 4
DEC_SEQ = 2048
PAST_LEN = 512

GRID_W = 64
N_MIXERS = 3
N_SSD = (DEPTH + 2) // 3
N_ATTN = (DEPTH + 1) // 3
N_MLSTM = DEPTH // 3
EPS = 1e-6
CHUNK = 128

SSD_D_INNER = 2 * D_MODEL
SSD_HEAD_DIM = 64
SSD_HEADS = SSD_D_INNER // SSD_HEAD_DIM
SSD_GROUPS = 8
SSD_D_STATE = 128
SSD_CONV_W = 3
SSD_CONV_DIM = SSD_D_INNER + 2 * SSD_GROUPS * SSD_D_STATE
SSD_IN_DIM = SSD_D_INNER + SSD_CONV_DIM + 2 * SSD_HEADS

ATTN_HEAD_DIM = 128
ATTN_HEADS = D_MODEL // ATTN_HEAD_DIM
ATTN_KV_HEADS = 4
Q_BLOCK = 128
ROPE_THETA = 10000.0

MLSTM_HEADS = 8
MLSTM_DK = D_MODEL // 2 // MLSTM_HEADS
MLSTM_DV = D_MODEL // MLSTM_HEADS
MLSTM_IN_DIM = 2 * MLSTM_HEADS * MLSTM_DK + 2 * MLSTM_HEADS * MLSTM_DV + 4 * MLSTM_HEADS

FFN_HIDDEN = ((8 * D_MODEL + 3 * 256 - 1) // (3 * 256)) * 256

kernel_name = 'hybrid_flow_ssd_gqa_mlstm_step'

F32 = jnp.float32


def rms_norm(x, w):
    xf = x.astype(F32)
    y = xf * lax.rsqrt(jnp.mean(xf * xf, axis=-1, keepdims=True) + EPS)
    return (y * w.astype(F32)).astype(x.dtype)


def group_rms_norm(y, w, groups):
    shp = y.shape
    yg = y.reshape(shp[:-1] + (groups, shp[-1] // groups))
    return rms_norm(yg, w.reshape(groups, -1)).reshape(shp)


def modulate(x, shift, scale):
    return x * (1 + scale) + shift


def swiglu(u, w_gate, w_up, w_down):
    return (jax.nn.silu(u @ w_gate) * (u @ w_up)) @ w_down


def depthwise_conv(x, w, b):
    pad = SSD_CONV_W // 2
    y = lax.conv_general_dilated(x, w[:, None, :], window_strides=(1,), padding=[(pad, pad)],
                                 dimension_numbers=('NWC', 'WIO', 'NWC'),
                                 feature_group_count=x.shape[-1])
    return y + b


def ssd_scan(x, dt, A, Bm, Cm, h0):
    b, L, H, P = x.shape
    G, N = Bm.shape[2], Bm.shape[3]
    R = H // G
    nc = L // CHUNK
    xc = x.reshape(b, nc, CHUNK, G, R, P)
    dtc = dt.reshape(b, nc, CHUNK, G, R)
    Bc = Bm.reshape(b, nc, CHUNK, G, N)
    Cc = Cm.reshape(b, nc, CHUNK, G, N)
    a_cs = jnp.cumsum(dtc * A.reshape(G, R), axis=2)
    seg = a_cs[:, :, :, None] - a_cs[:, :, None, :]
    mask = jnp.tril(jnp.ones((CHUNK, CHUNK), bool))[:, :, None, None]
    decay = jnp.exp(jnp.where(mask, seg, -jnp.inf))
    CB = jnp.einsum('bctgn,bcsgn->bctsg', Cc, Bc)
    xdt = xc * dtc[..., None]
    y_intra = jnp.einsum('bctsgr,bcsgrp->bctgrp', CB[..., None] * decay, xdt)
    decay_end = jnp.exp(a_cs[:, :, -1:] - a_cs)
    S = jnp.einsum('bcsgn,bcsgrp->bcgrpn', Bc, xdt * decay_end[..., None]).astype(F32)
    tot = jnp.exp(a_cs[:, :, -1])

    def step(h, inp):
        tot_c, S_c = inp
        return tot_c[..., None, None] * h + S_c, h

    h_init = h0.astype(F32).reshape(b, G, R, P, N)
    hT, h_prev = lax.scan(step, h_init, (jnp.moveaxis(tot, 1, 0), jnp.moveaxis(S, 1, 0)))
    h_prev = jnp.moveaxis(h_prev, 0, 1)
    y_inter = jnp.einsum('bctgn,bcgrpn->bctgrp', Cc, h_prev) * jnp.exp(a_cs)[..., None]
    y = (y_intra + y_inter).reshape(b, L, H, P).astype(x.dtype)
    return y, hT.reshape(b, H, P, N).astype(h0.dtype)


def ssd_mixer(u, h0, w_in, conv_w, conv_b, dt_bias, a_log, d_skip, norm_w, w_out):
    b, L, _ = u.shape
    GN = SSD_GROUPS * SSD_D_STATE
    proj = u @ w_in
    z = proj[..., :SSD_D_INNER]
    xbc = jax.nn.silu(depthwise_conv(proj[..., SSD_D_INNER:SSD_D_INNER + SSD_CONV_DIM], conv_w, conv_b))
    dt_raw = proj[..., SSD_D_INNER + SSD_CONV_DIM:].astype(F32).reshape(b, L, 2, SSD_HEADS)
    xs = xbc[..., :SSD_D_INNER].reshape(b, L, SSD_HEADS, SSD_HEAD_DIM)
    Bm = xbc[..., SSD_D_INNER:SSD_D_INNER + GN].reshape(b, L, SSD_GROUPS, SSD_D_STATE)
    Cm = xbc[..., SSD_D_INNER + GN:].reshape(b, L, SSD_GROUPS, SSD_D_STATE)
    dt = jax.nn.softplus(dt_raw + dt_bias.astype(F32))
    A = -jnp.exp(a_log.astype(F32))
    y_f, h_f = ssd_scan(xs, dt[:, :, 0], A[0], Bm, Cm, h0[:, 0])
    y_b, h_b = ssd_scan(xs[:, ::-1], dt[:, ::-1, 1], A[1], Bm[:, ::-1], Cm[:, ::-1], h0[:, 1])
    y = y_f + y_b[:, ::-1] + d_skip[:, None] * xs
    y = group_rms_norm(y.reshape(b, L, SSD_D_INNER) * jax.nn.silu(z), norm_w, SSD_GROUPS)
    return y @ w_out, jnp.stack([h_f, h_b], axis=1)


def attn_project(u, w_qkv, q_norm, k_norm):
    b, L, _ = u.shape
    nq = ATTN_HEADS * ATTN_HEAD_DIM
    nk = ATTN_KV_HEADS * ATTN_HEAD_DIM
    qkv = u @ w_qkv
    q = rms_norm(qkv[..., :nq].reshape(b, L, ATTN_HEADS, ATTN_HEAD_DIM), q_norm)
    k = rms_norm(qkv[..., nq:nq + nk].reshape(b, L, ATTN_KV_HEADS, ATTN_HEAD_DIM), k_norm)
    v = qkv[..., nq + nk:].reshape(b, L, ATTN_KV_HEADS, ATTN_HEAD_DIM)
    return q, k, v


def axial_rope(x, rows):
    half = ATTN_HEAD_DIM // 2
    quarter = half // 2
    pos_row = jnp.repeat(jnp.arange(rows), GRID_W)
    pos_col = jnp.tile(jnp.arange(GRID_W), rows)
    inv_freq = ROPE_THETA ** (-jnp.arange(quarter, dtype=F32) / quarter)

    def rotate(xa, pos):
        ang = pos.astype(F32)[:, None] * inv_freq
        cos = jnp.cos(ang)[None, :, None, :]
        sin = jnp.sin(ang)[None, :, None, :]
        xf = xa.astype(F32)
        x1, x2 = xf[..., :quarter], xf[..., quarter:]
        return jnp.concatenate([x1 * cos - x2 * sin, x2 * cos + x1 * sin], axis=-1)

    out = jnp.concatenate([rotate(x[..., :half], pos_row), rotate(x[..., half:], pos_col)], axis=-1)
    return out.astype(x.dtype)


def blocked_attention(q, k, v):
    b, Lq = q.shape[:2]
    grp = ATTN_HEADS // ATTN_KV_HEADS
    nb = Lq // Q_BLOCK
    qb = jnp.moveaxis(q.reshape(b, nb, Q_BLOCK, ATTN_KV_HEADS, grp, ATTN_HEAD_DIM), 1, 0)
    scale = ATTN_HEAD_DIM ** -0.5

    def one_block(qblk):
        s = jnp.einsum('bqkgd,bskd->bkgqs', qblk, k).astype(F32) * scale
        p = jax.nn.softmax(s, axis=-1).astype(v.dtype)
        return jnp.einsum('bkgqs,bskd->bqkgd', p, v)

    o = lax.map(one_block, qb)
    return jnp.moveaxis(o, 0, 1).reshape(b, Lq, ATTN_HEADS * ATTN_HEAD_DIM)


def mlstm_scan(q, k, v, i_raw, log_f, C0, n0, m0):
    b, L, H, _ = q.shape
    DV = v.shape[-1]
    nc = L // CHUNK

    def to_chunks(t):
        return jnp.moveaxis(t.reshape((b, nc, CHUNK) + t.shape[2:]), 1, 0)

    mask = jnp.tril(jnp.ones((CHUNK, CHUNK), bool))[None, :, :, None]

    def step(carry, inp):
        C, n, m = carry
        qc, kc, vc, ic, fc = inp
        bcum = jnp.cumsum(fc, axis=1)
        Dm = jnp.where(mask, bcum[:, :, None] - bcum[:, None, :] + ic[:, None, :], -jnp.inf)
        m_inter = bcum + m[:, None]
        m_t = jnp.maximum(m_inter, jnp.max(Dm, axis=2))
        W = jnp.exp(Dm - m_t[:, :, None])
        S = jnp.einsum('bthd,bshd->btsh', qc, kc) * W
        inter = jnp.exp(m_inter - m_t)
        num = jnp.einsum('btsh,bshv->bthv', S, vc) + inter[..., None] * jnp.einsum('bthd,bhdv->bthv', qc, C)
        den = jnp.sum(S, axis=2) + inter * jnp.einsum('bthd,bhd->bth', qc, n)
        h = num / jnp.maximum(jnp.abs(den), jnp.exp(-m_t))[..., None]
        bQ = bcum[:, -1]
        log_w = bQ[:, None] - bcum + ic
        m_new = jnp.maximum(bQ + m, jnp.max(log_w, axis=1))
        w = jnp.exp(log_w - m_new[:, None])
        carry_decay = jnp.exp(bQ + m - m_new)
        wk = w[..., None] * kc
        C_new = carry_decay[..., None, None] * C + jnp.einsum('bshd,bshv->bhdv', wk, vc)
        n_new = carry_decay[..., None] * n + jnp.sum(wk, axis=1)
        return (C_new, n_new, m_new), h

    init = (C0.astype(F32), n0.astype(F32), m0.astype(F32))
    (CT, nT, mT), h = lax.scan(step, init, (to_chunks(q), to_chunks(k), to_chunks(v),
                                             to_chunks(i_raw), to_chunks(log_f)))
    h = jnp.moveaxis(h, 0, 1).reshape(b, L, H, DV).astype(q.dtype)
    return h, CT.astype(C0.dtype), nT.astype(n0.dtype), mT.astype(m0.dtype)


def mlstm_mixer(u, C0, n0, m0, w_in, b_gates, head_norm, w_out):
    b, L, _ = u.shape
    H, DK, DV = MLSTM_HEADS, MLSTM_DK, MLSTM_DV
    o1, o2 = H * DK, 2 * H * DK
    o3, o4 = o2 + H * DV, o2 + 2 * H * DV
    proj = u @ w_in
    q = proj[..., :o1].reshape(b, L, H, DK) * (DK ** -0.5)
    k = proj[..., o1:o2].reshape(b, L, H, DK)
    v = proj[..., o2:o3].reshape(b, L, H, DV)
    o_gate = jax.nn.sigmoid(proj[..., o3:o4])
    gates = proj[..., o4:].astype(F32).reshape(b, L, 2, 2, H) + b_gates.astype(F32)
    i_raw = gates[:, :, :, 0]
    log_f = jax.nn.log_sigmoid(gates[:, :, :, 1])
    h_f, C_f, n_f, m_f = mlstm_scan(q, k, v, i_raw[:, :, 0], log_f[:, :, 0], C0[:, 0], n0[:, 0], m0[:, 0])
    h_b, C_b, n_b, m_b = mlstm_scan(q[:, ::-1], k[:, ::-1], v[:, ::-1], i_raw[:, ::-1, 1], log_f[:, ::-1, 1],
                                    C0[:, 1], n0[:, 1], m0[:, 1])
    h = rms_norm(h_f + h_b[:, ::-1], head_norm).reshape(b, L, H * DV) * o_gate
    return (h @ w_out, jnp.stack([C_f, C_b], axis=1), jnp.stack([n_f, n_b], axis=1),
            jnp.stack([m_f, m_b], axis=1))


def setup_inputs(seed: int = 0) -> dict:
    key = jax.random.key(seed)
    ks = iter(jax.random.split(key, 64))
    D = D_MODEL
    inv = D ** -0.5

    def nrm(shape, scale):
        return jax.random.normal(next(ks), shape, F32) * scale

    x_prompt = nrm((BATCH, SEQ, D), 1.0)
    x_sample = nrm((DEC_BATCH, DEC_SEQ, D), 1.0)
    c = nrm((DEC_BATCH, D), 1.0)
    state_ssd = nrm((DEC_BATCH, N_SSD, 2, SSD_HEADS, SSD_HEAD_DIM, SSD_D_STATE), 0.1)
    cache_attn_k = nrm((DEC_BATCH, N_ATTN, PAST_LEN, ATTN_KV_HEADS, ATTN_HEAD_DIM), 1.0)
    cache_attn_v = nrm((DEC_BATCH, N_ATTN, PAST_LEN, ATTN_KV_HEADS, ATTN_HEAD_DIM), 1.0)
    state_mlstm_C = nrm((DEC_BATCH, N_MLSTM, 2, MLSTM_HEADS, MLSTM_DK, MLSTM_DV), 0.1)
    state_mlstm_n = nrm((DEC_BATCH, N_MLSTM, 2, MLSTM_HEADS, MLSTM_DK), 0.5)
    state_mlstm_m = nrm((DEC_BATCH, N_MLSTM, 2, MLSTM_HEADS), 1.0)
    c_ctx = nrm((D,), 1.0)
    ada_w = nrm((DEPTH, D, 6 * D), 0.3 * inv)
    ada_b = nrm((DEPTH, 6 * D), 0.02)
    norm_mix_w = 1.0 + nrm((DEPTH, D), 0.02)
    norm_ffn_w = 1.0 + nrm((DEPTH, D), 0.02)
    ffn_w_gate = nrm((DEPTH, D, FFN_HIDDEN), inv)
    ffn_w_up = nrm((DEPTH, D, FFN_HIDDEN), inv)
    ffn_w_down = nrm((DEPTH, FFN_HIDDEN, D), FFN_HIDDEN ** -0.5)
    ssd_w_in = jnp.concatenate([nrm((N_SSD, D, SSD_D_INNER + SSD_CONV_DIM), inv),
                                nrm((N_SSD, D, 2 * SSD_HEADS), 0.1 * inv)], axis=-1)
    ssd_conv_w = nrm((N_SSD, SSD_CONV_W, SSD_CONV_DIM), SSD_CONV_W ** -0.5)
    ssd_conv_b = nrm((N_SSD, SSD_CONV_DIM), 0.02)
    dt0 = jnp.exp(jax.random.uniform(next(ks), (N_SSD, 2, SSD_HEADS), F32,
                                     minval=math.log(1e-3), maxval=math.log(1e-1)))
    ssd_dt_bias = dt0 + jnp.log(-jnp.expm1(-dt0))
    ssd_a_log = jnp.log(jax.random.uniform(next(ks), (N_SSD, 2, SSD_HEADS), F32, minval=1.0, maxval=16.0))
    ssd_d = 1.0 + nrm((N_SSD, SSD_HEADS), 0.1)
    ssd_norm_w = 1.0 + nrm((N_SSD, SSD_D_INNER), 0.02)
    ssd_w_out = nrm((N_SSD, SSD_D_INNER, D), SSD_D_INNER ** -0.5)
    attn_w_qkv = nrm((N_ATTN, D, (ATTN_HEADS + 2 * ATTN_KV_HEADS) * ATTN_HEAD_DIM), inv)
    attn_q_norm = 1.0 + nrm((N_ATTN, ATTN_HEAD_DIM), 0.02)
    attn_k_norm = 1.0 + nrm((N_ATTN, ATTN_HEAD_DIM), 0.02)
    attn_w_out = nrm((N_ATTN, ATTN_HEADS * ATTN_HEAD_DIM, D), (ATTN_HEADS * ATTN_HEAD_DIM) ** -0.5)
    mlstm_w_in = jnp.concatenate([nrm((N_MLSTM, D, MLSTM_IN_DIM - 4 * MLSTM_HEADS), inv),
                                  nrm((N_MLSTM, D, 4 * MLSTM_HEADS), 0.1 * inv)], axis=-1)
    i_bias = nrm((N_MLSTM, 2, 1, MLSTM_HEADS), 0.1)
    f_bias = jnp.linspace(3.0, 6.0, MLSTM_HEADS, dtype=F32) + nrm((N_MLSTM, 2, 1, MLSTM_HEADS), 0.1)
    mlstm_b_gates = jnp.concatenate([i_bias, f_bias], axis=2)
    mlstm_head_norm = 1.0 + nrm((N_MLSTM, MLSTM_HEADS, MLSTM_DV), 0.02)
    mlstm_w_out = nrm((N_MLSTM, MLSTM_HEADS * MLSTM_DV, D), (MLSTM_HEADS * MLSTM_DV) ** -0.5)
    return {'x_prompt': x_prompt, 'x_sample': x_sample, 'c': c,
            'state_ssd': state_ssd, 'cache_attn_k': cache_attn_k, 'cache_attn_v': cache_attn_v,
            'state_mlstm_C': state_mlstm_C, 'state_mlstm_n': state_mlstm_n, 'state_mlstm_m': state_mlstm_m,
            'c_ctx': c_ctx, 'ada_w': ada_w, 'ada_b': ada_b,
            'norm_mix_w': norm_mix_w, 'norm_ffn_w': norm_ffn_w,
            'ffn_w_gate': ffn_w_gate, 'ffn_w_up': ffn_w_up, 'ffn_w_down': ffn_w_down,
            'ssd_w_in': ssd_w_in, 'ssd_conv_w': ssd_conv_w, 'ssd_conv_b': ssd_conv_b,
            'ssd_dt_bias': ssd_dt_bias, 'ssd_a_log': ssd_a_log, 'ssd_d': ssd_d,
            'ssd_norm_w': ssd_norm_w, 'ssd_w_out': ssd_w_out,
            'attn_w_qkv': attn_w_qkv, 'attn_q_norm': attn_q_norm, 'attn_k_norm': attn_k_norm,
            'attn_w_out': attn_w_out,
            'mlstm_w_in': mlstm_w_in, 'mlstm_b_gates': mlstm_b_gates,
            'mlstm_head_norm': mlstm_head_norm, 'mlstm_w_out': mlstm_w_out}


def reference(x_prompt, x_sample, c, state_ssd, cache_attn_k, cache_attn_v, state_mlstm_C, state_mlstm_n,
              state_mlstm_m, c_ctx, ada_w, ada_b, norm_mix_w, norm_ffn_w, ffn_w_gate, ffn_w_up, ffn_w_down,
              ssd_w_in, ssd_conv_w, ssd_conv_b, ssd_dt_bias, ssd_a_log, ssd_d, ssd_norm_w, ssd_w_out,
              attn_w_qkv, attn_q_norm, attn_k_norm, attn_w_out,
              mlstm_w_in, mlstm_b_gates, mlstm_head_norm, mlstm_w_out):
    rows = x_sample.shape[1] // GRID_W
    bp = x_prompt.shape[0]
    xp, xs = x_prompt, x_sample
    new_ssd, new_k, new_v, new_C, new_n, new_m = [], [], [], [], [], []
    for l in range(DEPTH):
        kind, j = l % N_MIXERS, l // N_MIXERS
        mod_p = (jax.nn.silu(c_ctx) @ ada_w[l] + ada_b[l])[None, None, :]
        mod_s = (jax.nn.silu(c) @ ada_w[l] + ada_b[l])[:, None, :]
        sh1p, sc1p, g1p, sh2p, sc2p, g2p = jnp.split(mod_p, 6, axis=-1)
        sh1s, sc1s, g1s, sh2s, sc2s, g2s = jnp.split(mod_s, 6, axis=-1)
        up = modulate(rms_norm(xp, norm_mix_w[l]), sh1p, sc1p)
        us = modulate(rms_norm(xs, norm_mix_w[l]), sh1s, sc1s)
        if kind == 0:
            prm = (ssd_w_in[j], ssd_conv_w[j], ssd_conv_b[j], ssd_dt_bias[j], ssd_a_log[j], ssd_d[j],
                   ssd_norm_w[j], ssd_w_out[j])
            h_zero = jnp.zeros((bp, 2, SSD_HEADS, SSD_HEAD_DIM, SSD_D_STATE), xp.dtype)
            mp, st = ssd_mixer(up, h_zero, *prm)
            ms, _ = ssd_mixer(us, state_ssd[:, j], *prm)
            new_ssd.append(st)
        elif kind == 1:
            qp, kp, vp = attn_project(up, attn_w_qkv[j], attn_q_norm[j], attn_k_norm[j])
            mp = blocked_attention(qp, kp, vp) @ attn_w_out[j]
            qs, kl, vl = attn_project(us, attn_w_qkv[j], attn_q_norm[j], attn_k_norm[j])
            qs = axial_rope(qs, rows)
            kl = axial_rope(kl, rows)
            k_all = jnp.concatenate([cache_attn_k[:, j], kl], axis=1)
            v_all = jnp.concatenate([cache_attn_v[:, j], vl], axis=1)
            ms = blocked_attention(qs, k_all, v_all) @ attn_w_out[j]
            new_k.append(kp)
            new_v.append(vp)
        else:
            prm = (mlstm_w_in[j], mlstm_b_gates[j], mlstm_head_norm[j], mlstm_w_out[j])
            C_zero = jnp.zeros((bp, 2, MLSTM_HEADS, MLSTM_DK, MLSTM_DV), xp.dtype)
            n_zero = jnp.zeros((bp, 2, MLSTM_HEADS, MLSTM_DK), xp.dtype)
            m_zero = jnp.zeros((bp, 2, MLSTM_HEADS), xp.dtype)
            mp, Cc, nc_, mc = mlstm_mixer(up, C_zero, n_zero, m_zero, *prm)
            ms, _, _, _ = mlstm_mixer(us, state_mlstm_C[:, j], state_mlstm_n[:, j], state_mlstm_m[:, j], *prm)
            new_C.append(Cc)
            new_n.append(nc_)
            new_m.append(mc)
        xp = xp + g1p * mp
        xs = xs + g1s * ms
        xp = xp + g2p * swiglu(modulate(rms_norm(xp, norm_ffn_w[l]), sh2p, sc2p),
                               ffn_w_gate[l], ffn_w_up[l], ffn_w_down[l])
        xs = xs + g2s * swiglu(modulate(rms_norm(xs, norm_ffn_w[l]), sh2s, sc2s),
                               ffn_w_gate[l], ffn_w_up[l], ffn_w_down[l])
    return (xp, xs, jnp.stack(new_ssd, axis=1), jnp.stack(new_k, axis=1), jnp.stack(new_v, axis=1),
            jnp.stack(new_C, axis=1), jnp.stack(new_n, axis=1), jnp.stack(new_m, axis=1))
```

```python
import numpy as np
from contextlib import ExitStack
import concourse.bass as bass
import concourse.mybir as mybir
from concourse.bass_utils import run_bass_kernel_spmd

F32 = mybir.dt.float32
BF16 = mybir.dt.bfloat16
AF = mybir.ActivationFunctionType
ALU = mybir.AluOpType

D = 2048
NTOK = 2048
KC = 16
FH = 5632
HC = 44
DEPTH = 4
EPS = 1e-6
SLOT = 8192
NSLOT = 4
KSPLIT = [(0, 16), (16, 16), (32, 12)]


class Prog:
    def __init__(s, nc):
        s.nc = nc
        s.E = dict(pe=nc.tensor, act=nc.scalar, dve=nc.vector, pool=nc.gpsimd, sp=nc.sync)
        s.sem = {}
        s.val = {}
        for e in s.E:
            s.sem[e] = nc.alloc_semaphore("sem_" + e)
            s.val[e] = 0
        s.dpool = {}
        for q, n in (("sp", 16), ("pool", 16)):
            keys = []
            for i in range(n):
                k = "d_%s%d" % (q, i)
                s.sem[k] = nc.alloc_semaphore(k)
                s.val[k] = 0
                keys.append(k)
            s.dpool[q] = [keys, 0]
        s.waited = {e: {} for e in s.E}
        s.lastw = {}
        s.readers = {}
        s.nops = 0

    def _deps(s, reads, writes):
        deps = {}
        for r in reads:
            t = s.lastw.get(r)
            if t is not None and deps.get(t[0], 0) < t[1]:
                deps[t[0]] = t[1]
        for w in writes:
            t = s.lastw.get(w)
            if t is not None and deps.get(t[0], 0) < t[1]:
                deps[t[0]] = t[1]
            rd = s.readers.get(w)
            if rd:
                for k, v in rd.items():
                    if deps.get(k, 0) < v:
                        deps[k] = v
        return deps

    def _wait(s, eng, deps):
        E = s.E[eng]
        wd = s.waited[eng]
        for k, v in deps.items():
            if eng == "pe" and k == "pe":
                continue
            if wd.get(k, 0) < v:
                E.wait_ge(s.sem[k], v)
                wd[k] = v

    def _commit(s, tok, reads, writes):
        for r in reads:
            d = s.readers.setdefault(r, {})
            if d.get(tok[0], 0) < tok[1]:
                d[tok[0]] = tok[1]
        for w in writes:
            s.lastw[w] = tok
            s.readers[w] = {}

    def op(s, eng, reads, writes, fn):
        pr = [r for r in reads if isinstance(r, tuple) and r[0] == "ps"]
        if pr:
            reads = [r for r in reads if r not in pr]
            writes = list(writes) + [r for r in pr if r not in writes]
        s._wait(eng, s._deps(reads, writes))
        ins = fn(s.E[eng])
        s.val[eng] += 1
        ins.then_inc(s.sem[eng], 1)
        tok = (eng, s.val[eng])
        s._commit(tok, reads, writes)
        s.nops += 1
        return tok

    def dma(s, q, out, in_, reads, writes):
        s._wait(q, s._deps(reads, writes))
        keys, idx = s.dpool[q]
        k = keys[idx % len(keys)]
        s.dpool[q][1] = idx + 1
        if s.val[k] > 0 and s.waited[q].get(k, 0) < s.val[k]:
            s.E[q].wait_ge(s.sem[k], s.val[k])
            s.waited[q][k] = s.val[k]
        s.val[k] += 16
        s.E[q].dma_start(out=out, in_=in_).then_inc(s.sem[k], 16)
        tok = (k, s.val[k])
        s._commit(tok, reads, writes)
        return tok

    def barrier(s):
        for e in s.E:
            for k, v in s.val.items():
                if e == "pe" and k == "pe":
                    continue
                if v > 0 and s.waited[e].get(k, 0) < v:
                    s.E[e].wait_ge(s.sem[k], v)
                    s.waited[e][k] = v


class WStream:
    def __init__(s, P, nc, WT, plan):
        s.P = P
        s.WT = WT
        s.slots = [nc.alloc_sbuf_tensor("wring%d" % i, [128, SLOT], BF16) for i in range(NSLOT)]
        s.record = plan is None
        s.plan = [] if plan is None else plan
        s.issued = 0
        s.taken = 0

    def _issue(s, n):
        name, idx, kc, ncols = s.plan[n]
        ap = s.WT[name][idx]
        i = n % NSLOT
        dst = s.slots[i][:, 0:kc * ncols].rearrange("p (c n) -> p c n", c=kc)
        s.P.dma("pool", dst, ap.rearrange("(c p) n -> p c n", p=128), reads=[], writes=[("wr", i)])

    def take(s, name, idx, kc, ncols):
        assert kc * ncols <= SLOT
        n = s.taken
        if s.record:
            s.plan.append((name, idx, kc, ncols))
        else:
            assert s.plan[n] == (name, idx, kc, ncols), (n, s.plan[n], name, idx)
            lim = min(len(s.plan), n + NSLOT - 1)
            while s.issued < lim:
                s._issue(s.issued)
                s.issued += 1
        i = n % NSLOT
        s.taken += 1
        view = s.slots[i][:, 0:kc * ncols].rearrange("p (c n) -> p c n", c=kc)
        return view, ("wr", i)


def build_program(cfg, plan=None):
    nc = bass.Bass("TRN2", target_bir_lowering=False)
    P = Prog(nc)
    _uid = [0]

    def sbt(name, shape, dt):
        _uid[0] += 1
        return nc.sbuf_tensor("%s_%d" % (name, _uid[0]), shape, dt)

    def din(name, shape):
        return nc.dram_tensor(name, list(shape), F32, kind="ExternalInput").ap()

    xin = din("xin", [NTOK, D])
    cv = din("cv", [128, KC])
    consts = din("consts", [128, 128])
    layers = cfg.get("layers", list(range(DEPTH)))
    NL = len(layers)
    WT = {}
    ada_w = din("ada_w", [NL, D, 6 * D])
    ada_b = din("ada_b", [NL, 6 * D])
    adab_f = din("adab_f", [128, DEPTH, 96])
    nw_f = din("nw_f", [128, DEPTH, 2, KC])
    ffn_wg = din("ffn_w_gate", [NL, D, FH])
    ffn_wu = din("ffn_w_up", [NL, D, FH])
    ffn_wd = din("ffn_w_down", [NL, FH, D])
    WT.update(ada_w=ada_w, ffn_w_gate=ffn_wg, ffn_w_up=ffn_wu, ffn_w_down=ffn_wd)
    has_ssd = any(l % 3 == 0 for l in layers)
    ssd_js = sorted(set(l // 3 for l in layers if l % 3 == 0))
    if has_ssd:
        NJ = len(ssd_js)
        ssd_win = din("ssd_w_in", [NJ, D, 10368])
        ssd_wout = din("ssd_w_out", [NJ, 4096, D])
        ssd_cw = din("ssd_cw", [128, NJ, 48, 3])
        ssd_cb = din("ssd_cb", [128, NJ, 48])
        ssd_dtb = din("ssd_dtb", [NJ, 128])
        ssd_alog = din("ssd_alog", [NJ, 128])
        ssd_dskip = din("ssd_dskip", [NJ, 64])
        ssd_nw = din("ssd_nw", [NJ, 4096])
        ssd_h0 = din("ssd_h0", [NJ, 2, 4096, 128])
        ssd_consts = din("ssd_consts", [128, 4, 128])
        contd = din("cont", [128, 1])
        st_out = nc.dram_tensor("st_out", [NJ, 8, 2, 4096, 128], F32, kind="ExternalOutput").ap()
        XS = nc.dram_tensor("xs_scr", [8, 16, 128, 512], BF16).ap()
        ZS = nc.dram_tensor("zs_scr", [8, 16, 128, 512], BF16).ap()
        BS = nc.dram_tensor("bs_scr", [8, 16, 128, 128], BF16).ap()
        BTS = nc.dram_tensor("bts_scr", [8, 128, NTOK], BF16).ap()
        CTS = nc.dram_tensor("cts_scr", [8, 128, NTOK], BF16).ap()
        WT.update(ssd_w_in=ssd_win, ssd_w_out=ssd_wout)
    has_ml = 2 in layers
    if has_ml:
        ml_win = din("mlstm_w_in", [D, 6176])
        ml_wout = din("mlstm_w_out", [D, D])
        ml_bg = din("ml_bg", [32])
        ml_hn = din("ml_hn", [8, 256])
        ml_C0 = din("ml_C0", [2, 8, 128, 256])
        ml_n0 = din("ml_n0", [2, 8, 128])
        ml_m0 = din("ml_m0", [128, 16])
        ml_consts = din("ml_consts", [128, 4, 128])
        if not has_ssd:
            ssd_consts = din("ssd_consts", [128, 4, 128])
            contd = din("cont", [128, 1])
        mlC_out = nc.dram_tensor("mlC_out", [8, 2, 8, 128, 256], F32, kind="ExternalOutput").ap()
        mln_out = nc.dram_tensor("mln_out", [8, 2, 8, 128], F32, kind="ExternalOutput").ap()
        mlm_out = nc.dram_tensor("mlm_out", [1, 128], F32, kind="ExternalOutput").ap()
        QTS = nc.dram_tensor("qts_scr", [8, 128, NTOK], BF16).ap()
        KTS = nc.dram_tensor("kts_scr", [8, 128, NTOK], BF16).ap()
        KMS = nc.dram_tensor("kms_scr", [8, 16, 128, 128], BF16).ap()
        VS = nc.dram_tensor("vs_scr", [8, 16, 128, 256], BF16).ap()
        OGS = nc.dram_tensor("ogs_scr", [8, 16, 128, 256], BF16).ap()
        WT.update(mlstm_w_in=ml_win, mlstm_w_out=ml_wout)
    has_attn = 1 in layers
    YT = nc.dram_tensor("ytscr", [32, 128, NTOK], BF16).ap()
    UTS = nc.dram_tensor("utscr", [KC, 128, NTOK], BF16).ap()
    if has_attn:
        attn_wqkv = din("attn_w_qkv", [D, 3072])
        attn_wout = din("attn_w_out", [D, D])
        attn_qn = din("attn_qn", [128, 1])
        attn_kn = din("attn_kn", [128, 1])
        attn_knrow = din("attn_knrow", [128])
        ropecos = din("ropecos", [128, NTOK])
        ropesin = din("ropesin", [128, NTOK])
        maskb_d = din("maskb", [128, 160])
        perm_d = din("perm", [128, 128])
        cachek = din("cachek", [512, 512])
        cachev = din("cachev", [512, 512])
        kc_out = nc.dram_tensor("kc_out", [NTOK, 512], F32, kind="ExternalOutput").ap()
        vc_out = nc.dram_tensor("vc_out", [NTOK, 512], F32, kind="ExternalOutput").ap()
        WT.update(attn_w_qkv=attn_wqkv, attn_w_out=attn_wout)
    dbg = nc.dram_tensor("dbg", [128, 8192], F32, kind="ExternalOutput").ap() if cfg.get("debug") else None
    X = nc.dram_tensor("xout", [NTOK, D], F32, kind="ExternalOutput").ap()
    GS = nc.dram_tensor("gscr", [DEPTH, 2, 128, D], F32).ap()

    NTP = cfg.get("ntp", 4)
    W = WStream(P, nc, WT, plan)
    ps = [nc.alloc_psum_tensor("psb%d" % i, [128, 512], F32) for i in range(8)]

    ident_f = nc.alloc_sbuf_tensor("ident_f", [128, 128], F32)
    ident = nc.alloc_sbuf_tensor("ident", [128, 128], BF16)
    modf = nc.alloc_sbuf_tensor("modf", [128, DEPTH, 96], F32)
    adabf = nc.alloc_sbuf_tensor("adabf", [128, DEPTH, 96], F32)
    nwf = nc.alloc_sbuf_tensor("nwf", [128, DEPTH, 2, KC], F32)
    amod = nc.alloc_sbuf_tensor("amod", [128, DEPTH, 2, KC], F32)
    cvt = nc.alloc_sbuf_tensor("cvt", [128, KC], F32)
    svb = nc.alloc_sbuf_tensor("svb", [128, KC], BF16)
    ss = nc.alloc_sbuf_tensor("ss", [128, 8], F32)
    rstd = nc.alloc_sbuf_tensor("rstd", [128, 8], F32)
    epsb = nc.alloc_sbuf_tensor("epsb", [128, 1], F32)
    P.op("dve", [], ["epsb"], lambda E: E.memset(epsb[:], EPS))

    P.dma("sp", ident_f[:], consts[:, :], [], ["ident_f"])
    P.dma("sp", adabf[:], adab_f[:, :, :], [], ["adabf"])
    P.dma("sp", nwf[:], nw_f[:, :, :, :], [], ["nwf"])
    P.dma("sp", cvt[:], cv[:, :], [], ["cvt"])
    P.op("dve", ["ident_f"], ["ident"], lambda E: E.tensor_copy(out=ident[:], in_=ident_f[:]))
    P.op("act", ["cvt"], ["svb"], lambda E: E.activation(out=svb[:], in_=cvt[:], func=AF.Silu))
    with sbt("svrep", [128, KC, 128], BF16) as svrep, sbt("gtile0", [128, 512], F32) as gt0, sbt("gtile1", [128, 512], F32) as gt1, \
            sbt("abrow0", [128, 512], F32) as ab0, sbt("abrow1", [128, 512], F32) as ab1:
        P.op("dve", ["svb"], ["svrep"], lambda E: E.tensor_copy(
            out=svrep[:], in_=svb[:].unsqueeze(2).to_broadcast([128, KC, 128])))
        gts = [gt0, gt1]
        abs_ = [ab0, ab1]
        nrow = 0
        for l in layers:
            li = layers.index(l)
            for cb in range(24):
                blk, wk = W.take("ada_w", (li, slice(None), slice(cb * 512, (cb + 1) * 512)), KC, 512)
                pb = ps[cb % 2]
                pk = ("ps", cb % 2)
                if (cb % 12) < 8:
                    def mmf(E, blk=blk, pb=pb):
                        ins = None
                        for ft in range(4):
                            for k in range(KC):
                                ins = E.matmul(pb[:, ft:ft + 1], lhsT=blk[:, k, ft * 128:(ft + 1) * 128],
                                               rhs=svb[:, k:k + 1], start=(k == 0), stop=(k == KC - 1))
                        return ins
                    P.op("pe", [wk, "svb"], [pk], mmf)
                    P.op("dve", [pk, "adabf"], ["modf"], lambda E, pb=pb, l=l, cb=cb: E.tensor_tensor(
                        out=modf[:, l, cb * 4:(cb + 1) * 4], in0=pb[:, 0:4], in1=adabf[:, l, cb * 4:(cb + 1) * 4],
                        op=ALU.add))
                else:
                    which = 0 if cb < 12 else 1
                    c0 = (cb % 12 - 8) * 512
                    j = nrow % 2
                    nrow += 1
                    P.dma("sp", abs_[j][:], ada_b[li, cb * 512:(cb + 1) * 512].partition_broadcast(128),
                          [], [("abrow", j)])

                    def mmr(E, blk=blk, pb=pb):
                        ins = None
                        for k in range(KC):
                            ins = E.matmul(pb[:, :], lhsT=svrep[:, k, :], rhs=blk[:, k, :],
                                           start=(k == 0), stop=(k == KC - 1))
                        return ins
                    P.op("pe", [wk, "svrep"], [pk], mmr)
                    P.op("dve", [pk, ("abrow", j)], [("gtile", j)], lambda E, pb=pb, j=j: E.tensor_tensor(
                        out=gts[j][:], in0=pb[:, :], in1=abs_[j][:], op=ALU.add))
                    P.dma("sp", GS[l, which, :, c0:c0 + 512], gts[j][:], [("gtile", j)], [("gs", l, which)])
            for wh in range(2):
                P.op("dve", ["modf", "nwf"], [("amod", l, wh)], lambda E, l=l, wh=wh: E.scalar_tensor_tensor(
                    out=amod[:, l, wh, :], in0=modf[:, l, wh * 48 + 16:wh * 48 + 32], scalar=1.0,
                    in1=nwf[:, l, wh, :], op0=ALU.add, op1=ALU.mult))
        P.barrier()
    if dbg is not None:
        P.dma("sp", dbg[:, 0:DEPTH * 96], modf[:].rearrange("p l j -> p (l j)"), ["modf"], ["dbg0"])
        P.dma("sp", dbg[:, 512:512 + DEPTH * 2 * KC], amod[:].rearrange("p l w c -> p (l w c)"), [], ["dbg1"])

    def norm_mod(l, wh, src, i, xt, xk, j, xn, UT, utk, col0):
        P.dma("sp", xt[:], src[i * 128:(i + 1) * 128, :], [("X", i)], [xk])
        P.op("act", [xk], ["xn", ("ss", j)], lambda E: E.activation(
            out=xn[:], in_=xt[:], func=AF.Square, accum_out=ss[:, j:j + 1]))
        P.op("act", [("ss", j)], [("rstd", j)], lambda E: E.activation(
            out=rstd[:, j:j + 1], in_=ss[:, j:j + 1], func=AF.Sqrt, scale=1.0 / D, bias=epsb[:, 0:1]))
        P.op("dve", [("rstd", j)], [("rstd", j)], lambda E: E.reciprocal(out=rstd[:, j:j + 1], in_=rstd[:, j:j + 1]))
        P.op("act", [xk, ("rstd", j)], ["xn"], lambda E: E.mul(out=xn[:], in_=xt[:], mul=rstd[:, j:j + 1]))
        for c4 in range(4):
            pb = ps[4 + (c4 % 2)]
            pk = ("ps", 4 + (c4 % 2))
            pbb = pb.bitcast(BF16)

            def tr(E, c4=c4, pbb=pbb):
                ins = None
                for cc in range(4):
                    c = c4 * 4 + cc
                    ins = E.transpose(out=pbb[:, cc * 128:(cc + 1) * 128], in_=xn[:, c * 128:(c + 1) * 128],
                                      identity=ident[:])
                return ins
            P.op("pe", ["xn", "ident"], [pk], tr)
            for cc in range(4):
                c = c4 * 4 + cc
                last = (cc == 3)
                P.op("dve", [pk, ("amod", l, wh), "modf"], [utk] if not last else [utk],
                     lambda E, c=c, cc=cc, pbb=pbb: E.tensor_scalar(
                         out=UT[:, c, col0:col0 + 128], in0=pbb[:, cc * 128:(cc + 1) * 128],
                         scalar1=amod[:, l, wh, c:c + 1], scalar2=modf[:, l, wh * 48 + c:wh * 48 + c + 1],
                         op0=ALU.mult, op1=ALU.add))

    def ffn_phase(l, src):
        li = layers.index(l)
        with sbt("xt0", [128, D], F32) as xt0, sbt("xt1", [128, D], F32) as xt1, \
                sbt("xt2", [128, D], F32) as xt2, sbt("xt3", [128, D], F32) as xt3, \
                sbt("xn", [128, D], BF16) as xn, \
                sbt("UT", [128, KC, 512], BF16) as UT, sbt("HT", [128, HC, 512], BF16) as HT, \
                sbt("g2", [128, D], F32) as g2, sbt("sg0", [128, 512], BF16) as sg0, \
                sbt("sg1", [128, 512], BF16) as sg1, sbt("tmp0", [128, 512], F32) as tmp0, \
                sbt("tmp1", [128, 512], F32) as tmp1:
            xts = [xt0, xt1, xt2, xt3]
            sgs = [sg0, sg1]
            tmps = [tmp0, tmp1]
            P.dma("sp", g2[:], GS[l, 1, :, :], [("gs", l, 1)], ["g2"])
            nsg = 0
            ntmp = 0
            for tp in range(NTP):
                for tt in range(4):
                    i = tp * 4 + tt
                    norm_mod(l, 1, src, i, xts[tt], ("xt", tt), tt, xn, UT, ("UT", tt), tt * 128)
                utks = [("UT", tt) for tt in range(4)]
                for hb in range(11):
                    hsl = (li, slice(None), slice(hb * 512, (hb + 1) * 512))
                    wg, wgk = W.take("ffn_w_gate", hsl, KC, 512)
                    wu, wuk = W.take("ffn_w_up", hsl, KC, 512)
                    for ft in range(4):
                        hi = hb * 4 + ft
                        pg = ps[hi % 2]
                        pgk = ("ps", hi % 2)
                        pu = ps[2 + hi % 2]
                        puk = ("ps", 2 + hi % 2)

                        def mmg(E, w=wg, pb=pg, ft=ft):
                            ins = None
                            for k in range(KC):
                                ins = E.matmul(pb[:, :], lhsT=w[:, k, ft * 128:(ft + 1) * 128], rhs=UT[:, k, :],
                                               start=(k == 0), stop=(k == KC - 1))
                            return ins
                        P.op("pe", [wgk] + utks, [pgk], mmg)
                        P.op("pe", [wuk] + utks, [puk], lambda E, w=wu, pb=pu, ft=ft: mmg(E, w, pb, ft))
                        sj = nsg % 2
                        nsg += 1
                        P.op("act", [pgk], [("sg", sj)], lambda E, pb=pg, sj=sj: E.activation(
                            out=sgs[sj][:], in_=pb[:, :], func=AF.Silu))
                        P.op("dve", [("sg", sj), puk], [("HT", hi)], lambda E, pb=pu, sj=sj, hi=hi: E.tensor_tensor(
                            out=HT[:, hi, :], in0=sgs[sj][:], in1=pb[:, :], op=ALU.mult))
                        if dbg is not None and tp == 0 and hi == 0:
                            with sbt("dbgu", [128, 1536], F32) as dbgu:
                                P.op("dve", [pgk], ["dbgu"], lambda E: E.tensor_copy(out=dbgu[:, 0:512], in_=pg[:, :]))
                                P.op("dve", [("sg", sj)], ["dbgu"], lambda E: E.tensor_copy(out=dbgu[:, 512:1024], in_=sgs[sj][:]))
                                P.op("dve", [puk], ["dbgu"], lambda E: E.tensor_copy(out=dbgu[:, 1024:1536], in_=pu[:, :]))
                                P.dma("sp", dbg[:, 6144:7680], dbgu[:], ["dbgu"], ["dbg4"])
                                P.barrier()
                htks = [("HT", hi) for hi in range(HC)]
                if dbg is not None and tp == 0:
                    with sbt("dbgt", [128, 2048], F32) as dbgt:
                        P.op("dve", utks, ["dbgt"], lambda E: E.tensor_copy(out=dbgt[:, 0:512], in_=UT[:, 0, :]))
                        P.op("dve", utks, ["dbgt"], lambda E: E.tensor_copy(out=dbgt[:, 512:1024], in_=UT[:, 5, :]))
                        P.op("dve", htks, ["dbgt"], lambda E: E.tensor_copy(out=dbgt[:, 1024:1536], in_=HT[:, 0, :]))
                        P.op("dve", htks, ["dbgt"], lambda E: E.tensor_copy(out=dbgt[:, 1536:2048], in_=HT[:, 17, :]))
                        P.dma("sp", dbg[:, 1024:3072], dbgt[:], ["dbgt"], ["dbg2"])
                        P.dma("sp", dbg[:, 4096:6144], g2[:], ["g2"], ["dbg3"])
                        P.barrier()
                for cb in range(4):
                    base = 0 if cb % 2 == 0 else 4
                    for k0, kn in KSPLIT:
                        wd, wdk = W.take("ffn_w_down", (li, slice(k0 * 128, (k0 + kn) * 128),
                                                        slice(cb * 512, (cb + 1) * 512)), kn, 512)
                        for tt in range(4):
                            pb = ps[base + tt]
                            pk = ("ps", base + tt)

                            def mmd(E, w=wd, pb=pb, tt=tt, k0=k0, kn=kn):
                                ins = None
                                for k in range(kn):
                                    ins = E.matmul(pb[:, :], lhsT=HT[:, k0 + k, tt * 128:(tt + 1) * 128],
                                                   rhs=w[:, k, :], start=(k0 + k == 0),
                                                   stop=(k0 + k == HC - 1))
                                return ins
                            P.op("pe", [wdk] + htks, [pk], mmd)
                    for tt in range(4):
                        pb = ps[base + tt]
                        pk = ("ps", base + tt)
                        tj = ntmp % 2
                        ntmp += 1
                        P.op("dve", [pk, "g2"], [("tmp", tj)], lambda E, pb=pb, tj=tj, cb=cb: E.tensor_tensor(
                            out=tmps[tj][:], in0=pb[:, :], in1=g2[:, cb * 512:(cb + 1) * 512], op=ALU.mult))
                        P.op("pool", [("tmp", tj), ("xt", tt)], [("xt", tt)],
                             lambda E, tj=tj, tt=tt, cb=cb: E.tensor_tensor(
                                 out=xts[tt][:, cb * 512:(cb + 1) * 512], in0=xts[tt][:, cb * 512:(cb + 1) * 512],
                                 in1=tmps[tj][:], op=ALU.add))
                for tt in range(4):
                    i = tp * 4 + tt
                    P.dma("sp", X[i * 128:(i + 1) * 128, :], xts[tt][:], [("xt", tt)], [("X", i)])
            P.barrier()


    def outproj_phase(l, src, Kc, wname, widx=None):
        with sbt("xt0", [128, D], F32) as xt0, sbt("xt1", [128, D], F32) as xt1, \
                sbt("xt2", [128, D], F32) as xt2, sbt("xt3", [128, D], F32) as xt3, \
                sbt("AT", [128, Kc, 512], BF16) as AT, sbt("g1", [128, D], F32) as g1, \
                sbt("tmp0", [128, 512], F32) as tmp0, sbt("tmp1", [128, 512], F32) as tmp1:
            xts = [xt0, xt1, xt2, xt3]
            tmps = [tmp0, tmp1]
            P.dma("sp", g1[:], GS[l, 0, :, :], [("gs", l, 0)], ["g1"])
            ntmp = 0
            for tp in range(4):
                for tt in range(4):
                    i = tp * 4 + tt
                    P.dma("sp", xts[tt][:], src[i * 128:(i + 1) * 128, :], [("X", i)], [("xt", tt)])
                P.dma("sp", AT[:], YT[0:Kc, :, tp * 512:(tp + 1) * 512].rearrange("c p n -> p c n"), [], ["AT"])
                for cb in range(4):
                    base = 0 if cb % 2 == 0 else 4
                    for kb in range(Kc // 16):
                        wsl = (slice(kb * 2048, (kb + 1) * 2048), slice(cb * 512, (cb + 1) * 512))
                        if widx is not None:
                            wsl = (ssd_js.index(widx),) + wsl
                        w, wk = W.take(wname, wsl, 16, 512)
                        for tt in range(4):
                            pb = ps[base + tt]

                            def mmo(E, w=w, pb=pb, tt=tt, kb=kb):
                                ins = None
                                for k in range(16):
                                    ins = E.matmul(pb[:, :], lhsT=AT[:, kb * 16 + k, tt * 128:(tt + 1) * 128],
                                                   rhs=w[:, k, :], start=(kb == 0 and k == 0),
                                                   stop=(kb == Kc // 16 - 1 and k == 15))
                                return ins
                            P.op("pe", [wk, "AT"], [("ps", base + tt)], mmo)
                    for tt in range(4):
                        pb = ps[base + tt]
                        pk = ("ps", base + tt)
                        tj = ntmp % 2
                        ntmp += 1
                        P.op("dve", [pk, "g1"], [("tmp", tj)], lambda E, pb=pb, tj=tj, cb=cb: E.tensor_tensor(
                            out=tmps[tj][:], in0=pb[:, :], in1=g1[:, cb * 512:(cb + 1) * 512], op=ALU.mult))
                        P.op("pool", [("tmp", tj), ("xt", tt)], [("xt", tt)],
                             lambda E, tj=tj, tt=tt, cb=cb: E.tensor_tensor(
                                 out=xts[tt][:, cb * 512:(cb + 1) * 512], in0=xts[tt][:, cb * 512:(cb + 1) * 512],
                                 in1=tmps[tj][:], op=ALU.add))
                for tt in range(4):
                    i = tp * 4 + tt
                    P.dma("sp", X[i * 128:(i + 1) * 128, :], xts[tt][:], [("xt", tt)], [("X", i)])
            P.barrier()

    def attn_phase(l, src):
        SC = 128.0 ** -0.5
        with ExitStack() as st:
            UT = st.enter_context(sbt("UTa", [128, KC, 512], BF16))
            KT = st.enter_context(sbt("KT", [128, 4, 2560], BF16))
            VA = st.enter_context(sbt("VA", [128, 20, 512], BF16))
            cosT = st.enter_context(sbt("cosT", [128, 512], F32))
            sinT = st.enter_context(sbt("sinT", [128, 512], F32))
            maskb = st.enter_context(sbt("maskb_s", [128, 160], F32))
            permf = st.enter_context(sbt("permf", [128, 128], F32))
            permb = st.enter_context(sbt("permb", [128, 128], BF16))
            onesb = st.enter_context(sbt("onesb", [128, 128], BF16))
            qnw = st.enter_context(sbt("qnw", [128, 2], F32))
            knrow = st.enter_context(sbt("knrow", [128, 128], F32))
            sqb = st.enter_context(sbt("sqb", [128, 512], BF16))
            rt = st.enter_context(sbt("rt", [128, 512], F32))
            qn = st.enter_context(sbt("qn", [128, 512], BF16))
            t1 = st.enter_context(sbt("t1", [128, 512], F32))
            t2 = st.enter_context(sbt("t2", [128, 512], F32))
            QT0 = st.enter_context(sbt("QT0", [128, 512], BF16))
            QT1 = st.enter_context(sbt("QT1", [128, 512], BF16))
            PT0 = st.enter_context(sbt("PT0", [128, 256], BF16))
            PT1 = st.enter_context(sbt("PT1", [128, 256], BF16))
            PT2 = st.enter_context(sbt("PT2", [128, 256], BF16))
            rs = st.enter_context(sbt("rs", [128, 256], F32))
            AO0 = st.enter_context(sbt("AO0", [128, 256], BF16))
            AO1 = st.enter_context(sbt("AO1", [128, 256], BF16))
            kf = st.enter_context(sbt("kf", [128, 512], F32))
            vf = st.enter_context(sbt("vf", [128, 512], F32))
            ssk = st.enter_context(sbt("ssk", [128, 4], F32))
            ckf = st.enter_context(sbt("ckf", [128, 512], F32))
            ckb = st.enter_context(sbt("ckb", [128, 512], BF16))
            QTs = [QT0, QT1]
            PTs = [PT0, PT1, PT2]
            AOs = [AO0, AO1]
            P.dma("sp", maskb[:], maskb_d[:, :], [], ["maskb"])
            P.dma("sp", permf[:], perm_d[:, :], [], ["permf"])
            P.dma("sp", qnw[:, 0:1], attn_qn[:, :], [], ["qnw"])
            P.dma("sp", qnw[:, 1:2], attn_kn[:, :], ["qnw"], ["qnw"])
            P.dma("sp", knrow[:], attn_knrow.partition_broadcast(128), [], ["knrow"])
            P.op("dve", ["permf"], ["permb"], lambda E: E.tensor_copy(out=permb[:], in_=permf[:]))
            P.op("dve", [], ["onesb"], lambda E: E.memset(onesb[:], 1.0))
            P.op("dve", ["qnw"], ["qnw"], lambda E: E.tensor_scalar_mul(out=qnw[:, 0:1], in0=qnw[:, 0:1], scalar1=SC))
            def head_fm(w, wk, c0, tt, widx, dst, dstk):
                def mmh(E):
                    ins = None
                    for k in range(KC):
                        ins = E.matmul(ps[0][:, :], lhsT=w[:, k, c0:c0 + 128], rhs=UT[:, k, tt * 512:(tt + 1) * 512],
                                       start=(k == 0), stop=(k == KC - 1))
                    return ins
                P.op("pe", [wk] + utk, [("ps", 0)], mmh)
                P.op("act", [("ps", 0)], ["sqb"], lambda E: E.activation(out=sqb[:], in_=ps[0][:, :], func=AF.Square))
                P.op("pe", ["sqb", "onesb"], [("ps", 1)], lambda E: E.matmul(
                    ps[1][:, :], lhsT=onesb[:], rhs=sqb[:], start=True, stop=True))
                P.op("act", [("ps", 1)], ["rt"], lambda E: E.activation(
                    out=rt[:], in_=ps[1][:, :], func=AF.Sqrt, scale=1.0 / 128, bias=epsb[:, 0:1]))
                P.op("dve", ["rt"], ["rt"], lambda E: E.reciprocal(out=rt[:], in_=rt[:]))
                P.op("dve", [("ps", 0), "rt", "qnw"], ["qn"], lambda E: E.scalar_tensor_tensor(
                    out=qn[:], in0=ps[0][:, :], scalar=qnw[:, widx:widx + 1], in1=rt[:], op0=ALU.mult, op1=ALU.mult))
                P.op("pe", ["qn", "permb"], [("ps", 2)], lambda E: E.matmul(
                    ps[2][:, :], lhsT=permb[:], rhs=qn[:], start=True, stop=True))
                P.op("dve", [("ps", 2), "sinT"], ["t1"], lambda E: E.tensor_tensor(
                    out=t1[:], in0=ps[2][:, :], in1=sinT[:], op=ALU.mult))
                P.op("pool", ["qn", "cosT"], ["t2"], lambda E: E.tensor_tensor(
                    out=t2[:], in0=qn[:], in1=cosT[:], op=ALU.mult))
                P.op("dve", ["t1", "t2"], [dstk], lambda E: E.tensor_tensor(out=dst, in0=t1[:], in1=t2[:], op=ALU.add))

            def load_tables(tt):
                P.dma("sp", cosT[:], ropecos[:, tt * 512:(tt + 1) * 512], [], ["cosT"])
                P.dma("sp", sinT[:], ropesin[:, tt * 512:(tt + 1) * 512], [], ["sinT"])

            utk = [("UT", j) for j in range(4)]
            wkk, wkkk = W.take("attn_w_qkv", (slice(None), slice(2048, 2560)), KC, 512)
            wv, wvk = W.take("attn_w_qkv", (slice(None), slice(2560, 3072)), KC, 512)
            with sbt("xta", [128, D], F32) as xta, sbt("xtb", [128, D], F32) as xtb, \
                    sbt("xn", [128, D], BF16) as xn:
                xtl = [xta, xtb]
                for tt in range(4):
                    load_tables(tt)
                    for i4 in range(4):
                        i = tt * 4 + i4
                        norm_mod(l, 0, src, i, xtl[i % 2], ("xt", i % 2), i % 2, xn, UT, ("UT", i4), i4 * 128)
                    P.dma("sp", UTS[:, :, tt * 512:(tt + 1) * 512].rearrange("c p n -> p c n"), UT[:], utk, [("UTS", tt)])
                    for i4 in range(4):
                        i = tt * 4 + i4
                        pkb, pvb = ps[(i % 2) * 2], ps[(i % 2) * 2 + 1]
                        pkk, pvk = ("ps", (i % 2) * 2), ("ps", (i % 2) * 2 + 1)

                        def mmt(E, w, pb, i4=i4):
                            ins = None
                            for k in range(KC):
                                ins = E.matmul(pb[:, :], lhsT=UT[:, k, i4 * 128:(i4 + 1) * 128], rhs=w[:, k, :],
                                               start=(k == 0), stop=(k == KC - 1))
                            return ins
                        P.op("pe", [wkkk] + utk, [pkk], lambda E, pb=pkb, mmt=mmt: mmt(E, wkk, pb))
                        P.op("pe", [wvk] + utk, [pvk], lambda E, pb=pvb, mmt=mmt: mmt(E, wv, pb))
                        for hh in range(4):
                            P.op("act", [pkk], ["sqb", ("ssk", hh)], lambda E, pb=pkb, hh=hh: E.activation(
                                out=sqb[:, 0:128], in_=pb[:, hh * 128:(hh + 1) * 128], func=AF.Square,
                                accum_out=ssk[:, hh:hh + 1]))
                        sskk = [("ssk", hh) for hh in range(4)]
                        P.op("act", sskk, sskk, lambda E: E.activation(
                            out=ssk[:, :], in_=ssk[:, :], func=AF.Sqrt, scale=1.0 / 128, bias=epsb[:, 0:1]))
                        P.op("dve", sskk, sskk, lambda E: E.reciprocal(out=ssk[:, :], in_=ssk[:, :]))
                        for hh in range(4):
                            P.op("dve", [pkk, ("ssk", hh), "knrow"], ["kf"], lambda E, pb=pkb, hh=hh: E.scalar_tensor_tensor(
                                out=kf[:, hh * 128:(hh + 1) * 128], in0=pb[:, hh * 128:(hh + 1) * 128],
                                scalar=ssk[:, hh:hh + 1], in1=knrow[:], op0=ALU.mult, op1=ALU.mult))
                        P.dma("sp", kc_out[i * 128:(i + 1) * 128, :], kf[:], ["kf"], [("kco", i)])
                        P.op("act", [pvk], ["vf"], lambda E, pb=pvb: E.copy(out=vf[:], in_=pb[:, :]))
                        P.op("dve", [pvk], [("VA", 4 + i)], lambda E, pb=pvb, i=i: E.tensor_copy(out=VA[:, 4 + i, :], in_=pb[:, :]))
                        P.dma("sp", vc_out[i * 128:(i + 1) * 128, :], vf[:], ["vf"], [("vco", i)])
                    for hh in range(4):
                        head_fm(wkk, wkkk, hh * 128, 0, 1, KT[:, hh, 512 + tt * 512:512 + (tt + 1) * 512], ("KT", hh))
                P.barrier()
            for j in range(4):
                P.dma("pool", VA[:, j, :], cachev[j * 128:(j + 1) * 128, :], [], [("VA", j)])
                P.dma("sp", ckf[:], cachek[j * 128:(j + 1) * 128, :], [], ["ckf"])
                P.op("dve", ["ckf"], ["ckb"], lambda E: E.tensor_copy(out=ckb[:], in_=ckf[:]))
                pbb = ps[4].bitcast(BF16)

                def trc(E, pbb=pbb):
                    ins = None
                    for hh in range(4):
                        ins = E.transpose(out=pbb[:, hh * 128:(hh + 1) * 128], in_=ckb[:, hh * 128:(hh + 1) * 128],
                                          identity=ident[:])
                    return ins
                P.op("pe", ["ckb", "ident"], [("ps", 4)], trc)
                for hh in range(4):
                    P.op("dve", [("ps", 4)], [("KT", hh)], lambda E, hh=hh, j=j, pbb=pbb: E.tensor_copy(
                        out=KT[:, hh, j * 128:(j + 1) * 128], in_=pbb[:, hh * 128:(hh + 1) * 128]))

            nq = 0
            npt = 0
            nao = 0
            vak = [("VA", j) for j in range(20)]
            for tt in range(4):
                load_tables(tt)
                P.dma("sp", UT[:], UTS[:, :, tt * 512:(tt + 1) * 512].rearrange("c p n -> p c n"), [("UTS", tt)], utk)
                for qb in range(4):
                    wq, wqk = W.take("attn_w_qkv", (slice(None), slice(qb * 512, (qb + 1) * 512)), KC, 512)
                    for hq in range(4):
                        h = qb * 4 + hq
                        kv = h // 4
                        qj = nq % 2
                        nq += 1
                        head_fm(wq, wqk, hq * 128, 0, 0, QTs[qj][:], ("QT", qj))
                        for sub in range(2):
                            qt8 = tt * 2 + sub
                            for kt in range(20):
                                sb = ps[3 + kt % 2]
                                sk = ("ps", 3 + kt % 2)
                                P.op("pe", [("KT", kv), ("QT", qj)], [sk], lambda E, sb=sb, kt=kt, kv=kv, qj=qj, sub=sub: E.matmul(
                                    sb[:, 0:256], lhsT=KT[:, kv, kt * 128:(kt + 1) * 128],
                                    rhs=QTs[qj][:, sub * 256:(sub + 1) * 256], start=True, stop=True))
                                pj = npt % 3
                                npt += 1
                                P.op("act", [sk, "maskb"], [("PT", pj)], lambda E, sb=sb, pj=pj, qt8=qt8, kt=kt: E.activation(
                                    out=PTs[pj][:], in_=sb[:, 0:256], func=AF.Exp,
                                    bias=maskb[:, qt8 * 20 + kt:qt8 * 20 + kt + 1]))
                                P.op("pe", [("PT", pj)] + vak, [("ps", 5), ("ps", 6)],
                                     lambda E, pj=pj, kt=kt, kv=kv: (
                                         E.matmul(ps[5][:, 0:256], lhsT=VA[:, kt, kv * 128:(kv + 1) * 128], rhs=PTs[pj][:],
                                                  start=(kt == 0), stop=(kt == 19)),
                                         E.matmul(ps[6][:, 0:256], lhsT=onesb[:], rhs=PTs[pj][:],
                                                  start=(kt == 0), stop=(kt == 19)))[1])
                            P.op("dve", [("ps", 6)], ["rs"], lambda E: E.reciprocal(out=rs[:], in_=ps[6][:, 0:256]))
                            aj = nao % 2
                            nao += 1
                            P.op("dve", [("ps", 5), "rs"], [("AO", aj)], lambda E, aj=aj: E.tensor_tensor(
                                out=AOs[aj][:], in0=ps[5][:, 0:256], in1=rs[:], op=ALU.mult))
                            P.dma("sp", YT[h, :, qt8 * 256:(qt8 + 1) * 256], AOs[aj][:], [("AO", aj)], [("YT", h, qt8)])
            P.barrier()


    def v3(ap):
        return ap.rearrange("p (r q) -> p r q", q=64)

    def ssd_phase(l, src):
        jj = ssd_js.index(l // 3)
        with ExitStack() as so:
            def sb(name, shape, dt, st=so):
                return st.enter_context(sbt(name, shape, dt))
            DTt = sb("DTt", [128, 16, 128], F32)
            At = sb("At", [128, 16, 128], F32)
            EX = sb("EX", [128, 16, 4, 64], F32)
            ETOT = sb("ETOT", [128, 16, 128], F32)
            cst = sb("ssdc", [128, 4, 128], F32)
            onesf = sb("onesf", [128, 128], F32)
            cont = sb("contt", [128, 1], F32)
            ncont = sb("ncont", [128, 1], F32)
            maskF, maskB, SLf, SLb = cst[:, 0, :], cst[:, 1, :], cst[:, 2, :], cst[:, 3, :]
            P.dma("sp", cst[:], ssd_consts[:, :, :], [], ["ssdc"])
            P.dma("sp", cont[:], contd[:, :], [], ["cont"])
            P.op("dve", [], ["onesf"], lambda E: E.memset(onesf[:], 1.0))
            P.op("dve", ["cont"], ["ncont"], lambda E: E.tensor_scalar_add(out=ncont[:], in0=cont[:], scalar1=-1.0))

            with ExitStack() as s1:
                UT = sb("UTs", [128, KC, NTOK], BF16, s1)
                cw = sb("cw", [128, 48, 3], F32, s1)
                ncw = sb("ncw", [128, 48, 3], F32, s1)
                cbb = sb("cbb", [128, 48], F32, s1)
                dtb = sb("dtb", [128, 128], F32, s1)
                arow = sb("arow", [128, 128], F32, s1)
                P.dma("sp", cw[:], ssd_cw[:, jj, :, :], [], ["cw"])
                P.dma("sp", cbb[:], ssd_cb[:, jj, :], [], ["cbb"])
                P.dma("sp", dtb[:], ssd_dtb[jj, :].partition_broadcast(128), [], ["dtb"])
                P.dma("sp", arow[:], ssd_alog[jj, :].partition_broadcast(128), [], ["arow"])
                P.op("act", ["arow"], ["arow"], lambda E: E.activation(out=arow[:], in_=arow[:], func=AF.Exp))
                P.op("dve", ["arow"], ["arow"], lambda E: E.tensor_scalar_mul(out=arow[:], in0=arow[:], scalar1=-1.0))
                P.op("dve", ["cw", "ncont"], ["ncw"], lambda E: E.tensor_scalar_mul(
                    out=ncw[:].rearrange("p a b -> p (a b)"), in0=cw[:].rearrange("p a b -> p (a b)"), scalar1=ncont[:, 0:1]))
                with ExitStack() as s0:
                    xta = sb("xta", [128, D], F32, s0)
                    xtb = sb("xtb", [128, D], F32, s0)
                    xn = sb("xn", [128, D], BF16, s0)
                    xtl = [xta, xtb]
                    for i in range(16):
                        norm_mod(l, 0, src, i, xtl[i % 2], ("xt", i % 2), i % 2, xn, UT, ("UT", i // 4), i * 128)
                    P.barrier()
                utk = [("UT", q) for q in range(4)]
                with ExitStack() as s2:
                    t_a = sb("t_a", [128, 128], F32, s2)
                    t_b = sb("t_b", [128, 128], F32, s2)
                    t_c = sb("t_c", [128, 128], F32, s2)
                    wdt, wdtk = W.take("ssd_w_in", (jj, slice(None), slice(10240, 10368)), KC, 128)
                    for i in range(16):
                        pb = ps[i % 2]
                        pk = ("ps", i % 2)

                        def mmdt(E, pb=pb, i=i):
                            ins = None
                            for k in range(KC):
                                ins = E.matmul(pb[:, 0:128], lhsT=UT[:, k, i * 128:(i + 1) * 128], rhs=wdt[:, k, :],
                                               start=(k == 0), stop=(k == KC - 1))
                            return ins
                        P.op("pe", [wdtk] + utk, [pk], mmdt)
                        P.op("dve", [pk, "dtb"], ["t_a"], lambda E, pb=pb: E.tensor_tensor(
                            out=t_a[:], in0=pb[:, 0:128], in1=dtb[:], op=ALU.add))
                        P.op("act", ["t_a"], ["t_b"], lambda E: E.activation(out=t_b[:], in_=t_a[:], func=AF.Abs))
                        P.op("act", ["t_b"], ["t_b"], lambda E: E.activation(out=t_b[:], in_=t_b[:], func=AF.Exp, scale=-1.0))
                        P.op("act", ["t_b"], ["t_b"], lambda E: E.activation(out=t_b[:], in_=t_b[:], func=AF.Ln, bias=1.0))
                        P.op("dve", ["t_a"], ["t_c"], lambda E: E.tensor_scalar_max(out=t_c[:], in0=t_a[:], scalar1=0.0))
                        P.op("dve", ["t_b", "t_c"], [("DT", i)], lambda E, i=i: E.tensor_tensor(
                            out=DTt[:, i, :], in0=t_b[:], in1=t_c[:], op=ALU.add))
                        P.op("dve", [("DT", i), "arow"], [("A", i)], lambda E, i=i: E.tensor_tensor(
                            out=At[:, i, :], in0=DTt[:, i, :], in1=arow[:], op=ALU.mult))
                        pc = ps[2 + i % 2]
                        pck = ("ps", 2 + i % 2)

                        def mmcs(E, pc=pc, i=i):
                            E.matmul(pc[:, 0:64], lhsT=maskF, rhs=At[:, i, 0:64], start=True, stop=True)
                            E.matmul(pc[:, 64:128], lhsT=SLf, rhs=At[:, i, 0:64], start=True, stop=True)
                            E.matmul(pc[:, 128:192], lhsT=maskB, rhs=At[:, i, 64:128], start=True, stop=True)
                            E.matmul(pc[:, 192:256], lhsT=SLb, rhs=At[:, i, 64:128], start=True, stop=True)
                            return E.matmul(pc[:, 256:384], lhsT=onesf[:], rhs=At[:, i, :], start=True, stop=True)
                        P.op("pe", [("A", i), "ssdc", "onesf"], [pck], mmcs)
                        P.op("act", [pck], [("EX", i)], lambda E, pc=pc, i=i: E.activation(
                            out=EX[:, i, :, :].rearrange("p a b -> p (a b)"), in_=pc[:, 0:256], func=AF.Exp))
                        P.op("act", [pck], [("ETOT", i)], lambda E, pc=pc, i=i: E.activation(
                            out=ETOT[:, i, :], in_=pc[:, 256:384], func=AF.Exp))
                with ExitStack() as s3:
                    zt0 = sb("zt0", [128, 512], BF16, s3)
                    zt1 = sb("zt1", [128, 512], BF16, s3)
                    zts = [zt0, zt1]
                    nz = 0
                    for zb in range(8):
                        wz, wzk = W.take("ssd_w_in", (jj, slice(None), slice(zb * 512, (zb + 1) * 512)), KC, 512)
                        for i in range(16):
                            pb = ps[nz % 2]
                            pk = ("ps", nz % 2)
                            zj = nz % 2
                            nz += 1

                            def mmz(E, pb=pb, i=i, wz=wz):
                                ins = None
                                for k in range(KC):
                                    ins = E.matmul(pb[:, :], lhsT=UT[:, k, i * 128:(i + 1) * 128], rhs=wz[:, k, :],
                                                   start=(k == 0), stop=(k == KC - 1))
                                return ins
                            P.op("pe", [wzk] + utk, [pk], mmz)
                            P.op("act", [pk], [("zt", zj)], lambda E, pb=pb, zj=zj: E.activation(
                                out=zts[zj][:], in_=pb[:, :], func=AF.Silu))
                            P.dma("sp", ZS[zb, i, :, :], zts[zj][:], [("zt", zj)], [("ZS", zb, i)])
                    raw0 = sb("raw0", [128, NTOK], F32, s3)
                    raw1 = raw0
                    acc = sb("acc", [128, NTOK], F32, s3)
                    xb0 = sb("xb0", [128, NTOK], BF16, s3)
                    xb1 = xb0
                    tr0 = sb("tr0", [128, 512], BF16, s3)
                    tr1 = sb("tr1", [128, 512], BF16, s3)
                    raws = [raw0, raw1]
                    xbs = [xb0, xb1]
                    trs = [tr0, tr1]
                    nft = 0
                    ntr = 0
                    for cb in range(12):
                        wx, wxk = W.take("ssd_w_in", (jj, slice(None), slice(4096 + cb * 512, 4096 + (cb + 1) * 512)), KC, 512)
                        for ft in range(4):
                            ct = cb * 4 + ft
                            rj = 0
                            nft += 1
                            raw = raws[rj]
                            xb = xbs[rj]
                            for tt in range(4):
                                pb = ps[tt % 2]
                                pk = ("ps", tt % 2)

                                def mmx(E, pb=pb, tt=tt, ft=ft, wx=wx):
                                    ins = None
                                    for k in range(KC):
                                        ins = E.matmul(pb[:, :], lhsT=wx[:, k, ft * 128:(ft + 1) * 128],
                                                       rhs=UT[:, k, tt * 512:(tt + 1) * 512], start=(k == 0), stop=(k == KC - 1))
                                    return ins
                                P.op("pe", [wxk] + utk, [pk], mmx)
                                P.op("act", [pk], [("raw", rj)], lambda E, pb=pb, tt=tt, raw=raw: E.copy(
                                    out=raw[:, tt * 512:(tt + 1) * 512], in_=pb[:, :]))
                            rk = ("raw", rj)
                            P.op("dve", [rk, "cw"], ["acc"], lambda E, raw=raw, ct=ct: E.tensor_scalar_mul(
                                out=acc[:], in0=raw[:], scalar1=cw[:, ct, 1:2]))
                            P.op("dve", [rk, "cw", "acc"], ["acc"], lambda E, raw=raw, ct=ct: E.scalar_tensor_tensor(
                                out=acc[:, 1:NTOK], in0=raw[:, 0:NTOK - 1], scalar=cw[:, ct, 0:1], in1=acc[:, 1:NTOK],
                                op0=ALU.mult, op1=ALU.add))
                            P.op("dve", [rk, "cw", "acc"], ["acc"], lambda E, raw=raw, ct=ct: E.scalar_tensor_tensor(
                                out=acc[:, 0:NTOK - 1], in0=raw[:, 1:NTOK], scalar=cw[:, ct, 2:3], in1=acc[:, 0:NTOK - 1],
                                op0=ALU.mult, op1=ALU.add))
                            accv = acc[:].rearrange("p (s t) -> p s t", t=256)
                            rawv = raw[:].rearrange("p (s t) -> p s t", t=256)
                            P.op("dve", [rk, "ncw", "acc"], ["acc"], lambda E, accv=accv, rawv=rawv, ct=ct: E.scalar_tensor_tensor(
                                out=accv[:, 1:8, 0], in0=rawv[:, 0:7, 255], scalar=ncw[:, ct, 0:1], in1=accv[:, 1:8, 0],
                                op0=ALU.mult, op1=ALU.add))
                            P.op("dve", [rk, "ncw", "acc"], ["acc"], lambda E, accv=accv, rawv=rawv, ct=ct: E.scalar_tensor_tensor(
                                out=accv[:, 0:7, 255], in0=rawv[:, 1:8, 0], scalar=ncw[:, ct, 2:3], in1=accv[:, 0:7, 255],
                                op0=ALU.mult, op1=ALU.add))
                            P.op("act", ["acc", "cbb"], [("xb", rj)], lambda E, xb=xb, ct=ct: E.activation(
                                out=xb[:], in_=acc[:], func=AF.Silu, bias=cbb[:, ct:ct + 1]))
                            xk = ("xb", rj)
                            if ct >= 40:
                                P.dma("sp", CTS[ct - 40, :, :], xb[:], [xk], [("CTS", ct)])
                                continue
                            if ct >= 32:
                                P.dma("sp", BTS[ct - 32, :, :], xb[:], [xk], [("BTS", ct)])
                            for i4 in range(4):
                                pbb = ps[4 + i4 % 2].bitcast(BF16)
                                pk = ("ps", 4 + i4 % 2)

                                def trx(E, pbb=pbb, i4=i4, xb=xb):
                                    ins = None
                                    for q in range(4):
                                        i = i4 * 4 + q
                                        ins = E.transpose(out=pbb[:, q * 128:(q + 1) * 128], in_=xb[:, i * 128:(i + 1) * 128],
                                                          identity=ident[:])
                                    return ins
                                P.op("pe", [xk, "ident"], [pk], trx)
                                tj = ntr % 2
                                ntr += 1
                                P.op("act", [pk], [("tr", tj)], lambda E, pbb=pbb, tj=tj: E.copy(out=trs[tj][:], in_=pbb[:, 0:512]))
                                if ct < 32:
                                    g, q4 = ct // 4, ct % 4
                                    dst = XS[g, i4 * 4:(i4 + 1) * 4, :, q4 * 128:(q4 + 1) * 128].rearrange("i p n -> p i n")
                                else:
                                    dst = BS[ct - 32, i4 * 4:(i4 + 1) * 4, :, :].rearrange("i p n -> p i n")
                                P.dma("sp", dst, trs[tj][:].rearrange("p (i n) -> p i n", n=128), [("tr", tj)], [("XSw", ct, i4)])
                P.barrier()

            if cfg.get("ssd_s2", True) is False:
                return
            with ExitStack() as s4:
                xg = sb("xg", [128, 16, 512], BF16, s4)
                Bg = sb("Bg", [128, 16, 128], BF16, s4)
                BT = sb("BT", [128, NTOK], BF16, s4)
                CT = sb("CT", [128, NTOK], BF16, s4)
                szs = [sb("sz0", [128, 512], BF16, s4), sb("sz1", [128, 512], BF16, s4)]
                Y = sb("Y", [128, 16, 512], F32, s4)
                hs = [sb("hf", [128, 512], F32, s4), sb("hb", [128, 512], F32, s4)]
                h16 = sb("h16", [128, 512], BF16, s4)
                xdt = [sb("xdtf", [128, 512], BF16, s4), sb("xdtb", [128, 512], BF16, s4)]
                xdd = [sb("xddf", [128, 512], BF16, s4), sb("xddb", [128, 512], BF16, s4)]
                CBm = [sb("CBf", [128, 128], F32, s4), sb("CBb", [128, 128], F32, s4)]
                lh = [sb("lh0", [128, 128], F32, s4), sb("lh1", [128, 128], F32, s4),
                      sb("lh2", [128, 128], F32, s4), sb("lh3", [128, 128], F32, s4)]
                LT = sb("LT", [128, 1024], F32, s4)
                MT = [sb("MTf", [128, 8, 128], BF16, s4), sb("MTb", [128, 8, 128], BF16, s4)]
                tmpy = sb("tmpy", [128, 512], F32, s4)
                dx = sb("dx", [128, 512], F32, s4)
                G = sb("G", [128, 512], F32, s4)
                yn = sb("yn", [128, 512], BF16, s4)
                ynT = sb("ynT", [128, 512], BF16, s4)
                drow = sb("drow", [128, 64], F32, s4)
                nwrow = sb("nwrow", [128, 512], F32, s4)
                ssq = sb("ssq", [128, 2], F32, s4)
                h0t = sb("h0t", [128, 128], F32, s4)
                hot = sb("hot", [128, 128], F32, s4)
                P.dma("sp", drow[:], ssd_dskip[jj, :].partition_broadcast(128), [], ["drow"])
                dsel = [(0, maskF, SLf), (1, maskB, SLb)]
                nlh = 0
                for g in range(cfg.get("ssd_groups", 8)):
                    P.dma("sp", xg[:], XS[g, :, :, :].rearrange("i p n -> p i n"), [], ["xg"])
                    P.dma("sp", Bg[:], BS[g, :, :, :].rearrange("i p n -> p i n"), [], ["Bg"])
                    P.dma("sp", BT[:], BTS[g, :, :], [], ["BT"])
                    P.dma("sp", CT[:], CTS[g, :, :], [], ["CT"])
                    P.dma("sp", nwrow[:], ssd_nw[jj, g * 512:(g + 1) * 512].partition_broadcast(128), [], ["nwrow"])
                    for d_ in range(2):
                        for q in range(4):
                            P.dma("sp", h0t[:], ssd_h0[jj, d_, g * 512 + q * 128:g * 512 + (q + 1) * 128, :], [], ["h0t"])
                            P.op("pe", ["h0t", "ident_f"], [("ps", 7)], lambda E: E.transpose(
                                out=ps[7][:, 0:128], in_=h0t[:], identity=ident_f[:]))
                            P.op("dve", [("ps", 7)], [("h", d_)], lambda E, d_=d_, q=q: E.tensor_copy(
                                out=hs[d_][:, q * 128:(q + 1) * 128], in_=ps[7][:, 0:128]))

                    def emit_state(d_, seg):
                        for q in range(4):
                            P.op("pe", [("h", d_), "ident_f"], [("ps", 7)], lambda E, q=q: E.transpose(
                                out=ps[7][:, 0:128], in_=hs[d_][:, q * 128:(q + 1) * 128], identity=ident_f[:]))
                            P.op("act", [("ps", 7)], ["hot"], lambda E: E.copy(out=hot[:], in_=ps[7][:, 0:128]))
                            P.dma("sp", st_out[jj, seg, d_, g * 512 + q * 128:g * 512 + (q + 1) * 128, :], hot[:],
                                  ["hot"], [("sto", seg, d_, g, q)])

                    def xprep(c, d_, eng):
                        hsl = slice(d_ * 64 + g * 8, d_ * 64 + g * 8 + 8)
                        P.op(eng, ["xg", ("DT", c)], [("xdt", d_)], lambda E: E.tensor_tensor(
                            out=v3(xdt[d_][:]), in0=v3(xg[:, c, :]), in1=DTt[:, c, hsl].unsqueeze(2).to_broadcast([128, 8, 64]),
                            op=ALU.mult))
                        esl = EX[:, c, 1 if d_ == 0 else 3, g * 8:g * 8 + 8]
                        P.op(eng, [("xdt", d_), ("EX", c)], [("xdd", d_)], lambda E: E.tensor_tensor(
                            out=v3(xdd[d_][:]), in0=v3(xdt[d_][:]), in1=esl.unsqueeze(2).to_broadcast([128, 8, 64]), op=ALU.mult))

                    def state_step(c, d_):
                        hk = ("h", d_)
                        P.op("act", [hk], ["h16"], lambda E: E.copy(out=h16[:], in_=hs[d_][:]))
                        P.op("pe", ["CT", "h16"], [("ps", 6)], lambda E: E.matmul(
                            ps[6][:, :], lhsT=CT[:, c * 128:(c + 1) * 128], rhs=h16[:], start=True, stop=True))
                        esl = EX[:, c, 0 if d_ == 0 else 2, g * 8:g * 8 + 8]
                        P.op("dve", [("ps", 6), ("EX", c)], ["tmpy"], lambda E: E.tensor_tensor(
                            out=v3(tmpy[:]), in0=v3(ps[6][:, :]), in1=esl.unsqueeze(2).to_broadcast([128, 8, 64]), op=ALU.mult))
                        P.op("pe", ["Bg", ("xdd", d_)], [("ps", 7)], lambda E: E.matmul(
                            ps[7][:, :], lhsT=Bg[:, c, :], rhs=xdd[d_][:], start=True, stop=True))
                        tsl = ETOT[:, c, d_ * 64 + g * 8:d_ * 64 + g * 8 + 8]
                        P.op("dve", [hk, ("ETOT", c)], [hk], lambda E: E.tensor_tensor(
                            out=v3(hs[d_][:]), in0=v3(hs[d_][:]), in1=tsl.unsqueeze(2).to_broadcast([128, 8, 64]), op=ALU.mult))
                        P.op("dve", [hk, ("ps", 7)], [hk], lambda E: E.tensor_tensor(
                            out=hs[d_][:], in0=hs[d_][:], in1=ps[7][:, :], op=ALU.add))

                    for c in range(16):
                        if c > 0 and c % 2 == 0:
                            emit_state(0, c // 2 - 1)
                            P.op("dve", [("h", 0), "cont"], [("h", 0)], lambda E: E.tensor_scalar_mul(
                                out=hs[0][:], in0=hs[0][:], scalar1=cont[:, 0:1]))
                        xprep(c, 0, "dve")
                        xprep(c, 1, "pool")
                        P.op("pe", ["BT", "CT"], [("ps", 0)], lambda E, c=c: E.matmul(
                            ps[0][:, 0:128], lhsT=BT[:, c * 128:(c + 1) * 128], rhs=CT[:, c * 128:(c + 1) * 128],
                            start=True, stop=True))
                        for d_, mk, SL in dsel:
                            P.op("dve", [("ps", 0), "ssdc"], [("CBm", d_)], lambda E, d_=d_, mk=mk: E.tensor_tensor(
                                out=CBm[d_][:], in0=ps[0][:, 0:128], in1=mk, op=ALU.mult))
                        for d_, mk, SL in dsel:
                            for half in range(2):
                                pb = ps[1 + d_ * 2 + half]
                                pk = ("ps", 1 + d_ * 2 + half)
                                for r4 in range(4):
                                    r = half * 4 + r4
                                    lj = nlh % 4
                                    nlh += 1
                                    col = d_ * 64 + g * 8 + r
                                    if lj % 2 == 0:
                                        P.op("act", [("A", c), "ssdc"], [("lh", lj)], lambda E, lj=lj, SL=SL, col=col, c=c: E.mul(
                                            out=lh[lj][:], in_=SL, mul=At[:, c, col:col + 1]))
                                    else:
                                        P.op("dve", [("A", c), "ssdc"], [("lh", lj)], lambda E, lj=lj, SL=SL, col=col, c=c: E.tensor_scalar_mul(
                                            out=lh[lj][:], in0=SL, scalar1=At[:, c, col:col + 1]))
                                    P.op("pe", [("lh", lj), "ssdc"], [pk], lambda E, pb=pb, r4=r4, lj=lj, mk=mk: E.matmul(
                                        pb[:, r4 * 128:(r4 + 1) * 128], lhsT=lh[lj][:], rhs=mk, start=True, stop=True))
                                P.op("act", [pk], [("LT", half)], lambda E, pb=pb, half=half: E.activation(
                                    out=LT[:, half * 512:(half + 1) * 512], in_=pb[:, :], func=AF.Exp))
                                P.op("dve", [("LT", half), ("CBm", d_)], [("MT", d_)], lambda E, d_=d_, half=half: E.tensor_tensor(
                                    out=MT[d_][:, half * 4:(half + 1) * 4, :],
                                    in0=LT[:, half * 512:(half + 1) * 512].rearrange("p (r t) -> p r t", t=128),
                                    in1=CBm[d_][:].unsqueeze(1).to_broadcast([128, 4, 128]), op=ALU.mult))

                        def mmy(E):
                            ins = None
                            for r in range(8):
                                E.matmul(ps[5][:, r * 64:(r + 1) * 64], lhsT=MT[0][:, r, :], rhs=xdt[0][:, r * 64:(r + 1) * 64],
                                         start=True, stop=False)
                                ins = E.matmul(ps[5][:, r * 64:(r + 1) * 64], lhsT=MT[1][:, r, :], rhs=xdt[1][:, r * 64:(r + 1) * 64],
                                               start=False, stop=True)
                            return ins
                        P.op("pe", [("MT", 0), ("MT", 1), ("xdt", 0), ("xdt", 1)], [("ps", 5)], mmy)
                        P.op("pool", ["xg", "drow"], ["dx"], lambda E, c=c: E.tensor_tensor(
                            out=v3(dx[:]), in0=v3(xg[:, c, :]), in1=drow[:, g * 8:g * 8 + 8].unsqueeze(2).to_broadcast([128, 8, 64]),
                            op=ALU.mult))
                        P.op("dve", [("ps", 5), "dx"], [("Y", c)], lambda E, c=c: E.tensor_tensor(
                            out=Y[:, c, :], in0=ps[5][:, :], in1=dx[:], op=ALU.add))
                        state_step(c, 0)
                        P.op("pool", ["tmpy", ("Y", c)], [("Y", c)], lambda E, c=c: E.tensor_tensor(
                            out=Y[:, c, :], in0=Y[:, c, :], in1=tmpy[:], op=ALU.add))
                    emit_state(0, 7)
                    for c in range(15, -1, -1):
                        if c < 15 and c % 2 == 1:
                            emit_state(1, (c + 1) // 2)
                            P.op("dve", [("h", 1), "cont"], [("h", 1)], lambda E: E.tensor_scalar_mul(
                                out=hs[1][:], in0=hs[1][:], scalar1=cont[:, 0:1]))
                        P.dma("sp", szs[c % 2][:], ZS[g, c, :, :], [], [("sz", c % 2)])
                        xprep(c, 1, "pool")
                        state_step(c, 1)
                        P.op("pool", ["tmpy", ("Y", c)], [("Y", c)], lambda E, c=c: E.tensor_tensor(
                            out=Y[:, c, :], in0=Y[:, c, :], in1=tmpy[:], op=ALU.add))
                        P.op("pool", [("Y", c), ("sz", c % 2)], ["G"], lambda E, c=c: E.tensor_tensor(
                            out=G[:], in0=Y[:, c, :], in1=szs[c % 2][:], op=ALU.mult))
                        P.op("act", ["G"], ["yn", "ssq"], lambda E: E.activation(
                            out=yn[:], in_=G[:], func=AF.Square, accum_out=ssq[:, 0:1]))
                        P.op("act", ["ssq"], ["ssq"], lambda E: E.activation(
                            out=ssq[:, 0:1], in_=ssq[:, 0:1], func=AF.Sqrt, scale=1.0 / 512, bias=epsb[:, 0:1]))
                        P.op("dve", ["ssq"], ["ssq"], lambda E: E.reciprocal(out=ssq[:, 0:1], in_=ssq[:, 0:1]))
                        P.op("dve", ["G", "ssq", "nwrow"], ["yn"], lambda E: E.scalar_tensor_tensor(
                            out=yn[:], in0=G[:], scalar=ssq[:, 0:1], in1=nwrow[:], op0=ALU.mult, op1=ALU.mult))
                        pbb = ps[0].bitcast(BF16)

                        def try_(E, pbb=pbb):
                            ins = None
                            for q in range(4):
                                ins = E.transpose(out=pbb[:, q * 128:(q + 1) * 128], in_=yn[:, q * 128:(q + 1) * 128],
                                                  identity=ident[:])
                            return ins
                        P.op("pe", ["yn", "ident"], [("ps", 0)], try_)
                        P.op("act", [("ps", 0)], ["ynT"], lambda E, pbb=pbb: E.copy(out=ynT[:], in_=pbb[:, 0:512]))
                        P.dma("sp", YT[g * 4:(g + 1) * 4, :, c * 128:(c + 1) * 128].rearrange("q p n -> p q n"),
                              ynT[:].rearrange("p (q n) -> p q n", n=128), ["ynT"], [("YT", g, c)])
                    emit_state(1, 0)
                P.barrier()


    def mlstm_phase(l, src):
        with ExitStack() as so:
            def sb(name, shape, dt, st=so):
                return st.enter_context(sbt(name, shape, dt))
            GI = sb("GI", [128, 16, 16], F32)
            GF = sb("GF", [128, 16, 16], F32)
            BC = sb("BC", [128, 16, 16], F32)
            TOT = sb("TOT", [128, 16, 16], F32)
            cst = sb("mlc1", [128, 4, 128], F32)
            cst2 = sb("mlc2", [128, 4, 128], F32)
            onesf = sb("onesfm", [128, 128], F32)
            cont = sb("contm", [128, 1], F32)
            maskF, maskB, SLf, SLb = cst[:, 0, :], cst[:, 1, :], cst[:, 2, :], cst[:, 3, :]
            NEGf, NEGb, Self, Selb = cst2[:, 0, :], cst2[:, 1, :], cst2[:, 2, :], cst2[:, 3, :]
            P.dma("sp", cst[:], ssd_consts[:, :, :], [], ["mlc"])
            P.dma("sp", cst2[:], ml_consts[:, :, :], [], ["mlc"])
            P.dma("sp", cont[:], contd[:, :], [], ["contm"])
            P.op("dve", [], ["onesfm"], lambda E: E.memset(onesf[:], 1.0))
            with ExitStack() as s1:
                UT = sb("UTm", [128, KC, NTOK], BF16, s1)
                with ExitStack() as s0:
                    xta = sb("xta", [128, D], F32, s0)
                    xtb = sb("xtb", [128, D], F32, s0)
                    xn = sb("xn", [128, D], BF16, s0)
                    xtl = [xta, xtb]
                    for i in range(16):
                        norm_mod(l, 0, src, i, xtl[i % 2], ("xt", i % 2), i % 2, xn, UT, ("UT", i // 4), i * 128)
                    P.barrier()
                utk = [("UT", q) for q in range(4)]
                bgr = sb("bgr", [128, 32], F32, s1)
                t_a = sb("mt_a", [128, 32], F32, s1)
                t_b = sb("mt_b", [128, 16], F32, s1)
                t_c = sb("mt_c", [128, 16], F32, s1)
                ob0 = sb("ob0", [128, 512], BF16, s1)
                ob1 = sb("ob1", [128, 512], BF16, s1)
                obs = [ob0, ob1]
                fo0 = sb("fo0", [128, NTOK], BF16, s1)
                P.dma("sp", bgr[:], ml_bg.partition_broadcast(128), [], ["bgr"])
                wgt, wgtk = W.take("mlstm_w_in", (slice(None), slice(6144, 6176)), KC, 32)
                for i in range(16):
                    pb = ps[i % 2]
                    pk = ("ps", i % 2)

                    def mmg_(E, pb=pb, i=i):
                        ins = None
                        for k in range(KC):
                            ins = E.matmul(pb[:, 0:32], lhsT=UT[:, k, i * 128:(i + 1) * 128], rhs=wgt[:, k, :],
                                           start=(k == 0), stop=(k == KC - 1))
                        return ins
                    P.op("pe", [wgtk] + utk, [pk], mmg_)
                    P.op("dve", [pk, "bgr"], ["mt_a"], lambda E, pb=pb: E.tensor_tensor(
                        out=t_a[:], in0=pb[:, 0:32], in1=bgr[:], op=ALU.add))
                    t4 = t_a[:].rearrange("p (d g h) -> p d g h", d=2, g=2)
                    P.op("dve", ["mt_a"], [("GI", i)], lambda E, i=i, t4=t4: E.tensor_copy(
                        out=GI[:, i, :].rearrange("p (d h) -> p d h", d=2), in_=t4[:, :, 0, :]))
                    P.op("act", ["mt_a"], ["mt_b"], lambda E, t4=t4: E.activation(
                        out=t_b[:].rearrange("p (d h) -> p d h", d=2), in_=t4[:, :, 1, :], func=AF.Abs))
                    P.op("act", ["mt_b"], ["mt_b"], lambda E: E.activation(out=t_b[:], in_=t_b[:], func=AF.Exp, scale=-1.0))
                    P.op("act", ["mt_b"], ["mt_b"], lambda E: E.activation(out=t_b[:], in_=t_b[:], func=AF.Ln, bias=1.0))
                    P.op("dve", ["mt_a"], ["mt_c"], lambda E, t4=t4: E.tensor_scalar_min(
                        out=t_c[:].rearrange("p (d h) -> p d h", d=2), in0=t4[:, :, 1, :], scalar1=0.0))
                    P.op("dve", ["mt_b", "mt_c"], [("GF", i)], lambda E, i=i: E.tensor_tensor(
                        out=GF[:, i, :], in0=t_c[:], in1=t_b[:], op=ALU.subtract))
                    pc = ps[2 + i % 2]
                    pck = ("ps", 2 + i % 2)

                    def mmcs(E, pc=pc, i=i):
                        E.matmul(pc[:, 0:8], lhsT=maskF, rhs=GF[:, i, 0:8], start=True, stop=True)
                        E.matmul(pc[:, 8:16], lhsT=maskB, rhs=GF[:, i, 8:16], start=True, stop=True)
                        return E.matmul(pc[:, 16:32], lhsT=onesf[:], rhs=GF[:, i, :], start=True, stop=True)
                    P.op("pe", [("GF", i), "mlc", "onesfm"], [pck], mmcs)
                    P.op("dve", [pck], [("BC", i)], lambda E, pc=pc, i=i: E.tensor_copy(out=BC[:, i, :], in_=pc[:, 0:16]))
                    P.op("dve", [pck], [("TOT", i)], lambda E, pc=pc, i=i: E.tensor_copy(out=TOT[:, i, :], in_=pc[:, 16:32]))
                nob = 0
                for blk in range(4):
                    wq, wqk = W.take("mlstm_w_in", (slice(None), slice(blk * 512, (blk + 1) * 512)), KC, 512)
                    isq = blk < 2
                    for ft in range(4):
                        hh = (blk % 2) * 4 + ft
                        for tt in range(4):
                            pb = ps[tt % 2]
                            pk = ("ps", tt % 2)

                            def mmx(E, pb=pb, tt=tt, ft=ft, wq=wq):
                                ins = None
                                for k in range(KC):
                                    ins = E.matmul(pb[:, :], lhsT=wq[:, k, ft * 128:(ft + 1) * 128],
                                                   rhs=UT[:, k, tt * 512:(tt + 1) * 512], start=(k == 0), stop=(k == KC - 1))
                                return ins
                            P.op("pe", [wqk] + utk, [pk], mmx)
                            P.op("act", [pk], ["fo0"], lambda E, pb=pb, tt=tt, isq=isq: E.mul(
                                out=fo0[:, tt * 512:(tt + 1) * 512], in_=pb[:, :], mul=(128.0 ** -0.5) if isq else 1.0))
                        P.dma("sp", (QTS if isq else KTS)[hh, :, :], fo0[:], ["fo0"], [("QKS", blk, ft)])
                    if not isq:
                        for i in range(16):
                            pb = ps[2 + i % 2]
                            pk = ("ps", 2 + i % 2)
                            oj = nob % 2
                            nob += 1

                            def mmk(E, pb=pb, i=i, wq=wq):
                                ins = None
                                for k in range(KC):
                                    ins = E.matmul(pb[:, :], lhsT=UT[:, k, i * 128:(i + 1) * 128], rhs=wq[:, k, :],
                                                   start=(k == 0), stop=(k == KC - 1))
                                return ins
                            P.op("pe", [wqk] + utk, [pk], mmk)
                            P.op("dve", [pk], [("ob", oj)], lambda E, pb=pb, oj=oj: E.tensor_copy(out=obs[oj][:], in_=pb[:, :]))
                            h0_ = (blk % 2) * 4
                            P.dma("sp", KMS[h0_:h0_ + 4, i, :, :].rearrange("h p n -> p h n"),
                                  obs[oj][:].rearrange("p (h n) -> p h n", n=128), [("ob", oj)], [("KMS", blk, i)])
                for blk in range(8):
                    c0 = 2048 + blk * 512
                    wv_, wvk_ = W.take("mlstm_w_in", (slice(None), slice(c0, c0 + 512)), KC, 512)
                    isv = blk < 4
                    for i in range(16):
                        pb = ps[i % 2]
                        pk = ("ps", i % 2)
                        oj = nob % 2
                        nob += 1

                        def mmv(E, pb=pb, i=i, wv_=wv_):
                            ins = None
                            for k in range(KC):
                                ins = E.matmul(pb[:, :], lhsT=UT[:, k, i * 128:(i + 1) * 128], rhs=wv_[:, k, :],
                                               start=(k == 0), stop=(k == KC - 1))
                            return ins
                        P.op("pe", [wvk_] + utk, [pk], mmv)
                        if isv:
                            P.op("act", [pk], [("ob", oj)], lambda E, pb=pb, oj=oj: E.copy(out=obs[oj][:], in_=pb[:, :]))
                        else:
                            P.op("act", [pk], [("ob", oj)], lambda E, pb=pb, oj=oj: E.activation(
                                out=obs[oj][:], in_=pb[:, :], func=AF.Sigmoid))
                        h0_ = (blk % 4) * 2
                        P.dma("sp", (VS if isv else OGS)[h0_:h0_ + 2, i, :, :].rearrange("h p n -> p h n"),
                              obs[oj][:].rearrange("p (h n) -> p h n", n=256), [("ob", oj)], [("VOS", blk, i)])
                P.barrier()

            with ExitStack() as s4:
                QT = sb("QTm", [128, NTOK], BF16, s4)
                KTm = sb("KTm", [128, NTOK], BF16, s4)
                Km = sb("Km", [128, 16, 128], BF16, s4)
                V1 = sb("V1", [128, 16, 257], BF16, s4)
                Hs = sb("Hs", [128, 16, 256], F32, s4)
                CN = [sb("CNf", [128, 257], F32, s4), sb("CNb", [128, 257], F32, s4)]
                CN16 = sb("CN16", [128, 257], BF16, s4)
                Mst = sb("Mst", [128, 2], F32, s4)
                m0t = sb("m0t", [128, 16], F32, s4)
                mout = sb("mout", [1, 128], F32, s4)
                nout = sb("nout", [128, 128], F32, s4)
                nouts = sb("nouts", [128, 128], F32, s4)
                r1 = sb("r1", [128, 128], F32, s4)
                rD = sb("rD", [128, 128], F32, s4)
                Dmm = sb("Dmm", [128, 128], F32, s4)
                Wt = sb("Wt", [128, 128], BF16, s4)
                WTs = sb("WTs", [128, 128], BF16, s4)
                ST = sb("ST", [128, 128], BF16, s4)
                P1s = sb("P1s", [128, 257], F32, s4)
                R = sb("R", [128, 257], F32, s4)
                kw = sb("kw", [128, 128], BF16, s4)
                sm = sb("sm", [128, 16], F32, s4)
                ogs = [sb("og0", [128, 256], BF16, s4), sb("og1", [128, 256], BF16, s4)]
                hn = sb("hn", [128, 256], F32, s4)
                hg = sb("hg", [128, 256], BF16, s4)
                hgT = sb("hgT", [128, 256], BF16, s4)
                hnrow = sb("hnrow", [128, 256], F32, s4)
                P.dma("sp", m0t[:], ml_m0[:, :], [], ["m0t"])
                P.op("dve", [], ["V1"], lambda E: E.memset(V1[:, :, 256:257], 1.0))
                dsel = [(0, maskF, SLf, NEGf, Self), (1, maskB, SLb, NEGb, Selb)]
                MX, MI, MT_, NMT, INTER, EMT, TM, LW, MNEW, E2a, E2b, ADEN, RDN, SSQ = range(14)

                def col(j):
                    return sm[:, j:j + 1]

                for h in range(cfg.get("ml_heads", 8)):
                    P.dma("sp", QT[:], QTS[h, :, :], [], ["QTm"])
                    P.dma("sp", KTm[:], KTS[h, :, :], [], ["KTm"])
                    P.dma("sp", Km[:], KMS[h, :, :, :].rearrange("i p n -> p i n"), [], ["Km"])
                    P.dma("sp", V1[:, :, 0:256], VS[h, :, :, :].rearrange("i p n -> p i n"), ["V1"], ["V1"])
                    P.dma("sp", hnrow[:], ml_hn[h, :].partition_broadcast(128), [], ["hnrow"])
                    for d_ in range(2):
                        P.dma("sp", CN[d_][:, 0:256], ml_C0[d_, h, :, :], [], [("CN", d_)])
                        P.dma("sp", CN[d_][:, 256:257], ml_n0[d_, h, :].rearrange("(p o) -> p o", o=1), [("CN", d_)], [("CN", d_)])
                        P.op("dve", ["m0t"], [("M", d_)], lambda E, d_=d_: E.tensor_copy(
                            out=Mst[:, d_:d_ + 1], in_=m0t[:, d_ * 8 + h:d_ * 8 + h + 1]))

                    def emit_state(d_, seg):
                        P.dma("sp", mlC_out[seg, d_, h, :, :], CN[d_][:, 0:256], [("CN", d_)], [("mlCo", seg, d_, h)])
                        idx = (seg * 2 + d_) * 8 + h
                        P.op("dve", [("CN", d_)], ["nout"], lambda E: E.tensor_copy(
                            out=nout[:, idx:idx + 1], in_=CN[d_][:, 256:257]))
                        P.op("dve", [("M", d_)], ["mout"], lambda E: E.tensor_copy(
                            out=mout[0:1, idx:idx + 1], in_=Mst[0:1, d_:d_ + 1]))

                    def step(c, d_, first_dir):
                        _, mk, SL, NEG, Sel = dsel[d_]
                        gcol = d_ * 8 + h
                        fcol = GF[:, c, gcol:gcol + 1]
                        icol = GI[:, c, gcol:gcol + 1]
                        bcol = BC[:, c, gcol:gcol + 1]
                        tcol = TOT[:, c, gcol:gcol + 1]
                        mcol = Mst[:, d_:d_ + 1]
                        gk = [("GF", c), ("GI", c), ("BC", c), ("TOT", c)]
                        cs = slice(c * 128, (c + 1) * 128)
                        P.op("act", gk + ["mlc"], ["r1"], lambda E: E.mul(out=r1[:], in_=SL, mul=fcol))
                        P.op("dve", gk + ["r1", "ident_f"], ["rD"], lambda E: E.scalar_tensor_tensor(
                            out=rD[:], in0=ident_f[:], scalar=icol, in1=r1[:], op0=ALU.mult, op1=ALU.add))
                        P.op("pe", ["rD", "mlc"], [("ps", 0)], lambda E: E.matmul(
                            ps[0][:, 0:128], lhsT=mk, rhs=rD[:], start=True, stop=True))
                        P.op("dve", [("ps", 0), "mlc"], ["Dmm"], lambda E: E.tensor_tensor(
                            out=Dmm[:], in0=ps[0][:, 0:128], in1=NEG, op=ALU.add))
                        P.op("dve", ["Dmm"], [("sm", MX)], lambda E: E.reduce_max(
                            out=col(MX), in_=Dmm[:], axis=mybir.AxisListType.X))
                        P.op("dve", gk + [("M", d_)], [("sm", MI)], lambda E: E.tensor_tensor(
                            out=col(MI), in0=bcol, in1=mcol, op=ALU.add))
                        P.op("dve", [("sm", MX), ("sm", MI)], [("sm", MT_)], lambda E: E.tensor_tensor(
                            out=col(MT_), in0=col(MX), in1=col(MI), op=ALU.max))
                        P.op("dve", [("sm", MT_)], [("sm", NMT)], lambda E: E.tensor_scalar_mul(
                            out=col(NMT), in0=col(MT_), scalar1=-1.0))
                        P.op("act", [("sm", MI), ("sm", NMT)], [("sm", INTER)], lambda E: E.activation(
                            out=col(INTER), in_=col(MI), func=AF.Exp, bias=col(NMT)))
                        P.op("act", [("sm", NMT)], [("sm", EMT)], lambda E: E.activation(
                            out=col(EMT), in_=col(NMT), func=AF.Exp))
                        P.op("act", ["Dmm", ("sm", NMT)], ["Wt"], lambda E: E.activation(
                            out=Wt[:], in_=Dmm[:], func=AF.Exp, bias=col(NMT)))
                        pbb = ps[1].bitcast(BF16)
                        P.op("pe", ["Wt", "ident"], [("ps", 1)], lambda E: E.transpose(
                            out=pbb[:, 0:128], in_=Wt[:], identity=ident[:]))
                        P.op("act", [("ps", 1)], ["WTs"], lambda E: E.copy(out=WTs[:], in_=pbb[:, 0:128]))
                        P.op("pe", ["KTm", "QTm"], [("ps", 2)], lambda E: E.matmul(
                            ps[2][:, 0:128], lhsT=KTm[:, cs], rhs=QT[:, cs], start=True, stop=True))
                        P.op("dve", [("ps", 2), "WTs"], ["ST"], lambda E: E.tensor_tensor(
                            out=ST[:], in0=ps[2][:, 0:128], in1=WTs[:], op=ALU.mult))
                        P.op("act", [("CN", d_)], ["CN16"], lambda E: E.copy(out=CN16[:], in_=CN[d_][:]))
                        P.op("pe", ["ST", "V1"], [("ps", 3)], lambda E: E.matmul(
                            ps[3][:, 0:257], lhsT=ST[:], rhs=V1[:, c, :], start=True, stop=True))
                        P.op("pe", ["QTm", "CN16"], [("ps", 4)], lambda E: E.matmul(
                            ps[4][:, 0:257], lhsT=QT[:, cs], rhs=CN16[:], start=True, stop=True))
                        P.op("act", [("ps", 3)], ["P1s"], lambda E: E.copy(out=P1s[:], in_=ps[3][:, 0:257]))
                        P.op("dve", [("ps", 4), "P1s", ("sm", INTER)], ["R"], lambda E: E.scalar_tensor_tensor(
                            out=R[:], in0=ps[4][:, 0:257], scalar=col(INTER), in1=P1s[:], op0=ALU.mult, op1=ALU.add))
                        P.op("act", ["R"], [("sm", ADEN)], lambda E: E.activation(
                            out=col(ADEN), in_=R[:, 256:257], func=AF.Abs))
                        P.op("dve", [("sm", ADEN), ("sm", EMT)], [("sm", RDN)], lambda E: E.tensor_tensor(
                            out=col(RDN), in0=col(ADEN), in1=col(EMT), op=ALU.max))
                        P.op("dve", [("sm", RDN)], [("sm", RDN)], lambda E: E.reciprocal(out=col(RDN), in_=col(RDN)))
                        if first_dir:
                            P.op("dve", ["R", ("sm", RDN)], [("Hs", c)], lambda E: E.tensor_scalar_mul(
                                out=Hs[:, c, :], in0=R[:, 0:256], scalar1=col(RDN)))
                        else:
                            P.op("dve", ["R", ("sm", RDN), ("Hs", c)], [("Hs", c)], lambda E: E.scalar_tensor_tensor(
                                out=Hs[:, c, :], in0=R[:, 0:256], scalar=col(RDN), in1=Hs[:, c, :], op0=ALU.mult, op1=ALU.add))
                        P.op("dve", gk + [("M", d_)], [("sm", TM)], lambda E: E.tensor_tensor(
                            out=col(TM), in0=tcol, in1=mcol, op=ALU.add))
                        P.op("dve", gk, [("sm", LW)], lambda E: E.tensor_tensor(
                            out=col(LW), in0=tcol, in1=bcol, op=ALU.subtract))
                        P.op("dve", gk + [("sm", LW)], [("sm", LW)], lambda E: E.tensor_tensor(
                            out=col(LW), in0=col(LW), in1=icol, op=ALU.add))
                        P.op("pe", [("sm", MX), "mlc"], [("ps", 5)], lambda E: E.matmul(
                            ps[5][:, 0:1], lhsT=Sel, rhs=col(MX), start=True, stop=True))
                        P.op("dve", [("ps", 5), ("sm", TM)], [("sm", MNEW)], lambda E: E.tensor_tensor(
                            out=col(MNEW), in0=ps[5][:, 0:1], in1=col(TM), op=ALU.max))
                        P.op("dve", [("sm", TM), ("sm", LW), ("sm", MNEW)], [("sm", TM), ("sm", LW)], lambda E: E.tensor_scalar(
                            out=sm[:, TM:TM + 2], in0=sm[:, TM:TM + 2], scalar1=col(MNEW), scalar2=None, op0=ALU.subtract))
                        P.op("act", [("sm", TM), ("sm", LW)], [("sm", E2a), ("sm", E2b)], lambda E: E.activation(
                            out=sm[:, E2a:E2a + 2], in_=sm[:, TM:TM + 2], func=AF.Exp))
                        P.op("dve", ["Km", ("sm", E2b)], ["kw"], lambda E: E.tensor_scalar_mul(
                            out=kw[:], in0=Km[:, c, :], scalar1=col(E2b)))
                        P.op("pe", ["kw", "V1"], [("ps", 6)], lambda E: E.matmul(
                            ps[6][:, 0:257], lhsT=kw[:], rhs=V1[:, c, :], start=True, stop=True))
                        P.op("dve", [("CN", d_), ("sm", E2a), ("ps", 6)], [("CN", d_)], lambda E: E.scalar_tensor_tensor(
                            out=CN[d_][:], in0=CN[d_][:], scalar=col(E2a), in1=ps[6][:, 0:257], op0=ALU.mult, op1=ALU.add))
                        P.op("dve", [("sm", MNEW)], [("M", d_)], lambda E: E.tensor_copy(out=mcol, in_=col(MNEW)))

                    def seg_reset(d_):
                        P.op("dve", [("CN", d_), "contm"], [("CN", d_)], lambda E: E.tensor_scalar_mul(
                            out=CN[d_][:], in0=CN[d_][:], scalar1=cont[:, 0:1]))
                        P.op("dve", [("M", d_), "contm"], [("M", d_)], lambda E: E.tensor_scalar_mul(
                            out=Mst[:, d_:d_ + 1], in0=Mst[:, d_:d_ + 1], scalar1=cont[:, 0:1]))

                    for c in range(16):
                        if c > 0 and c % 2 == 0:
                            emit_state(0, c // 2 - 1)
                            seg_reset(0)
                        step(c, 0, True)
                    emit_state(0, 7)
                    for c in range(15, -1, -1):
                        if c < 15 and c % 2 == 1:
                            emit_state(1, (c + 1) // 2)
                            seg_reset(1)
                        P.dma("sp", ogs[c % 2][:], OGS[h, c, :, :], [], [("og", c % 2)])
                        step(c, 1, False)
                        P.op("act", [("Hs", c)], ["hn", ("sm", SSQ)], lambda E, c=c: E.activation(
                            out=hn[:], in_=Hs[:, c, :], func=AF.Square, accum_out=col(SSQ)))
                        P.op("act", [("sm", SSQ)], [("sm", SSQ)], lambda E: E.activation(
                            out=col(SSQ), in_=col(SSQ), func=AF.Sqrt, scale=1.0 / 256, bias=epsb[:, 0:1]))
                        P.op("dve", [("sm", SSQ)], [("sm", SSQ)], lambda E: E.reciprocal(out=col(SSQ), in_=col(SSQ)))
                        P.op("dve", [("Hs", c), ("sm", SSQ), "hnrow"], ["hn"], lambda E, c=c: E.scalar_tensor_tensor(
                            out=hn[:], in0=Hs[:, c, :], scalar=col(SSQ), in1=hnrow[:], op0=ALU.mult, op1=ALU.mult))
                        P.op("pool", ["hn", ("og", c % 2)], ["hg"], lambda E, c=c: E.tensor_tensor(
                            out=hg[:], in0=hn[:], in1=ogs[c % 2][:], op=ALU.mult))
                        pbb = ps[7].bitcast(BF16)

                        def trh(E, pbb=pbb):
                            E.transpose(out=pbb[:, 0:128], in_=hg[:, 0:128], identity=ident[:])
                            return E.transpose(out=pbb[:, 128:256], in_=hg[:, 128:256], identity=ident[:])
                        P.op("pe", ["hg", "ident"], [("ps", 7)], trh)
                        P.op("act", [("ps", 7)], ["hgT"], lambda E, pbb=pbb: E.copy(out=hgT[:], in_=pbb[:, 0:256]))
                        P.dma("sp", YT[h * 2:h * 2 + 2, :, c * 128:(c + 1) * 128].rearrange("q p n -> p q n"),
                              hgT[:].rearrange("p (q n) -> p q n", n=128), ["hgT"], [("YT", h, c)])
                    emit_state(1, 0)
                P.op("pe", ["nout", "ident_f"], [("ps", 0)], lambda E: E.transpose(
                    out=ps[0][:, 0:128], in_=nout[:], identity=ident_f[:]))
                P.op("dve", [("ps", 0)], ["nouts"], lambda E: E.tensor_copy(out=nouts[:], in_=ps[0][:, 0:128]))
                P.dma("sp", mln_out.rearrange("s d h k -> (s d h) k"), nouts[:], ["nouts"], ["mlno"])
                P.dma("sp", mlm_out[:, :], mout[:], ["mout"], ["mlmo"])
                P.barrier()

    cur = xin
    for l in layers:
        if cfg.get("mixer", True):
            if l % 3 == 0:
                ssd_phase(l, cur)
                if cfg.get("ssd_out", True):
                    outproj_phase(l, cur, 32, "ssd_w_out", l // 3)
                cur = X
            if l % 3 == 2:
                mlstm_phase(l, cur)
                outproj_phase(l, cur, 16, "mlstm_w_out")
                cur = X
            if l % 3 == 1:
                attn_phase(l, cur)
                outproj_phase(l, cur, 16, "attn_w_out")
                cur = X
        if cfg.get("ffn", True):
            ffn_phase(l, cur)
            cur = X
    P.barrier()
    return nc, W.plan


def build(cfg):
    _, plan = build_program(cfg, None)
    nc, _ = build_program(cfg, plan)
    return nc


def _rope_tables():
    half, quarter = 64, 32
    t = np.arange(NTOK)
    pos_row = (t // 64).astype(np.float32)
    pos_col = (t % 64).astype(np.float32)
    inv_freq = (10000.0 ** (-np.arange(quarter, dtype=np.float32) / quarter)).astype(np.float32)
    cos = np.zeros((128, NTOK), np.float32)
    sin = np.zeros((128, NTOK), np.float32)
    for p in range(128):
        pos = pos_row if p < 64 else pos_col
        ang = (pos * inv_freq[p % 32]).astype(np.float32)
        cos[p] = np.cos(ang)
        sin[p] = np.sin(ang) * (-1.0 if (p % 64) < 32 else 1.0)
    return cos, sin


def _perm():
    pm = np.zeros((128, 128), np.float32)
    for m in range(128):
        sw = m + 32 if (m % 64) < 32 else m - 32
        pm[sw, m] = 1.0
    return pm


def _core_inputs(inp, core, layers):
    m = {}
    if core < 4:
        m["xin"] = np.ascontiguousarray(inp["x_sample"][core])
        cvec = inp["c"][core]
    else:
        j = core - 4
        m["xin"] = np.ascontiguousarray(inp["x_prompt"][8 * j:8 * j + 8].reshape(NTOK, D))
        cvec = inp["c_ctx"]
    m["cv"] = np.ascontiguousarray(cvec.reshape(KC, 128).T)
    js = sorted(set(l // 3 for l in layers if l % 3 == 0))
    if js:
        if core < 4:
            m["ssd_h0"] = np.ascontiguousarray(inp["state_ssd"][core][js].reshape(len(js), 2, 4096, 128))
            m["cont"] = np.ones((128, 1), np.float32)
        else:
            m["ssd_h0"] = np.zeros((len(js), 2, 4096, 128), np.float32)
            m["cont"] = np.zeros((128, 1), np.float32)
    if (2 in layers) and not js:
        m["cont"] = np.ones((128, 1), np.float32) if core < 4 else np.zeros((128, 1), np.float32)
    if 2 in layers:
        if core < 4:
            m["ml_C0"] = np.ascontiguousarray(inp["state_mlstm_C"][core, 0])
            m["ml_n0"] = np.ascontiguousarray(inp["state_mlstm_n"][core, 0])
            m["ml_m0"] = np.ascontiguousarray(np.broadcast_to(inp["state_mlstm_m"][core, 0].reshape(1, 16), (128, 16)))
        else:
            m["ml_C0"] = np.zeros((2, 8, 128, 256), np.float32)
            m["ml_n0"] = np.zeros((2, 8, 128), np.float32)
            m["ml_m0"] = np.zeros((128, 16), np.float32)
    if 1 in layers:
        mb = np.zeros((128, 160), np.float32)
        if core < 4:
            cos, sin = _rope_tables()
            m["cachek"] = np.ascontiguousarray(inp["cache_attn_k"][core, 0].reshape(512, 512))
            m["cachev"] = np.ascontiguousarray(inp["cache_attn_v"][core, 0].reshape(512, 512))
        else:
            cos = np.ones((128, NTOK), np.float32)
            sin = np.zeros((128, NTOK), np.float32)
            m["cachek"] = np.zeros((512, 512), np.float32)
            m["cachev"] = np.zeros((512, 512), np.float32)
            for q8 in range(8):
                for kt in range(20):
                    ok = kt >= 4 and (kt - 4) // 2 == q8
                    mb[:, q8 * 20 + kt] = 0.0 if ok else -30000.0
        m["ropecos"], m["ropesin"], m["maskb"] = cos, sin, mb
    return m


def _shared_inputs(inp, layers):
    L = list(layers)
    sh = {}
    sh["consts"] = np.eye(128, dtype=np.float32)
    sh["ada_w"] = inp["ada_w"][L] if len(L) < DEPTH else inp["ada_w"]
    sh["ada_b"] = inp["ada_b"][L]
    sh["adab_f"] = np.ascontiguousarray(inp["ada_b"].reshape(DEPTH, 96, 128).transpose(2, 0, 1))
    nw = np.stack([inp["norm_mix_w"], inp["norm_ffn_w"]], axis=1)
    sh["nw_f"] = np.ascontiguousarray(nw.reshape(DEPTH, 2, KC, 128).transpose(3, 0, 1, 2))
    for k in ("ffn_w_gate", "ffn_w_up", "ffn_w_down"):
        sh[k] = inp[k][L] if len(L) < DEPTH else inp[k]
    js = sorted(set(l // 3 for l in L if l % 3 == 0))
    if js:
        sh["ssd_w_in"] = inp["ssd_w_in"][js]
        sh["ssd_w_out"] = inp["ssd_w_out"][js]
        cw = inp["ssd_conv_w"][js]
        sh["ssd_cw"] = np.ascontiguousarray(cw.reshape(len(js), 3, 48, 128).transpose(3, 0, 2, 1))
        sh["ssd_cb"] = np.ascontiguousarray(inp["ssd_conv_b"][js].reshape(len(js), 48, 128).transpose(2, 0, 1))
        sh["ssd_dtb"] = np.ascontiguousarray(inp["ssd_dt_bias"][js].reshape(len(js), 128))
        sh["ssd_alog"] = np.ascontiguousarray(inp["ssd_a_log"][js].reshape(len(js), 128))
        sh["ssd_dskip"] = np.ascontiguousarray(inp["ssd_d"][js])
        sh["ssd_nw"] = np.ascontiguousarray(inp["ssd_norm_w"][js])
        idx = np.arange(128)
        mF = (idx[:, None] <= idx[None, :]).astype(np.float32)
        mB = (idx[:, None] >= idx[None, :]).astype(np.float32)
        slf = (idx[:, None] > idx[None, :]).astype(np.float32)
        slb = (idx[:, None] < idx[None, :]).astype(np.float32)
        sh["ssd_consts"] = np.ascontiguousarray(np.stack([mF, mB, slf, slb], axis=1))
    if 2 in L or js:
        idx = np.arange(128)
        mF = (idx[:, None] <= idx[None, :]).astype(np.float32)
        mB = (idx[:, None] >= idx[None, :]).astype(np.float32)
        slf = (idx[:, None] > idx[None, :]).astype(np.float32)
        slb = (idx[:, None] < idx[None, :]).astype(np.float32)
        sh["ssd_consts"] = np.ascontiguousarray(np.stack([mF, mB, slf, slb], axis=1))
    if 2 in L:
        sh["mlstm_w_in"] = inp["mlstm_w_in"][0]
        sh["mlstm_w_out"] = inp["mlstm_w_out"][0]
        sh["ml_bg"] = np.ascontiguousarray(inp["mlstm_b_gates"][0].reshape(32))
        sh["ml_hn"] = np.ascontiguousarray(inp["mlstm_head_norm"][0])
        idx = np.arange(128)
        negf = np.where(idx[None, :] > idx[:, None], -30000.0, 0.0).astype(np.float32)
        negb = np.where(idx[None, :] < idx[:, None], -30000.0, 0.0).astype(np.float32)
        self_ = np.zeros((128, 128), np.float32)
        self_[127, :] = 1.0
        selb = np.zeros((128, 128), np.float32)
        selb[0, :] = 1.0
        sh["ml_consts"] = np.ascontiguousarray(np.stack([negf, negb, self_, selb], axis=1))
    if 1 in L:
        sh["attn_w_qkv"] = inp["attn_w_qkv"][0]
        sh["attn_w_out"] = inp["attn_w_out"][0]
        sh["attn_qn"] = np.ascontiguousarray(inp["attn_q_norm"][0].reshape(128, 1))
        sh["attn_kn"] = np.ascontiguousarray(inp["attn_k_norm"][0].reshape(128, 1))
        sh["attn_knrow"] = np.ascontiguousarray(inp["attn_k_norm"][0])
        sh["perm"] = _perm()
    return sh


def run(inp, cfg, cores=None):
    inp = {k: np.asarray(v) for k, v in inp.items()}
    cores = list(range(8)) if cores is None else cores
    layers = cfg.get("layers", list(range(DEPTH)))
    nc = build(cfg)
    sh = _shared_inputs(inp, layers)
    in_maps = []
    for core in cores:
        m = dict(sh)
        m.update(_core_inputs(inp, core, layers))
        in_maps.append(m)
    res = run_bass_kernel_spmd(nc, in_maps, core_ids=list(range(len(cores))))
    return res


def kernel(**inputs):
    res = run(inputs, {})
    r = res.results
    y_sample = np.stack([r[c]["xout"] for c in range(4)], axis=0)
    y_prompt = np.concatenate([r[c]["xout"].reshape(8, 256, D) for c in range(4, 8)], axis=0)
    st = np.concatenate([r[c]["st_out"].reshape(2, 8, 2, 64, 64, 128).transpose(1, 0, 2, 3, 4, 5)
                         for c in range(4, 8)], axis=0)
    kc = np.concatenate([r[c]["kc_out"].reshape(8, 1, 256, 4, 128) for c in range(4, 8)], axis=0)
    vc = np.concatenate([r[c]["vc_out"].reshape(8, 1, 256, 4, 128) for c in range(4, 8)], axis=0)
    mC = np.concatenate([r[c]["mlC_out"].reshape(8, 1, 2, 8, 128, 256) for c in range(4, 8)], axis=0)
    mn = np.concatenate([r[c]["mln_out"].reshape(8, 1, 2, 8, 128) for c in range(4, 8)], axis=0)
    mm = np.concatenate([r[c]["mlm_out"].reshape(8, 1, 2, 8) for c in range(4, 8)], axis=0)
    return (np.ascontiguousarray(y_prompt), y_sample, np.ascontiguousarray(st), kc, vc, mC, mn, mm)
```

```python
import numpy as np
from contextlib import ExitStack
import concourse.bass as bass
import concourse.mybir as mybir
from concourse.bass_utils import run_bass_kernel_spmd

F32 = mybir.dt.float32
BF16 = mybir.dt.bfloat16
AF = mybir.ActivationFunctionType
ALU = mybir.AluOpType

D = 2048
NTOK = 2048
KC = 16
FH = 5632
HC = 44
DEPTH = 4
EPS = 1e-6
SLOT = 8192
NSLOT = 4
KSPLIT = [(0, 16), (16, 16), (32, 12)]


class Prog:
    def __init__(s, nc):
        s.nc = nc
        s.E = dict(pe=nc.tensor, act=nc.scalar, dve=nc.vector, pool=nc.gpsimd, sp=nc.sync)
        s.sem = {}
        s.val = {}
        for e in s.E:
            s.sem[e] = nc.alloc_semaphore("sem_" + e)
            s.val[e] = 0
        s.dpool = {}
        for q, n in (("sp", 16), ("pool", 16)):
            keys = []
            for i in range(n):
                k = "d_%s%d" % (q, i)
                s.sem[k] = nc.alloc_semaphore(k)
                s.val[k] = 0
                keys.append(k)
            s.dpool[q] = [keys, 0]
        s.waited = {e: {} for e in s.E}
        s.lastw = {}
        s.readers = {}
        s.nops = 0

    def _deps(s, reads, writes):
        deps = {}
        for r in reads:
            t = s.lastw.get(r)
            if t is not None and deps.get(t[0], 0) < t[1]:
                deps[t[0]] = t[1]
        for w in writes:
            t = s.lastw.get(w)
            if t is not None and deps.get(t[0], 0) < t[1]:
                deps[t[0]] = t[1]
            rd = s.readers.get(w)
            if rd:
                for k, v in rd.items():
                    if deps.get(k, 0) < v:
                        deps[k] = v
        return deps

    def _wait(s, eng, deps):
        E = s.E[eng]
        wd = s.waited[eng]
        for k, v in deps.items():
            if eng == "pe" and k == "pe":
                continue
            if wd.get(k, 0) < v:
                E.wait_ge(s.sem[k], v)
                wd[k] = v

    def _commit(s, tok, reads, writes):
        for r in reads:
            d = s.readers.setdefault(r, {})
            if d.get(tok[0], 0) < tok[1]:
                d[tok[0]] = tok[1]
        for w in writes:
            s.lastw[w] = tok
            s.readers[w] = {}

    def op(s, eng, reads, writes, fn):
        pr = [r for r in reads if isinstance(r, tuple) and r[0] == "ps"]
        if pr:
            reads = [r for r in reads if r not in pr]
            writes = list(writes) + [r for r in pr if r not in writes]
        s._wait(eng, s._deps(reads, writes))
        ins = fn(s.E[eng])
        s.val[eng] += 1
        ins.then_inc(s.sem[eng], 1)
        tok = (eng, s.val[eng])
        s._commit(tok, reads, writes)
        s.nops += 1
        return tok

    def dma(s, q, out, in_, reads, writes):
        s._wait(q, s._deps(reads, writes))
        keys, idx = s.dpool[q]
        k = keys[idx % len(keys)]
        s.dpool[q][1] = idx + 1
        if s.val[k] > 0 and s.waited[q].get(k, 0) < s.val[k]:
            s.E[q].wait_ge(s.sem[k], s.val[k])
            s.waited[q][k] = s.val[k]
        s.val[k] += 16
        s.E[q].dma_start(out=out, in_=in_).then_inc(s.sem[k], 16)
        tok = (k, s.val[k])
        s._commit(tok, reads, writes)
        return tok

    def barrier(s):
        for e in s.E:
            for k, v in s.val.items():
                if e == "pe" and k == "pe":
                    continue
                if v > 0 and s.waited[e].get(k, 0) < v:
                    s.E[e].wait_ge(s.sem[k], v)
                    s.waited[e][k] = v


class WStream:
    def __init__(s, P, nc, WT, plan):
        s.P = P
        s.WT = WT
        s.slots = [nc.alloc_sbuf_tensor("wring%d" % i, [128, SLOT], BF16) for i in range(NSLOT)]
        s.record = plan is None
        s.plan = [] if plan is None else plan
        s.issued = 0
        s.taken = 0

    def _issue(s, n):
        name, idx, kc, ncols = s.plan[n]
        ap = s.WT[name][idx]
        i = n % NSLOT
        dst = s.slots[i][:, 0:kc * ncols].rearrange("p (c n) -> p c n", c=kc)
        s.P.dma("pool", dst, ap.rearrange("(c p) n -> p c n", p=128), reads=[], writes=[("wr", i)])

    def take(s, name, idx, kc, ncols):
        assert kc * ncols <= SLOT
        n = s.taken
        if s.record:
            s.plan.append((name, idx, kc, ncols))
        else:
            assert s.plan[n] == (name, idx, kc, ncols), (n, s.plan[n], name, idx)
            lim = min(len(s.plan), n + NSLOT - 1)
            while s.issued < lim:
                s._issue(s.issued)
                s.issued += 1
        i = n % NSLOT
        s.taken += 1
        view = s.slots[i][:, 0:kc * ncols].rearrange("p (c n) -> p c n", c=kc)
        return view, ("wr", i)


def build_program(cfg, plan=None):
    nc = bass.Bass("TRN2", target_bir_lowering=False)
    P = Prog(nc)
    _uid = [0]

    def sbt(name, shape, dt):
        _uid[0] += 1
        return nc.sbuf_tensor("%s_%d" % (name, _uid[0]), shape, dt)

    def din(name, shape):
        return nc.dram_tensor(name, list(shape), F32, kind="ExternalInput").ap()

    xin = din("xin", [NTOK, D])
    cv = din("cv", [128, KC])
    consts = din("consts", [128, 128])
    layers = cfg.get("layers", list(range(DEPTH)))
    NL = len(layers)
    WT = {}
    ada_w = din("ada_w", [NL, D, 6 * D])
    ada_b = din("ada_b", [NL, 6 * D])
    adab_f = din("adab_f", [128, DEPTH, 96])
    nw_f = din("nw_f", [128, DEPTH, 2, KC])
    ffn_wg = din("ffn_w_gate", [NL, D, FH])
    ffn_wu = din("ffn_w_up", [NL, D, FH])
    ffn_wd = din("ffn_w_down", [NL, FH, D])
    WT.update(ada_w=ada_w, ffn_w_gate=ffn_wg, ffn_w_up=ffn_wu, ffn_w_down=ffn_wd)
    has_ssd = any(l % 3 == 0 for l in layers)
    ssd_js = sorted(set(l // 3 for l in layers if l % 3 == 0))
    if has_ssd:
        NJ = len(ssd_js)
        ssd_win = din("ssd_w_in", [NJ, D, 10368])
        ssd_wout = din("ssd_w_out", [NJ, 4096, D])
        ssd_cw = din("ssd_cw", [128, NJ, 48, 3])
        ssd_cb = din("ssd_cb", [128, NJ, 48])
        ssd_dtb = din("ssd_dtb", [NJ, 128])
        ssd_alog = din("ssd_alog", [NJ, 128])
        ssd_dskip = din("ssd_dskip", [NJ, 64])
        ssd_nw = din("ssd_nw", [NJ, 4096])
        ssd_h0 = din("ssd_h0", [NJ, 2, 4096, 128])
        ssd_consts = din("ssd_consts", [128, 4, 128])
        contd = din("cont", [128, 1])
        st_out = nc.dram_tensor("st_out", [NJ, 8, 2, 4096, 128], F32, kind="ExternalOutput").ap()
        XS = nc.dram_tensor("xs_scr", [8, 16, 128, 512], BF16).ap()
        ZS = nc.dram_tensor("zs_scr", [8, 16, 128, 512], BF16).ap()
        BS = nc.dram_tensor("bs_scr", [8, 16, 128, 128], BF16).ap()
        BTS = nc.dram_tensor("bts_scr", [8, 128, NTOK], BF16).ap()
        CTS = nc.dram_tensor("cts_scr", [8, 128, NTOK], BF16).ap()
        WT.update(ssd_w_in=ssd_win, ssd_w_out=ssd_wout)
    has_ml = 2 in layers
    if has_ml:
        ml_win = din("mlstm_w_in", [D, 6176])
        ml_wout = din("mlstm_w_out", [D, D])
        ml_bg = din("ml_bg", [32])
        ml_hn = din("ml_hn", [8, 256])
        ml_C0 = din("ml_C0", [2, 8, 128, 256])
        ml_n0 = din("ml_n0", [2, 8, 128])
        ml_m0 = din("ml_m0", [128, 16])
        ml_consts = din("ml_consts", [128, 4, 128])
        if not has_ssd:
            ssd_consts = din("ssd_consts", [128, 4, 128])
            contd = din("cont", [128, 1])
        mlC_out = nc.dram_tensor("mlC_out", [8, 2, 8, 128, 256], F32, kind="ExternalOutput").ap()
        mln_out = nc.dram_tensor("mln_out", [8, 2, 8, 128], F32, kind="ExternalOutput").ap()
        mlm_out = nc.dram_tensor("mlm_out", [1, 128], F32, kind="ExternalOutput").ap()
        QTS = nc.dram_tensor("qts_scr", [8, 128, NTOK], BF16).ap()
        KTS = nc.dram_tensor("kts_scr", [8, 128, NTOK], BF16).ap()
        KMS = nc.dram_tensor("kms_scr", [8, 16, 128, 128], BF16).ap()
        VS = nc.dram_tensor("vs_scr", [8, 16, 128, 256], BF16).ap()
        OGS = nc.dram_tensor("ogs_scr", [8, 16, 128, 256], BF16).ap()
        WT.update(mlstm_w_in=ml_win, mlstm_w_out=ml_wout)
    has_attn = 1 in layers
    YT = nc.dram_tensor("ytscr", [32, 128, NTOK], BF16).ap()
    UTS = nc.dram_tensor("utscr", [KC, 128, NTOK], BF16).ap()
    if has_attn:
        attn_wqkv = din("attn_w_qkv", [D, 3072])
        attn_wout = din("attn_w_out", [D, D])
        attn_qn = din("attn_qn", [128, 1])
        attn_kn = din("attn_kn", [128, 1])
        attn_knrow = din("attn_knrow", [128])
        ropecos = din("ropecos", [128, NTOK])
        ropesin = din("ropesin", [128, NTOK])
        maskb_d = din("maskb", [128, 160])
        perm_d = din("perm", [128, 128])
        cachek = din("cachek", [512, 512])
        cachev = din("cachev", [512, 512])
        kc_out = nc.dram_tensor("kc_out", [NTOK, 512], F32, kind="ExternalOutput").ap()
        vc_out = nc.dram_tensor("vc_out", [NTOK, 512], F32, kind="ExternalOutput").ap()
        WT.update(attn_w_qkv=attn_wqkv, attn_w_out=attn_wout)
    dbg = nc.dram_tensor("dbg", [128, 8192], F32, kind="ExternalOutput").ap() if cfg.get("debug") else None
    X = nc.dram_tensor("xout", [NTOK, D], F32, kind="ExternalOutput").ap()
    GS = nc.dram_tensor("gscr", [DEPTH, 2, 128, D], F32).ap()

    NTP = cfg.get("ntp", 4)
    W = WStream(P, nc, WT, plan)
    ps = [nc.alloc_psum_tensor("psb%d" % i, [128, 512], F32) for i in range(8)]

    ident_f = nc.alloc_sbuf_tensor("ident_f", [128, 128], F32)
    ident = nc.alloc_sbuf_tensor("ident", [128, 128], BF16)
    modf = nc.alloc_sbuf_tensor("modf", [128, DEPTH, 96], F32)
    adabf = nc.alloc_sbuf_tensor("adabf", [128, DEPTH, 96], F32)
    nwf = nc.alloc_sbuf_tensor("nwf", [128, DEPTH, 2, KC], F32)
    amod = nc.alloc_sbuf_tensor("amod", [128, DEPTH, 2, KC], F32)
    cvt = nc.alloc_sbuf_tensor("cvt", [128, KC], F32)
    svb = nc.alloc_sbuf_tensor("svb", [128, KC], BF16)
    ss = nc.alloc_sbuf_tensor("ss", [128, 8], F32)
    rstd = nc.alloc_sbuf_tensor("rstd", [128, 8], F32)
    epsb = nc.alloc_sbuf_tensor("epsb", [128, 1], F32)
    P.op("dve", [], ["epsb"], lambda E: E.memset(epsb[:], EPS))

    P.dma("sp", ident_f[:], consts[:, :], [], ["ident_f"])
    P.dma("sp", adabf[:], adab_f[:, :, :], [], ["adabf"])
    P.dma("sp", nwf[:], nw_f[:, :, :, :], [], ["nwf"])
    P.dma("sp", cvt[:], cv[:, :], [], ["cvt"])
    P.op("dve", ["ident_f"], ["ident"], lambda E: E.tensor_copy(out=ident[:], in_=ident_f[:]))
    P.op("act", ["cvt"], ["svb"], lambda E: E.activation(out=svb[:], in_=cvt[:], func=AF.Silu))
    with sbt("svrep", [128, KC, 128], BF16) as svrep, sbt("gtile0", [128, 512], F32) as gt0, sbt("gtile1", [128, 512], F32) as gt1, \
            sbt("abrow0", [128, 512], F32) as ab0, sbt("abrow1", [128, 512], F32) as ab1:
        P.op("dve", ["svb"], ["svrep"], lambda E: E.tensor_copy(
            out=svrep[:], in_=svb[:].unsqueeze(2).to_broadcast([128, KC, 128])))
        gts = [gt0, gt1]
        abs_ = [ab0, ab1]
        nrow = 0
        for l in layers:
            li = layers.index(l)
            for cb in range(24):
                blk, wk = W.take("ada_w", (li, slice(None), slice(cb * 512, (cb + 1) * 512)), KC, 512)
                pb = ps[cb % 2]
                pk = ("ps", cb % 2)
                if (cb % 12) < 8:
                    def mmf(E, blk=blk, pb=pb):
                        ins = None
                        for ft in range(4):
                            for k in range(KC):
                                ins = E.matmul(pb[:, ft:ft + 1], lhsT=blk[:, k, ft * 128:(ft + 1) * 128],
                                               rhs=svb[:, k:k + 1], start=(k == 0), stop=(k == KC - 1))
                        return ins
                    P.op("pe", [wk, "svb"], [pk], mmf)
                    P.op("dve", [pk, "adabf"], ["modf"], lambda E, pb=pb, l=l, cb=cb: E.tensor_tensor(
                        out=modf[:, l, cb * 4:(cb + 1) * 4], in0=pb[:, 0:4], in1=adabf[:, l, cb * 4:(cb + 1) * 4],
                        op=ALU.add))
                else:
                    which = 0 if cb < 12 else 1
                    c0 = (cb % 12 - 8) * 512
                    j = nrow % 2
                    nrow += 1
                    P.dma("sp", abs_[j][:], ada_b[li, cb * 512:(cb + 1) * 512].partition_broadcast(128),
                          [], [("abrow", j)])

                    def mmr(E, blk=blk, pb=pb):
                        ins = None
                        for k in range(KC):
                            ins = E.matmul(pb[:, :], lhsT=svrep[:, k, :], rhs=blk[:, k, :],
                                           start=(k == 0), stop=(k == KC - 1))
                        return ins
                    P.op("pe", [wk, "svrep"], [pk], mmr)
                    P.op("dve", [pk, ("abrow", j)], [("gtile", j)], lambda E, pb=pb, j=j: E.tensor_tensor(
                        out=gts[j][:], in0=pb[:, :], in1=abs_[j][:], op=ALU.add))
                    P.dma("sp", GS[l, which, :, c0:c0 + 512], gts[j][:], [("gtile", j)], [("gs", l, which)])
            for wh in range(2):
                P.op("dve", ["modf", "nwf"], [("amod", l, wh)], lambda E, l=l, wh=wh: E.scalar_tensor_tensor(
                    out=amod[:, l, wh, :], in0=modf[:, l, wh * 48 + 16:wh * 48 + 32], scalar=1.0,
                    in1=nwf[:, l, wh, :], op0=ALU.add, op1=ALU.mult))
        P.barrier()
    if dbg is not None:
        P.dma("sp", dbg[:, 0:DEPTH * 96], modf[:].rearrange("p l j -> p (l j)"), ["modf"], ["dbg0"])
        P.dma("sp", dbg[:, 512:512 + DEPTH * 2 * KC], amod[:].rearrange("p l w c -> p (l w c)"), [], ["dbg1"])

    def norm_mod(l, wh, src, i, xt, xk, j, xn, UT, utk, col0):
        P.dma("sp", xt[:], src[i * 128:(i + 1) * 128, :], [("X", i)], [xk])
        P.op("act", [xk], ["xn", ("ss", j)], lambda E: E.activation(
            out=xn[:], in_=xt[:], func=AF.Square, accum_out=ss[:, j:j + 1]))
        P.op("act", [("ss", j)], [("rstd", j)], lambda E: E.activation(
            out=rstd[:, j:j + 1], in_=ss[:, j:j + 1], func=AF.Sqrt, scale=1.0 / D, bias=epsb[:, 0:1]))
        P.op("dve", [("rstd", j)], [("rstd", j)], lambda E: E.reciprocal(out=rstd[:, j:j + 1], in_=rstd[:, j:j + 1]))
        P.op("act", [xk, ("rstd", j)], ["xn"], lambda E: E.mul(out=xn[:], in_=xt[:], mul=rstd[:, j:j + 1]))
        for c4 in range(4):
            pb = ps[4 + (c4 % 2)]
            pk = ("ps", 4 + (c4 % 2))
            pbb = pb.bitcast(BF16)

            def tr(E, c4=c4, pbb=pbb):
                ins = None
                for cc in range(4):
                    c = c4 * 4 + cc
                    ins = E.transpose(out=pbb[:, cc * 128:(cc + 1) * 128], in_=xn[:, c * 128:(c + 1) * 128],
                                      identity=ident[:])
                return ins
            P.op("pe", ["xn", "ident"], [pk], tr)
            for cc in range(4):
                c = c4 * 4 + cc
                last = (cc == 3)
                P.op("dve", [pk, ("amod", l, wh), "modf"], [utk] if not last else [utk],
                     lambda E, c=c, cc=cc, pbb=pbb: E.tensor_scalar(
                         out=UT[:, c, col0:col0 + 128], in0=pbb[:, cc * 128:(cc + 1) * 128],
                         scalar1=amod[:, l, wh, c:c + 1], scalar2=modf[:, l, wh * 48 + c:wh * 48 + c + 1],
                         op0=ALU.mult, op1=ALU.add))

    def ffn_phase(l, src):
        li = layers.index(l)
        with sbt("xt0", [128, D], F32) as xt0, sbt("xt1", [128, D], F32) as xt1, \
                sbt("xt2", [128, D], F32) as xt2, sbt("xt3", [128, D], F32) as xt3, \
                sbt("xn", [128, D], BF16) as xn, \
                sbt("UT", [128, KC, 512], BF16) as UT, sbt("HT", [128, HC, 512], BF16) as HT, \
                sbt("g2", [128, D], F32) as g2, sbt("sg0", [128, 512], BF16) as sg0, \
                sbt("sg1", [128, 512], BF16) as sg1, sbt("tmp0", [128, 512], F32) as tmp0, \
                sbt("tmp1", [128, 512], F32) as tmp1:
            xts = [xt0, xt1, xt2, xt3]
            sgs = [sg0, sg1]
            tmps = [tmp0, tmp1]
            P.dma("sp", g2[:], GS[l, 1, :, :], [("gs", l, 1)], ["g2"])
            nsg = 0
            ntmp = 0
            for tp in range(NTP):
                for tt in range(4):
                    i = tp * 4 + tt
                    norm_mod(l, 1, src, i, xts[tt], ("xt", tt), tt, xn, UT, ("UT", tt), tt * 128)
                utks = [("UT", tt) for tt in range(4)]
                for hb in range(11):
                    hsl = (li, slice(None), slice(hb * 512, (hb + 1) * 512))
                    wg, wgk = W.take("ffn_w_gate", hsl, KC, 512)
                    wu, wuk = W.take("ffn_w_up", hsl, KC, 512)
                    for ft in range(4):
                        hi = hb * 4 + ft
                        pg = ps[hi % 2]
                        pgk = ("ps", hi % 2)
                        pu = ps[2 + hi % 2]
                        puk = ("ps", 2 + hi % 2)

                        def mmg(E, w=wg, pb=pg, ft=ft):
                            ins = None
                            for k in range(KC):
                                ins = E.matmul(pb[:, :], lhsT=w[:, k, ft * 128:(ft + 1) * 128], rhs=UT[:, k, :],
                                               start=(k == 0), stop=(k == KC - 1))
                            return ins
                        P.op("pe", [wgk] + utks, [pgk], mmg)
                        P.op("pe", [wuk] + utks, [puk], lambda E, w=wu, pb=pu, ft=ft: mmg(E, w, pb, ft))
                        sj = nsg % 2
                        nsg += 1
                        P.op("act", [pgk], [("sg", sj)], lambda E, pb=pg, sj=sj: E.activation(
                            out=sgs[sj][:], in_=pb[:, :], func=AF.Silu))
                        P.op("dve", [("sg", sj), puk], [("HT", hi)], lambda E, pb=pu, sj=sj, hi=hi: E.tensor_tensor(
                            out=HT[:, hi, :], in0=sgs[sj][:], in1=pb[:, :], op=ALU.mult))
                        if dbg is not None and tp == 0 and hi == 0:
                            with sbt("dbgu", [128, 1536], F32) as dbgu:
                                P.op("dve", [pgk], ["dbgu"], lambda E: E.tensor_copy(out=dbgu[:, 0:512], in_=pg[:, :]))
                                P.op("dve", [("sg", sj)], ["dbgu"], lambda E: E.tensor_copy(out=dbgu[:, 512:1024], in_=sgs[sj][:]))
                                P.op("dve", [puk], ["dbgu"], lambda E: E.tensor_copy(out=dbgu[:, 1024:1536], in_=pu[:, :]))
                                P.dma("sp", dbg[:, 6144:7680], dbgu[:], ["dbgu"], ["dbg4"])
                                P.barrier()
                htks = [("HT", hi) for hi in range(HC)]
                if dbg is not None and tp == 0:
                    with sbt("dbgt", [128, 2048], F32) as dbgt:
                        P.op("dve", utks, ["dbgt"], lambda E: E.tensor_copy(out=dbgt[:, 0:512], in_=UT[:, 0, :]))
                        P.op("dve", utks, ["dbgt"], lambda E: E.tensor_copy(out=dbgt[:, 512:1024], in_=UT[:, 5, :]))
                        P.op("dve", htks, ["dbgt"], lambda E: E.tensor_copy(out=dbgt[:, 1024:1536], in_=HT[:, 0, :]))
                        P.op("dve", htks, ["dbgt"], lambda E: E.tensor_copy(out=dbgt[:, 1536:2048], in_=HT[:, 17, :]))
                        P.dma("sp", dbg[:, 1024:3072], dbgt[:], ["dbgt"], ["dbg2"])
                        P.dma("sp", dbg[:, 4096:6144], g2[:], ["g2"], ["dbg3"])
                        P.barrier()
                for cb in range(4):
                    base = 0 if cb % 2 == 0 else 4
                    for k0, kn in KSPLIT:
                        wd, wdk = W.take("ffn_w_down", (li, slice(k0 * 128, (k0 + kn) * 128),
                                                        slice(cb * 512, (cb + 1) * 512)), kn, 512)
                        for tt in range(4):
                            pb = ps[base + tt]
                            pk = ("ps", base + tt)

                            def mmd(E, w=wd, pb=pb, tt=tt, k0=k0, kn=kn):
                                ins = None
                                for k in range(kn):
                                    ins = E.matmul(pb[:, :], lhsT=HT[:, k0 + k, tt * 128:(tt + 1) * 128],
                                                   rhs=w[:, k, :], start=(k0 + k == 0),
                                                   stop=(k0 + k == HC - 1))
                                return ins
                            P.op("pe", [wdk] + htks, [pk], mmd)
                    for tt in range(4):
                        pb = ps[base + tt]
                        pk = ("ps", base + tt)
                        tj = ntmp % 2
                        ntmp += 1
                        P.op("dve", [pk, "g2"], [("tmp", tj)], lambda E, pb=pb, tj=tj, cb=cb: E.tensor_tensor(
                            out=tmps[tj][:], in0=pb[:, :], in1=g2[:, cb * 512:(cb + 1) * 512], op=ALU.mult))
                        P.op("pool", [("tmp", tj), ("xt", tt)], [("xt", tt)],
                             lambda E, tj=tj, tt=tt, cb=cb: E.tensor_tensor(
                                 out=xts[tt][:, cb * 512:(cb + 1) * 512], in0=xts[tt][:, cb * 512:(cb + 1) * 512],
                                 in1=tmps[tj][:], op=ALU.add))
                for tt in range(4):
                    i = tp * 4 + tt
                    P.dma("sp", X[i * 128:(i + 1) * 128, :], xts[tt][:], [("xt", tt)], [("X", i)])
            P.barrier()


    def outproj_phase(l, src, Kc, wname, widx=None):
        with sbt("xt0", [128, D], F32) as xt0, sbt("xt1", [128, D], F32) as xt1, \
                sbt("xt2", [128, D], F32) as xt2, sbt("xt3", [128, D], F32) as xt3, \
                sbt("AT", [128, Kc, 512], BF16) as AT, sbt("g1", [128, D], F32) as g1, \
                sbt("tmp0", [128, 512], F32) as tmp0, sbt("tmp1", [128, 512], F32) as tmp1:
            xts = [xt0, xt1, xt2, xt3]
            tmps = [tmp0, tmp1]
            P.dma("sp", g1[:], GS[l, 0, :, :], [("gs", l, 0)], ["g1"])
            ntmp = 0
            for tp in range(4):
                for tt in range(4):
                    i = tp * 4 + tt
                    P.dma("sp", xts[tt][:], src[i * 128:(i + 1) * 128, :], [("X", i)], [("xt", tt)])
                P.dma("sp", AT[:], YT[0:Kc, :, tp * 512:(tp + 1) * 512].rearrange("c p n -> p c n"), [], ["AT"])
                for cb in range(4):
                    base = 0 if cb % 2 == 0 else 4
                    for kb in range(Kc // 16):
                        wsl = (slice(kb * 2048, (kb + 1) * 2048), slice(cb * 512, (cb + 1) * 512))
                        if widx is not None:
                            wsl = (ssd_js.index(widx),) + wsl
                        w, wk = W.take(wname, wsl, 16, 512)
                        for tt in range(4):
                            pb = ps[base + tt]

                            def mmo(E, w=w, pb=pb, tt=tt, kb=kb):
                                ins = None
                                for k in range(16):
                                    ins = E.matmul(pb[:, :], lhsT=AT[:, kb * 16 + k, tt * 128:(tt + 1) * 128],
                                                   rhs=w[:, k, :], start=(kb == 0 and k == 0),
                                                   stop=(kb == Kc // 16 - 1 and k == 15))
                                return ins
                            P.op("pe", [wk, "AT"], [("ps", base + tt)], mmo)
                    for tt in range(4):
                        pb = ps[base + tt]
                        pk = ("ps", base + tt)
                        tj = ntmp % 2
                        ntmp += 1
                        P.op("dve", [pk, "g1"], [("tmp", tj)], lambda E, pb=pb, tj=tj, cb=cb: E.tensor_tensor(
                            out=tmps[tj][:], in0=pb[:, :], in1=g1[:, cb * 512:(cb + 1) * 512], op=ALU.mult))
                        P.op("pool", [("tmp", tj), ("xt", tt)], [("xt", tt)],
                             lambda E, tj=tj, tt=tt, cb=cb: E.tensor_tensor(
                                 out=xts[tt][:, cb * 512:(cb + 1) * 512], in0=xts[tt][:, cb * 512:(cb + 1) * 512],
                                 in1=tmps[tj][:], op=ALU.add))
                for tt in range(4):
                    i = tp * 4 + tt
                    P.dma("sp", X[i * 128:(i + 1) * 128, :], xts[tt][:], [("xt", tt)], [("X", i)])
            P.barrier()

    def attn_phase(l, src):
        SC = 128.0 ** -0.5
        with ExitStack() as st:
            UT = st.enter_context(sbt("UTa", [128, KC, 512], BF16))
            KT = st.enter_context(sbt("KT", [128, 4, 2560], BF16))
            VA = st.enter_context(sbt("VA", [128, 20, 512], BF16))
            cosT = st.enter_context(sbt("cosT", [128, 512], F32))
            sinT = st.enter_context(sbt("sinT", [128, 512], F32))
            maskb = st.enter_context(sbt("maskb_s", [128, 160], F32))
            permf = st.enter_context(sbt("permf", [128, 128], F32))
            permb = st.enter_context(sbt("permb", [128, 128], BF16))
            onesb = st.enter_context(sbt("onesb", [128, 128], BF16))
            qnw = st.enter_context(sbt("qnw", [128, 2], F32))
            knrow = st.enter_context(sbt("knrow", [128, 128], F32))
            sqb = st.enter_context(sbt("sqb", [128, 512], BF16))
            rt = st.enter_context(sbt("rt", [128, 512], F32))
            qn = st.enter_context(sbt("qn", [128, 512], BF16))
            t1 = st.enter_context(sbt("t1", [128, 512], F32))
            t2 = st.enter_context(sbt("t2", [128, 512], F32))
            QT0 = st.enter_context(sbt("QT0", [128, 512], BF16))
            QT1 = st.enter_context(sbt("QT1", [128, 512], BF16))
            PT0 = st.enter_context(sbt("PT0", [128, 256], BF16))
            PT1 = st.enter_context(sbt("PT1", [128, 256], BF16))
            PT2 = st.enter_context(sbt("PT2", [128, 256], BF16))
            rs = st.enter_context(sbt("rs", [128, 256], F32))
            AO0 = st.enter_context(sbt("AO0", [128, 256], BF16))
            AO1 = st.enter_context(sbt("AO1", [128, 256], BF16))
            kf = st.enter_context(sbt("kf", [128, 512], F32))
            vf = st.enter_context(sbt("vf", [128, 512], F32))
            ssk = st.enter_context(sbt("ssk", [128, 4], F32))
            ckf = st.enter_context(sbt("ckf", [128, 512], F32))
            ckb = st.enter_context(sbt("ckb", [128, 512], BF16))
            QTs = [QT0, QT1]
            PTs = [PT0, PT1, PT2]
            AOs = [AO0, AO1]
            P.dma("sp", maskb[:], maskb_d[:, :], [], ["maskb"])
            P.dma("sp", permf[:], perm_d[:, :], [], ["permf"])
            P.dma("sp", qnw[:, 0:1], attn_qn[:, :], [], ["qnw"])
            P.dma("sp", qnw[:, 1:2], attn_kn[:, :], ["qnw"], ["qnw"])
            P.dma("sp", knrow[:], attn_knrow.partition_broadcast(128), [], ["knrow"])
            P.op("dve", ["permf"], ["permb"], lambda E: E.tensor_copy(out=permb[:], in_=permf[:]))
            P.op("dve", [], ["onesb"], lambda E: E.memset(onesb[:], 1.0))
            P.op("dve", ["qnw"], ["qnw"], lambda E: E.tensor_scalar_mul(out=qnw[:, 0:1], in0=qnw[:, 0:1], scalar1=SC))
            def head_fm(w, wk, c0, tt, widx, dst, dstk):
                def mmh(E):
                    ins = None
                    for k in range(KC):
                        ins = E.matmul(ps[0][:, :], lhsT=w[:, k, c0:c0 + 128], rhs=UT[:, k, tt * 512:(tt + 1) * 512],
                                       start=(k == 0), stop=(k == KC - 1))
                    return ins
                P.op("pe", [wk] + utk, [("ps", 0)], mmh)
                P.op("act", [("ps", 0)], ["sqb"], lambda E: E.activation(out=sqb[:], in_=ps[0][:, :], func=AF.Square))
                P.op("pe", ["sqb", "onesb"], [("ps", 1)], lambda E: E.matmul(
                    ps[1][:, :], lhsT=onesb[:], rhs=sqb[:], start=True, stop=True))
                P.op("act", [("ps", 1)], ["rt"], lambda E: E.activation(
                    out=rt[:], in_=ps[1][:, :], func=AF.Sqrt, scale=1.0 / 128, bias=epsb[:, 0:1]))
                P.op("dve", ["rt"], ["rt"], lambda E: E.reciprocal(out=rt[:], in_=rt[:]))
                P.op("dve", [("ps", 0), "rt", "qnw"], ["qn"], lambda E: E.scalar_tensor_tensor(
                    out=qn[:], in0=ps[0][:, :], scalar=qnw[:, widx:widx + 1], in1=rt[:], op0=ALU.mult, op1=ALU.mult))
                P.op("pe", ["qn", "permb"], [("ps", 2)], lambda E: E.matmul(
                    ps[2][:, :], lhsT=permb[:], rhs=qn[:], start=True, stop=True))
                P.op("dve", [("ps", 2), "sinT"], ["t1"], lambda E: E.tensor_tensor(
                    out=t1[:], in0=ps[2][:, :], in1=sinT[:], op=ALU.mult))
                P.op("pool", ["qn", "cosT"], ["t2"], lambda E: E.tensor_tensor(
                    out=t2[:], in0=qn[:], in1=cosT[:], op=ALU.mult))
                P.op("dve", ["t1", "t2"], [dstk], lambda E: E.tensor_tensor(out=dst, in0=t1[:], in1=t2[:], op=ALU.add))

            def load_tables(tt):
                P.dma("sp", cosT[:], ropecos[:, tt * 512:(tt + 1) * 512], [], ["cosT"])
                P.dma("sp", sinT[:], ropesin[:, tt * 512:(tt + 1) * 512], [], ["sinT"])

            utk = [("UT", j) for j in range(4)]
            wkk, wkkk = W.take("attn_w_qkv", (slice(None), slice(2048, 2560)), KC, 512)
            wv, wvk = W.take("attn_w_qkv", (slice(None), slice(2560, 3072)), KC, 512)
            with sbt("xta", [128, D], F32) as xta, sbt("xtb", [128, D], F32) as xtb, \
                    sbt("xn", [128, D], BF16) as xn:
                xtl = [xta, xtb]
                for tt in range(4):
                    load_tables(tt)
                    for i4 in range(4):
                        i = tt * 4 + i4
                        norm_mod(l, 0, src, i, xtl[i % 2], ("xt", i % 2), i % 2, xn, UT, ("UT", i4), i4 * 128)
                    P.dma("sp", UTS[:, :, tt * 512:(tt + 1) * 512].rearrange("c p n -> p c n"), UT[:], utk, [("UTS", tt)])
                    for i4 in range(4):
                        i = tt * 4 + i4
                        pkb, pvb = ps[(i % 2) * 2], ps[(i % 2) * 2 + 1]
                        pkk, pvk = ("ps", (i % 2) * 2), ("ps", (i % 2) * 2 + 1)

                        def mmt(E, w, pb, i4=i4):
                            ins = None
                            for k in range(KC):
                                ins = E.matmul(pb[:, :], lhsT=UT[:, k, i4 * 128:(i4 + 1) * 128], rhs=w[:, k, :],
                                               start=(k == 0), stop=(k == KC - 1))
                            return ins
                        P.op("pe", [wkkk] + utk, [pkk], lambda E, pb=pkb, mmt=mmt: mmt(E, wkk, pb))
                        P.op("pe", [wvk] + utk, [pvk], lambda E, pb=pvb, mmt=mmt: mmt(E, wv, pb))
                        for hh in range(4):
                            P.op("act", [pkk], ["sqb", ("ssk", hh)], lambda E, pb=pkb, hh=hh: E.activation(
                                out=sqb[:, 0:128], in_=pb[:, hh * 128:(hh + 1) * 128], func=AF.Square,
                                accum_out=ssk[:, hh:hh + 1]))
                        sskk = [("ssk", hh) for hh in range(4)]
                        P.op("act", sskk, sskk, lambda E: E.activation(
                            out=ssk[:, :], in_=ssk[:, :], func=AF.Sqrt, scale=1.0 / 128, bias=epsb[:, 0:1]))
                        P.op("dve", sskk, sskk, lambda E: E.reciprocal(out=ssk[:, :], in_=ssk[:, :]))
                        for hh in range(4):
                            P.op("dve", [pkk, ("ssk", hh), "knrow"], ["kf"], lambda E, pb=pkb, hh=hh: E.scalar_tensor_tensor(
                                out=kf[:, hh * 128:(hh + 1) * 128], in0=pb[:, hh * 128:(hh + 1) * 128],
                                scalar=ssk[:, hh:hh + 1], in1=knrow[:], op0=ALU.mult, op1=ALU.mult))
                        P.dma("sp", kc_out[i * 128:(i + 1) * 128, :], kf[:], ["kf"], [("kco", i)])
                        P.op("act", [pvk], ["vf"], lambda E, pb=pvb: E.copy(out=vf[:], in_=pb[:, :]))
                        P.op("dve", [pvk], [("VA", 4 + i)], lambda E, pb=pvb, i=i: E.tensor_copy(out=VA[:, 4 + i, :], in_=pb[:, :]))
                        P.dma("sp", vc_out[i * 128:(i + 1) * 128, :], vf[:], ["vf"], [("vco", i)])
                    for hh in range(4):
                        head_fm(wkk, wkkk, hh * 128, 0, 1, KT[:, hh, 512 + tt * 512:512 + (tt + 1) * 512], ("KT", hh))
                P.barrier()
            for j in range(4):
                P.dma("pool", VA[:, j, :], cachev[j * 128:(j + 1) * 128, :], [], [("VA", j)])
                P.dma("sp", ckf[:], cachek[j * 128:(j + 1) * 128, :], [], ["ckf"])
                P.op("dve", ["ckf"], ["ckb"], lambda E: E.tensor_copy(out=ckb[:], in_=ckf[:]))
                pbb = ps[4].bitcast(BF16)

                def trc(E, pbb=pbb):
                    ins = None
                    for hh in range(4):
                        ins = E.transpose(out=pbb[:, hh * 128:(hh + 1) * 128], in_=ckb[:, hh * 128:(hh + 1) * 128],
                                          identity=ident[:])
                    return ins
                P.op("pe", ["ckb", "ident"], [("ps", 4)], trc)
                for hh in range(4):
                    P.op("dve", [("ps", 4)], [("KT", hh)], lambda E, hh=hh, j=j, pbb=pbb: E.tensor_copy(
                        out=KT[:, hh, j * 128:(j + 1) * 128], in_=pbb[:, hh * 128:(hh + 1) * 128]))

            nq = 0
            npt = 0
            nao = 0
            vak = [("VA", j) for j in range(20)]
            for tt in range(4):
                load_tables(tt)
                P.dma("sp", UT[:], UTS[:, :, tt * 512:(tt + 1) * 512].rearrange("c p n -> p c n"), [("UTS", tt)], utk)
                for qb in range(4):
                    wq, wqk = W.take("attn_w_qkv", (slice(None), slice(qb * 512, (qb + 1) * 512)), KC, 512)
                    for hq in range(4):
                        h = qb * 4 + hq
                        kv = h // 4
                        qj = nq % 2
                        nq += 1
                        head_fm(wq, wqk, hq * 128, 0, 0, QTs[qj][:], ("QT", qj))
                        for sub in range(2):
                            qt8 = tt * 2 + sub
                            SB = [3, 4, 7]

                            def emit_S(kt, kv=kv, qj=qj, sub=sub):
                                bi = SB[kt % 3]
                                P.op("pe", [("KT", kv), ("QT", qj)], [("ps", bi)], lambda E: E.matmul(
                                    ps[bi][:, 0:256], lhsT=KT[:, kv, kt * 128:(kt + 1) * 128],
                                    rhs=QTs[qj][:, sub * 256:(sub + 1) * 256], start=True, stop=True))

                            emit_S(0)
                            emit_S(1)
                            for kt in range(20):
                                if kt + 2 < 20:
                                    emit_S(kt + 2)
                                bi = SB[kt % 3]
                                sb = ps[bi]
                                sk = ("ps", bi)
                                pj = npt % 3
                                npt += 1
                                P.op("act", [sk, "maskb"], [("PT", pj)], lambda E, sb=sb, pj=pj, qt8=qt8, kt=kt: E.activation(
                                    out=PTs[pj][:], in_=sb[:, 0:256], func=AF.Exp,
                                    bias=maskb[:, qt8 * 20 + kt:qt8 * 20 + kt + 1]))
                                P.op("pe", [("PT", pj)] + vak, [("ps", 5), ("ps", 6)],
                                     lambda E, pj=pj, kt=kt, kv=kv: (
                                         E.matmul(ps[5][:, 0:256], lhsT=VA[:, kt, kv * 128:(kv + 1) * 128], rhs=PTs[pj][:],
                                                  start=(kt == 0), stop=(kt == 19)),
                                         E.matmul(ps[6][:, 0:256], lhsT=onesb[:], rhs=PTs[pj][:],
                                                  start=(kt == 0), stop=(kt == 19)))[1])
                            P.op("dve", [("ps", 6)], ["rs"], lambda E: E.reciprocal(out=rs[:], in_=ps[6][:, 0:256]))
                            aj = nao % 2
                            nao += 1
                            P.op("dve", [("ps", 5), "rs"], [("AO", aj)], lambda E, aj=aj: E.tensor_tensor(
                                out=AOs[aj][:], in0=ps[5][:, 0:256], in1=rs[:], op=ALU.mult))
                            P.dma("sp", YT[h, :, qt8 * 256:(qt8 + 1) * 256], AOs[aj][:], [("AO", aj)], [("YT", h, qt8)])
            P.barrier()


    def v3(ap):
        return ap.rearrange("p (r q) -> p r q", q=64)

    def ssd_phase(l, src):
        jj = ssd_js.index(l // 3)
        with ExitStack() as so:
            def sb(name, shape, dt, st=so):
                return st.enter_context(sbt(name, shape, dt))
            DTt = sb("DTt", [128, 16, 128], F32)
            At = sb("At", [128, 16, 128], F32)
            EX = sb("EX", [128, 16, 4, 64], F32)
            ETOT = sb("ETOT", [128, 16, 128], F32)
            cst = sb("ssdc", [128, 4, 128], F32)
            onesf = sb("onesf", [128, 128], F32)
            cont = sb("contt", [128, 1], F32)
            ncont = sb("ncont", [128, 1], F32)
            maskF, maskB, SLf, SLb = cst[:, 0, :], cst[:, 1, :], cst[:, 2, :], cst[:, 3, :]
            P.dma("sp", cst[:], ssd_consts[:, :, :], [], ["ssdc"])
            P.dma("sp", cont[:], contd[:, :], [], ["cont"])
            P.op("dve", [], ["onesf"], lambda E: E.memset(onesf[:], 1.0))
            P.op("dve", ["cont"], ["ncont"], lambda E: E.tensor_scalar_add(out=ncont[:], in0=cont[:], scalar1=-1.0))

            with ExitStack() as s1:
                UT = sb("UTs", [128, KC, NTOK], BF16, s1)
                cw = sb("cw", [128, 48, 3], F32, s1)
                ncw = sb("ncw", [128, 48, 3], F32, s1)
                cbb = sb("cbb", [128, 48], F32, s1)
                dtb = sb("dtb", [128, 128], F32, s1)
                arow = sb("arow", [128, 128], F32, s1)
                P.dma("sp", cw[:], ssd_cw[:, jj, :, :], [], ["cw"])
                P.dma("sp", cbb[:], ssd_cb[:, jj, :], [], ["cbb"])
                P.dma("sp", dtb[:], ssd_dtb[jj, :].partition_broadcast(128), [], ["dtb"])
                P.dma("sp", arow[:], ssd_alog[jj, :].partition_broadcast(128), [], ["arow"])
                P.op("act", ["arow"], ["arow"], lambda E: E.activation(out=arow[:], in_=arow[:], func=AF.Exp))
                P.op("dve", ["arow"], ["arow"], lambda E: E.tensor_scalar_mul(out=arow[:], in0=arow[:], scalar1=-1.0))
                P.op("dve", ["cw", "ncont"], ["ncw"], lambda E: E.tensor_scalar_mul(
                    out=ncw[:].rearrange("p a b -> p (a b)"), in0=cw[:].rearrange("p a b -> p (a b)"), scalar1=ncont[:, 0:1]))
                with ExitStack() as s0:
                    xta = sb("xta", [128, D], F32, s0)
                    xtb = sb("xtb", [128, D], F32, s0)
                    xn = sb("xn", [128, D], BF16, s0)
                    xtl = [xta, xtb]
                    for i in range(16):
                        norm_mod(l, 0, src, i, xtl[i % 2], ("xt", i % 2), i % 2, xn, UT, ("UT", i // 4), i * 128)
                    P.barrier()
                utk = [("UT", q) for q in range(4)]
                with ExitStack() as s2:
                    t_a = sb("t_a", [128, 128], F32, s2)
                    t_b = sb("t_b", [128, 128], F32, s2)
                    t_c = sb("t_c", [128, 128], F32, s2)
                    wdt, wdtk = W.take("ssd_w_in", (jj, slice(None), slice(10240, 10368)), KC, 128)
                    for i in range(16):
                        pb = ps[i % 2]
                        pk = ("ps", i % 2)

                        def mmdt(E, pb=pb, i=i):
                            ins = None
                            for k in range(KC):
                                ins = E.matmul(pb[:, 0:128], lhsT=UT[:, k, i * 128:(i + 1) * 128], rhs=wdt[:, k, :],
                                               start=(k == 0), stop=(k == KC - 1))
                            return ins
                        P.op("pe", [wdtk] + utk, [pk], mmdt)
                        P.op("dve", [pk, "dtb"], ["t_a"], lambda E, pb=pb: E.tensor_tensor(
                            out=t_a[:], in0=pb[:, 0:128], in1=dtb[:], op=ALU.add))
                        P.op("act", ["t_a"], ["t_b"], lambda E: E.activation(out=t_b[:], in_=t_a[:], func=AF.Abs))
                        P.op("act", ["t_b"], ["t_b"], lambda E: E.activation(out=t_b[:], in_=t_b[:], func=AF.Exp, scale=-1.0))
                        P.op("act", ["t_b"], ["t_b"], lambda E: E.activation(out=t_b[:], in_=t_b[:], func=AF.Ln, bias=1.0))
                        P.op("dve", ["t_a"], ["t_c"], lambda E: E.tensor_scalar_max(out=t_c[:], in0=t_a[:], scalar1=0.0))
                        P.op("dve", ["t_b", "t_c"], [("DT", i)], lambda E, i=i: E.tensor_tensor(
                            out=DTt[:, i, :], in0=t_b[:], in1=t_c[:], op=ALU.add))
                        P.op("dve", [("DT", i), "arow"], [("A", i)], lambda E, i=i: E.tensor_tensor(
                            out=At[:, i, :], in0=DTt[:, i, :], in1=arow[:], op=ALU.mult))
                        pc = ps[2 + i % 2]
                        pck = ("ps", 2 + i % 2)

                        def mmcs(E, pc=pc, i=i):
                            E.matmul(pc[:, 0:64], lhsT=maskF, rhs=At[:, i, 0:64], start=True, stop=True)
                            E.matmul(pc[:, 64:128], lhsT=SLf, rhs=At[:, i, 0:64], start=True, stop=True)
                            E.matmul(pc[:, 128:192], lhsT=maskB, rhs=At[:, i, 64:128], start=True, stop=True)
                            E.matmul(pc[:, 192:256], lhsT=SLb, rhs=At[:, i, 64:128], start=True, stop=True)
                            return E.matmul(pc[:, 256:384], lhsT=onesf[:], rhs=At[:, i, :], start=True, stop=True)
                        P.op("pe", [("A", i), "ssdc", "onesf"], [pck], mmcs)
                        P.op("act", [pck], [("EX", i)], lambda E, pc=pc, i=i: E.activation(
                            out=EX[:, i, :, :].rearrange("p a b -> p (a b)"), in_=pc[:, 0:256], func=AF.Exp))
                        P.op("act", [pck], [("ETOT", i)], lambda E, pc=pc, i=i: E.activation(
                            out=ETOT[:, i, :], in_=pc[:, 256:384], func=AF.Exp))
                with ExitStack() as s3:
                    zt0 = sb("zt0", [128, 512], BF16, s3)
                    zt1 = sb("zt1", [128, 512], BF16, s3)
                    zts = [zt0, zt1]
                    nz = 0
                    for zb in range(8):
                        wz, wzk = W.take("ssd_w_in", (jj, slice(None), slice(zb * 512, (zb + 1) * 512)), KC, 512)
                        for i in range(16):
                            pb = ps[nz % 2]
                            pk = ("ps", nz % 2)
                            zj = nz % 2
                            nz += 1

                            def mmz(E, pb=pb, i=i, wz=wz):
                                ins = None
                                for k in range(KC):
                                    ins = E.matmul(pb[:, :], lhsT=UT[:, k, i * 128:(i + 1) * 128], rhs=wz[:, k, :],
                                                   start=(k == 0), stop=(k == KC - 1))
                                return ins
                            P.op("pe", [wzk] + utk, [pk], mmz)
                            P.op("act", [pk], [("zt", zj)], lambda E, pb=pb, zj=zj: E.activation(
                                out=zts[zj][:], in_=pb[:, :], func=AF.Silu))
                            P.dma("sp", ZS[zb, i, :, :], zts[zj][:], [("zt", zj)], [("ZS", zb, i)])
                    raw0 = sb("raw0", [128, NTOK], F32, s3)
                    raw1 = raw0
                    acc = sb("acc", [128, NTOK], F32, s3)
                    xb0 = sb("xb0", [128, NTOK], BF16, s3)
                    xb1 = xb0
                    tr0 = sb("tr0", [128, 512], BF16, s3)
                    tr1 = sb("tr1", [128, 512], BF16, s3)
                    raws = [raw0, raw1]
                    xbs = [xb0, xb1]
                    trs = [tr0, tr1]
                    nft = 0
                    ntr = 0
                    for cb in range(12):
                        wx, wxk = W.take("ssd_w_in", (jj, slice(None), slice(4096 + cb * 512, 4096 + (cb + 1) * 512)), KC, 512)
                        for ft in range(4):
                            ct = cb * 4 + ft
                            rj = 0
                            nft += 1
                            raw = raws[rj]
                            xb = xbs[rj]
                            for tt in range(4):
                                pb = ps[tt % 2]
                                pk = ("ps", tt % 2)

                                def mmx(E, pb=pb, tt=tt, ft=ft, wx=wx):
                                    ins = None
                                    for k in range(KC):
                                        ins = E.matmul(pb[:, :], lhsT=wx[:, k, ft * 128:(ft + 1) * 128],
                                                       rhs=UT[:, k, tt * 512:(tt + 1) * 512], start=(k == 0), stop=(k == KC - 1))
                                    return ins
                                P.op("pe", [wxk] + utk, [pk], mmx)
                                P.op("act", [pk], [("raw", rj)], lambda E, pb=pb, tt=tt, raw=raw: E.copy(
                                    out=raw[:, tt * 512:(tt + 1) * 512], in_=pb[:, :]))
                            rk = ("raw", rj)
                            P.op("dve", [rk, "cw"], ["acc"], lambda E, raw=raw, ct=ct: E.tensor_scalar_mul(
                                out=acc[:], in0=raw[:], scalar1=cw[:, ct, 1:2]))
                            P.op("dve", [rk, "cw", "acc"], ["acc"], lambda E, raw=raw, ct=ct: E.scalar_tensor_tensor(
                                out=acc[:, 1:NTOK], in0=raw[:, 0:NTOK - 1], scalar=cw[:, ct, 0:1], in1=acc[:, 1:NTOK],
                                op0=ALU.mult, op1=ALU.add))
                            P.op("dve", [rk, "cw", "acc"], ["acc"], lambda E, raw=raw, ct=ct: E.scalar_tensor_tensor(
                                out=acc[:, 0:NTOK - 1], in0=raw[:, 1:NTOK], scalar=cw[:, ct, 2:3], in1=acc[:, 0:NTOK - 1],
                                op0=ALU.mult, op1=ALU.add))
                            accv = acc[:].rearrange("p (s t) -> p s t", t=256)
                            rawv = raw[:].rearrange("p (s t) -> p s t", t=256)
                            P.op("dve", [rk, "ncw", "acc"], ["acc"], lambda E, accv=accv, rawv=rawv, ct=ct: E.scalar_tensor_tensor(
                                out=accv[:, 1:8, 0], in0=rawv[:, 0:7, 255], scalar=ncw[:, ct, 0:1], in1=accv[:, 1:8, 0],
                                op0=ALU.mult, op1=ALU.add))
                            P.op("dve", [rk, "ncw", "acc"], ["acc"], lambda E, accv=accv, rawv=rawv, ct=ct: E.scalar_tensor_tensor(
                                out=accv[:, 0:7, 255], in0=rawv[:, 1:8, 0], scalar=ncw[:, ct, 2:3], in1=accv[:, 0:7, 255],
                                op0=ALU.mult, op1=ALU.add))
                            P.op("act", ["acc", "cbb"], [("xb", rj)], lambda E, xb=xb, ct=ct: E.activation(
                                out=xb[:], in_=acc[:], func=AF.Silu, bias=cbb[:, ct:ct + 1]))
                            xk = ("xb", rj)
                            if ct >= 40:
                                P.dma("sp", CTS[ct - 40, :, :], xb[:], [xk], [("CTS", ct)])
                                continue
                            if ct >= 32:
                                P.dma("sp", BTS[ct - 32, :, :], xb[:], [xk], [("BTS", ct)])
                            for i4 in range(4):
                                pbb = ps[4 + i4 % 2].bitcast(BF16)
                                pk = ("ps", 4 + i4 % 2)

                                def trx(E, pbb=pbb, i4=i4, xb=xb):
                                    ins = None
                                    for q in range(4):
                                        i = i4 * 4 + q
                                        ins = E.transpose(out=pbb[:, q * 128:(q + 1) * 128], in_=xb[:, i * 128:(i + 1) * 128],
                                                          identity=ident[:])
                                    return ins
                                P.op("pe", [xk, "ident"], [pk], trx)
                                tj = ntr % 2
                                ntr += 1
                                P.op("act", [pk], [("tr", tj)], lambda E, pbb=pbb, tj=tj: E.copy(out=trs[tj][:], in_=pbb[:, 0:512]))
                                if ct < 32:
                                    g, q4 = ct // 4, ct % 4
                                    dst = XS[g, i4 * 4:(i4 + 1) * 4, :, q4 * 128:(q4 + 1) * 128].rearrange("i p n -> p i n")
                                else:
                                    dst = BS[ct - 32, i4 * 4:(i4 + 1) * 4, :, :].rearrange("i p n -> p i n")
                                P.dma("sp", dst, trs[tj][:].rearrange("p (i n) -> p i n", n=128), [("tr", tj)], [("XSw", ct, i4)])
                P.barrier()

            if cfg.get("ssd_s2", True) is False:
                return
            with ExitStack() as s4:
                xg = sb("xg", [128, 16, 512], BF16, s4)
                Bg = sb("Bg", [128, 16, 128], BF16, s4)
                BT = sb("BT", [128, NTOK], BF16, s4)
                CT = sb("CT", [128, NTOK], BF16, s4)
                szs = [sb("sz0", [128, 512], BF16, s4), sb("sz1", [128, 512], BF16, s4)]
                Y = sb("Y", [128, 16, 512], F32, s4)
                hs = [sb("hf", [128, 512], F32, s4), sb("hb", [128, 512], F32, s4)]
                h16 = sb("h16", [128, 512], BF16, s4)
                xdt = [sb("xdtf", [128, 512], BF16, s4), sb("xdtb", [128, 512], BF16, s4)]
                xdd = [sb("xddf", [128, 512], BF16, s4), sb("xddb", [128, 512], BF16, s4)]
                xdd2 = [xdd[0], sb("xddf2", [128, 512], BF16, s4)]
                CBm = [sb("CBf", [128, 128], F32, s4), sb("CBb", [128, 128], F32, s4)]
                lh = [sb("lh0", [128, 128], F32, s4), sb("lh1", [128, 128], F32, s4),
                      sb("lh2", [128, 128], F32, s4), sb("lh3", [128, 128], F32, s4)]
                LT = sb("LT", [128, 1024], F32, s4)
                MT = [sb("MTf", [128, 8, 128], BF16, s4), sb("MTb", [128, 8, 128], BF16, s4)]
                tmpy = sb("tmpy", [128, 512], F32, s4)
                dx = sb("dx", [128, 512], F32, s4)
                G = sb("G", [128, 512], F32, s4)
                yn = sb("yn", [128, 512], BF16, s4)
                ynT = sb("ynT", [128, 512], BF16, s4)
                drow = sb("drow", [128, 64], F32, s4)
                nwrow = sb("nwrow", [128, 512], F32, s4)
                ssq = sb("ssq", [128, 2], F32, s4)
                h0t = sb("h0t", [128, 128], F32, s4)
                hot = sb("hot", [128, 128], F32, s4)
                P.dma("sp", drow[:], ssd_dskip[jj, :].partition_broadcast(128), [], ["drow"])
                dsel = [(0, maskF, SLf), (1, maskB, SLb)]
                nlh = [0]
                for g in range(cfg.get("ssd_groups", 8)):
                    P.dma("sp", xg[:], XS[g, :, :, :].rearrange("i p n -> p i n"), [], ["xg"])
                    P.dma("sp", Bg[:], BS[g, :, :, :].rearrange("i p n -> p i n"), [], ["Bg"])
                    P.dma("sp", BT[:], BTS[g, :, :], [], ["BT"])
                    P.dma("sp", CT[:], CTS[g, :, :], [], ["CT"])
                    P.dma("sp", nwrow[:], ssd_nw[jj, g * 512:(g + 1) * 512].partition_broadcast(128), [], ["nwrow"])
                    for d_ in range(2):
                        for q in range(4):
                            P.dma("sp", h0t[:], ssd_h0[jj, d_, g * 512 + q * 128:g * 512 + (q + 1) * 128, :], [], ["h0t"])
                            P.op("pe", ["h0t", "ident_f"], [("ps", 7)], lambda E: E.transpose(
                                out=ps[7][:, 0:128], in_=h0t[:], identity=ident_f[:]))
                            P.op("dve", [("ps", 7)], [("h", d_)], lambda E, d_=d_, q=q: E.tensor_copy(
                                out=hs[d_][:, q * 128:(q + 1) * 128], in_=ps[7][:, 0:128]))

                    def emit_state(d_, seg):
                        for q in range(4):
                            P.op("pe", [("h", d_), "ident_f"], [("ps", 7)], lambda E, q=q: E.transpose(
                                out=ps[7][:, 0:128], in_=hs[d_][:, q * 128:(q + 1) * 128], identity=ident_f[:]))
                            P.op("act", [("ps", 7)], ["hot"], lambda E: E.copy(out=hot[:], in_=ps[7][:, 0:128]))
                            P.dma("sp", st_out[jj, seg, d_, g * 512 + q * 128:g * 512 + (q + 1) * 128, :], hot[:],
                                  ["hot"], [("sto", seg, d_, g, q)])

                    def xprep(c, d_, eng, par=0, do_dd=True):
                        hsl = slice(d_ * 64 + g * 8, d_ * 64 + g * 8 + 8)
                        P.op(eng, ["xg", ("DT", c)], [("xdt", d_)], lambda E: E.tensor_tensor(
                            out=v3(xdt[d_][:]), in0=v3(xg[:, c, :]), in1=DTt[:, c, hsl].unsqueeze(2).to_broadcast([128, 8, 64]),
                            op=ALU.mult))
                        if not do_dd:
                            return
                        esl = EX[:, c, 1 if d_ == 0 else 3, g * 8:g * 8 + 8]
                        xd = xdd2[par] if d_ == 0 else xdd[1]
                        P.op(eng, [("xdt", d_), ("EX", c)], [("xdd", d_, par)], lambda E: E.tensor_tensor(
                            out=v3(xd[:]), in0=v3(xdt[d_][:]), in1=esl.unsqueeze(2).to_broadcast([128, 8, 64]), op=ALU.mult))

                    def state_step(c, d_, par=0):
                        hk = ("h", d_)
                        P.op("act", [hk], ["h16"], lambda E: E.copy(out=h16[:], in_=hs[d_][:]))
                        P.op("pe", ["CT", "h16"], [("ps", 6)], lambda E: E.matmul(
                            ps[6][:, :], lhsT=CT[:, c * 128:(c + 1) * 128], rhs=h16[:], start=True, stop=True))
                        esl = EX[:, c, 0 if d_ == 0 else 2, g * 8:g * 8 + 8]
                        P.op("dve", [("ps", 6), ("EX", c)], ["tmpy"], lambda E: E.tensor_tensor(
                            out=v3(tmpy[:]), in0=v3(ps[6][:, :]), in1=esl.unsqueeze(2).to_broadcast([128, 8, 64]), op=ALU.mult))
                        xd = xdd2[par] if d_ == 0 else xdd[1]
                        P.op("pe", ["Bg", ("xdd", d_, par)], [("ps", 7)], lambda E: E.matmul(
                            ps[7][:, :], lhsT=Bg[:, c, :], rhs=xd[:], start=True, stop=True))
                        tsl = ETOT[:, c, d_ * 64 + g * 8:d_ * 64 + g * 8 + 8]
                        P.op("dve", [hk, ("ETOT", c)], [hk], lambda E: E.tensor_tensor(
                            out=v3(hs[d_][:]), in0=v3(hs[d_][:]), in1=tsl.unsqueeze(2).to_broadcast([128, 8, 64]), op=ALU.mult))
                        P.op("dve", [hk, ("ps", 7)], [hk], lambda E: E.tensor_tensor(
                            out=hs[d_][:], in0=hs[d_][:], in1=ps[7][:, :], op=ALU.add))

                    def intra(c):
                        xprep(c, 0, "dve", c % 2)
                        xprep(c, 1, "pool", 0, False)
                        P.op("pe", ["BT", "CT"], [("ps", 0)], lambda E, c=c: E.matmul(
                            ps[0][:, 0:128], lhsT=BT[:, c * 128:(c + 1) * 128], rhs=CT[:, c * 128:(c + 1) * 128],
                            start=True, stop=True))
                        for d_, mk, SL in dsel:
                            P.op("dve", [("ps", 0), "ssdc"], [("CBm", d_)], lambda E, d_=d_, mk=mk: E.tensor_tensor(
                                out=CBm[d_][:], in0=ps[0][:, 0:128], in1=mk, op=ALU.mult))
                        for d_, mk, SL in dsel:
                            for half in range(2):
                                pb = ps[1 + d_ * 2 + half]
                                pk = ("ps", 1 + d_ * 2 + half)
                                for r4 in range(4):
                                    r = half * 4 + r4
                                    lj = nlh[0] % 4
                                    nlh[0] += 1
                                    col = d_ * 64 + g * 8 + r
                                    if lj % 2 == 0:
                                        P.op("act", [("A", c), "ssdc"], [("lh", lj)], lambda E, lj=lj, SL=SL, col=col, c=c: E.mul(
                                            out=lh[lj][:], in_=SL, mul=At[:, c, col:col + 1]))
                                    else:
                                        P.op("dve", [("A", c), "ssdc"], [("lh", lj)], lambda E, lj=lj, SL=SL, col=col, c=c: E.tensor_scalar_mul(
                                            out=lh[lj][:], in0=SL, scalar1=At[:, c, col:col + 1]))
                                    P.op("pe", [("lh", lj), "ssdc"], [pk], lambda E, pb=pb, r4=r4, lj=lj, mk=mk: E.matmul(
                                        pb[:, r4 * 128:(r4 + 1) * 128], lhsT=lh[lj][:], rhs=mk, start=True, stop=True))
                                P.op("act", [pk], [("LT", half)], lambda E, pb=pb, half=half: E.activation(
                                    out=LT[:, half * 512:(half + 1) * 512], in_=pb[:, :], func=AF.Exp))
                                P.op("dve", [("LT", half), ("CBm", d_)], [("MT", d_)], lambda E, d_=d_, half=half: E.tensor_tensor(
                                    out=MT[d_][:, half * 4:(half + 1) * 4, :],
                                    in0=LT[:, half * 512:(half + 1) * 512].rearrange("p (r t) -> p r t", t=128),
                                    in1=CBm[d_][:].unsqueeze(1).to_broadcast([128, 4, 128]), op=ALU.mult))

                        def mmy(E):
                            ins = None
                            for r in range(8):
                                E.matmul(ps[5][:, r * 64:(r + 1) * 64], lhsT=MT[0][:, r, :], rhs=xdt[0][:, r * 64:(r + 1) * 64],
                                         start=True, stop=False)
                                ins = E.matmul(ps[5][:, r * 64:(r + 1) * 64], lhsT=MT[1][:, r, :], rhs=xdt[1][:, r * 64:(r + 1) * 64],
                                               start=False, stop=True)
                            return ins
                        P.op("pe", [("MT", 0), ("MT", 1), ("xdt", 0), ("xdt", 1)], [("ps", 5)], mmy)
                        P.op("pool", ["xg", "drow"], ["dx"], lambda E, c=c: E.tensor_tensor(
                            out=v3(dx[:]), in0=v3(xg[:, c, :]), in1=drow[:, g * 8:g * 8 + 8].unsqueeze(2).to_broadcast([128, 8, 64]),
                            op=ALU.mult))
                        P.op("dve", [("ps", 5), "dx"], [("Y", c)], lambda E, c=c: E.tensor_tensor(
                            out=Y[:, c, :], in0=ps[5][:, :], in1=dx[:], op=ALU.add))

                    def rec_f(c):
                        if c > 0 and c % 2 == 0:
                            emit_state(0, c // 2 - 1)
                            P.op("dve", [("h", 0), "cont"], [("h", 0)], lambda E: E.tensor_scalar_mul(
                                out=hs[0][:], in0=hs[0][:], scalar1=cont[:, 0:1]))
                        state_step(c, 0, c % 2)
                        P.op("pool", ["tmpy", ("Y", c)], [("Y", c)], lambda E, c=c: E.tensor_tensor(
                            out=Y[:, c, :], in0=Y[:, c, :], in1=tmpy[:], op=ALU.add))

                    intra(0)
                    for c in range(16):
                        if c + 1 < 16:
                            intra(c + 1)
                        rec_f(c)
                    emit_state(0, 7)

                    def rec_b(c):
                        if c < 15 and c % 2 == 1:
                            emit_state(1, (c + 1) // 2)
                            P.op("dve", [("h", 1), "cont"], [("h", 1)], lambda E: E.tensor_scalar_mul(
                                out=hs[1][:], in0=hs[1][:], scalar1=cont[:, 0:1]))
                        xprep(c, 1, "pool", 0)
                        state_step(c, 1, 0)
                        P.op("pool", ["tmpy", ("Y", c)], [("Y", c)], lambda E, c=c: E.tensor_tensor(
                            out=Y[:, c, :], in0=Y[:, c, :], in1=tmpy[:], op=ALU.add))

                    def fin(c):
                        P.dma("sp", szs[c % 2][:], ZS[g, c, :, :], [], [("sz", c % 2)])
                        P.op("pool", [("Y", c), ("sz", c % 2)], ["G"], lambda E, c=c: E.tensor_tensor(
                            out=G[:], in0=Y[:, c, :], in1=szs[c % 2][:], op=ALU.mult))
                        P.op("act", ["G"], ["yn", "ssq"], lambda E: E.activation(
                            out=yn[:], in_=G[:], func=AF.Square, accum_out=ssq[:, 0:1]))
                        P.op("act", ["ssq"], ["ssq"], lambda E: E.activation(
                            out=ssq[:, 0:1], in_=ssq[:, 0:1], func=AF.Sqrt, scale=1.0 / 512, bias=epsb[:, 0:1]))
                        P.op("dve", ["ssq"], ["ssq"], lambda E: E.reciprocal(out=ssq[:, 0:1], in_=ssq[:, 0:1]))
                        P.op("dve", ["G", "ssq", "nwrow"], ["yn"], lambda E: E.scalar_tensor_tensor(
                            out=yn[:], in0=G[:], scalar=ssq[:, 0:1], in1=nwrow[:], op0=ALU.mult, op1=ALU.mult))
                        pbb = ps[0].bitcast(BF16)

                        def try_(E, pbb=pbb):
                            ins = None
                            for q in range(4):
                                ins = E.transpose(out=pbb[:, q * 128:(q + 1) * 128], in_=yn[:, q * 128:(q + 1) * 128],
                                                  identity=ident[:])
                            return ins
                        P.op("pe", ["yn", "ident"], [("ps", 0)], try_)
                        P.op("act", [("ps", 0)], ["ynT"], lambda E, pbb=pbb: E.copy(out=ynT[:], in_=pbb[:, 0:512]))
                        P.dma("sp", YT[g * 4:(g + 1) * 4, :, c * 128:(c + 1) * 128].rearrange("q p n -> p q n"),
                              ynT[:].rearrange("p (q n) -> p q n", n=128), ["ynT"], [("YT", g, c)])

                    for c in range(15, -1, -1):
                        rec_b(c)
                        if c < 15:
                            fin(c + 1)
                    fin(0)
                    emit_state(1, 0)
                P.barrier()


    def mlstm_phase(l, src):
        with ExitStack() as so:
            def sb(name, shape, dt, st=so):
                return st.enter_context(sbt(name, shape, dt))
            GI = sb("GI", [128, 16, 16], F32)
            GF = sb("GF", [128, 16, 16], F32)
            BC = sb("BC", [128, 16, 16], F32)
            TOT = sb("TOT", [128, 16, 16], F32)
            cst = sb("mlc1", [128, 4, 128], F32)
            cst2 = sb("mlc2", [128, 4, 128], F32)
            onesf = sb("onesfm", [128, 128], F32)
            cont = sb("contm", [128, 1], F32)
            maskF, maskB, SLf, SLb = cst[:, 0, :], cst[:, 1, :], cst[:, 2, :], cst[:, 3, :]
            NEGf, NEGb, Self, Selb = cst2[:, 0, :], cst2[:, 1, :], cst2[:, 2, :], cst2[:, 3, :]
            P.dma("sp", cst[:], ssd_consts[:, :, :], [], ["mlc"])
            P.dma("sp", cst2[:], ml_consts[:, :, :], [], ["mlc"])
            P.dma("sp", cont[:], contd[:, :], [], ["contm"])
            P.op("dve", [], ["onesfm"], lambda E: E.memset(onesf[:], 1.0))
            with ExitStack() as s1:
                UT = sb("UTm", [128, KC, NTOK], BF16, s1)
                with ExitStack() as s0:
                    xta = sb("xta", [128, D], F32, s0)
                    xtb = sb("xtb", [128, D], F32, s0)
                    xn = sb("xn", [128, D], BF16, s0)
                    xtl = [xta, xtb]
                    for i in range(16):
                        norm_mod(l, 0, src, i, xtl[i % 2], ("xt", i % 2), i % 2, xn, UT, ("UT", i // 4), i * 128)
                    P.barrier()
                utk = [("UT", q) for q in range(4)]
                bgr = sb("bgr", [128, 32], F32, s1)
                t_a = sb("mt_a", [128, 32], F32, s1)
                t_b = sb("mt_b", [128, 16], F32, s1)
                t_c = sb("mt_c", [128, 16], F32, s1)
                ob0 = sb("ob0", [128, 512], BF16, s1)
                ob1 = sb("ob1", [128, 512], BF16, s1)
                obs = [ob0, ob1]
                fo0 = sb("fo0", [128, NTOK], BF16, s1)
                P.dma("sp", bgr[:], ml_bg.partition_broadcast(128), [], ["bgr"])
                wgt, wgtk = W.take("mlstm_w_in", (slice(None), slice(6144, 6176)), KC, 32)
                for i in range(16):
                    pb = ps[i % 2]
                    pk = ("ps", i % 2)

                    def mmg_(E, pb=pb, i=i):
                        ins = None
                        for k in range(KC):
                            ins = E.matmul(pb[:, 0:32], lhsT=UT[:, k, i * 128:(i + 1) * 128], rhs=wgt[:, k, :],
                                           start=(k == 0), stop=(k == KC - 1))
                        return ins
                    P.op("pe", [wgtk] + utk, [pk], mmg_)
                    P.op("dve", [pk, "bgr"], ["mt_a"], lambda E, pb=pb: E.tensor_tensor(
                        out=t_a[:], in0=pb[:, 0:32], in1=bgr[:], op=ALU.add))
                    t4 = t_a[:].rearrange("p (d g h) -> p d g h", d=2, g=2)
                    P.op("dve", ["mt_a"], [("GI", i)], lambda E, i=i, t4=t4: E.tensor_copy(
                        out=GI[:, i, :].rearrange("p (d h) -> p d h", d=2), in_=t4[:, :, 0, :]))
                    P.op("act", ["mt_a"], ["mt_b"], lambda E, t4=t4: E.activation(
                        out=t_b[:].rearrange("p (d h) -> p d h", d=2), in_=t4[:, :, 1, :], func=AF.Abs))
                    P.op("act", ["mt_b"], ["mt_b"], lambda E: E.activation(out=t_b[:], in_=t_b[:], func=AF.Exp, scale=-1.0))
                    P.op("act", ["mt_b"], ["mt_b"], lambda E: E.activation(out=t_b[:], in_=t_b[:], func=AF.Ln, bias=1.0))
                    P.op("dve", ["mt_a"], ["mt_c"], lambda E, t4=t4: E.tensor_scalar_min(
                        out=t_c[:].rearrange("p (d h) -> p d h", d=2), in0=t4[:, :, 1, :], scalar1=0.0))
                    P.op("dve", ["mt_b", "mt_c"], [("GF", i)], lambda E, i=i: E.tensor_tensor(
                        out=GF[:, i, :], in0=t_c[:], in1=t_b[:], op=ALU.subtract))
                    pc = ps[2 + i % 2]
                    pck = ("ps", 2 + i % 2)

                    def mmcs(E, pc=pc, i=i):
                        E.matmul(pc[:, 0:8], lhsT=maskF, rhs=GF[:, i, 0:8], start=True, stop=True)
                        E.matmul(pc[:, 8:16], lhsT=maskB, rhs=GF[:, i, 8:16], start=True, stop=True)
                        return E.matmul(pc[:, 16:32], lhsT=onesf[:], rhs=GF[:, i, :], start=True, stop=True)
                    P.op("pe", [("GF", i), "mlc", "onesfm"], [pck], mmcs)
                    P.op("dve", [pck], [("BC", i)], lambda E, pc=pc, i=i: E.tensor_copy(out=BC[:, i, :], in_=pc[:, 0:16]))
                    P.op("dve", [pck], [("TOT", i)], lambda E, pc=pc, i=i: E.tensor_copy(out=TOT[:, i, :], in_=pc[:, 16:32]))
                nob = 0
                for blk in range(4):
                    wq, wqk = W.take("mlstm_w_in", (slice(None), slice(blk * 512, (blk + 1) * 512)), KC, 512)
                    isq = blk < 2
                    for ft in range(4):
                        hh = (blk % 2) * 4 + ft
                        for tt in range(4):
                            pb = ps[tt % 2]
                            pk = ("ps", tt % 2)

                            def mmx(E, pb=pb, tt=tt, ft=ft, wq=wq):
                                ins = None
                                for k in range(KC):
                                    ins = E.matmul(pb[:, :], lhsT=wq[:, k, ft * 128:(ft + 1) * 128],
                                                   rhs=UT[:, k, tt * 512:(tt + 1) * 512], start=(k == 0), stop=(k == KC - 1))
                                return ins
                            P.op("pe", [wqk] + utk, [pk], mmx)
                            P.op("act", [pk], ["fo0"], lambda E, pb=pb, tt=tt, isq=isq: E.mul(
                                out=fo0[:, tt * 512:(tt + 1) * 512], in_=pb[:, :], mul=(128.0 ** -0.5) if isq else 1.0))
                        P.dma("sp", (QTS if isq else KTS)[hh, :, :], fo0[:], ["fo0"], [("QKS", blk, ft)])
                    if not isq:
                        for i in range(16):
                            pb = ps[2 + i % 2]
                            pk = ("ps", 2 + i % 2)
                            oj = nob % 2
                            nob += 1

                            def mmk(E, pb=pb, i=i, wq=wq):
                                ins = None
                                for k in range(KC):
                                    ins = E.matmul(pb[:, :], lhsT=UT[:, k, i * 128:(i + 1) * 128], rhs=wq[:, k, :],
                                                   start=(k == 0), stop=(k == KC - 1))
                                return ins
                            P.op("pe", [wqk] + utk, [pk], mmk)
                            P.op("dve", [pk], [("ob", oj)], lambda E, pb=pb, oj=oj: E.tensor_copy(out=obs[oj][:], in_=pb[:, :]))
                            h0_ = (blk % 2) * 4
                            P.dma("sp", KMS[h0_:h0_ + 4, i, :, :].rearrange("h p n -> p h n"),
                                  obs[oj][:].rearrange("p (h n) -> p h n", n=128), [("ob", oj)], [("KMS", blk, i)])
                for blk in range(8):
                    c0 = 2048 + blk * 512
                    wv_, wvk_ = W.take("mlstm_w_in", (slice(None), slice(c0, c0 + 512)), KC, 512)
                    isv = blk < 4
                    for i in range(16):
                        pb = ps[i % 2]
                        pk = ("ps", i % 2)
                        oj = nob % 2
                        nob += 1

                        def mmv(E, pb=pb, i=i, wv_=wv_):
                            ins = None
                            for k in range(KC):
                                ins = E.matmul(pb[:, :], lhsT=UT[:, k, i * 128:(i + 1) * 128], rhs=wv_[:, k, :],
                                               start=(k == 0), stop=(k == KC - 1))
                            return ins
                        P.op("pe", [wvk_] + utk, [pk], mmv)
                        if isv:
                            P.op("act", [pk], [("ob", oj)], lambda E, pb=pb, oj=oj: E.copy(out=obs[oj][:], in_=pb[:, :]))
                        else:
                            P.op("act", [pk], [("ob", oj)], lambda E, pb=pb, oj=oj: E.activation(
                                out=obs[oj][:], in_=pb[:, :], func=AF.Sigmoid))
                        h0_ = (blk % 4) * 2
                        P.dma("sp", (VS if isv else OGS)[h0_:h0_ + 2, i, :, :].rearrange("h p n -> p h n"),
                              obs[oj][:].rearrange("p (h n) -> p h n", n=256), [("ob", oj)], [("VOS", blk, i)])
                P.barrier()

            with ExitStack() as s4:
                QT = sb("QTm", [128, NTOK], BF16, s4)
                KTm = sb("KTm", [128, NTOK], BF16, s4)
                Km = sb("Km", [128, 16, 128], BF16, s4)
                V1 = sb("V1", [128, 16, 257], BF16, s4)
                Hs = sb("Hs", [128, 16, 256], F32, s4)
                CN = [sb("CNf", [128, 257], F32, s4), sb("CNb", [128, 257], F32, s4)]
                CN16 = sb("CN16", [128, 257], BF16, s4)
                Mst = sb("Mst", [128, 2], F32, s4)
                m0t = sb("m0t", [128, 16], F32, s4)
                mout = sb("mout", [1, 128], F32, s4)
                nout = sb("nout", [128, 128], F32, s4)
                nouts = sb("nouts", [128, 128], F32, s4)
                r1 = sb("r1", [128, 128], F32, s4)
                rD = sb("rD", [128, 128], F32, s4)
                Dmm = sb("Dmm", [128, 128], F32, s4)
                Wt = sb("Wt", [128, 128], BF16, s4)
                WTs = sb("WTs", [128, 128], BF16, s4)
                ST = sb("ST", [128, 128], BF16, s4)
                P1s = sb("P1s", [128, 257], F32, s4)
                R = sb("R", [128, 257], F32, s4)
                kw = sb("kw", [128, 128], BF16, s4)
                sm = sb("sm", [128, 16], F32, s4)
                ogs = [sb("og0", [128, 256], BF16, s4), sb("og1", [128, 256], BF16, s4)]
                hn = sb("hn", [128, 256], F32, s4)
                hg = sb("hg", [128, 256], BF16, s4)
                hgT = sb("hgT", [128, 256], BF16, s4)
                hnrow = sb("hnrow", [128, 256], F32, s4)
                P.dma("sp", m0t[:], ml_m0[:, :], [], ["m0t"])
                P.op("dve", [], ["V1"], lambda E: E.memset(V1[:, :, 256:257], 1.0))
                dsel = [(0, maskF, SLf, NEGf, Self), (1, maskB, SLb, NEGb, Selb)]
                MX, MI, MT_, NMT, INTER, EMT, TM, LW, MNEW, E2a, E2b, ADEN, RDN, SSQ = range(14)

                def col(j):
                    return sm[:, j:j + 1]

                for h in range(cfg.get("ml_heads", 8)):
                    P.dma("sp", QT[:], QTS[h, :, :], [], ["QTm"])
                    P.dma("sp", KTm[:], KTS[h, :, :], [], ["KTm"])
                    P.dma("sp", Km[:], KMS[h, :, :, :].rearrange("i p n -> p i n"), [], ["Km"])
                    P.dma("sp", V1[:, :, 0:256], VS[h, :, :, :].rearrange("i p n -> p i n"), ["V1"], ["V1"])
                    P.dma("sp", hnrow[:], ml_hn[h, :].partition_broadcast(128), [], ["hnrow"])
                    for d_ in range(2):
                        P.dma("sp", CN[d_][:, 0:256], ml_C0[d_, h, :, :], [], [("CN", d_)])
                        P.dma("sp", CN[d_][:, 256:257], ml_n0[d_, h, :].rearrange("(p o) -> p o", o=1), [("CN", d_)], [("CN", d_)])
                        P.op("dve", ["m0t"], [("M", d_)], lambda E, d_=d_: E.tensor_copy(
                            out=Mst[:, d_:d_ + 1], in_=m0t[:, d_ * 8 + h:d_ * 8 + h + 1]))

                    def emit_state(d_, seg):
                        P.dma("sp", mlC_out[seg, d_, h, :, :], CN[d_][:, 0:256], [("CN", d_)], [("mlCo", seg, d_, h)])
                        idx = (seg * 2 + d_) * 8 + h
                        P.op("dve", [("CN", d_)], ["nout"], lambda E: E.tensor_copy(
                            out=nout[:, idx:idx + 1], in_=CN[d_][:, 256:257]))
                        P.op("dve", [("M", d_)], ["mout"], lambda E: E.tensor_copy(
                            out=mout[0:1, idx:idx + 1], in_=Mst[0:1, d_:d_ + 1]))

                    def step(c, d_, first_dir):
                        _, mk, SL, NEG, Sel = dsel[d_]
                        gcol = d_ * 8 + h
                        fcol = GF[:, c, gcol:gcol + 1]
                        icol = GI[:, c, gcol:gcol + 1]
                        bcol = BC[:, c, gcol:gcol + 1]
                        tcol = TOT[:, c, gcol:gcol + 1]
                        mcol = Mst[:, d_:d_ + 1]
                        gk = [("GF", c), ("GI", c), ("BC", c), ("TOT", c)]
                        cs = slice(c * 128, (c + 1) * 128)
                        P.op("act", gk + ["mlc"], ["r1"], lambda E: E.mul(out=r1[:], in_=SL, mul=fcol))
                        P.op("dve", gk + ["r1", "ident_f"], ["rD"], lambda E: E.scalar_tensor_tensor(
                            out=rD[:], in0=ident_f[:], scalar=icol, in1=r1[:], op0=ALU.mult, op1=ALU.add))
                        P.op("pe", ["rD", "mlc"], [("ps", 0)], lambda E: E.matmul(
                            ps[0][:, 0:128], lhsT=mk, rhs=rD[:], start=True, stop=True))
                        P.op("dve", [("ps", 0), "mlc"], ["Dmm"], lambda E: E.tensor_tensor(
                            out=Dmm[:], in0=ps[0][:, 0:128], in1=NEG, op=ALU.add))
                        P.op("dve", ["Dmm"], [("sm", MX)], lambda E: E.reduce_max(
                            out=col(MX), in_=Dmm[:], axis=mybir.AxisListType.X))
                        P.op("dve", gk + [("M", d_)], [("sm", MI)], lambda E: E.tensor_tensor(
                            out=col(MI), in0=bcol, in1=mcol, op=ALU.add))
                        P.op("dve", [("sm", MX), ("sm", MI)], [("sm", MT_)], lambda E: E.tensor_tensor(
                            out=col(MT_), in0=col(MX), in1=col(MI), op=ALU.max))
                        P.op("dve", [("sm", MT_)], [("sm", NMT)], lambda E: E.tensor_scalar_mul(
                            out=col(NMT), in0=col(MT_), scalar1=-1.0))
                        P.op("act", [("sm", MI), ("sm", NMT)], [("sm", INTER)], lambda E: E.activation(
                            out=col(INTER), in_=col(MI), func=AF.Exp, bias=col(NMT)))
                        P.op("act", [("sm", NMT)], [("sm", EMT)], lambda E: E.activation(
                            out=col(EMT), in_=col(NMT), func=AF.Exp))
                        P.op("act", ["Dmm", ("sm", NMT)], ["Wt"], lambda E: E.activation(
                            out=Wt[:], in_=Dmm[:], func=AF.Exp, bias=col(NMT)))
                        pbb = ps[1].bitcast(BF16)
                        P.op("pe", ["Wt", "ident"], [("ps", 1)], lambda E: E.transpose(
                            out=pbb[:, 0:128], in_=Wt[:], identity=ident[:]))
                        P.op("act", [("ps", 1)], ["WTs"], lambda E: E.copy(out=WTs[:], in_=pbb[:, 0:128]))
                        P.op("pe", ["KTm", "QTm"], [("ps", 2)], lambda E: E.matmul(
                            ps[2][:, 0:128], lhsT=KTm[:, cs], rhs=QT[:, cs], start=True, stop=True))
                        P.op("dve", [("ps", 2), "WTs"], ["ST"], lambda E: E.tensor_tensor(
                            out=ST[:], in0=ps[2][:, 0:128], in1=WTs[:], op=ALU.mult))
                        P.op("act", [("CN", d_)], ["CN16"], lambda E: E.copy(out=CN16[:], in_=CN[d_][:]))
                        P.op("pe", ["ST", "V1"], [("ps", 3)], lambda E: E.matmul(
                            ps[3][:, 0:257], lhsT=ST[:], rhs=V1[:, c, :], start=True, stop=True))
                        P.op("pe", ["QTm", "CN16"], [("ps", 4)], lambda E: E.matmul(
                            ps[4][:, 0:257], lhsT=QT[:, cs], rhs=CN16[:], start=True, stop=True))
                        P.op("act", [("ps", 3)], ["P1s"], lambda E: E.copy(out=P1s[:], in_=ps[3][:, 0:257]))
                        P.op("dve", [("ps", 4), "P1s", ("sm", INTER)], ["R"], lambda E: E.scalar_tensor_tensor(
                            out=R[:], in0=ps[4][:, 0:257], scalar=col(INTER), in1=P1s[:], op0=ALU.mult, op1=ALU.add))
                        P.op("act", ["R"], [("sm", ADEN)], lambda E: E.activation(
                            out=col(ADEN), in_=R[:, 256:257], func=AF.Abs))
                        P.op("dve", [("sm", ADEN), ("sm", EMT)], [("sm", RDN)], lambda E: E.tensor_tensor(
                            out=col(RDN), in0=col(ADEN), in1=col(EMT), op=ALU.max))
                        P.op("dve", [("sm", RDN)], [("sm", RDN)], lambda E: E.reciprocal(out=col(RDN), in_=col(RDN)))
                        if first_dir:
                            P.op("dve", ["R", ("sm", RDN)], [("Hs", c)], lambda E: E.tensor_scalar_mul(
                                out=Hs[:, c, :], in0=R[:, 0:256], scalar1=col(RDN)))
                        else:
                            P.op("dve", ["R", ("sm", RDN), ("Hs", c)], [("Hs", c)], lambda E: E.scalar_tensor_tensor(
                                out=Hs[:, c, :], in0=R[:, 0:256], scalar=col(RDN), in1=Hs[:, c, :], op0=ALU.mult, op1=ALU.add))
                        P.op("dve", gk + [("M", d_)], [("sm", TM)], lambda E: E.tensor_tensor(
                            out=col(TM), in0=tcol, in1=mcol, op=ALU.add))
                        P.op("dve", gk, [("sm", LW)], lambda E: E.tensor_tensor(
                            out=col(LW), in0=tcol, in1=bcol, op=ALU.subtract))
                        P.op("dve", gk + [("sm", LW)], [("sm", LW)], lambda E: E.tensor_tensor(
                            out=col(LW), in0=col(LW), in1=icol, op=ALU.add))
                        P.op("pe", [("sm", MX), "mlc"], [("ps", 5)], lambda E: E.matmul(
                            ps[5][:, 0:1], lhsT=Sel, rhs=col(MX), start=True, stop=True))
                        P.op("dve", [("ps", 5), ("sm", TM)], [("sm", MNEW)], lambda E: E.tensor_tensor(
                            out=col(MNEW), in0=ps[5][:, 0:1], in1=col(TM), op=ALU.max))
                        P.op("dve", [("sm", TM), ("sm", LW), ("sm", MNEW)], [("sm", TM), ("sm", LW)], lambda E: E.tensor_scalar(
                            out=sm[:, TM:TM + 2], in0=sm[:, TM:TM + 2], scalar1=col(MNEW), scalar2=None, op0=ALU.subtract))
                        P.op("act", [("sm", TM), ("sm", LW)], [("sm", E2a), ("sm", E2b)], lambda E: E.activation(
                            out=sm[:, E2a:E2a + 2], in_=sm[:, TM:TM + 2], func=AF.Exp))
                        P.op("dve", ["Km", ("sm", E2b)], ["kw"], lambda E: E.tensor_scalar_mul(
                            out=kw[:], in0=Km[:, c, :], scalar1=col(E2b)))
                        P.op("pe", ["kw", "V1"], [("ps", 6)], lambda E: E.matmul(
                            ps[6][:, 0:257], lhsT=kw[:], rhs=V1[:, c, :], start=True, stop=True))
                        P.op("dve", [("CN", d_), ("sm", E2a), ("ps", 6)], [("CN", d_)], lambda E: E.scalar_tensor_tensor(
                            out=CN[d_][:], in0=CN[d_][:], scalar=col(E2a), in1=ps[6][:, 0:257], op0=ALU.mult, op1=ALU.add))
                        P.op("dve", [("sm", MNEW)], [("M", d_)], lambda E: E.tensor_copy(out=mcol, in_=col(MNEW)))

                    def seg_reset(d_):
                        P.op("dve", [("CN", d_), "contm"], [("CN", d_)], lambda E: E.tensor_scalar_mul(
                            out=CN[d_][:], in0=CN[d_][:], scalar1=cont[:, 0:1]))
                        P.op("dve", [("M", d_), "contm"], [("M", d_)], lambda E: E.tensor_scalar_mul(
                            out=Mst[:, d_:d_ + 1], in0=Mst[:, d_:d_ + 1], scalar1=cont[:, 0:1]))

                    for c in range(16):
                        if c > 0 and c % 2 == 0:
                            emit_state(0, c // 2 - 1)
                            seg_reset(0)
                        step(c, 0, True)
                    emit_state(0, 7)
                    for c in range(15, -1, -1):
                        if c < 15 and c % 2 == 1:
                            emit_state(1, (c + 1) // 2)
                            seg_reset(1)
                        P.dma("sp", ogs[c % 2][:], OGS[h, c, :, :], [], [("og", c % 2)])
                        step(c, 1, False)
                        P.op("act", [("Hs", c)], ["hn", ("sm", SSQ)], lambda E, c=c: E.activation(
                            out=hn[:], in_=Hs[:, c, :], func=AF.Square, accum_out=col(SSQ)))
                        P.op("act", [("sm", SSQ)], [("sm", SSQ)], lambda E: E.activation(
                            out=col(SSQ), in_=col(SSQ), func=AF.Sqrt, scale=1.0 / 256, bias=epsb[:, 0:1]))
                        P.op("dve", [("sm", SSQ)], [("sm", SSQ)], lambda E: E.reciprocal(out=col(SSQ), in_=col(SSQ)))
                        P.op("dve", [("Hs", c), ("sm", SSQ), "hnrow"], ["hn"], lambda E, c=c: E.scalar_tensor_tensor(
                            out=hn[:], in0=Hs[:, c, :], scalar=col(SSQ), in1=hnrow[:], op0=ALU.mult, op1=ALU.mult))
                        P.op("pool", ["hn", ("og", c % 2)], ["hg"], lambda E, c=c: E.tensor_tensor(
                            out=hg[:], in0=hn[:], in1=ogs[c % 2][:], op=ALU.mult))
                        pbb = ps[7].bitcast(BF16)

                        def trh(E, pbb=pbb):
                            E.transpose(out=pbb[:, 0:128], in_=hg[:, 0:128], identity=ident[:])
                            return E.transpose(out=pbb[:, 128:256], in_=hg[:, 128:256], identity=ident[:])
                        P.op("pe", ["hg", "ident"], [("ps", 7)], trh)
                        P.op("act", [("ps", 7)], ["hgT"], lambda E, pbb=pbb: E.copy(out=hgT[:], in_=pbb[:, 0:256]))
                        P.dma("sp", YT[h * 2:h * 2 + 2, :, c * 128:(c + 1) * 128].rearrange("q p n -> p q n"),
                              hgT[:].rearrange("p (q n) -> p q n", n=128), ["hgT"], [("YT", h, c)])
                    emit_state(1, 0)
                P.op("pe", ["nout", "ident_f"], [("ps", 0)], lambda E: E.transpose(
                    out=ps[0][:, 0:128], in_=nout[:], identity=ident_f[:]))
                P.op("dve", [("ps", 0)], ["nouts"], lambda E: E.tensor_copy(out=nouts[:], in_=ps[0][:, 0:128]))
                P.dma("sp", mln_out.rearrange("s d h k -> (s d h) k"), nouts[:], ["nouts"], ["mlno"])
                P.dma("sp", mlm_out[:, :], mout[:], ["mout"], ["mlmo"])
                P.barrier()

    cur = xin
    for l in layers:
        if cfg.get("mixer", True):
            if l % 3 == 0:
                ssd_phase(l, cur)
                if cfg.get("ssd_out", True):
                    outproj_phase(l, cur, 32, "ssd_w_out", l // 3)
                cur = X
            if l % 3 == 2:
                mlstm_phase(l, cur)
                outproj_phase(l, cur, 16, "mlstm_w_out")
                cur = X
            if l % 3 == 1:
                attn_phase(l, cur)
                outproj_phase(l, cur, 16, "attn_w_out")
                cur = X
        if cfg.get("ffn", True):
            ffn_phase(l, cur)
            cur = X
    P.barrier()
    return nc, W.plan


def build(cfg):
    _, plan = build_program(cfg, None)
    nc, _ = build_program(cfg, plan)
    return nc


def _rope_tables():
    half, quarter = 64, 32
    t = np.arange(NTOK)
    pos_row = (t // 64).astype(np.float32)
    pos_col = (t % 64).astype(np.float32)
    inv_freq = (10000.0 ** (-np.arange(quarter, dtype=np.float32) / quarter)).astype(np.float32)
    cos = np.zeros((128, NTOK), np.float32)
    sin = np.zeros((128, NTOK), np.float32)
    for p in range(128):
        pos = pos_row if p < 64 else pos_col
        ang = (pos * inv_freq[p % 32]).astype(np.float32)
        cos[p] = np.cos(ang)
        sin[p] = np.sin(ang) * (-1.0 if (p % 64) < 32 else 1.0)
    return cos, sin


def _perm():
    pm = np.zeros((128, 128), np.float32)
    for m in range(128):
        sw = m + 32 if (m % 64) < 32 else m - 32
        pm[sw, m] = 1.0
    return pm


def _core_inputs(inp, core, layers):
    m = {}
    if core < 4:
        m["xin"] = np.ascontiguousarray(inp["x_sample"][core])
        cvec = inp["c"][core]
    else:
        j = core - 4
        m["xin"] = np.ascontiguousarray(inp["x_prompt"][8 * j:8 * j + 8].reshape(NTOK, D))
        cvec = inp["c_ctx"]
    m["cv"] = np.ascontiguousarray(cvec.reshape(KC, 128).T)
    js = sorted(set(l // 3 for l in layers if l % 3 == 0))
    if js:
        if core < 4:
            m["ssd_h0"] = np.ascontiguousarray(inp["state_ssd"][core][js].reshape(len(js), 2, 4096, 128))
            m["cont"] = np.ones((128, 1), np.float32)
        else:
            m["ssd_h0"] = np.zeros((len(js), 2, 4096, 128), np.float32)
            m["cont"] = np.zeros((128, 1), np.float32)
    if (2 in layers) and not js:
        m["cont"] = np.ones((128, 1), np.float32) if core < 4 else np.zeros((128, 1), np.float32)
    if 2 in layers:
        if core < 4:
            m["ml_C0"] = np.ascontiguousarray(inp["state_mlstm_C"][core, 0])
            m["ml_n0"] = np.ascontiguousarray(inp["state_mlstm_n"][core, 0])
            m["ml_m0"] = np.ascontiguousarray(np.broadcast_to(inp["state_mlstm_m"][core, 0].reshape(1, 16), (128, 16)))
        else:
            m["ml_C0"] = np.zeros((2, 8, 128, 256), np.float32)
            m["ml_n0"] = np.zeros((2, 8, 128), np.float32)
            m["ml_m0"] = np.zeros((128, 16), np.float32)
    if 1 in layers:
        mb = np.zeros((128, 160), np.float32)
        if core < 4:
            cos, sin = _rope_tables()
            m["cachek"] = np.ascontiguousarray(inp["cache_attn_k"][core, 0].reshape(512, 512))
            m["cachev"] = np.ascontiguousarray(inp["cache_attn_v"][core, 0].reshape(512, 512))
        else:
            cos = np.ones((128, NTOK), np.float32)
            sin = np.zeros((128, NTOK), np.float32)
            m["cachek"] = np.zeros((512, 512), np.float32)
            m["cachev"] = np.zeros((512, 512), np.float32)
            for q8 in range(8):
                for kt in range(20):
                    ok = kt >= 4 and (kt - 4) // 2 == q8
                    mb[:, q8 * 20 + kt] = 0.0 if ok else -30000.0
        m["ropecos"], m["ropesin"], m["maskb"] = cos, sin, mb
    return m


def _shared_inputs(inp, layers):
    L = list(layers)
    sh = {}
    sh["consts"] = np.eye(128, dtype=np.float32)
    sh["ada_w"] = inp["ada_w"][L] if len(L) < DEPTH else inp["ada_w"]
    sh["ada_b"] = inp["ada_b"][L]
    sh["adab_f"] = np.ascontiguousarray(inp["ada_b"].reshape(DEPTH, 96, 128).transpose(2, 0, 1))
    nw = np.stack([inp["norm_mix_w"], inp["norm_ffn_w"]], axis=1)
    sh["nw_f"] = np.ascontiguousarray(nw.reshape(DEPTH, 2, KC, 128).transpose(3, 0, 1, 2))
    for k in ("ffn_w_gate", "ffn_w_up", "ffn_w_down"):
        sh[k] = inp[k][L] if len(L) < DEPTH else inp[k]
    js = sorted(set(l // 3 for l in L if l % 3 == 0))
    if js:
        sh["ssd_w_in"] = inp["ssd_w_in"][js]
        sh["ssd_w_out"] = inp["ssd_w_out"][js]
        cw = inp["ssd_conv_w"][js]
        sh["ssd_cw"] = np.ascontiguousarray(cw.reshape(len(js), 3, 48, 128).transpose(3, 0, 2, 1))
        sh["ssd_cb"] = np.ascontiguousarray(inp["ssd_conv_b"][js].reshape(len(js), 48, 128).transpose(2, 0, 1))
        sh["ssd_dtb"] = np.ascontiguousarray(inp["ssd_dt_bias"][js].reshape(len(js), 128))
        sh["ssd_alog"] = np.ascontiguousarray(inp["ssd_a_log"][js].reshape(len(js), 128))
        sh["ssd_dskip"] = np.ascontiguousarray(inp["ssd_d"][js])
        sh["ssd_nw"] = np.ascontiguousarray(inp["ssd_norm_w"][js])
        idx = np.arange(128)
        mF = (idx[:, None] <= idx[None, :]).astype(np.float32)
        mB = (idx[:, None] >= idx[None, :]).astype(np.float32)
        slf = (idx[:, None] > idx[None, :]).astype(np.float32)
        slb = (idx[:, None] < idx[None, :]).astype(np.float32)
        sh["ssd_consts"] = np.ascontiguousarray(np.stack([mF, mB, slf, slb], axis=1))
    if 2 in L or js:
        idx = np.arange(128)
        mF = (idx[:, None] <= idx[None, :]).astype(np.float32)
        mB = (idx[:, None] >= idx[None, :]).astype(np.float32)
        slf = (idx[:, None] > idx[None, :]).astype(np.float32)
        slb = (idx[:, None] < idx[None, :]).astype(np.float32)
        sh["ssd_consts"] = np.ascontiguousarray(np.stack([mF, mB, slf, slb], axis=1))
    if 2 in L:
        sh["mlstm_w_in"] = inp["mlstm_w_in"][0]
        sh["mlstm_w_out"] = inp["mlstm_w_out"][0]
        sh["ml_bg"] = np.ascontiguousarray(inp["mlstm_b_gates"][0].reshape(32))
        sh["ml_hn"] = np.ascontiguousarray(inp["mlstm_head_norm"][0])
        idx = np.arange(128)
        negf = np.where(idx[None, :] > idx[:, None], -30000.0, 0.0).astype(np.float32)
        negb = np.where(idx[None, :] < idx[:, None], -30000.0, 0.0).astype(np.float32)
        self_ = np.zeros((128, 128), np.float32)
        self_[127, :] = 1.0
        selb = np.zeros((128, 128), np.float32)
        selb[0, :] = 1.0
        sh["ml_consts"] = np.ascontiguousarray(np.stack([negf, negb, self_, selb], axis=1))
    if 1 in L:
        sh["attn_w_qkv"] = inp["attn_w_qkv"][0]
        sh["attn_w_out"] = inp["attn_w_out"][0]
        sh["attn_qn"] = np.ascontiguousarray(inp["attn_q_norm"][0].reshape(128, 1))
        sh["attn_kn"] = np.ascontiguousarray(inp["attn_k_norm"][0].reshape(128, 1))
        sh["attn_knrow"] = np.ascontiguousarray(inp["attn_k_norm"][0])
        sh["perm"] = _perm()
    return sh


def run(inp, cfg, cores=None):
    inp = {k: np.asarray(v) for k, v in inp.items()}
    cores = list(range(8)) if cores is None else cores
    layers = cfg.get("layers", list(range(DEPTH)))
    nc = build(cfg)
    sh = _shared_inputs(inp, layers)
    in_maps = []
    for core in cores:
        m = dict(sh)
        m.update(_core_inputs(inp, core, layers))
        in_maps.append(m)
    res = run_bass_kernel_spmd(nc, in_maps, core_ids=list(range(len(cores))))
    return res


def kernel(**inputs):
    res = run(inputs, {})
    r = res.results
    y_sample = np.stack([r[c]["xout"] for c in range(4)], axis=0)
    y_prompt = np.concatenate([r[c]["xout"].reshape(8, 256, D) for c in range(4, 8)], axis=0)
    st = np.concatenate([r[c]["st_out"].reshape(2, 8, 2, 64, 64, 128).transpose(1, 0, 2, 3, 4, 5)
                         for c in range(4, 8)], axis=0)
    kc = np.concatenate([r[c]["kc_out"].reshape(8, 1, 256, 4, 128) for c in range(4, 8)], axis=0)
    vc = np.concatenate([r[c]["vc_out"].reshape(8, 1, 256, 4, 128) for c in range(4, 8)], axis=0)
    mC = np.concatenate([r[c]["mlC_out"].reshape(8, 1, 2, 8, 128, 256) for c in range(4, 8)], axis=0)
    mn = np.concatenate([r[c]["mln_out"].reshape(8, 1, 2, 8, 128) for c in range(4, 8)], axis=0)
    mm = np.concatenate([r[c]["mlm_out"].reshape(8, 1, 2, 8) for c in range(4, 8)], axis=0)
    return (np.ascontiguousarray(y_prompt), y_sample, np.ascontiguousarray(st), kc, vc, mC, mn, mm)
```

```python
import numpy as np
from contextlib import ExitStack
import concourse.bass as bass
import concourse.mybir as mybir
from concourse.bass_utils import run_bass_kernel_spmd

F32 = mybir.dt.float32
BF16 = mybir.dt.bfloat16
AF = mybir.ActivationFunctionType
ALU = mybir.AluOpType

D = 2048
NTOK = 2048
KC = 16
FH = 5632
HC = 44
DEPTH = 4
EPS = 1e-6
SLOT = 8192
NSLOT = 4
KSPLIT = [(0, 16), (16, 16), (32, 12)]


class Prog:
    def __init__(s, nc):
        s.nc = nc
        s.E = dict(pe=nc.tensor, act=nc.scalar, dve=nc.vector, pool=nc.gpsimd, sp=nc.sync)
        s.sem = {}
        s.val = {}
        for e in s.E:
            s.sem[e] = nc.alloc_semaphore("sem_" + e)
            s.val[e] = 0
        s.dpool = {}
        for q, n in (("sp", 16), ("pool", 16)):
            keys = []
            for i in range(n):
                k = "d_%s%d" % (q, i)
                s.sem[k] = nc.alloc_semaphore(k)
                s.val[k] = 0
                keys.append(k)
            s.dpool[q] = [keys, 0]
        s.waited = {e: {} for e in s.E}
        s.lastw = {}
        s.readers = {}
        s.nops = 0

    def _deps(s, reads, writes):
        deps = {}
        for r in reads:
            t = s.lastw.get(r)
            if t is not None and deps.get(t[0], 0) < t[1]:
                deps[t[0]] = t[1]
        for w in writes:
            t = s.lastw.get(w)
            if t is not None and deps.get(t[0], 0) < t[1]:
                deps[t[0]] = t[1]
            rd = s.readers.get(w)
            if rd:
                for k, v in rd.items():
                    if deps.get(k, 0) < v:
                        deps[k] = v
        return deps

    def _wait(s, eng, deps):
        E = s.E[eng]
        wd = s.waited[eng]
        for k, v in deps.items():
            if eng == "pe" and k == "pe":
                continue
            if wd.get(k, 0) < v:
                E.wait_ge(s.sem[k], v)
                wd[k] = v

    def _commit(s, tok, reads, writes):
        for r in reads:
            d = s.readers.setdefault(r, {})
            if d.get(tok[0], 0) < tok[1]:
                d[tok[0]] = tok[1]
        for w in writes:
            s.lastw[w] = tok
            s.readers[w] = {}

    def op(s, eng, reads, writes, fn):
        pr = [r for r in reads if isinstance(r, tuple) and r[0] == "ps"]
        if pr:
            reads = [r for r in reads if r not in pr]
            writes = list(writes) + [r for r in pr if r not in writes]
        s._wait(eng, s._deps(reads, writes))
        ins = fn(s.E[eng])
        s.val[eng] += 1
        ins.then_inc(s.sem[eng], 1)
        tok = (eng, s.val[eng])
        s._commit(tok, reads, writes)
        s.nops += 1
        return tok

    def dma(s, q, out, in_, reads, writes):
        s._wait(q, s._deps(reads, writes))
        keys, idx = s.dpool[q]
        k = keys[idx % len(keys)]
        s.dpool[q][1] = idx + 1
        if s.val[k] > 0 and s.waited[q].get(k, 0) < s.val[k]:
            s.E[q].wait_ge(s.sem[k], s.val[k])
            s.waited[q][k] = s.val[k]
        s.val[k] += 16
        s.E[q].dma_start(out=out, in_=in_).then_inc(s.sem[k], 16)
        tok = (k, s.val[k])
        s._commit(tok, reads, writes)
        return tok

    def barrier(s):
        for e in s.E:
            for k, v in s.val.items():
                if e == "pe" and k == "pe":
                    continue
                if v > 0 and s.waited[e].get(k, 0) < v:
                    s.E[e].wait_ge(s.sem[k], v)
                    s.waited[e][k] = v


class WStream:
    def __init__(s, P, nc, WT, plan):
        s.P = P
        s.WT = WT
        s.slots = [nc.alloc_sbuf_tensor("wring%d" % i, [128, SLOT], BF16) for i in range(NSLOT)]
        s.record = plan is None
        s.plan = [] if plan is None else plan
        s.issued = 0
        s.taken = 0

    def _issue(s, n):
        name, idx, kc, ncols = s.plan[n]
        ap = s.WT[name][idx]
        i = n % NSLOT
        dst = s.slots[i][:, 0:kc * ncols].rearrange("p (c n) -> p c n", c=kc)
        s.P.dma("pool", dst, ap.rearrange("(c p) n -> p c n", p=128), reads=[], writes=[("wr", i)])

    def take(s, name, idx, kc, ncols):
        assert kc * ncols <= SLOT
        n = s.taken
        if s.record:
            s.plan.append((name, idx, kc, ncols))
        else:
            assert s.plan[n] == (name, idx, kc, ncols), (n, s.plan[n], name, idx)
            lim = min(len(s.plan), n + NSLOT - 1)
            while s.issued < lim:
                s._issue(s.issued)
                s.issued += 1
        i = n % NSLOT
        s.taken += 1
        view = s.slots[i][:, 0:kc * ncols].rearrange("p (c n) -> p c n", c=kc)
        return view, ("wr", i)


def build_program(cfg, plan=None):
    nc = bass.Bass("TRN2", target_bir_lowering=False)
    P = Prog(nc)
    _uid = [0]

    def sbt(name, shape, dt):
        _uid[0] += 1
        return nc.sbuf_tensor("%s_%d" % (name, _uid[0]), shape, dt)

    def din(name, shape):
        return nc.dram_tensor(name, list(shape), F32, kind="ExternalInput").ap()

    xin = din("xin", [NTOK, D])
    cv = din("cv", [128, KC])
    consts = din("consts", [128, 128])
    layers = cfg.get("layers", list(range(DEPTH)))
    NL = len(layers)
    WT = {}
    ada_w = din("ada_w", [NL, D, 6 * D])
    ada_b = din("ada_b", [NL, 6 * D])
    adab_f = din("adab_f", [128, DEPTH, 96])
    nw_f = din("nw_f", [128, DEPTH, 2, KC])
    ffn_wg = din("ffn_w_gate", [NL, D, FH])
    ffn_wu = din("ffn_w_up", [NL, D, FH])
    ffn_wd = din("ffn_w_down", [NL, FH, D])
    WT.update(ada_w=ada_w, ffn_w_gate=ffn_wg, ffn_w_up=ffn_wu, ffn_w_down=ffn_wd)
    has_ssd = any(l % 3 == 0 for l in layers)
    ssd_js = sorted(set(l // 3 for l in layers if l % 3 == 0))
    if has_ssd:
        NJ = len(ssd_js)
        ssd_win = din("ssd_w_in", [NJ, D, 10368])
        ssd_wout = din("ssd_w_out", [NJ, 4096, D])
        ssd_cw = din("ssd_cw", [128, NJ, 48, 3])
        ssd_cb = din("ssd_cb", [128, NJ, 48])
        ssd_dtb = din("ssd_dtb", [NJ, 128])
        ssd_alog = din("ssd_alog", [NJ, 128])
        ssd_dskip = din("ssd_dskip", [NJ, 64])
        ssd_nw = din("ssd_nw", [NJ, 4096])
        ssd_h0 = din("ssd_h0", [NJ, 2, 4096, 128])
        ssd_consts = din("ssd_consts", [128, 4, 128])
        contd = din("cont", [128, 1])
        st_out = nc.dram_tensor("st_out", [NJ, 8, 2, 4096, 128], F32, kind="ExternalOutput").ap()
        XS = nc.dram_tensor("xs_scr", [8, 16, 128, 512], BF16).ap()
        ZS = nc.dram_tensor("zs_scr", [8, 16, 128, 512], BF16).ap()
        BS = nc.dram_tensor("bs_scr", [8, 16, 128, 128], BF16).ap()
        BTS = nc.dram_tensor("bts_scr", [8, 128, NTOK], BF16).ap()
        CTS = nc.dram_tensor("cts_scr", [8, 128, NTOK], BF16).ap()
        WT.update(ssd_w_in=ssd_win, ssd_w_out=ssd_wout)
    has_ml = 2 in layers
    if has_ml:
        ml_win = din("mlstm_w_in", [D, 6176])
        ml_wout = din("mlstm_w_out", [D, D])
        ml_bg = din("ml_bg", [32])
        ml_hn = din("ml_hn", [8, 256])
        ml_C0 = din("ml_C0", [2, 8, 128, 256])
        ml_n0 = din("ml_n0", [2, 8, 128])
        ml_m0 = din("ml_m0", [128, 16])
        ml_consts = din("ml_consts", [128, 4, 128])
        if not has_ssd:
            ssd_consts = din("ssd_consts", [128, 4, 128])
            contd = din("cont", [128, 1])
        mlC_out = nc.dram_tensor("mlC_out", [8, 2, 8, 128, 256], F32, kind="ExternalOutput").ap()
        mln_out = nc.dram_tensor("mln_out", [8, 2, 8, 128], F32, kind="ExternalOutput").ap()
        mlm_out = nc.dram_tensor("mlm_out", [1, 128], F32, kind="ExternalOutput").ap()
        QTS = nc.dram_tensor("qts_scr", [8, 128, NTOK], BF16).ap()
        KTS = nc.dram_tensor("kts_scr", [8, 128, NTOK], BF16).ap()
        KMS = nc.dram_tensor("kms_scr", [8, 16, 128, 128], BF16).ap()
        VS = nc.dram_tensor("vs_scr", [8, 16, 128, 256], BF16).ap()
        OGS = nc.dram_tensor("ogs_scr", [8, 16, 128, 256], BF16).ap()
        WT.update(mlstm_w_in=ml_win, mlstm_w_out=ml_wout)
    has_attn = 1 in layers
    YT = nc.dram_tensor("ytscr", [32, 128, NTOK], BF16).ap()
    UTS = nc.dram_tensor("utscr", [KC, 128, NTOK], BF16).ap()
    if has_attn:
        attn_wqkv = din("attn_w_qkv", [D, 3072])
        attn_wout = din("attn_w_out", [D, D])
        attn_qn = din("attn_qn", [128, 1])
        attn_kn = din("attn_kn", [128, 1])
        attn_knrow = din("attn_knrow", [128])
        ropecos = din("ropecos", [128, NTOK])
        ropesin = din("ropesin", [128, NTOK])
        maskb_d = din("maskb", [128, 160])
        perm_d = din("perm", [128, 128])
        cachek = din("cachek", [512, 512])
        cachev = din("cachev", [512, 512])
        kc_out = nc.dram_tensor("kc_out", [NTOK, 512], F32, kind="ExternalOutput").ap()
        vc_out = nc.dram_tensor("vc_out", [NTOK, 512], F32, kind="ExternalOutput").ap()
        WT.update(attn_w_qkv=attn_wqkv, attn_w_out=attn_wout)
    dbg = nc.dram_tensor("dbg", [128, 8192], F32, kind="ExternalOutput").ap() if cfg.get("debug") else None
    X = nc.dram_tensor("xout", [NTOK, D], F32, kind="ExternalOutput").ap()
    GS = nc.dram_tensor("gscr", [DEPTH, 2, 128, D], F32).ap()

    NTP = cfg.get("ntp", 4)
    W = WStream(P, nc, WT, plan)
    ps = [nc.alloc_psum_tensor("psb%d" % i, [128, 512], F32) for i in range(8)]

    ident_f = nc.alloc_sbuf_tensor("ident_f", [128, 128], F32)
    ident = nc.alloc_sbuf_tensor("ident", [128, 128], BF16)
    modf = nc.alloc_sbuf_tensor("modf", [128, DEPTH, 96], F32)
    adabf = nc.alloc_sbuf_tensor("adabf", [128, DEPTH, 96], F32)
    nwf = nc.alloc_sbuf_tensor("nwf", [128, DEPTH, 2, KC], F32)
    amod = nc.alloc_sbuf_tensor("amod", [128, DEPTH, 2, KC], F32)
    cvt = nc.alloc_sbuf_tensor("cvt", [128, KC], F32)
    svb = nc.alloc_sbuf_tensor("svb", [128, KC], BF16)
    ss = nc.alloc_sbuf_tensor("ss", [128, 8], F32)
    rstd = nc.alloc_sbuf_tensor("rstd", [128, 8], F32)
    epsb = nc.alloc_sbuf_tensor("epsb", [128, 1], F32)
    P.op("dve", [], ["epsb"], lambda E: E.memset(epsb[:], EPS))

    P.dma("sp", ident_f[:], consts[:, :], [], ["ident_f"])
    P.dma("sp", adabf[:], adab_f[:, :, :], [], ["adabf"])
    P.dma("sp", nwf[:], nw_f[:, :, :, :], [], ["nwf"])
    P.dma("sp", cvt[:], cv[:, :], [], ["cvt"])
    P.op("dve", ["ident_f"], ["ident"], lambda E: E.tensor_copy(out=ident[:], in_=ident_f[:]))
    P.op("act", ["cvt"], ["svb"], lambda E: E.activation(out=svb[:], in_=cvt[:], func=AF.Silu))
    with sbt("svrep", [128, KC, 128], BF16) as svrep, sbt("gtile0", [128, 512], F32) as gt0, sbt("gtile1", [128, 512], F32) as gt1, \
            sbt("abrow0", [128, 512], F32) as ab0, sbt("abrow1", [128, 512], F32) as ab1:
        P.op("dve", ["svb"], ["svrep"], lambda E: E.tensor_copy(
            out=svrep[:], in_=svb[:].unsqueeze(2).to_broadcast([128, KC, 128])))
        gts = [gt0, gt1]
        abs_ = [ab0, ab1]
        nrow = 0
        for l in layers:
            li = layers.index(l)
            for cb in range(24):
                blk, wk = W.take("ada_w", (li, slice(None), slice(cb * 512, (cb + 1) * 512)), KC, 512)
                pb = ps[cb % 2]
                pk = ("ps", cb % 2)
                if (cb % 12) < 8:
                    def mmf(E, blk=blk, pb=pb):
                        ins = None
                        for ft in range(4):
                            for k in range(KC):
                                ins = E.matmul(pb[:, ft:ft + 1], lhsT=blk[:, k, ft * 128:(ft + 1) * 128],
                                               rhs=svb[:, k:k + 1], start=(k == 0), stop=(k == KC - 1))
                        return ins
                    P.op("pe", [wk, "svb"], [pk], mmf)
                    P.op("dve", [pk, "adabf"], ["modf"], lambda E, pb=pb, l=l, cb=cb: E.tensor_tensor(
                        out=modf[:, l, cb * 4:(cb + 1) * 4], in0=pb[:, 0:4], in1=adabf[:, l, cb * 4:(cb + 1) * 4],
                        op=ALU.add))
                else:
                    which = 0 if cb < 12 else 1
                    c0 = (cb % 12 - 8) * 512
                    j = nrow % 2
                    nrow += 1
                    P.dma("sp", abs_[j][:], ada_b[li, cb * 512:(cb + 1) * 512].partition_broadcast(128),
                          [], [("abrow", j)])

                    def mmr(E, blk=blk, pb=pb):
                        ins = None
                        for k in range(KC):
                            ins = E.matmul(pb[:, :], lhsT=svrep[:, k, :], rhs=blk[:, k, :],
                                           start=(k == 0), stop=(k == KC - 1))
                        return ins
                    P.op("pe", [wk, "svrep"], [pk], mmr)
                    P.op("dve", [pk, ("abrow", j)], [("gtile", j)], lambda E, pb=pb, j=j: E.tensor_tensor(
                        out=gts[j][:], in0=pb[:, :], in1=abs_[j][:], op=ALU.add))
                    P.dma("sp", GS[l, which, :, c0:c0 + 512], gts[j][:], [("gtile", j)], [("gs", l, which)])
            for wh in range(2):
                P.op("dve", ["modf", "nwf"], [("amod", l, wh)], lambda E, l=l, wh=wh: E.scalar_tensor_tensor(
                    out=amod[:, l, wh, :], in0=modf[:, l, wh * 48 + 16:wh * 48 + 32], scalar=1.0,
                    in1=nwf[:, l, wh, :], op0=ALU.add, op1=ALU.mult))
        P.barrier()
    if dbg is not None:
        P.dma("sp", dbg[:, 0:DEPTH * 96], modf[:].rearrange("p l j -> p (l j)"), ["modf"], ["dbg0"])
        P.dma("sp", dbg[:, 512:512 + DEPTH * 2 * KC], amod[:].rearrange("p l w c -> p (l w c)"), [], ["dbg1"])

    def norm_mod(l, wh, src, i, xt, xk, j, xn, UT, utk, col0):
        P.dma("sp", xt[:], src[i * 128:(i + 1) * 128, :], [("X", i)], [xk])
        P.op("act", [xk], ["xn", ("ss", j)], lambda E: E.activation(
            out=xn[:], in_=xt[:], func=AF.Square, accum_out=ss[:, j:j + 1]))
        P.op("act", [("ss", j)], [("rstd", j)], lambda E: E.activation(
            out=rstd[:, j:j + 1], in_=ss[:, j:j + 1], func=AF.Ln, scale=1.0 / D, bias=epsb[:, 0:1]))
        P.op("act", [("rstd", j)], [("rstd", j)], lambda E: E.activation(
            out=rstd[:, j:j + 1], in_=rstd[:, j:j + 1], func=AF.Exp, scale=-0.5))
        P.op("act", [xk, ("rstd", j)], ["xn"], lambda E: E.mul(out=xn[:], in_=xt[:], mul=rstd[:, j:j + 1]))
        for c4 in range(4):
            pb = ps[4 + (c4 % 2)]
            pk = ("ps", 4 + (c4 % 2))
            pbb = pb.bitcast(BF16)

            def tr(E, c4=c4, pbb=pbb):
                ins = None
                for cc in range(4):
                    c = c4 * 4 + cc
                    ins = E.transpose(out=pbb[:, cc * 128:(cc + 1) * 128], in_=xn[:, c * 128:(c + 1) * 128],
                                      identity=ident[:])
                return ins
            P.op("pe", ["xn", "ident"], [pk], tr)
            for cc in range(4):
                c = c4 * 4 + cc
                last = (cc == 3)
                P.op("dve", [pk, ("amod", l, wh), "modf"], [utk] if not last else [utk],
                     lambda E, c=c, cc=cc, pbb=pbb: E.tensor_scalar(
                         out=UT[:, c, col0:col0 + 128], in0=pbb[:, cc * 128:(cc + 1) * 128],
                         scalar1=amod[:, l, wh, c:c + 1], scalar2=modf[:, l, wh * 48 + c:wh * 48 + c + 1],
                         op0=ALU.mult, op1=ALU.add))

    def ffn_phase(l, src):
        li = layers.index(l)
        with sbt("xt0", [128, D], F32) as xt0, sbt("xt1", [128, D], F32) as xt1, \
                sbt("xt2", [128, D], F32) as xt2, sbt("xt3", [128, D], F32) as xt3, \
                sbt("xn", [128, D], BF16) as xn, \
                sbt("UT", [128, KC, 512], BF16) as UT, sbt("HT", [128, HC, 512], BF16) as HT, \
                sbt("g2", [128, D], F32) as g2, sbt("sg0", [128, 512], BF16) as sg0, \
                sbt("sg1", [128, 512], BF16) as sg1, sbt("tmp0", [128, 512], F32) as tmp0, \
                sbt("tmp1", [128, 512], F32) as tmp1:
            xts = [xt0, xt1, xt2, xt3]
            sgs = [sg0, sg1]
            tmps = [tmp0, tmp1]
            P.dma("sp", g2[:], GS[l, 1, :, :], [("gs", l, 1)], ["g2"])
            nsg = 0
            ntmp = 0
            for tp in range(NTP):
                for tt in range(4):
                    i = tp * 4 + tt
                    norm_mod(l, 1, src, i, xts[tt], ("xt", tt), tt, xn, UT, ("UT", tt), tt * 128)
                utks = [("UT", tt) for tt in range(4)]
                for hb in range(11):
                    hsl = (li, slice(None), slice(hb * 512, (hb + 1) * 512))
                    wg, wgk = W.take("ffn_w_gate", hsl, KC, 512)
                    wu, wuk = W.take("ffn_w_up", hsl, KC, 512)
                    for ft in range(4):
                        hi = hb * 4 + ft
                        pg = ps[hi % 2]
                        pgk = ("ps", hi % 2)
                        pu = ps[2 + hi % 2]
                        puk = ("ps", 2 + hi % 2)

                        def mmg(E, w=wg, pb=pg, ft=ft):
                            ins = None
                            for k in range(KC):
                                ins = E.matmul(pb[:, :], lhsT=w[:, k, ft * 128:(ft + 1) * 128], rhs=UT[:, k, :],
                                               start=(k == 0), stop=(k == KC - 1))
                            return ins
                        P.op("pe", [wgk] + utks, [pgk], mmg)
                        P.op("pe", [wuk] + utks, [puk], lambda E, w=wu, pb=pu, ft=ft: mmg(E, w, pb, ft))
                        sj = nsg % 2
                        nsg += 1
                        P.op("act", [pgk], [("sg", sj)], lambda E, pb=pg, sj=sj: E.activation(
                            out=sgs[sj][:], in_=pb[:, :], func=AF.Silu))
                        P.op("dve", [("sg", sj), puk], [("HT", hi)], lambda E, pb=pu, sj=sj, hi=hi: E.tensor_tensor(
                            out=HT[:, hi, :], in0=sgs[sj][:], in1=pb[:, :], op=ALU.mult))
                        if dbg is not None and tp == 0 and hi == 0:
                            with sbt("dbgu", [128, 1536], F32) as dbgu:
                                P.op("dve", [pgk], ["dbgu"], lambda E: E.tensor_copy(out=dbgu[:, 0:512], in_=pg[:, :]))
                                P.op("dve", [("sg", sj)], ["dbgu"], lambda E: E.tensor_copy(out=dbgu[:, 512:1024], in_=sgs[sj][:]))
                                P.op("dve", [puk], ["dbgu"], lambda E: E.tensor_copy(out=dbgu[:, 1024:1536], in_=pu[:, :]))
                                P.dma("sp", dbg[:, 6144:7680], dbgu[:], ["dbgu"], ["dbg4"])
                                P.barrier()
                htks = [("HT", hi) for hi in range(HC)]
                if dbg is not None and tp == 0:
                    with sbt("dbgt", [128, 2048], F32) as dbgt:
                        P.op("dve", utks, ["dbgt"], lambda E: E.tensor_copy(out=dbgt[:, 0:512], in_=UT[:, 0, :]))
                        P.op("dve", utks, ["dbgt"], lambda E: E.tensor_copy(out=dbgt[:, 512:1024], in_=UT[:, 5, :]))
                        P.op("dve", htks, ["dbgt"], lambda E: E.tensor_copy(out=dbgt[:, 1024:1536], in_=HT[:, 0, :]))
                        P.op("dve", htks, ["dbgt"], lambda E: E.tensor_copy(out=dbgt[:, 1536:2048], in_=HT[:, 17, :]))
                        P.dma("sp", dbg[:, 1024:3072], dbgt[:], ["dbgt"], ["dbg2"])
                        P.dma("sp", dbg[:, 4096:6144], g2[:], ["g2"], ["dbg3"])
                        P.barrier()
                for cb in range(4):
                    base = 0 if cb % 2 == 0 else 4
                    for k0, kn in KSPLIT:
                        wd, wdk = W.take("ffn_w_down", (li, slice(k0 * 128, (k0 + kn) * 128),
                                                        slice(cb * 512, (cb + 1) * 512)), kn, 512)
                        for tt in range(4):
                            pb = ps[base + tt]
                            pk = ("ps", base + tt)

                            def mmd(E, w=wd, pb=pb, tt=tt, k0=k0, kn=kn):
                                ins = None
                                for k in range(kn):
                                    ins = E.matmul(pb[:, :], lhsT=HT[:, k0 + k, tt * 128:(tt + 1) * 128],
                                                   rhs=w[:, k, :], start=(k0 + k == 0),
                                                   stop=(k0 + k == HC - 1))
                                return ins
                            P.op("pe", [wdk] + htks, [pk], mmd)
                    for tt in range(4):
                        pb = ps[base + tt]
                        pk = ("ps", base + tt)
                        tj = ntmp % 2
                        ntmp += 1
                        P.op("dve", [pk, "g2"], [("tmp", tj)], lambda E, pb=pb, tj=tj, cb=cb: E.tensor_tensor(
                            out=tmps[tj][:], in0=pb[:, :], in1=g2[:, cb * 512:(cb + 1) * 512], op=ALU.mult))
                        P.op("pool", [("tmp", tj), ("xt", tt)], [("xt", tt)],
                             lambda E, tj=tj, tt=tt, cb=cb: E.tensor_tensor(
                                 out=xts[tt][:, cb * 512:(cb + 1) * 512], in0=xts[tt][:, cb * 512:(cb + 1) * 512],
                                 in1=tmps[tj][:], op=ALU.add))
                for tt in range(4):
                    i = tp * 4 + tt
                    P.dma("sp", X[i * 128:(i + 1) * 128, :], xts[tt][:], [("xt", tt)], [("X", i)])
            P.barrier()


    def outproj_phase(l, src, Kc, wname, widx=None):
        with sbt("xt0", [128, D], F32) as xt0, sbt("xt1", [128, D], F32) as xt1, \
                sbt("xt2", [128, D], F32) as xt2, sbt("xt3", [128, D], F32) as xt3, \
                sbt("AT", [128, Kc, 512], BF16) as AT, sbt("g1", [128, D], F32) as g1, \
                sbt("tmp0", [128, 512], F32) as tmp0, sbt("tmp1", [128, 512], F32) as tmp1:
            xts = [xt0, xt1, xt2, xt3]
            tmps = [tmp0, tmp1]
            P.dma("sp", g1[:], GS[l, 0, :, :], [("gs", l, 0)], ["g1"])
            ntmp = 0
            for tp in range(4):
                for tt in range(4):
                    i = tp * 4 + tt
                    P.dma("sp", xts[tt][:], src[i * 128:(i + 1) * 128, :], [("X", i)], [("xt", tt)])
                P.dma("sp", AT[:], YT[0:Kc, :, tp * 512:(tp + 1) * 512].rearrange("c p n -> p c n"), [], ["AT"])
                for cb in range(4):
                    base = 0 if cb % 2 == 0 else 4
                    for kb in range(Kc // 16):
                        wsl = (slice(kb * 2048, (kb + 1) * 2048), slice(cb * 512, (cb + 1) * 512))
                        if widx is not None:
                            wsl = (ssd_js.index(widx),) + wsl
                        w, wk = W.take(wname, wsl, 16, 512)
                        for tt in range(4):
                            pb = ps[base + tt]

                            def mmo(E, w=w, pb=pb, tt=tt, kb=kb):
                                ins = None
                                for k in range(16):
                                    ins = E.matmul(pb[:, :], lhsT=AT[:, kb * 16 + k, tt * 128:(tt + 1) * 128],
                                                   rhs=w[:, k, :], start=(kb == 0 and k == 0),
                                                   stop=(kb == Kc // 16 - 1 and k == 15))
                                return ins
                            P.op("pe", [wk, "AT"], [("ps", base + tt)], mmo)
                    for tt in range(4):
                        pb = ps[base + tt]
                        pk = ("ps", base + tt)
                        tj = ntmp % 2
                        ntmp += 1
                        P.op("dve", [pk, "g1"], [("tmp", tj)], lambda E, pb=pb, tj=tj, cb=cb: E.tensor_tensor(
                            out=tmps[tj][:], in0=pb[:, :], in1=g1[:, cb * 512:(cb + 1) * 512], op=ALU.mult))
                        P.op("pool", [("tmp", tj), ("xt", tt)], [("xt", tt)],
                             lambda E, tj=tj, tt=tt, cb=cb: E.tensor_tensor(
                                 out=xts[tt][:, cb * 512:(cb + 1) * 512], in0=xts[tt][:, cb * 512:(cb + 1) * 512],
                                 in1=tmps[tj][:], op=ALU.add))
                for tt in range(4):
                    i = tp * 4 + tt
                    P.dma("sp", X[i * 128:(i + 1) * 128, :], xts[tt][:], [("xt", tt)], [("X", i)])
            P.barrier()

    def attn_phase(l, src):
        SC = 128.0 ** -0.5
        with ExitStack() as st:
            UT = st.enter_context(sbt("UTa", [128, KC, 512], BF16))
            KT = st.enter_context(sbt("KT", [128, 4, 2560], BF16))
            VA = st.enter_context(sbt("VA", [128, 20, 512], BF16))
            cosT = st.enter_context(sbt("cosT", [128, 512], F32))
            sinT = st.enter_context(sbt("sinT", [128, 512], F32))
            maskb = st.enter_context(sbt("maskb_s", [128, 160], F32))
            permf = st.enter_context(sbt("permf", [128, 128], F32))
            permb = st.enter_context(sbt("permb", [128, 128], BF16))
            onesb = st.enter_context(sbt("onesb", [128, 128], BF16))
            qnw = st.enter_context(sbt("qnw", [128, 2], F32))
            knrow = st.enter_context(sbt("knrow", [128, 128], F32))
            sqb = st.enter_context(sbt("sqb", [128, 512], BF16))
            rt = st.enter_context(sbt("rt", [128, 512], F32))
            qn = st.enter_context(sbt("qn", [128, 512], BF16))
            t1 = st.enter_context(sbt("t1", [128, 512], F32))
            t2 = st.enter_context(sbt("t2", [128, 512], F32))
            QT0 = st.enter_context(sbt("QT0", [128, 512], BF16))
            QT1 = st.enter_context(sbt("QT1", [128, 512], BF16))
            PT0 = st.enter_context(sbt("PT0", [128, 256], BF16))
            PT1 = st.enter_context(sbt("PT1", [128, 256], BF16))
            PT2 = st.enter_context(sbt("PT2", [128, 256], BF16))
            rs = st.enter_context(sbt("rs", [128, 256], F32))
            AO0 = st.enter_context(sbt("AO0", [128, 256], BF16))
            AO1 = st.enter_context(sbt("AO1", [128, 256], BF16))
            kf = st.enter_context(sbt("kf", [128, 512], F32))
            vf = st.enter_context(sbt("vf", [128, 512], F32))
            ssk = st.enter_context(sbt("ssk", [128, 4], F32))
            ckf = st.enter_context(sbt("ckf", [128, 512], F32))
            ckb = st.enter_context(sbt("ckb", [128, 512], BF16))
            QTs = [QT0, QT1]
            PTs = [PT0, PT1, PT2]
            AOs = [AO0, AO1]
            P.dma("sp", maskb[:], maskb_d[:, :], [], ["maskb"])
            P.dma("sp", permf[:], perm_d[:, :], [], ["permf"])
            P.dma("sp", qnw[:, 0:1], attn_qn[:, :], [], ["qnw"])
            P.dma("sp", qnw[:, 1:2], attn_kn[:, :], ["qnw"], ["qnw"])
            P.dma("sp", knrow[:], attn_knrow.partition_broadcast(128), [], ["knrow"])
            P.op("dve", ["permf"], ["permb"], lambda E: E.tensor_copy(out=permb[:], in_=permf[:]))
            P.op("dve", [], ["onesb"], lambda E: E.memset(onesb[:], 1.0))
            P.op("dve", ["qnw"], ["qnw"], lambda E: E.tensor_scalar_mul(out=qnw[:, 0:1], in0=qnw[:, 0:1], scalar1=SC))
            def head_fm(w, wk, c0, tt, widx, dst, dstk):
                def mmh(E):
                    ins = None
                    for k in range(KC):
                        ins = E.matmul(ps[0][:, :], lhsT=w[:, k, c0:c0 + 128], rhs=UT[:, k, tt * 512:(tt + 1) * 512],
                                       start=(k == 0), stop=(k == KC - 1))
                    return ins
                P.op("pe", [wk] + utk, [("ps", 0)], mmh)
                P.op("act", [("ps", 0)], ["sqb"], lambda E: E.activation(out=sqb[:], in_=ps[0][:, :], func=AF.Square))
                P.op("pe", ["sqb", "onesb"], [("ps", 1)], lambda E: E.matmul(
                    ps[1][:, :], lhsT=onesb[:], rhs=sqb[:], start=True, stop=True))
                P.op("act", [("ps", 1)], ["rt"], lambda E: E.activation(
                    out=rt[:], in_=ps[1][:, :], func=AF.Ln, scale=1.0 / 128, bias=epsb[:, 0:1]))
                P.op("act", ["rt"], ["rt"], lambda E: E.activation(out=rt[:], in_=rt[:], func=AF.Exp, scale=-0.5))
                P.op("dve", [("ps", 0), "rt", "qnw"], ["qn"], lambda E: E.scalar_tensor_tensor(
                    out=qn[:], in0=ps[0][:, :], scalar=qnw[:, widx:widx + 1], in1=rt[:], op0=ALU.mult, op1=ALU.mult))
                P.op("pe", ["qn", "permb"], [("ps", 2)], lambda E: E.matmul(
                    ps[2][:, :], lhsT=permb[:], rhs=qn[:], start=True, stop=True))
                P.op("dve", [("ps", 2), "sinT"], ["t1"], lambda E: E.tensor_tensor(
                    out=t1[:], in0=ps[2][:, :], in1=sinT[:], op=ALU.mult))
                P.op("pool", ["qn", "cosT"], ["t2"], lambda E: E.tensor_tensor(
                    out=t2[:], in0=qn[:], in1=cosT[:], op=ALU.mult))
                P.op("dve", ["t1", "t2"], [dstk], lambda E: E.tensor_tensor(out=dst, in0=t1[:], in1=t2[:], op=ALU.add))

            def load_tables(tt):
                P.dma("sp", cosT[:], ropecos[:, tt * 512:(tt + 1) * 512], [], ["cosT"])
                P.dma("sp", sinT[:], ropesin[:, tt * 512:(tt + 1) * 512], [], ["sinT"])

            utk = [("UT", j) for j in range(4)]
            wkk, wkkk = W.take("attn_w_qkv", (slice(None), slice(2048, 2560)), KC, 512)
            wv, wvk = W.take("attn_w_qkv", (slice(None), slice(2560, 3072)), KC, 512)
            with sbt("xta", [128, D], F32) as xta, sbt("xtb", [128, D], F32) as xtb, \
                    sbt("xn", [128, D], BF16) as xn:
                xtl = [xta, xtb]
                for tt in range(4):
                    load_tables(tt)
                    for i4 in range(4):
                        i = tt * 4 + i4
                        norm_mod(l, 0, src, i, xtl[i % 2], ("xt", i % 2), i % 2, xn, UT, ("UT", i4), i4 * 128)
                    P.dma("sp", UTS[:, :, tt * 512:(tt + 1) * 512].rearrange("c p n -> p c n"), UT[:], utk, [("UTS", tt)])
                    for i4 in range(4):
                        i = tt * 4 + i4
                        pkb, pvb = ps[(i % 2) * 2], ps[(i % 2) * 2 + 1]
                        pkk, pvk = ("ps", (i % 2) * 2), ("ps", (i % 2) * 2 + 1)

                        def mmt(E, w, pb, i4=i4):
                            ins = None
                            for k in range(KC):
                                ins = E.matmul(pb[:, :], lhsT=UT[:, k, i4 * 128:(i4 + 1) * 128], rhs=w[:, k, :],
                                               start=(k == 0), stop=(k == KC - 1))
                            return ins
                        P.op("pe", [wkkk] + utk, [pkk], lambda E, pb=pkb, mmt=mmt: mmt(E, wkk, pb))
                        P.op("pe", [wvk] + utk, [pvk], lambda E, pb=pvb, mmt=mmt: mmt(E, wv, pb))
                        for hh in range(4):
                            P.op("act", [pkk], ["sqb", ("ssk", hh)], lambda E, pb=pkb, hh=hh: E.activation(
                                out=sqb[:, 0:128], in_=pb[:, hh * 128:(hh + 1) * 128], func=AF.Square,
                                accum_out=ssk[:, hh:hh + 1]))
                        sskk = [("ssk", hh) for hh in range(4)]
                        P.op("act", sskk, sskk, lambda E: E.activation(
                            out=ssk[:, :], in_=ssk[:, :], func=AF.Ln, scale=1.0 / 128, bias=epsb[:, 0:1]))
                        P.op("act", sskk, sskk, lambda E: E.activation(out=ssk[:, :], in_=ssk[:, :], func=AF.Exp, scale=-0.5))
                        for hh in range(4):
                            P.op("dve", [pkk, ("ssk", hh), "knrow"], ["kf"], lambda E, pb=pkb, hh=hh: E.scalar_tensor_tensor(
                                out=kf[:, hh * 128:(hh + 1) * 128], in0=pb[:, hh * 128:(hh + 1) * 128],
                                scalar=ssk[:, hh:hh + 1], in1=knrow[:], op0=ALU.mult, op1=ALU.mult))
                        P.dma("sp", kc_out[i * 128:(i + 1) * 128, :], kf[:], ["kf"], [("kco", i)])
                        P.op("act", [pvk], ["vf"], lambda E, pb=pvb: E.copy(out=vf[:], in_=pb[:, :]))
                        P.op("dve", [pvk], [("VA", 4 + i)], lambda E, pb=pvb, i=i: E.tensor_copy(out=VA[:, 4 + i, :], in_=pb[:, :]))
                        P.dma("sp", vc_out[i * 128:(i + 1) * 128, :], vf[:], ["vf"], [("vco", i)])
                    for hh in range(4):
                        head_fm(wkk, wkkk, hh * 128, 0, 1, KT[:, hh, 512 + tt * 512:512 + (tt + 1) * 512], ("KT", hh))
                P.barrier()
            for j in range(4):
                P.dma("pool", VA[:, j, :], cachev[j * 128:(j + 1) * 128, :], [], [("VA", j)])
                P.dma("sp", ckf[:], cachek[j * 128:(j + 1) * 128, :], [], ["ckf"])
                P.op("dve", ["ckf"], ["ckb"], lambda E: E.tensor_copy(out=ckb[:], in_=ckf[:]))
                pbb = ps[4].bitcast(BF16)

                def trc(E, pbb=pbb):
                    ins = None
                    for hh in range(4):
                        ins = E.transpose(out=pbb[:, hh * 128:(hh + 1) * 128], in_=ckb[:, hh * 128:(hh + 1) * 128],
                                          identity=ident[:])
                    return ins
                P.op("pe", ["ckb", "ident"], [("ps", 4)], trc)
                for hh in range(4):
                    P.op("dve", [("ps", 4)], [("KT", hh)], lambda E, hh=hh, j=j, pbb=pbb: E.tensor_copy(
                        out=KT[:, hh, j * 128:(j + 1) * 128], in_=pbb[:, hh * 128:(hh + 1) * 128]))

            nq = 0
            npt = 0
            nao = 0
            vak = [("VA", j) for j in range(20)]
            for tt in range(4):
                load_tables(tt)
                P.dma("sp", UT[:], UTS[:, :, tt * 512:(tt + 1) * 512].rearrange("c p n -> p c n"), [("UTS", tt)], utk)
                for qb in range(4):
                    wq, wqk = W.take("attn_w_qkv", (slice(None), slice(qb * 512, (qb + 1) * 512)), KC, 512)
                    for hq in range(4):
                        h = qb * 4 + hq
                        kv = h // 4
                        qj = nq % 2
                        nq += 1
                        head_fm(wq, wqk, hq * 128, 0, 0, QTs[qj][:], ("QT", qj))
                        for sub in range(2):
                            qt8 = tt * 2 + sub
                            SB = [3, 4, 7]

                            def emit_S(kt, kv=kv, qj=qj, sub=sub):
                                bi = SB[kt % 3]
                                P.op("pe", [("KT", kv), ("QT", qj)], [("ps", bi)], lambda E: E.matmul(
                                    ps[bi][:, 0:256], lhsT=KT[:, kv, kt * 128:(kt + 1) * 128],
                                    rhs=QTs[qj][:, sub * 256:(sub + 1) * 256], start=True, stop=True))

                            emit_S(0)
                            emit_S(1)
                            for kt in range(20):
                                if kt + 2 < 20:
                                    emit_S(kt + 2)
                                bi = SB[kt % 3]
                                sb = ps[bi]
                                sk = ("ps", bi)
                                pj = npt % 3
                                npt += 1
                                P.op("act", [sk, "maskb"], [("PT", pj)], lambda E, sb=sb, pj=pj, qt8=qt8, kt=kt: E.activation(
                                    out=PTs[pj][:], in_=sb[:, 0:256], func=AF.Exp,
                                    bias=maskb[:, qt8 * 20 + kt:qt8 * 20 + kt + 1]))
                                P.op("pe", [("PT", pj)] + vak, [("ps", 5), ("ps", 6)],
                                     lambda E, pj=pj, kt=kt, kv=kv: (
                                         E.matmul(ps[5][:, 0:256], lhsT=VA[:, kt, kv * 128:(kv + 1) * 128], rhs=PTs[pj][:],
                                                  start=(kt == 0), stop=(kt == 19)),
                                         E.matmul(ps[6][:, 0:256], lhsT=onesb[:], rhs=PTs[pj][:],
                                                  start=(kt == 0), stop=(kt == 19)))[1])
                            P.op("dve", [("ps", 6)], ["rs"], lambda E: E.reciprocal(out=rs[:], in_=ps[6][:, 0:256]))
                            aj = nao % 2
                            nao += 1
                            P.op("dve", [("ps", 5), "rs"], [("AO", aj)], lambda E, aj=aj: E.tensor_tensor(
                                out=AOs[aj][:], in0=ps[5][:, 0:256], in1=rs[:], op=ALU.mult))
                            P.dma("sp", YT[h, :, qt8 * 256:(qt8 + 1) * 256], AOs[aj][:], [("AO", aj)], [("YT", h, qt8)])
            P.barrier()


    def v3(ap):
        return ap.rearrange("p (r q) -> p r q", q=64)

    def ssd_phase(l, src):
        jj = ssd_js.index(l // 3)
        with ExitStack() as so:
            def sb(name, shape, dt, st=so):
                return st.enter_context(sbt(name, shape, dt))
            DTt = sb("DTt", [128, 16, 128], F32)
            At = sb("At", [128, 16, 128], F32)
            EX = sb("EX", [128, 16, 4, 64], F32)
            ETOT = sb("ETOT", [128, 16, 128], F32)
            cst = sb("ssdc", [128, 4, 128], F32)
            onesf = sb("onesf", [128, 128], F32)
            cont = sb("contt", [128, 1], F32)
            ncont = sb("ncont", [128, 1], F32)
            maskF, maskB, SLf, SLb = cst[:, 0, :], cst[:, 1, :], cst[:, 2, :], cst[:, 3, :]
            P.dma("sp", cst[:], ssd_consts[:, :, :], [], ["ssdc"])
            P.dma("sp", cont[:], contd[:, :], [], ["cont"])
            P.op("dve", [], ["onesf"], lambda E: E.memset(onesf[:], 1.0))
            P.op("dve", ["cont"], ["ncont"], lambda E: E.tensor_scalar_add(out=ncont[:], in0=cont[:], scalar1=-1.0))

            with ExitStack() as s1:
                UT = sb("UTs", [128, KC, NTOK], BF16, s1)
                cw = sb("cw", [128, 48, 3], F32, s1)
                ncw = sb("ncw", [128, 48, 3], F32, s1)
                cbb = sb("cbb", [128, 48], F32, s1)
                dtb = sb("dtb", [128, 128], F32, s1)
                arow = sb("arow", [128, 128], F32, s1)
                P.dma("sp", cw[:], ssd_cw[:, jj, :, :], [], ["cw"])
                P.dma("sp", cbb[:], ssd_cb[:, jj, :], [], ["cbb"])
                P.dma("sp", dtb[:], ssd_dtb[jj, :].partition_broadcast(128), [], ["dtb"])
                P.dma("sp", arow[:], ssd_alog[jj, :].partition_broadcast(128), [], ["arow"])
                P.op("act", ["arow"], ["arow"], lambda E: E.activation(out=arow[:], in_=arow[:], func=AF.Exp))
                P.op("dve", ["arow"], ["arow"], lambda E: E.tensor_scalar_mul(out=arow[:], in0=arow[:], scalar1=-1.0))
                P.op("dve", ["cw", "ncont"], ["ncw"], lambda E: E.tensor_scalar_mul(
                    out=ncw[:].rearrange("p a b -> p (a b)"), in0=cw[:].rearrange("p a b -> p (a b)"), scalar1=ncont[:, 0:1]))
                with ExitStack() as s0:
                    xta = sb("xta", [128, D], F32, s0)
                    xtb = sb("xtb", [128, D], F32, s0)
                    xn = sb("xn", [128, D], BF16, s0)
                    xtl = [xta, xtb]
                    for i in range(16):
                        norm_mod(l, 0, src, i, xtl[i % 2], ("xt", i % 2), i % 2, xn, UT, ("UT", i // 4), i * 128)
                    P.barrier()
                utk = [("UT", q) for q in range(4)]
                with ExitStack() as s2:
                    t_a = sb("t_a", [128, 128], F32, s2)
                    t_b = sb("t_b", [128, 128], F32, s2)
                    t_c = sb("t_c", [128, 128], F32, s2)
                    wdt, wdtk = W.take("ssd_w_in", (jj, slice(None), slice(10240, 10368)), KC, 128)
                    for i in range(16):
                        pb = ps[i % 2]
                        pk = ("ps", i % 2)

                        def mmdt(E, pb=pb, i=i):
                            ins = None
                            for k in range(KC):
                                ins = E.matmul(pb[:, 0:128], lhsT=UT[:, k, i * 128:(i + 1) * 128], rhs=wdt[:, k, :],
                                               start=(k == 0), stop=(k == KC - 1))
                            return ins
                        P.op("pe", [wdtk] + utk, [pk], mmdt)
                        P.op("dve", [pk, "dtb"], ["t_a"], lambda E, pb=pb: E.tensor_tensor(
                            out=t_a[:], in0=pb[:, 0:128], in1=dtb[:], op=ALU.add))
                        P.op("act", ["t_a"], ["t_b"], lambda E: E.activation(out=t_b[:], in_=t_a[:], func=AF.Abs))
                        P.op("act", ["t_b"], ["t_b"], lambda E: E.activation(out=t_b[:], in_=t_b[:], func=AF.Exp, scale=-1.0))
                        P.op("act", ["t_b"], ["t_b"], lambda E: E.activation(out=t_b[:], in_=t_b[:], func=AF.Ln, bias=1.0))
                        P.op("dve", ["t_a"], ["t_c"], lambda E: E.tensor_scalar_max(out=t_c[:], in0=t_a[:], scalar1=0.0))
                        P.op("dve", ["t_b", "t_c"], [("DT", i)], lambda E, i=i: E.tensor_tensor(
                            out=DTt[:, i, :], in0=t_b[:], in1=t_c[:], op=ALU.add))
                        P.op("dve", [("DT", i), "arow"], [("A", i)], lambda E, i=i: E.tensor_tensor(
                            out=At[:, i, :], in0=DTt[:, i, :], in1=arow[:], op=ALU.mult))
                        pc = ps[2 + i % 2]
                        pck = ("ps", 2 + i % 2)

                        def mmcs(E, pc=pc, i=i):
                            E.matmul(pc[:, 0:64], lhsT=maskF, rhs=At[:, i, 0:64], start=True, stop=True)
                            E.matmul(pc[:, 64:128], lhsT=SLf, rhs=At[:, i, 0:64], start=True, stop=True)
                            E.matmul(pc[:, 128:192], lhsT=maskB, rhs=At[:, i, 64:128], start=True, stop=True)
                            E.matmul(pc[:, 192:256], lhsT=SLb, rhs=At[:, i, 64:128], start=True, stop=True)
                            return E.matmul(pc[:, 256:384], lhsT=onesf[:], rhs=At[:, i, :], start=True, stop=True)
                        P.op("pe", [("A", i), "ssdc", "onesf"], [pck], mmcs)
                        P.op("act", [pck], [("EX", i)], lambda E, pc=pc, i=i: E.activation(
                            out=EX[:, i, :, :].rearrange("p a b -> p (a b)"), in_=pc[:, 0:256], func=AF.Exp))
                        P.op("act", [pck], [("ETOT", i)], lambda E, pc=pc, i=i: E.activation(
                            out=ETOT[:, i, :], in_=pc[:, 256:384], func=AF.Exp))
                with ExitStack() as s3:
                    zt0 = sb("zt0", [128, 512], BF16, s3)
                    zt1 = sb("zt1", [128, 512], BF16, s3)
                    zts = [zt0, zt1]
                    nz = 0
                    for zb in range(8):
                        wz, wzk = W.take("ssd_w_in", (jj, slice(None), slice(zb * 512, (zb + 1) * 512)), KC, 512)
                        for i in range(16):
                            pb = ps[nz % 2]
                            pk = ("ps", nz % 2)
                            zj = nz % 2
                            nz += 1

                            def mmz(E, pb=pb, i=i, wz=wz):
                                ins = None
                                for k in range(KC):
                                    ins = E.matmul(pb[:, :], lhsT=UT[:, k, i * 128:(i + 1) * 128], rhs=wz[:, k, :],
                                                   start=(k == 0), stop=(k == KC - 1))
                                return ins
                            P.op("pe", [wzk] + utk, [pk], mmz)
                            P.op("act", [pk], [("zt", zj)], lambda E, pb=pb, zj=zj: E.activation(
                                out=zts[zj][:], in_=pb[:, :], func=AF.Silu))
                            P.dma("sp", ZS[zb, i, :, :], zts[zj][:], [("zt", zj)], [("ZS", zb, i)])
                    raw0 = sb("raw0", [128, NTOK], F32, s3)
                    raw1 = raw0
                    acc = sb("acc", [128, NTOK], F32, s3)
                    xb0 = sb("xb0", [128, NTOK], BF16, s3)
                    xb1 = xb0
                    tr0 = sb("tr0", [128, 512], BF16, s3)
                    tr1 = sb("tr1", [128, 512], BF16, s3)
                    raws = [raw0, raw1]
                    xbs = [xb0, xb1]
                    trs = [tr0, tr1]
                    nft = 0
                    ntr = 0
                    for cb in range(12):
                        wx, wxk = W.take("ssd_w_in", (jj, slice(None), slice(4096 + cb * 512, 4096 + (cb + 1) * 512)), KC, 512)
                        for ft in range(4):
                            ct = cb * 4 + ft
                            rj = 0
                            nft += 1
                            raw = raws[rj]
                            xb = xbs[rj]
                            for tt in range(4):
                                pb = ps[tt % 2]
                                pk = ("ps", tt % 2)

                                def mmx(E, pb=pb, tt=tt, ft=ft, wx=wx):
                                    ins = None
                                    for k in range(KC):
                                        ins = E.matmul(pb[:, :], lhsT=wx[:, k, ft * 128:(ft + 1) * 128],
                                                       rhs=UT[:, k, tt * 512:(tt + 1) * 512], start=(k == 0), stop=(k == KC - 1))
                                    return ins
                                P.op("pe", [wxk] + utk, [pk], mmx)
                                P.op("act", [pk], [("raw", rj)], lambda E, pb=pb, tt=tt, raw=raw: E.copy(
                                    out=raw[:, tt * 512:(tt + 1) * 512], in_=pb[:, :]))
                            rk = ("raw", rj)
                            P.op("dve", [rk, "cw"], ["acc"], lambda E, raw=raw, ct=ct: E.tensor_scalar_mul(
                                out=acc[:], in0=raw[:], scalar1=cw[:, ct, 1:2]))
                            P.op("dve", [rk, "cw", "acc"], ["acc"], lambda E, raw=raw, ct=ct: E.scalar_tensor_tensor(
                                out=acc[:, 1:NTOK], in0=raw[:, 0:NTOK - 1], scalar=cw[:, ct, 0:1], in1=acc[:, 1:NTOK],
                                op0=ALU.mult, op1=ALU.add))
                            P.op("dve", [rk, "cw", "acc"], ["acc"], lambda E, raw=raw, ct=ct: E.scalar_tensor_tensor(
                                out=acc[:, 0:NTOK - 1], in0=raw[:, 1:NTOK], scalar=cw[:, ct, 2:3], in1=acc[:, 0:NTOK - 1],
                                op0=ALU.mult, op1=ALU.add))
                            accv = acc[:].rearrange("p (s t) -> p s t", t=256)
                            rawv = raw[:].rearrange("p (s t) -> p s t", t=256)
                            P.op("dve", [rk, "ncw", "acc"], ["acc"], lambda E, accv=accv, rawv=rawv, ct=ct: E.scalar_tensor_tensor(
                                out=accv[:, 1:8, 0], in0=rawv[:, 0:7, 255], scalar=ncw[:, ct, 0:1], in1=accv[:, 1:8, 0],
                                op0=ALU.mult, op1=ALU.add))
                            P.op("dve", [rk, "ncw", "acc"], ["acc"], lambda E, accv=accv, rawv=rawv, ct=ct: E.scalar_tensor_tensor(
                                out=accv[:, 0:7, 255], in0=rawv[:, 1:8, 0], scalar=ncw[:, ct, 2:3], in1=accv[:, 0:7, 255],
                                op0=ALU.mult, op1=ALU.add))
                            P.op("act", ["acc", "cbb"], [("xb", rj)], lambda E, xb=xb, ct=ct: E.activation(
                                out=xb[:], in_=acc[:], func=AF.Silu, bias=cbb[:, ct:ct + 1]))
                            xk = ("xb", rj)
                            if ct >= 40:
                                P.dma("sp", CTS[ct - 40, :, :], xb[:], [xk], [("CTS", ct)])
                                continue
                            if ct >= 32:
                                P.dma("sp", BTS[ct - 32, :, :], xb[:], [xk], [("BTS", ct)])
                            for i4 in range(4):
                                pbb = ps[4 + i4 % 2].bitcast(BF16)
                                pk = ("ps", 4 + i4 % 2)

                                def trx(E, pbb=pbb, i4=i4, xb=xb):
                                    ins = None
                                    for q in range(4):
                                        i = i4 * 4 + q
                                        ins = E.transpose(out=pbb[:, q * 128:(q + 1) * 128], in_=xb[:, i * 128:(i + 1) * 128],
                                                          identity=ident[:])
                                    return ins
                                P.op("pe", [xk, "ident"], [pk], trx)
                                tj = ntr % 2
                                ntr += 1
                                P.op("act", [pk], [("tr", tj)], lambda E, pbb=pbb, tj=tj: E.copy(out=trs[tj][:], in_=pbb[:, 0:512]))
                                if ct < 32:
                                    g, q4 = ct // 4, ct % 4
                                    dst = XS[g, i4 * 4:(i4 + 1) * 4, :, q4 * 128:(q4 + 1) * 128].rearrange("i p n -> p i n")
                                else:
                                    dst = BS[ct - 32, i4 * 4:(i4 + 1) * 4, :, :].rearrange("i p n -> p i n")
                                P.dma("sp", dst, trs[tj][:].rearrange("p (i n) -> p i n", n=128), [("tr", tj)], [("XSw", ct, i4)])
                P.barrier()

            if cfg.get("ssd_s2", True) is False:
                return
            with ExitStack() as s4:
                xg = sb("xg", [128, 16, 512], BF16, s4)
                Bg = sb("Bg", [128, 16, 128], BF16, s4)
                BT = sb("BT", [128, NTOK], BF16, s4)
                CT = sb("CT", [128, NTOK], BF16, s4)
                szs = [sb("sz0", [128, 512], BF16, s4), sb("sz1", [128, 512], BF16, s4)]
                Y = sb("Y", [128, 16, 512], F32, s4)
                hs = [sb("hf", [128, 512], F32, s4), sb("hb", [128, 512], F32, s4)]
                h16 = sb("h16", [128, 512], BF16, s4)
                xdt = [sb("xdtf", [128, 512], BF16, s4), sb("xdtb", [128, 512], BF16, s4)]
                xdd = [sb("xddf", [128, 512], BF16, s4), sb("xddb", [128, 512], BF16, s4)]
                xdd2 = [xdd[0], sb("xddf2", [128, 512], BF16, s4)]
                CBm = [sb("CBf", [128, 128], F32, s4), sb("CBb", [128, 128], F32, s4)]
                lh = [sb("lh0", [128, 128], F32, s4), sb("lh1", [128, 128], F32, s4),
                      sb("lh2", [128, 128], F32, s4), sb("lh3", [128, 128], F32, s4)]
                LT = sb("LT", [128, 1024], F32, s4)
                MT = [sb("MTf", [128, 8, 128], BF16, s4), sb("MTb", [128, 8, 128], BF16, s4)]
                tmpy = sb("tmpy", [128, 512], F32, s4)
                dx = sb("dx", [128, 512], F32, s4)
                G = sb("G", [128, 512], F32, s4)
                yn = sb("yn", [128, 512], BF16, s4)
                ynT = sb("ynT", [128, 512], BF16, s4)
                drow = sb("drow", [128, 64], F32, s4)
                nwrow = sb("nwrow", [128, 512], F32, s4)
                ssq = sb("ssq", [128, 2], F32, s4)
                h0t = sb("h0t", [128, 128], F32, s4)
                hot0_ = sb("hot0", [128, 512], F32, s4)
                hots = [hot0_, hot0_]
                nhot = [0]
                P.dma("sp", drow[:], ssd_dskip[jj, :].partition_broadcast(128), [], ["drow"])
                dsel = [(0, maskF, SLf), (1, maskB, SLb)]
                nlh = [0]
                for g in range(cfg.get("ssd_groups", 8)):
                    P.dma("sp", xg[:], XS[g, :, :, :].rearrange("i p n -> p i n"), [], ["xg"])
                    P.dma("sp", Bg[:], BS[g, :, :, :].rearrange("i p n -> p i n"), [], ["Bg"])
                    P.dma("sp", BT[:], BTS[g, :, :], [], ["BT"])
                    P.dma("sp", CT[:], CTS[g, :, :], [], ["CT"])
                    P.dma("sp", nwrow[:], ssd_nw[jj, g * 512:(g + 1) * 512].partition_broadcast(128), [], ["nwrow"])
                    for d_ in range(2):
                        for q in range(4):
                            P.dma("sp", h0t[:], ssd_h0[jj, d_, g * 512 + q * 128:g * 512 + (q + 1) * 128, :], [], ["h0t"])
                            P.op("pe", ["h0t", "ident_f"], [("ps", 7)], lambda E: E.transpose(
                                out=ps[7][:, 0:128], in_=h0t[:], identity=ident_f[:]))
                            P.op("dve", [("ps", 7)], [("h", d_)], lambda E, d_=d_, q=q: E.tensor_copy(
                                out=hs[d_][:, q * 128:(q + 1) * 128], in_=ps[7][:, 0:128]))

                    def emit_state(d_, seg):
                        def tr4(E):
                            ins = None
                            for q in range(4):
                                ins = E.transpose(out=ps[7][:, q * 128:(q + 1) * 128], in_=hs[d_][:, q * 128:(q + 1) * 128],
                                                  identity=ident_f[:])
                            return ins
                        P.op("pe", [("h", d_), "ident_f"], [("ps", 7)], tr4)
                        hj = 0
                        P.op("act", [("ps", 7)], [("hot", hj)], lambda E: E.copy(out=hots[hj][:], in_=ps[7][:, :]))
                        P.dma("sp", st_out[jj, seg, d_, g * 512:(g + 1) * 512, :].rearrange("(q p) n -> p q n", p=128),
                              hots[hj][:].rearrange("p (q n) -> p q n", n=128), [("hot", hj)], [("sto", seg, d_, g)])

                    def xprep(c, d_, eng, par=0, do_dd=True):
                        hsl = slice(d_ * 64 + g * 8, d_ * 64 + g * 8 + 8)
                        P.op(eng, ["xg", ("DT", c)], [("xdt", d_)], lambda E: E.tensor_tensor(
                            out=v3(xdt[d_][:]), in0=v3(xg[:, c, :]), in1=DTt[:, c, hsl].unsqueeze(2).to_broadcast([128, 8, 64]),
                            op=ALU.mult))
                        if not do_dd:
                            return
                        esl = EX[:, c, 1 if d_ == 0 else 3, g * 8:g * 8 + 8]
                        xd = xdd2[par] if d_ == 0 else xdd[1]
                        P.op(eng, [("xdt", d_), ("EX", c)], [("xdd", d_, par)], lambda E: E.tensor_tensor(
                            out=v3(xd[:]), in0=v3(xdt[d_][:]), in1=esl.unsqueeze(2).to_broadcast([128, 8, 64]), op=ALU.mult))

                    def state_step(c, d_, par=0):
                        hk = ("h", d_)
                        P.op("act", [hk], ["h16"], lambda E: E.copy(out=h16[:], in_=hs[d_][:]))
                        P.op("pe", ["CT", "h16"], [("ps", 6)], lambda E: E.matmul(
                            ps[6][:, :], lhsT=CT[:, c * 128:(c + 1) * 128], rhs=h16[:], start=True, stop=True))
                        esl = EX[:, c, 0 if d_ == 0 else 2, g * 8:g * 8 + 8]
                        P.op("dve", [("ps", 6), ("EX", c)], ["tmpy"], lambda E: E.tensor_tensor(
                            out=v3(tmpy[:]), in0=v3(ps[6][:, :]), in1=esl.unsqueeze(2).to_broadcast([128, 8, 64]), op=ALU.mult))
                        xd = xdd2[par] if d_ == 0 else xdd[1]
                        P.op("pe", ["Bg", ("xdd", d_, par)], [("ps", 7)], lambda E: E.matmul(
                            ps[7][:, :], lhsT=Bg[:, c, :], rhs=xd[:], start=True, stop=True))
                        tsl = ETOT[:, c, d_ * 64 + g * 8:d_ * 64 + g * 8 + 8]
                        P.op("dve", [hk, ("ETOT", c)], [hk], lambda E: E.tensor_tensor(
                            out=v3(hs[d_][:]), in0=v3(hs[d_][:]), in1=tsl.unsqueeze(2).to_broadcast([128, 8, 64]), op=ALU.mult))
                        P.op("dve", [hk, ("ps", 7)], [hk], lambda E: E.tensor_tensor(
                            out=hs[d_][:], in0=hs[d_][:], in1=ps[7][:, :], op=ALU.add))

                    def intra(c):
                        xprep(c, 0, "dve", c % 2)
                        xprep(c, 1, "pool", 0, False)
                        P.op("pe", ["BT", "CT"], [("ps", 0)], lambda E, c=c: E.matmul(
                            ps[0][:, 0:128], lhsT=BT[:, c * 128:(c + 1) * 128], rhs=CT[:, c * 128:(c + 1) * 128],
                            start=True, stop=True))
                        for d_, mk, SL in dsel:
                            P.op("dve", [("ps", 0), "ssdc"], [("CBm", d_)], lambda E, d_=d_, mk=mk: E.tensor_tensor(
                                out=CBm[d_][:], in0=ps[0][:, 0:128], in1=mk, op=ALU.mult))
                        for d_, mk, SL in dsel:
                            for half in range(2):
                                pb = ps[1 + d_ * 2 + half]
                                pk = ("ps", 1 + d_ * 2 + half)
                                for r4 in range(4):
                                    r = half * 4 + r4
                                    lj = nlh[0] % 4
                                    nlh[0] += 1
                                    col = d_ * 64 + g * 8 + r
                                    if lj % 2 == 0:
                                        P.op("act", [("A", c), "ssdc"], [("lh", lj)], lambda E, lj=lj, SL=SL, col=col, c=c: E.mul(
                                            out=lh[lj][:], in_=SL, mul=At[:, c, col:col + 1]))
                                    else:
                                        P.op("dve", [("A", c), "ssdc"], [("lh", lj)], lambda E, lj=lj, SL=SL, col=col, c=c: E.tensor_scalar_mul(
                                            out=lh[lj][:], in0=SL, scalar1=At[:, c, col:col + 1]))
                                    P.op("pe", [("lh", lj), "ssdc"], [pk], lambda E, pb=pb, r4=r4, lj=lj, mk=mk: E.matmul(
                                        pb[:, r4 * 128:(r4 + 1) * 128], lhsT=lh[lj][:], rhs=mk, start=True, stop=True))
                                P.op("act", [pk], [("LT", half)], lambda E, pb=pb, half=half: E.activation(
                                    out=LT[:, half * 512:(half + 1) * 512], in_=pb[:, :], func=AF.Exp))
                                P.op("dve", [("LT", half), ("CBm", d_)], [("MT", d_)], lambda E, d_=d_, half=half: E.tensor_tensor(
                                    out=MT[d_][:, half * 4:(half + 1) * 4, :],
                                    in0=LT[:, half * 512:(half + 1) * 512].rearrange("p (r t) -> p r t", t=128),
                                    in1=CBm[d_][:].unsqueeze(1).to_broadcast([128, 4, 128]), op=ALU.mult))

                        def mmy(E):
                            ins = None
                            for r in range(8):
                                E.matmul(ps[5][:, r * 64:(r + 1) * 64], lhsT=MT[0][:, r, :], rhs=xdt[0][:, r * 64:(r + 1) * 64],
                                         start=True, stop=False)
                                ins = E.matmul(ps[5][:, r * 64:(r + 1) * 64], lhsT=MT[1][:, r, :], rhs=xdt[1][:, r * 64:(r + 1) * 64],
                                               start=False, stop=True)
                            return ins
                        P.op("pe", [("MT", 0), ("MT", 1), ("xdt", 0), ("xdt", 1)], [("ps", 5)], mmy)
                        P.op("pool", ["xg", "drow"], ["dx"], lambda E, c=c: E.tensor_tensor(
                            out=v3(dx[:]), in0=v3(xg[:, c, :]), in1=drow[:, g * 8:g * 8 + 8].unsqueeze(2).to_broadcast([128, 8, 64]),
                            op=ALU.mult))
                        P.op("dve", [("ps", 5), "dx"], [("Y", c)], lambda E, c=c: E.tensor_tensor(
                            out=Y[:, c, :], in0=ps[5][:, :], in1=dx[:], op=ALU.add))

                    def rec_f(c):
                        if c > 0 and c % 2 == 0:
                            emit_state(0, c // 2 - 1)
                            P.op("dve", [("h", 0), "cont"], [("h", 0)], lambda E: E.tensor_scalar_mul(
                                out=hs[0][:], in0=hs[0][:], scalar1=cont[:, 0:1]))
                        state_step(c, 0, c % 2)
                        P.op("pool", ["tmpy", ("Y", c)], [("Y", c)], lambda E, c=c: E.tensor_tensor(
                            out=Y[:, c, :], in0=Y[:, c, :], in1=tmpy[:], op=ALU.add))

                    intra(0)
                    for c in range(16):
                        if c + 1 < 16:
                            intra(c + 1)
                        rec_f(c)
                    emit_state(0, 7)

                    def rec_b(c):
                        if c < 15 and c % 2 == 1:
                            emit_state(1, (c + 1) // 2)
                            P.op("dve", [("h", 1), "cont"], [("h", 1)], lambda E: E.tensor_scalar_mul(
                                out=hs[1][:], in0=hs[1][:], scalar1=cont[:, 0:1]))
                        xprep(c, 1, "pool", 0)
                        state_step(c, 1, 0)
                        P.op("pool", ["tmpy", ("Y", c)], [("Y", c)], lambda E, c=c: E.tensor_tensor(
                            out=Y[:, c, :], in0=Y[:, c, :], in1=tmpy[:], op=ALU.add))

                    def fin(c):
                        P.dma("sp", szs[c % 2][:], ZS[g, c, :, :], [], [("sz", c % 2)])
                        P.op("pool", [("Y", c), ("sz", c % 2)], ["G"], lambda E, c=c: E.tensor_tensor(
                            out=G[:], in0=Y[:, c, :], in1=szs[c % 2][:], op=ALU.mult))
                        P.op("act", ["G"], ["yn", "ssq"], lambda E: E.activation(
                            out=yn[:], in_=G[:], func=AF.Square, accum_out=ssq[:, 0:1]))
                        P.op("act", ["ssq"], ["ssq"], lambda E: E.activation(
                            out=ssq[:, 0:1], in_=ssq[:, 0:1], func=AF.Ln, scale=1.0 / 512, bias=epsb[:, 0:1]))
                        P.op("act", ["ssq"], ["ssq"], lambda E: E.activation(
                            out=ssq[:, 0:1], in_=ssq[:, 0:1], func=AF.Exp, scale=-0.5))
                        P.op("dve", ["G", "ssq", "nwrow"], ["yn"], lambda E: E.scalar_tensor_tensor(
                            out=yn[:], in0=G[:], scalar=ssq[:, 0:1], in1=nwrow[:], op0=ALU.mult, op1=ALU.mult))
                        pbb = ps[0].bitcast(BF16)

                        def try_(E, pbb=pbb):
                            ins = None
                            for q in range(4):
                                ins = E.transpose(out=pbb[:, q * 128:(q + 1) * 128], in_=yn[:, q * 128:(q + 1) * 128],
                                                  identity=ident[:])
                            return ins
                        P.op("pe", ["yn", "ident"], [("ps", 0)], try_)
                        P.op("act", [("ps", 0)], ["ynT"], lambda E, pbb=pbb: E.copy(out=ynT[:], in_=pbb[:, 0:512]))
                        P.dma("sp", YT[g * 4:(g + 1) * 4, :, c * 128:(c + 1) * 128].rearrange("q p n -> p q n"),
                              ynT[:].rearrange("p (q n) -> p q n", n=128), ["ynT"], [("YT", g, c)])

                    for c in range(15, -1, -1):
                        rec_b(c)
                        if c < 15:
                            fin(c + 1)
                    fin(0)
                    emit_state(1, 0)
                P.barrier()


    def mlstm_phase(l, src):
        with ExitStack() as so:
            def sb(name, shape, dt, st=so):
                return st.enter_context(sbt(name, shape, dt))
            GI = sb("GI", [128, 16, 16], F32)
            GF = sb("GF", [128, 16, 16], F32)
            BC = sb("BC", [128, 16, 16], F32)
            TOT = sb("TOT", [128, 16, 16], F32)
            cst = sb("mlc1", [128, 4, 128], F32)
            cst2 = sb("mlc2", [128, 4, 128], F32)
            onesf = sb("onesfm", [128, 128], F32)
            cont = sb("contm", [128, 1], F32)
            maskF, maskB, SLf, SLb = cst[:, 0, :], cst[:, 1, :], cst[:, 2, :], cst[:, 3, :]
            NEGf, NEGb, Self, Selb = cst2[:, 0, :], cst2[:, 1, :], cst2[:, 2, :], cst2[:, 3, :]
            P.dma("sp", cst[:], ssd_consts[:, :, :], [], ["mlc"])
            P.dma("sp", cst2[:], ml_consts[:, :, :], [], ["mlc"])
            P.dma("sp", cont[:], contd[:, :], [], ["contm"])
            P.op("dve", [], ["onesfm"], lambda E: E.memset(onesf[:], 1.0))
            with ExitStack() as s1:
                UT = sb("UTm", [128, KC, NTOK], BF16, s1)
                with ExitStack() as s0:
                    xta = sb("xta", [128, D], F32, s0)
                    xtb = sb("xtb", [128, D], F32, s0)
                    xn = sb("xn", [128, D], BF16, s0)
                    xtl = [xta, xtb]
                    for i in range(16):
                        norm_mod(l, 0, src, i, xtl[i % 2], ("xt", i % 2), i % 2, xn, UT, ("UT", i // 4), i * 128)
                    P.barrier()
                utk = [("UT", q) for q in range(4)]
                bgr = sb("bgr", [128, 32], F32, s1)
                t_a = sb("mt_a", [128, 32], F32, s1)
                t_b = sb("mt_b", [128, 16], F32, s1)
                t_c = sb("mt_c", [128, 16], F32, s1)
                ob0 = sb("ob0", [128, 512], BF16, s1)
                ob1 = sb("ob1", [128, 512], BF16, s1)
                obs = [ob0, ob1]
                fo0 = sb("fo0", [128, NTOK], BF16, s1)
                P.dma("sp", bgr[:], ml_bg.partition_broadcast(128), [], ["bgr"])
                wgt, wgtk = W.take("mlstm_w_in", (slice(None), slice(6144, 6176)), KC, 32)
                for i in range(16):
                    pb = ps[i % 2]
                    pk = ("ps", i % 2)

                    def mmg_(E, pb=pb, i=i):
                        ins = None
                        for k in range(KC):
                            ins = E.matmul(pb[:, 0:32], lhsT=UT[:, k, i * 128:(i + 1) * 128], rhs=wgt[:, k, :],
                                           start=(k == 0), stop=(k == KC - 1))
                        return ins
                    P.op("pe", [wgtk] + utk, [pk], mmg_)
                    P.op("dve", [pk, "bgr"], ["mt_a"], lambda E, pb=pb: E.tensor_tensor(
                        out=t_a[:], in0=pb[:, 0:32], in1=bgr[:], op=ALU.add))
                    t4 = t_a[:].rearrange("p (d g h) -> p d g h", d=2, g=2)
                    P.op("dve", ["mt_a"], [("GI", i)], lambda E, i=i, t4=t4: E.tensor_copy(
                        out=GI[:, i, :].rearrange("p (d h) -> p d h", d=2), in_=t4[:, :, 0, :]))
                    P.op("act", ["mt_a"], ["mt_b"], lambda E, t4=t4: E.activation(
                        out=t_b[:].rearrange("p (d h) -> p d h", d=2), in_=t4[:, :, 1, :], func=AF.Abs))
                    P.op("act", ["mt_b"], ["mt_b"], lambda E: E.activation(out=t_b[:], in_=t_b[:], func=AF.Exp, scale=-1.0))
                    P.op("act", ["mt_b"], ["mt_b"], lambda E: E.activation(out=t_b[:], in_=t_b[:], func=AF.Ln, bias=1.0))
                    P.op("dve", ["mt_a"], ["mt_c"], lambda E, t4=t4: E.tensor_scalar_min(
                        out=t_c[:].rearrange("p (d h) -> p d h", d=2), in0=t4[:, :, 1, :], scalar1=0.0))
                    P.op("dve", ["mt_b", "mt_c"], [("GF", i)], lambda E, i=i: E.tensor_tensor(
                        out=GF[:, i, :], in0=t_c[:], in1=t_b[:], op=ALU.subtract))
                    pc = ps[2 + i % 2]
                    pck = ("ps", 2 + i % 2)

                    def mmcs(E, pc=pc, i=i):
                        E.matmul(pc[:, 0:8], lhsT=maskF, rhs=GF[:, i, 0:8], start=True, stop=True)
                        E.matmul(pc[:, 8:16], lhsT=maskB, rhs=GF[:, i, 8:16], start=True, stop=True)
                        return E.matmul(pc[:, 16:32], lhsT=onesf[:], rhs=GF[:, i, :], start=True, stop=True)
                    P.op("pe", [("GF", i), "mlc", "onesfm"], [pck], mmcs)
                    P.op("dve", [pck], [("BC", i)], lambda E, pc=pc, i=i: E.tensor_copy(out=BC[:, i, :], in_=pc[:, 0:16]))
                    P.op("dve", [pck], [("TOT", i)], lambda E, pc=pc, i=i: E.tensor_copy(out=TOT[:, i, :], in_=pc[:, 16:32]))
                nob = 0
                for blk in range(4):
                    wq, wqk = W.take("mlstm_w_in", (slice(None), slice(blk * 512, (blk + 1) * 512)), KC, 512)
                    isq = blk < 2
                    for ft in range(4):
                        hh = (blk % 2) * 4 + ft
                        for tt in range(4):
                            pb = ps[tt % 2]
                            pk = ("ps", tt % 2)

                            def mmx(E, pb=pb, tt=tt, ft=ft, wq=wq):
                                ins = None
                                for k in range(KC):
                                    ins = E.matmul(pb[:, :], lhsT=wq[:, k, ft * 128:(ft + 1) * 128],
                                                   rhs=UT[:, k, tt * 512:(tt + 1) * 512], start=(k == 0), stop=(k == KC - 1))
                                return ins
                            P.op("pe", [wqk] + utk, [pk], mmx)
                            P.op("act", [pk], ["fo0"], lambda E, pb=pb, tt=tt, isq=isq: E.mul(
                                out=fo0[:, tt * 512:(tt + 1) * 512], in_=pb[:, :], mul=(128.0 ** -0.5) if isq else 1.0))
                        P.dma("sp", (QTS if isq else KTS)[hh, :, :], fo0[:], ["fo0"], [("QKS", blk, ft)])
                    if not isq:
                        for i in range(16):
                            pb = ps[2 + i % 2]
                            pk = ("ps", 2 + i % 2)
                            oj = nob % 2
                            nob += 1

                            def mmk(E, pb=pb, i=i, wq=wq):
                                ins = None
                                for k in range(KC):
                                    ins = E.matmul(pb[:, :], lhsT=UT[:, k, i * 128:(i + 1) * 128], rhs=wq[:, k, :],
                                                   start=(k == 0), stop=(k == KC - 1))
                                return ins
                            P.op("pe", [wqk] + utk, [pk], mmk)
                            P.op("dve", [pk], [("ob", oj)], lambda E, pb=pb, oj=oj: E.tensor_copy(out=obs[oj][:], in_=pb[:, :]))
                            h0_ = (blk % 2) * 4
                            P.dma("sp", KMS[h0_:h0_ + 4, i, :, :].rearrange("h p n -> p h n"),
                                  obs[oj][:].rearrange("p (h n) -> p h n", n=128), [("ob", oj)], [("KMS", blk, i)])
                for blk in range(8):
                    c0 = 2048 + blk * 512
                    wv_, wvk_ = W.take("mlstm_w_in", (slice(None), slice(c0, c0 + 512)), KC, 512)
                    isv = blk < 4
                    for i in range(16):
                        pb = ps[i % 2]
                        pk = ("ps", i % 2)
                        oj = nob % 2
                        nob += 1

                        def mmv(E, pb=pb, i=i, wv_=wv_):
                            ins = None
                            for k in range(KC):
                                ins = E.matmul(pb[:, :], lhsT=UT[:, k, i * 128:(i + 1) * 128], rhs=wv_[:, k, :],
                                               start=(k == 0), stop=(k == KC - 1))
                            return ins
                        P.op("pe", [wvk_] + utk, [pk], mmv)
                        if isv:
                            P.op("act", [pk], [("ob", oj)], lambda E, pb=pb, oj=oj: E.copy(out=obs[oj][:], in_=pb[:, :]))
                        else:
                            P.op("act", [pk], [("ob", oj)], lambda E, pb=pb, oj=oj: E.activation(
                                out=obs[oj][:], in_=pb[:, :], func=AF.Sigmoid))
                        h0_ = (blk % 4) * 2
                        P.dma("sp", (VS if isv else OGS)[h0_:h0_ + 2, i, :, :].rearrange("h p n -> p h n"),
                              obs[oj][:].rearrange("p (h n) -> p h n", n=256), [("ob", oj)], [("VOS", blk, i)])
                P.barrier()

            with ExitStack() as s4:
                QT = sb("QTm", [128, NTOK], BF16, s4)
                KTm = sb("KTm", [128, NTOK], BF16, s4)
                Km = sb("Km", [128, 16, 128], BF16, s4)
                V1 = sb("V1", [128, 16, 257], BF16, s4)
                Hs = sb("Hs", [128, 16, 256], F32, s4)
                CN = [sb("CNf", [128, 257], F32, s4), sb("CNb", [128, 257], F32, s4)]
                CN16 = sb("CN16", [128, 257], BF16, s4)
                Mst = sb("Mst", [128, 2], F32, s4)
                m0t = sb("m0t", [128, 16], F32, s4)
                mout = sb("mout", [1, 128], F32, s4)
                nout = sb("nout", [128, 128], F32, s4)
                nouts = sb("nouts", [128, 128], F32, s4)
                r1 = sb("r1", [128, 128], F32, s4)
                rD = sb("rD", [128, 128], F32, s4)
                Dmm = sb("Dmm", [128, 128], F32, s4)
                Wt = sb("Wt", [128, 128], BF16, s4)
                WTs = sb("WTs", [128, 128], BF16, s4)
                ST = sb("ST", [128, 128], BF16, s4)
                P1s = sb("P1s", [128, 257], F32, s4)
                R = sb("R", [128, 257], F32, s4)
                kw = sb("kw", [128, 128], BF16, s4)
                sm = sb("sm", [128, 16], F32, s4)
                ogs = [sb("og0", [128, 256], BF16, s4), sb("og1", [128, 256], BF16, s4)]
                hn = sb("hn", [128, 256], F32, s4)
                hg = sb("hg", [128, 256], BF16, s4)
                hgT = sb("hgT", [128, 256], BF16, s4)
                hnrow = sb("hnrow", [128, 256], F32, s4)
                P.dma("sp", m0t[:], ml_m0[:, :], [], ["m0t"])
                P.op("dve", [], ["V1"], lambda E: E.memset(V1[:, :, 256:257], 1.0))
                dsel = [(0, maskF, SLf, NEGf, Self), (1, maskB, SLb, NEGb, Selb)]
                MX, MI, MT_, NMT, INTER, EMT, TM, LW, MNEW, E2a, E2b, ADEN, RDN, SSQ = range(14)

                def col(j):
                    return sm[:, j:j + 1]

                for h in range(cfg.get("ml_heads", 8)):
                    P.dma("sp", QT[:], QTS[h, :, :], [], ["QTm"])
                    P.dma("sp", KTm[:], KTS[h, :, :], [], ["KTm"])
                    P.dma("sp", Km[:], KMS[h, :, :, :].rearrange("i p n -> p i n"), [], ["Km"])
                    P.dma("sp", V1[:, :, 0:256], VS[h, :, :, :].rearrange("i p n -> p i n"), ["V1"], ["V1"])
                    P.dma("sp", hnrow[:], ml_hn[h, :].partition_broadcast(128), [], ["hnrow"])
                    for d_ in range(2):
                        P.dma("sp", CN[d_][:, 0:256], ml_C0[d_, h, :, :], [], [("CN", d_)])
                        P.dma("sp", CN[d_][:, 256:257], ml_n0[d_, h, :].rearrange("(p o) -> p o", o=1), [("CN", d_)], [("CN", d_)])
                        P.op("dve", ["m0t"], [("M", d_)], lambda E, d_=d_: E.tensor_copy(
                            out=Mst[:, d_:d_ + 1], in_=m0t[:, d_ * 8 + h:d_ * 8 + h + 1]))

                    def emit_state(d_, seg):
                        P.dma("sp", mlC_out[seg, d_, h, :, :], CN[d_][:, 0:256], [("CN", d_)], [("mlCo", seg, d_, h)])
                        idx = (seg * 2 + d_) * 8 + h
                        P.op("dve", [("CN", d_)], ["nout"], lambda E: E.tensor_copy(
                            out=nout[:, idx:idx + 1], in_=CN[d_][:, 256:257]))
                        P.op("dve", [("M", d_)], ["mout"], lambda E: E.tensor_copy(
                            out=mout[0:1, idx:idx + 1], in_=Mst[0:1, d_:d_ + 1]))

                    def step(c, d_, first_dir):
                        _, mk, SL, NEG, Sel = dsel[d_]
                        gcol = d_ * 8 + h
                        fcol = GF[:, c, gcol:gcol + 1]
                        icol = GI[:, c, gcol:gcol + 1]
                        bcol = BC[:, c, gcol:gcol + 1]
                        tcol = TOT[:, c, gcol:gcol + 1]
                        mcol = Mst[:, d_:d_ + 1]
                        gk = [("GF", c), ("GI", c), ("BC", c), ("TOT", c)]
                        cs = slice(c * 128, (c + 1) * 128)
                        P.op("act", gk + ["mlc"], ["r1"], lambda E: E.mul(out=r1[:], in_=SL, mul=fcol))
                        P.op("dve", gk + ["r1", "ident_f"], ["rD"], lambda E: E.scalar_tensor_tensor(
                            out=rD[:], in0=ident_f[:], scalar=icol, in1=r1[:], op0=ALU.mult, op1=ALU.add))
                        P.op("pe", ["rD", "mlc"], [("ps", 0)], lambda E: E.matmul(
                            ps[0][:, 0:128], lhsT=mk, rhs=rD[:], start=True, stop=True))
                        P.op("dve", [("ps", 0), "mlc"], ["Dmm"], lambda E: E.tensor_tensor(
                            out=Dmm[:], in0=ps[0][:, 0:128], in1=NEG, op=ALU.add))
                        P.op("dve", ["Dmm"], [("sm", MX)], lambda E: E.reduce_max(
                            out=col(MX), in_=Dmm[:], axis=mybir.AxisListType.X))
                        P.op("dve", gk + [("M", d_)], [("sm", MI)], lambda E: E.tensor_tensor(
                            out=col(MI), in0=bcol, in1=mcol, op=ALU.add))
                        P.op("dve", [("sm", MX), ("sm", MI)], [("sm", MT_)], lambda E: E.tensor_tensor(
                            out=col(MT_), in0=col(MX), in1=col(MI), op=ALU.max))
                        P.op("dve", [("sm", MT_)], [("sm", NMT)], lambda E: E.tensor_scalar_mul(
                            out=col(NMT), in0=col(MT_), scalar1=-1.0))
                        P.op("act", [("sm", MI), ("sm", NMT)], [("sm", INTER)], lambda E: E.activation(
                            out=col(INTER), in_=col(MI), func=AF.Exp, bias=col(NMT)))
                        P.op("act", [("sm", NMT)], [("sm", EMT)], lambda E: E.activation(
                            out=col(EMT), in_=col(NMT), func=AF.Exp))
                        P.op("act", ["Dmm", ("sm", NMT)], ["Wt"], lambda E: E.activation(
                            out=Wt[:], in_=Dmm[:], func=AF.Exp, bias=col(NMT)))
                        pbb = ps[1].bitcast(BF16)
                        P.op("pe", ["Wt", "ident"], [("ps", 1)], lambda E: E.transpose(
                            out=pbb[:, 0:128], in_=Wt[:], identity=ident[:]))
                        P.op("act", [("ps", 1)], ["WTs"], lambda E: E.copy(out=WTs[:], in_=pbb[:, 0:128]))
                        P.op("pe", ["KTm", "QTm"], [("ps", 2)], lambda E: E.matmul(
                            ps[2][:, 0:128], lhsT=KTm[:, cs], rhs=QT[:, cs], start=True, stop=True))
                        P.op("dve", [("ps", 2), "WTs"], ["ST"], lambda E: E.tensor_tensor(
                            out=ST[:], in0=ps[2][:, 0:128], in1=WTs[:], op=ALU.mult))
                        P.op("act", [("CN", d_)], ["CN16"], lambda E: E.copy(out=CN16[:], in_=CN[d_][:]))
                        P.op("pe", ["ST", "V1"], [("ps", 3)], lambda E: E.matmul(
                            ps[3][:, 0:257], lhsT=ST[:], rhs=V1[:, c, :], start=True, stop=True))
                        P.op("pe", ["QTm", "CN16"], [("ps", 4)], lambda E: E.matmul(
                            ps[4][:, 0:257], lhsT=QT[:, cs], rhs=CN16[:], start=True, stop=True))
                        P.op("act", [("ps", 3)], ["P1s"], lambda E: E.copy(out=P1s[:], in_=ps[3][:, 0:257]))
                        P.op("dve", [("ps", 4), "P1s", ("sm", INTER)], ["R"], lambda E: E.scalar_tensor_tensor(
                            out=R[:], in0=ps[4][:, 0:257], scalar=col(INTER), in1=P1s[:], op0=ALU.mult, op1=ALU.add))
                        P.op("act", ["R"], [("sm", ADEN)], lambda E: E.activation(
                            out=col(ADEN), in_=R[:, 256:257], func=AF.Abs))
                        P.op("dve", [("sm", ADEN), ("sm", EMT)], [("sm", RDN)], lambda E: E.tensor_tensor(
                            out=col(RDN), in0=col(ADEN), in1=col(EMT), op=ALU.max))
                        P.op("dve", [("sm", RDN)], [("sm", RDN)], lambda E: E.reciprocal(out=col(RDN), in_=col(RDN)))
                        if first_dir:
                            P.op("dve", ["R", ("sm", RDN)], [("Hs", c)], lambda E: E.tensor_scalar_mul(
                                out=Hs[:, c, :], in0=R[:, 0:256], scalar1=col(RDN)))
                        else:
                            P.op("dve", ["R", ("sm", RDN), ("Hs", c)], [("Hs", c)], lambda E: E.scalar_tensor_tensor(
                                out=Hs[:, c, :], in0=R[:, 0:256], scalar=col(RDN), in1=Hs[:, c, :], op0=ALU.mult, op1=ALU.add))
                        P.op("dve", gk + [("M", d_)], [("sm", TM)], lambda E: E.tensor_tensor(
                            out=col(TM), in0=tcol, in1=mcol, op=ALU.add))
                        P.op("dve", gk, [("sm", LW)], lambda E: E.tensor_tensor(
                            out=col(LW), in0=tcol, in1=bcol, op=ALU.subtract))
                        P.op("dve", gk + [("sm", LW)], [("sm", LW)], lambda E: E.tensor_tensor(
                            out=col(LW), in0=col(LW), in1=icol, op=ALU.add))
                        P.op("pe", [("sm", MX), "mlc"], [("ps", 5)], lambda E: E.matmul(
                            ps[5][:, 0:1], lhsT=Sel, rhs=col(MX), start=True, stop=True))
                        P.op("dve", [("ps", 5), ("sm", TM)], [("sm", MNEW)], lambda E: E.tensor_tensor(
                            out=col(MNEW), in0=ps[5][:, 0:1], in1=col(TM), op=ALU.max))
                        P.op("dve", [("sm", TM), ("sm", LW), ("sm", MNEW)], [("sm", TM), ("sm", LW)], lambda E: E.tensor_scalar(
                            out=sm[:, TM:TM + 2], in0=sm[:, TM:TM + 2], scalar1=col(MNEW), scalar2=None, op0=ALU.subtract))
                        P.op("act", [("sm", TM), ("sm", LW)], [("sm", E2a), ("sm", E2b)], lambda E: E.activation(
                            out=sm[:, E2a:E2a + 2], in_=sm[:, TM:TM + 2], func=AF.Exp))
                        P.op("dve", ["Km", ("sm", E2b)], ["kw"], lambda E: E.tensor_scalar_mul(
                            out=kw[:], in0=Km[:, c, :], scalar1=col(E2b)))
                        P.op("pe", ["kw", "V1"], [("ps", 6)], lambda E: E.matmul(
                            ps[6][:, 0:257], lhsT=kw[:], rhs=V1[:, c, :], start=True, stop=True))
                        P.op("dve", [("CN", d_), ("sm", E2a), ("ps", 6)], [("CN", d_)], lambda E: E.scalar_tensor_tensor(
                            out=CN[d_][:], in0=CN[d_][:], scalar=col(E2a), in1=ps[6][:, 0:257], op0=ALU.mult, op1=ALU.add))
                        P.op("dve", [("sm", MNEW)], [("M", d_)], lambda E: E.tensor_copy(out=mcol, in_=col(MNEW)))

                    def seg_reset(d_):
                        P.op("dve", [("CN", d_), "contm"], [("CN", d_)], lambda E: E.tensor_scalar_mul(
                            out=CN[d_][:], in0=CN[d_][:], scalar1=cont[:, 0:1]))
                        P.op("dve", [("M", d_), "contm"], [("M", d_)], lambda E: E.tensor_scalar_mul(
                            out=Mst[:, d_:d_ + 1], in0=Mst[:, d_:d_ + 1], scalar1=cont[:, 0:1]))

                    for c in range(16):
                        if c > 0 and c % 2 == 0:
                            emit_state(0, c // 2 - 1)
                            seg_reset(0)
                        step(c, 0, True)
                    emit_state(0, 7)
                    for c in range(15, -1, -1):
                        if c < 15 and c % 2 == 1:
                            emit_state(1, (c + 1) // 2)
                            seg_reset(1)
                        P.dma("sp", ogs[c % 2][:], OGS[h, c, :, :], [], [("og", c % 2)])
                        step(c, 1, False)
                        P.op("act", [("Hs", c)], ["hn", ("sm", SSQ)], lambda E, c=c: E.activation(
                            out=hn[:], in_=Hs[:, c, :], func=AF.Square, accum_out=col(SSQ)))
                        P.op("act", [("sm", SSQ)], [("sm", SSQ)], lambda E: E.activation(
                            out=col(SSQ), in_=col(SSQ), func=AF.Ln, scale=1.0 / 256, bias=epsb[:, 0:1]))
                        P.op("act", [("sm", SSQ)], [("sm", SSQ)], lambda E: E.activation(
                            out=col(SSQ), in_=col(SSQ), func=AF.Exp, scale=-0.5))
                        P.op("dve", [("Hs", c), ("sm", SSQ), "hnrow"], ["hn"], lambda E, c=c: E.scalar_tensor_tensor(
                            out=hn[:], in0=Hs[:, c, :], scalar=col(SSQ), in1=hnrow[:], op0=ALU.mult, op1=ALU.mult))
                        P.op("pool", ["hn", ("og", c % 2)], ["hg"], lambda E, c=c: E.tensor_tensor(
                            out=hg[:], in0=hn[:], in1=ogs[c % 2][:], op=ALU.mult))
                        pbb = ps[7].bitcast(BF16)

                        def trh(E, pbb=pbb):
                            E.transpose(out=pbb[:, 0:128], in_=hg[:, 0:128], identity=ident[:])
                            return E.transpose(out=pbb[:, 128:256], in_=hg[:, 128:256], identity=ident[:])
                        P.op("pe", ["hg", "ident"], [("ps", 7)], trh)
                        P.op("act", [("ps", 7)], ["hgT"], lambda E, pbb=pbb: E.copy(out=hgT[:], in_=pbb[:, 0:256]))
                        P.dma("sp", YT[h * 2:h * 2 + 2, :, c * 128:(c + 1) * 128].rearrange("q p n -> p q n"),
                              hgT[:].rearrange("p (q n) -> p q n", n=128), ["hgT"], [("YT", h, c)])
                    emit_state(1, 0)
                P.op("pe", ["nout", "ident_f"], [("ps", 0)], lambda E: E.transpose(
                    out=ps[0][:, 0:128], in_=nout[:], identity=ident_f[:]))
                P.op("dve", [("ps", 0)], ["nouts"], lambda E: E.tensor_copy(out=nouts[:], in_=ps[0][:, 0:128]))
                P.dma("sp", mln_out.rearrange("s d h k -> (s d h) k"), nouts[:], ["nouts"], ["mlno"])
                P.dma("sp", mlm_out[:, :], mout[:], ["mout"], ["mlmo"])
                P.barrier()

    cur = xin
    for l in layers:
        if cfg.get("mixer", True):
            if l % 3 == 0:
                ssd_phase(l, cur)
                if cfg.get("ssd_out", True):
                    outproj_phase(l, cur, 32, "ssd_w_out", l // 3)
                cur = X
            if l % 3 == 2:
                mlstm_phase(l, cur)
                outproj_phase(l, cur, 16, "mlstm_w_out")
                cur = X
            if l % 3 == 1:
                attn_phase(l, cur)
                outproj_phase(l, cur, 16, "attn_w_out")
                cur = X
        if cfg.get("ffn", True):
            ffn_phase(l, cur)
            cur = X
    P.barrier()
    return nc, W.plan


def build(cfg):
    _, plan = build_program(cfg, None)
    nc, _ = build_program(cfg, plan)
    return nc


def _rope_tables():
    half, quarter = 64, 32
    t = np.arange(NTOK)
    pos_row = (t // 64).astype(np.float32)
    pos_col = (t % 64).astype(np.float32)
    inv_freq = (10000.0 ** (-np.arange(quarter, dtype=np.float32) / quarter)).astype(np.float32)
    cos = np.zeros((128, NTOK), np.float32)
    sin = np.zeros((128, NTOK), np.float32)
    for p in range(128):
        pos = pos_row if p < 64 else pos_col
        ang = (pos * inv_freq[p % 32]).astype(np.float32)
        cos[p] = np.cos(ang)
        sin[p] = np.sin(ang) * (-1.0 if (p % 64) < 32 else 1.0)
    return cos, sin


def _perm():
    pm = np.zeros((128, 128), np.float32)
    for m in range(128):
        sw = m + 32 if (m % 64) < 32 else m - 32
        pm[sw, m] = 1.0
    return pm


def _core_inputs(inp, core, layers):
    m = {}
    if core < 4:
        m["xin"] = np.ascontiguousarray(inp["x_sample"][core])
        cvec = inp["c"][core]
    else:
        j = core - 4
        m["xin"] = np.ascontiguousarray(inp["x_prompt"][8 * j:8 * j + 8].reshape(NTOK, D))
        cvec = inp["c_ctx"]
    m["cv"] = np.ascontiguousarray(cvec.reshape(KC, 128).T)
    js = sorted(set(l // 3 for l in layers if l % 3 == 0))
    if js:
        if core < 4:
            m["ssd_h0"] = np.ascontiguousarray(inp["state_ssd"][core][js].reshape(len(js), 2, 4096, 128))
            m["cont"] = np.ones((128, 1), np.float32)
        else:
            m["ssd_h0"] = np.zeros((len(js), 2, 4096, 128), np.float32)
            m["cont"] = np.zeros((128, 1), np.float32)
    if (2 in layers) and not js:
        m["cont"] = np.ones((128, 1), np.float32) if core < 4 else np.zeros((128, 1), np.float32)
    if 2 in layers:
        if core < 4:
            m["ml_C0"] = np.ascontiguousarray(inp["state_mlstm_C"][core, 0])
            m["ml_n0"] = np.ascontiguousarray(inp["state_mlstm_n"][core, 0])
            m["ml_m0"] = np.ascontiguousarray(np.broadcast_to(inp["state_mlstm_m"][core, 0].reshape(1, 16), (128, 16)))
        else:
            m["ml_C0"] = np.zeros((2, 8, 128, 256), np.float32)
            m["ml_n0"] = np.zeros((2, 8, 128), np.float32)
            m["ml_m0"] = np.zeros((128, 16), np.float32)
    if 1 in layers:
        mb = np.zeros((128, 160), np.float32)
        if core < 4:
            cos, sin = _rope_tables()
            m["cachek"] = np.ascontiguousarray(inp["cache_attn_k"][core, 0].reshape(512, 512))
            m["cachev"] = np.ascontiguousarray(inp["cache_attn_v"][core, 0].reshape(512, 512))
        else:
            cos = np.ones((128, NTOK), np.float32)
            sin = np.zeros((128, NTOK), np.float32)
            m["cachek"] = np.zeros((512, 512), np.float32)
            m["cachev"] = np.zeros((512, 512), np.float32)
            for q8 in range(8):
                for kt in range(20):
                    ok = kt >= 4 and (kt - 4) // 2 == q8
                    mb[:, q8 * 20 + kt] = 0.0 if ok else -30000.0
        m["ropecos"], m["ropesin"], m["maskb"] = cos, sin, mb
    return m


def _shared_inputs(inp, layers):
    L = list(layers)
    sh = {}
    sh["consts"] = np.eye(128, dtype=np.float32)
    sh["ada_w"] = inp["ada_w"][L] if len(L) < DEPTH else inp["ada_w"]
    sh["ada_b"] = inp["ada_b"][L]
    sh["adab_f"] = np.ascontiguousarray(inp["ada_b"].reshape(DEPTH, 96, 128).transpose(2, 0, 1))
    nw = np.stack([inp["norm_mix_w"], inp["norm_ffn_w"]], axis=1)
    sh["nw_f"] = np.ascontiguousarray(nw.reshape(DEPTH, 2, KC, 128).transpose(3, 0, 1, 2))
    for k in ("ffn_w_gate", "ffn_w_up", "ffn_w_down"):
        sh[k] = inp[k][L] if len(L) < DEPTH else inp[k]
    js = sorted(set(l // 3 for l in L if l % 3 == 0))
    if js:
        sh["ssd_w_in"] = inp["ssd_w_in"][js]
        sh["ssd_w_out"] = inp["ssd_w_out"][js]
        cw = inp["ssd_conv_w"][js]
        sh["ssd_cw"] = np.ascontiguousarray(cw.reshape(len(js), 3, 48, 128).transpose(3, 0, 2, 1))
        sh["ssd_cb"] = np.ascontiguousarray(inp["ssd_conv_b"][js].reshape(len(js), 48, 128).transpose(2, 0, 1))
        sh["ssd_dtb"] = np.ascontiguousarray(inp["ssd_dt_bias"][js].reshape(len(js), 128))
        sh["ssd_alog"] = np.ascontiguousarray(inp["ssd_a_log"][js].reshape(len(js), 128))
        sh["ssd_dskip"] = np.ascontiguousarray(inp["ssd_d"][js])
        sh["ssd_nw"] = np.ascontiguousarray(inp["ssd_norm_w"][js])
        idx = np.arange(128)
        mF = (idx[:, None] <= idx[None, :]).astype(np.float32)
        mB = (idx[:, None] >= idx[None, :]).astype(np.float32)
        slf = (idx[:, None] > idx[None, :]).astype(np.float32)
        slb = (idx[:, None] < idx[None, :]).astype(np.float32)
        sh["ssd_consts"] = np.ascontiguousarray(np.stack([mF, mB, slf, slb], axis=1))
    if 2 in L or js:
        idx = np.arange(128)
        mF = (idx[:, None] <= idx[None, :]).astype(np.float32)
        mB = (idx[:, None] >= idx[None, :]).astype(np.float32)
        slf = (idx[:, None] > idx[None, :]).astype(np.float32)
        slb = (idx[:, None] < idx[None, :]).astype(np.float32)
        sh["ssd_consts"] = np.ascontiguousarray(np.stack([mF, mB, slf, slb], axis=1))
    if 2 in L:
        sh["mlstm_w_in"] = inp["mlstm_w_in"][0]
        sh["mlstm_w_out"] = inp["mlstm_w_out"][0]
        sh["ml_bg"] = np.ascontiguousarray(inp["mlstm_b_gates"][0].reshape(32))
        sh["ml_hn"] = np.ascontiguousarray(inp["mlstm_head_norm"][0])
        idx = np.arange(128)
        negf = np.where(idx[None, :] > idx[:, None], -30000.0, 0.0).astype(np.float32)
        negb = np.where(idx[None, :] < idx[:, None], -30000.0, 0.0).astype(np.float32)
        self_ = np.zeros((128, 128), np.float32)
        self_[127, :] = 1.0
        selb = np.zeros((128, 128), np.float32)
        selb[0, :] = 1.0
        sh["ml_consts"] = np.ascontiguousarray(np.stack([negf, negb, self_, selb], axis=1))
    if 1 in L:
        sh["attn_w_qkv"] = inp["attn_w_qkv"][0]
        sh["attn_w_out"] = inp["attn_w_out"][0]
        sh["attn_qn"] = np.ascontiguousarray(inp["attn_q_norm"][0].reshape(128, 1))
        sh["attn_kn"] = np.ascontiguousarray(inp["attn_k_norm"][0].reshape(128, 1))
        sh["attn_knrow"] = np.ascontiguousarray(inp["attn_k_norm"][0])
        sh["perm"] = _perm()
    return sh


def run(inp, cfg, cores=None):
    inp = {k: np.asarray(v) for k, v in inp.items()}
    cores = list(range(8)) if cores is None else cores
    layers = cfg.get("layers", list(range(DEPTH)))
    nc = build(cfg)
    sh = _shared_inputs(inp, layers)
    in_maps = []
    for core in cores:
        m = dict(sh)
        m.update(_core_inputs(inp, core, layers))
        in_maps.append(m)
    res = run_bass_kernel_spmd(nc, in_maps, core_ids=list(range(len(cores))))
    return res


def kernel(**inputs):
    res = run(inputs, {})
    r = res.results
    y_sample = np.stack([r[c]["xout"] for c in range(4)], axis=0)
    y_prompt = np.concatenate([r[c]["xout"].reshape(8, 256, D) for c in range(4, 8)], axis=0)
    st = np.concatenate([r[c]["st_out"].reshape(2, 8, 2, 64, 64, 128).transpose(1, 0, 2, 3, 4, 5)
                         for c in range(4, 8)], axis=0)
    kc = np.concatenate([r[c]["kc_out"].reshape(8, 1, 256, 4, 128) for c in range(4, 8)], axis=0)
    vc = np.concatenate([r[c]["vc_out"].reshape(8, 1, 256, 4, 128) for c in range(4, 8)], axis=0)
    mC = np.concatenate([r[c]["mlC_out"].reshape(8, 1, 2, 8, 128, 256) for c in range(4, 8)], axis=0)
    mn = np.concatenate([r[c]["mln_out"].reshape(8, 1, 2, 8, 128) for c in range(4, 8)], axis=0)
    mm = np.concatenate([r[c]["mlm_out"].reshape(8, 1, 2, 8) for c in range(4, 8)], axis=0)
    return (np.ascontiguousarray(y_prompt), y_sample, np.ascontiguousarray(st), kc, vc, mC, mn, mm)
```

```python
import numpy as np
from contextlib import ExitStack
import concourse.bass as bass
import concourse.mybir as mybir
from concourse.bass_utils import run_bass_kernel_spmd

F32 = mybir.dt.float32
BF16 = mybir.dt.bfloat16
AF = mybir.ActivationFunctionType
ALU = mybir.AluOpType

D = 2048
NTOK = 2048
KC = 16
FH = 5632
HC = 44
DEPTH = 4
EPS = 1e-6
SLOT = 8192
NSLOT = 4
KSPLIT = [(0, 16), (16, 16), (32, 12)]


class Prog:
    def __init__(s, nc):
        s.nc = nc
        s.E = dict(pe=nc.tensor, act=nc.scalar, dve=nc.vector, pool=nc.gpsimd, sp=nc.sync)
        s.sem = {}
        s.val = {}
        for e in s.E:
            s.sem[e] = nc.alloc_semaphore("sem_" + e)
            s.val[e] = 0
        s.dpool = {}
        for q, n in (("sp", 16), ("pool", 16)):
            keys = []
            for i in range(n):
                k = "d_%s%d" % (q, i)
                s.sem[k] = nc.alloc_semaphore(k)
                s.val[k] = 0
                keys.append(k)
            s.dpool[q] = [keys, 0]
        s.waited = {e: {} for e in s.E}
        s.lastw = {}
        s.readers = {}
        s.nops = 0

    def _deps(s, reads, writes):
        deps = {}
        for r in reads:
            t = s.lastw.get(r)
            if t is not None and deps.get(t[0], 0) < t[1]:
                deps[t[0]] = t[1]
        for w in writes:
            t = s.lastw.get(w)
            if t is not None and deps.get(t[0], 0) < t[1]:
                deps[t[0]] = t[1]
            rd = s.readers.get(w)
            if rd:
                for k, v in rd.items():
                    if deps.get(k, 0) < v:
                        deps[k] = v
        return deps

    def _wait(s, eng, deps):
        E = s.E[eng]
        wd = s.waited[eng]
        for k, v in deps.items():
            if eng == "pe" and k == "pe":
                continue
            if wd.get(k, 0) < v:
                E.wait_ge(s.sem[k], v)
                wd[k] = v

    def _commit(s, tok, reads, writes):
        for r in reads:
            d = s.readers.setdefault(r, {})
            if d.get(tok[0], 0) < tok[1]:
                d[tok[0]] = tok[1]
        for w in writes:
            s.lastw[w] = tok
            s.readers[w] = {}

    def op(s, eng, reads, writes, fn):
        pr = [r for r in reads if isinstance(r, tuple) and r[0] == "ps"]
        if pr:
            reads = [r for r in reads if r not in pr]
            writes = list(writes) + [r for r in pr if r not in writes]
        s._wait(eng, s._deps(reads, writes))
        ins = fn(s.E[eng])
        s.val[eng] += 1
        ins.then_inc(s.sem[eng], 1)
        tok = (eng, s.val[eng])
        s._commit(tok, reads, writes)
        s.nops += 1
        return tok

    def dma(s, q, out, in_, reads, writes):
        s._wait(q, s._deps(reads, writes))
        keys, idx = s.dpool[q]
        k = keys[idx % len(keys)]
        s.dpool[q][1] = idx + 1
        if s.val[k] > 0 and s.waited[q].get(k, 0) < s.val[k]:
            s.E[q].wait_ge(s.sem[k], s.val[k])
            s.waited[q][k] = s.val[k]
        s.val[k] += 16
        s.E[q].dma_start(out=out, in_=in_).then_inc(s.sem[k], 16)
        tok = (k, s.val[k])
        s._commit(tok, reads, writes)
        return tok

    def barrier(s):
        for e in s.E:
            for k, v in s.val.items():
                if e == "pe" and k == "pe":
                    continue
                if v > 0 and s.waited[e].get(k, 0) < v:
                    s.E[e].wait_ge(s.sem[k], v)
                    s.waited[e][k] = v


class WStream:
    def __init__(s, P, nc, WT, plan):
        s.P = P
        s.WT = WT
        s.slots = [nc.alloc_sbuf_tensor("wring%d" % i, [128, SLOT], BF16) for i in range(NSLOT)]
        s.record = plan is None
        s.plan = [] if plan is None else plan
        s.issued = 0
        s.taken = 0

    def _issue(s, n):
        name, idx, kc, ncols = s.plan[n]
        ap = s.WT[name][idx]
        i = n % NSLOT
        dst = s.slots[i][:, 0:kc * ncols].rearrange("p (c n) -> p c n", c=kc)
        s.P.dma("pool", dst, ap.rearrange("(c p) n -> p c n", p=128), reads=[], writes=[("wr", i)])

    def take(s, name, idx, kc, ncols):
        assert kc * ncols <= SLOT
        n = s.taken
        if s.record:
            s.plan.append((name, idx, kc, ncols))
        else:
            assert s.plan[n] == (name, idx, kc, ncols), (n, s.plan[n], name, idx)
            lim = min(len(s.plan), n + NSLOT - 1)
            while s.issued < lim:
                s._issue(s.issued)
                s.issued += 1
        i = n % NSLOT
        s.taken += 1
        view = s.slots[i][:, 0:kc * ncols].rearrange("p (c n) -> p c n", c=kc)
        return view, ("wr", i)


def build_program(cfg, plan=None):
    nc = bass.Bass("TRN2", target_bir_lowering=False)
    P = Prog(nc)
    _uid = [0]

    def sbt(name, shape, dt):
        _uid[0] += 1
        return nc.sbuf_tensor("%s_%d" % (name, _uid[0]), shape, dt)

    def din(name, shape):
        return nc.dram_tensor(name, list(shape), F32, kind="ExternalInput").ap()

    xin = din("xin", [NTOK, D])
    cv = din("cv", [128, KC])
    consts = din("consts", [128, 128])
    layers = cfg.get("layers", list(range(DEPTH)))
    NL = len(layers)
    WT = {}
    ada_w = din("ada_w", [NL, D, 6 * D])
    ada_b = din("ada_b", [NL, 6 * D])
    adab_f = din("adab_f", [128, DEPTH, 96])
    nw_f = din("nw_f", [128, DEPTH, 2, KC])
    ffn_wg = din("ffn_w_gate", [NL, D, FH])
    ffn_wu = din("ffn_w_up", [NL, D, FH])
    ffn_wd = din("ffn_w_down", [NL, FH, D])
    WT.update(ada_w=ada_w, ffn_w_gate=ffn_wg, ffn_w_up=ffn_wu, ffn_w_down=ffn_wd)
    has_ssd = any(l % 3 == 0 for l in layers)
    ssd_js = sorted(set(l // 3 for l in layers if l % 3 == 0))
    if has_ssd:
        NJ = len(ssd_js)
        ssd_win = din("ssd_w_in", [NJ, D, 10368])
        ssd_wout = din("ssd_w_out", [NJ, 4096, D])
        ssd_cw = din("ssd_cw", [128, NJ, 48, 3])
        ssd_cb = din("ssd_cb", [128, NJ, 48])
        ssd_dtb = din("ssd_dtb", [NJ, 128])
        ssd_alog = din("ssd_alog", [NJ, 128])
        ssd_dskip = din("ssd_dskip", [NJ, 64])
        ssd_nw = din("ssd_nw", [NJ, 4096])
        ssd_h0 = din("ssd_h0", [NJ, 2, 4096, 128])
        ssd_consts = din("ssd_consts", [128, 4, 128])
        contd = din("cont", [128, 1])
        st_out = nc.dram_tensor("st_out", [NJ, 8, 2, 4096, 128], F32, kind="ExternalOutput").ap()
        XS = nc.dram_tensor("xs_scr", [8, 16, 128, 512], BF16).ap()
        ZS = nc.dram_tensor("zs_scr", [8, 16, 128, 512], BF16).ap()
        BS = nc.dram_tensor("bs_scr", [8, 16, 128, 128], BF16).ap()
        BTS = nc.dram_tensor("bts_scr", [8, 128, NTOK], BF16).ap()
        CTS = nc.dram_tensor("cts_scr", [8, 128, NTOK], BF16).ap()
        WT.update(ssd_w_in=ssd_win, ssd_w_out=ssd_wout)
    has_ml = 2 in layers
    if has_ml:
        ml_win = din("mlstm_w_in", [D, 6176])
        ml_wout = din("mlstm_w_out", [D, D])
        ml_bg = din("ml_bg", [32])
        ml_hn = din("ml_hn", [8, 256])
        ml_C0 = din("ml_C0", [2, 8, 128, 256])
        ml_n0 = din("ml_n0", [2, 8, 128])
        ml_m0 = din("ml_m0", [128, 16])
        ml_consts = din("ml_consts", [128, 4, 128])
        if not has_ssd:
            ssd_consts = din("ssd_consts", [128, 4, 128])
            contd = din("cont", [128, 1])
        mlC_out = nc.dram_tensor("mlC_out", [8, 2, 8, 128, 256], F32, kind="ExternalOutput").ap()
        mln_out = nc.dram_tensor("mln_out", [8, 2, 8, 128], F32, kind="ExternalOutput").ap()
        mlm_out = nc.dram_tensor("mlm_out", [1, 128], F32, kind="ExternalOutput").ap()
        QTS = nc.dram_tensor("qts_scr", [8, 128, NTOK], BF16).ap()
        KTS = nc.dram_tensor("kts_scr", [8, 128, NTOK], BF16).ap()
        KMS = nc.dram_tensor("kms_scr", [8, 16, 128, 128], BF16).ap()
        VS = nc.dram_tensor("vs_scr", [8, 16, 128, 256], BF16).ap()
        OGS = nc.dram_tensor("ogs_scr", [8, 16, 128, 256], BF16).ap()
        WT.update(mlstm_w_in=ml_win, mlstm_w_out=ml_wout)
    has_attn = 1 in layers
    YT = nc.dram_tensor("ytscr", [32, 128, NTOK], BF16).ap()
    UTS = nc.dram_tensor("utscr", [KC, 128, NTOK], BF16).ap()
    if has_attn:
        attn_wqkv = din("attn_w_qkv", [D, 3072])
        attn_wout = din("attn_w_out", [D, D])
        attn_qn = din("attn_qn", [128, 1])
        attn_kn = din("attn_kn", [128, 1])
        attn_knrow = din("attn_knrow", [128])
        ropecos = din("ropecos", [128, NTOK])
        ropesin = din("ropesin", [128, NTOK])
        maskb_d = din("maskb", [128, 160])
        perm_d = din("perm", [128, 128])
        cachek = din("cachek", [512, 512])
        cachev = din("cachev", [512, 512])
        kc_out = nc.dram_tensor("kc_out", [NTOK, 512], F32, kind="ExternalOutput").ap()
        vc_out = nc.dram_tensor("vc_out", [NTOK, 512], F32, kind="ExternalOutput").ap()
        WT.update(attn_w_qkv=attn_wqkv, attn_w_out=attn_wout)
    dbg = nc.dram_tensor("dbg", [128, 8192], F32, kind="ExternalOutput").ap() if cfg.get("debug") else None
    X = nc.dram_tensor("xout", [NTOK, D], F32, kind="ExternalOutput").ap()
    GS = nc.dram_tensor("gscr", [DEPTH, 2, 128, D], F32).ap()

    NTP = cfg.get("ntp", 4)
    W = WStream(P, nc, WT, plan)
    ps = [nc.alloc_psum_tensor("psb%d" % i, [128, 512], F32) for i in range(8)]

    ident_f = nc.alloc_sbuf_tensor("ident_f", [128, 128], F32)
    ident = nc.alloc_sbuf_tensor("ident", [128, 128], BF16)
    modf = nc.alloc_sbuf_tensor("modf", [128, DEPTH, 96], F32)
    adabf = nc.alloc_sbuf_tensor("adabf", [128, DEPTH, 96], F32)
    nwf = nc.alloc_sbuf_tensor("nwf", [128, DEPTH, 2, KC], F32)
    amod = nc.alloc_sbuf_tensor("amod", [128, DEPTH, 2, KC], F32)
    cvt = nc.alloc_sbuf_tensor("cvt", [128, KC], F32)
    svb = nc.alloc_sbuf_tensor("svb", [128, KC], BF16)
    ss = nc.alloc_sbuf_tensor("ss", [128, 8], F32)
    rstd = nc.alloc_sbuf_tensor("rstd", [128, 8], F32)
    epsb = nc.alloc_sbuf_tensor("epsb", [128, 1], F32)
    P.op("dve", [], ["epsb"], lambda E: E.memset(epsb[:], EPS))

    P.dma("sp", ident_f[:], consts[:, :], [], ["ident_f"])
    P.dma("sp", adabf[:], adab_f[:, :, :], [], ["adabf"])
    P.dma("sp", nwf[:], nw_f[:, :, :, :], [], ["nwf"])
    P.dma("sp", cvt[:], cv[:, :], [], ["cvt"])
    P.op("dve", ["ident_f"], ["ident"], lambda E: E.tensor_copy(out=ident[:], in_=ident_f[:]))
    P.op("act", ["cvt"], ["svb"], lambda E: E.activation(out=svb[:], in_=cvt[:], func=AF.Silu))
    def ada_block(l, cb, pbi, gtb, abb):
        li = layers.index(l)
        blk, wk = W.take("ada_w", (li, slice(None), slice(cb * 512, (cb + 1) * 512)), KC, 512)
        pb = ps[pbi]
        pk = ("ps", pbi)
        if (cb % 12) < 8:
            def mmf(E):
                ins = None
                for ft in range(4):
                    for k in range(KC):
                        ins = E.matmul(pb[:, ft:ft + 1], lhsT=blk[:, k, ft * 128:(ft + 1) * 128],
                                       rhs=svb[:, k:k + 1], start=(k == 0), stop=(k == KC - 1))
                return ins
            P.op("pe", [wk, "svb"], [pk], mmf)
            P.op("dve", [pk, "adabf"], [("modf", l)], lambda E: E.tensor_tensor(
                out=modf[:, l, cb * 4:(cb + 1) * 4], in0=pb[:, 0:4], in1=adabf[:, l, cb * 4:(cb + 1) * 4],
                op=ALU.add))
        else:
            which = 0 if cb < 12 else 1
            c0 = (cb % 12 - 8) * 512
            P.dma("sp", abb[0:1, :], ada_b[li:li + 1, cb * 512:(cb + 1) * 512], [], [("abrow", pbi)])

            def mmr(E):
                ins = None
                for k in range(KC):
                    ins = E.matmul(pb[0:1, :], lhsT=svb[:, k:k + 1], rhs=blk[:, k, :],
                                   start=(k == 0), stop=(k == KC - 1))
                return ins
            P.op("pe", [wk, "svb"], [pk], mmr)
            P.op("dve", [pk, ("abrow", pbi)], [("gtile", pbi)], lambda E: E.tensor_tensor(
                out=gtb[0:1, :], in0=pb[0:1, :], in1=abb[0:1, :], op=ALU.add))
            P.dma("sp", GS[l, which, 0:1, c0:c0 + 512], gtb[0:1, :], [("gtile", pbi)], [("gs", l, which)])
        if cb == 23:
            for wh in range(2):
                P.op("dve", [("modf", l), "nwf"], [("amod", l, wh)], lambda E, wh=wh: E.scalar_tensor_tensor(
                    out=amod[:, l, wh, :], in0=modf[:, l, wh * 48 + 16:wh * 48 + 32], scalar=1.0,
                    in1=nwf[:, l, wh, :], op0=ALU.add, op1=ALU.mult))

    with sbt("gtile0", [1, 512], F32) as gt0, sbt("gtile1", [1, 512], F32) as gt1, \
            sbt("abrow0", [1, 512], F32) as ab0, sbt("abrow1", [1, 512], F32) as ab1:
        for cb in range(24):
            ada_block(layers[0], cb, cb % 2, [gt0, gt1][cb % 2], [ab0, ab1][cb % 2])
        P.barrier()
    if dbg is not None:
        P.dma("sp", dbg[:, 0:DEPTH * 96], modf[:].rearrange("p l j -> p (l j)"), ["modf"], ["dbg0"])
        P.dma("sp", dbg[:, 512:512 + DEPTH * 2 * KC], amod[:].rearrange("p l w c -> p (l w c)"), [], ["dbg1"])

    def norm_mod(l, wh, src, i, xt, xk, j, xn, UT, utk, col0):
        P.dma("sp", xt[:], src[i * 128:(i + 1) * 128, :], [("X", i)], [xk])
        P.op("act", [xk], ["xn", ("ss", j)], lambda E: E.activation(
            out=xn[:], in_=xt[:], func=AF.Square, accum_out=ss[:, j:j + 1]))
        P.op("act", [("ss", j)], [("rstd", j)], lambda E: E.activation(
            out=rstd[:, j:j + 1], in_=ss[:, j:j + 1], func=AF.Ln, scale=1.0 / D, bias=epsb[:, 0:1]))
        P.op("act", [("rstd", j)], [("rstd", j)], lambda E: E.activation(
            out=rstd[:, j:j + 1], in_=rstd[:, j:j + 1], func=AF.Exp, scale=-0.5))
        P.op("act", [xk, ("rstd", j)], ["xn"], lambda E: E.mul(out=xn[:], in_=xt[:], mul=rstd[:, j:j + 1]))
        for c4 in range(4):
            pb = ps[4 + (c4 % 2)]
            pk = ("ps", 4 + (c4 % 2))
            pbb = pb.bitcast(BF16)

            def tr(E, c4=c4, pbb=pbb):
                ins = None
                for cc in range(4):
                    c = c4 * 4 + cc
                    ins = E.transpose(out=pbb[:, cc * 128:(cc + 1) * 128], in_=xn[:, c * 128:(c + 1) * 128],
                                      identity=ident[:])
                return ins
            P.op("pe", ["xn", "ident"], [pk], tr)
            for cc in range(4):
                c = c4 * 4 + cc
                last = (cc == 3)
                P.op("dve", [pk, ("amod", l, wh), ("modf", l)], [utk],
                     lambda E, c=c, cc=cc, pbb=pbb: E.tensor_scalar(
                         out=UT[:, c, col0:col0 + 128], in0=pbb[:, cc * 128:(cc + 1) * 128],
                         scalar1=amod[:, l, wh, c:c + 1], scalar2=modf[:, l, wh * 48 + c:wh * 48 + c + 1],
                         op0=ALU.mult, op1=ALU.add))

    def ffn_phase(l, src):
        li = layers.index(l)
        with sbt("xt0", [128, D], F32) as xt0, sbt("xt1", [128, D], F32) as xt1, \
                sbt("xt2", [128, D], F32) as xt2, sbt("xt3", [128, D], F32) as xt3, \
                sbt("xn", [128, D], BF16) as xn, \
                sbt("UT", [128, KC, 512], BF16) as UT, sbt("HT", [128, HC, 512], BF16) as HT, \
                sbt("g2", [128, D], F32) as g2, sbt("sg0", [128, 512], BF16) as sg0, \
                sbt("sg1", [128, 512], BF16) as sg1, sbt("tmp0", [128, 512], F32) as tmp0, \
                sbt("tmp1", [128, 512], F32) as tmp1, sbt("agt0", [1, 512], F32) as agt0, \
                sbt("agt1", [1, 512], F32) as agt1, sbt("aab0", [1, 512], F32) as aab0, sbt("aab1", [1, 512], F32) as aab1:
            agt = [agt0, agt1]
            aab = [aab0, aab1]
            xts = [xt0, xt1, xt2, xt3]
            sgs = [sg0, sg1]
            tmps = [tmp0, tmp1]
            lpos = layers.index(l)
            lnext = layers[lpos + 1] if lpos + 1 < len(layers) else None
            ada_todo = list(range(24)) if lnext is not None else []
            P.dma("sp", g2[:], GS[l, 1, 0, :].partition_broadcast(128), [("gs", l, 1)], ["g2"])
            nsg = 0
            ntmp = 0
            for tp in range(NTP):
                for tt in range(4):
                    i = tp * 4 + tt
                    norm_mod(l, 1, src, i, xts[tt], ("xt", tt), tt, xn, UT, ("UT", tt), tt * 128)
                utks = [("UT", tt) for tt in range(4)]
                for hb in range(11):
                    hsl = (li, slice(None), slice(hb * 512, (hb + 1) * 512))
                    wg, wgk = W.take("ffn_w_gate", hsl, KC, 512)
                    wu, wuk = W.take("ffn_w_up", hsl, KC, 512)
                    for ft in range(4):
                        hi = hb * 4 + ft
                        pg = ps[hi % 2]
                        pgk = ("ps", hi % 2)
                        pu = ps[2 + hi % 2]
                        puk = ("ps", 2 + hi % 2)

                        def mmg(E, w=wg, pb=pg, ft=ft):
                            ins = None
                            for k in range(KC):
                                ins = E.matmul(pb[:, :], lhsT=w[:, k, ft * 128:(ft + 1) * 128], rhs=UT[:, k, :],
                                               start=(k == 0), stop=(k == KC - 1))
                            return ins
                        P.op("pe", [wgk] + utks, [pgk], mmg)
                        P.op("pe", [wuk] + utks, [puk], lambda E, w=wu, pb=pu, ft=ft: mmg(E, w, pb, ft))
                        sj = nsg % 2
                        nsg += 1
                        P.op("act", [pgk], [("sg", sj)], lambda E, pb=pg, sj=sj: E.activation(
                            out=sgs[sj][:], in_=pb[:, :], func=AF.Silu))
                        P.op("dve", [("sg", sj), puk], [("HT", hi)], lambda E, pb=pu, sj=sj, hi=hi: E.tensor_tensor(
                            out=HT[:, hi, :], in0=sgs[sj][:], in1=pb[:, :], op=ALU.mult))
                        if dbg is not None and tp == 0 and hi == 0:
                            with sbt("dbgu", [128, 1536], F32) as dbgu:
                                P.op("dve", [pgk], ["dbgu"], lambda E: E.tensor_copy(out=dbgu[:, 0:512], in_=pg[:, :]))
                                P.op("dve", [("sg", sj)], ["dbgu"], lambda E: E.tensor_copy(out=dbgu[:, 512:1024], in_=sgs[sj][:]))
                                P.op("dve", [puk], ["dbgu"], lambda E: E.tensor_copy(out=dbgu[:, 1024:1536], in_=pu[:, :]))
                                P.dma("sp", dbg[:, 6144:7680], dbgu[:], ["dbgu"], ["dbg4"])
                                P.barrier()
                    if ada_todo:
                        cbn = ada_todo.pop(0)
                        ada_block(lnext, cbn, 6 + cbn % 2, agt[cbn % 2], aab[cbn % 2])
                htks = [("HT", hi) for hi in range(HC)]
                if dbg is not None and tp == 0:
                    with sbt("dbgt", [128, 2048], F32) as dbgt:
                        P.op("dve", utks, ["dbgt"], lambda E: E.tensor_copy(out=dbgt[:, 0:512], in_=UT[:, 0, :]))
                        P.op("dve", utks, ["dbgt"], lambda E: E.tensor_copy(out=dbgt[:, 512:1024], in_=UT[:, 5, :]))
                        P.op("dve", htks, ["dbgt"], lambda E: E.tensor_copy(out=dbgt[:, 1024:1536], in_=HT[:, 0, :]))
                        P.op("dve", htks, ["dbgt"], lambda E: E.tensor_copy(out=dbgt[:, 1536:2048], in_=HT[:, 17, :]))
                        P.dma("sp", dbg[:, 1024:3072], dbgt[:], ["dbgt"], ["dbg2"])
                        P.dma("sp", dbg[:, 4096:6144], g2[:], ["g2"], ["dbg3"])
                        P.barrier()
                for cb in range(4):
                    base = 0 if cb % 2 == 0 else 4
                    for k0, kn in KSPLIT:
                        wd, wdk = W.take("ffn_w_down", (li, slice(k0 * 128, (k0 + kn) * 128),
                                                        slice(cb * 512, (cb + 1) * 512)), kn, 512)
                        for tt in range(4):
                            pb = ps[base + tt]
                            pk = ("ps", base + tt)

                            def mmd(E, w=wd, pb=pb, tt=tt, k0=k0, kn=kn):
                                ins = None
                                for k in range(kn):
                                    ins = E.matmul(pb[:, :], lhsT=HT[:, k0 + k, tt * 128:(tt + 1) * 128],
                                                   rhs=w[:, k, :], start=(k0 + k == 0),
                                                   stop=(k0 + k == HC - 1))
                                return ins
                            P.op("pe", [wdk] + htks, [pk], mmd)
                    for tt in range(4):
                        pb = ps[base + tt]
                        pk = ("ps", base + tt)
                        tj = ntmp % 2
                        ntmp += 1
                        P.op("dve", [pk, "g2"], [("tmp", tj)], lambda E, pb=pb, tj=tj, cb=cb: E.tensor_tensor(
                            out=tmps[tj][:], in0=pb[:, :], in1=g2[:, cb * 512:(cb + 1) * 512], op=ALU.mult))
                        P.op("pool", [("tmp", tj), ("xt", tt)], [("xt", tt)],
                             lambda E, tj=tj, tt=tt, cb=cb: E.tensor_tensor(
                                 out=xts[tt][:, cb * 512:(cb + 1) * 512], in0=xts[tt][:, cb * 512:(cb + 1) * 512],
                                 in1=tmps[tj][:], op=ALU.add))
                for tt in range(4):
                    i = tp * 4 + tt
                    P.dma("sp", X[i * 128:(i + 1) * 128, :], xts[tt][:], [("xt", tt)], [("X", i)])
            P.barrier()


    def outproj_phase(l, src, Kc, wname, widx=None):
        with sbt("xt0", [128, D], F32) as xt0, sbt("xt1", [128, D], F32) as xt1, \
                sbt("xt2", [128, D], F32) as xt2, sbt("xt3", [128, D], F32) as xt3, \
                sbt("AT", [128, Kc, 512], BF16) as AT, sbt("g1", [128, D], F32) as g1, \
                sbt("tmp0", [128, 512], F32) as tmp0, sbt("tmp1", [128, 512], F32) as tmp1:
            xts = [xt0, xt1, xt2, xt3]
            tmps = [tmp0, tmp1]
            P.dma("sp", g1[:], GS[l, 0, 0, :].partition_broadcast(128), [("gs", l, 0)], ["g1"])
            ntmp = 0
            for tp in range(4):
                for tt in range(4):
                    i = tp * 4 + tt
                    P.dma("sp", xts[tt][:], src[i * 128:(i + 1) * 128, :], [("X", i)], [("xt", tt)])
                P.dma("sp", AT[:], YT[0:Kc, :, tp * 512:(tp + 1) * 512].rearrange("c p n -> p c n"), [], ["AT"])
                for cb in range(4):
                    base = 0 if cb % 2 == 0 else 4
                    for kb in range(Kc // 16):
                        wsl = (slice(kb * 2048, (kb + 1) * 2048), slice(cb * 512, (cb + 1) * 512))
                        if widx is not None:
                            wsl = (ssd_js.index(widx),) + wsl
                        w, wk = W.take(wname, wsl, 16, 512)
                        for tt in range(4):
                            pb = ps[base + tt]

                            def mmo(E, w=w, pb=pb, tt=tt, kb=kb):
                                ins = None
                                for k in range(16):
                                    ins = E.matmul(pb[:, :], lhsT=AT[:, kb * 16 + k, tt * 128:(tt + 1) * 128],
                                                   rhs=w[:, k, :], start=(kb == 0 and k == 0),
                                                   stop=(kb == Kc // 16 - 1 and k == 15))
                                return ins
                            P.op("pe", [wk, "AT"], [("ps", base + tt)], mmo)
                    for tt in range(4):
                        pb = ps[base + tt]
                        pk = ("ps", base + tt)
                        tj = ntmp % 2
                        ntmp += 1
                        P.op("dve", [pk, "g1"], [("tmp", tj)], lambda E, pb=pb, tj=tj, cb=cb: E.tensor_tensor(
                            out=tmps[tj][:], in0=pb[:, :], in1=g1[:, cb * 512:(cb + 1) * 512], op=ALU.mult))
                        P.op("pool", [("tmp", tj), ("xt", tt)], [("xt", tt)],
                             lambda E, tj=tj, tt=tt, cb=cb: E.tensor_tensor(
                                 out=xts[tt][:, cb * 512:(cb + 1) * 512], in0=xts[tt][:, cb * 512:(cb + 1) * 512],
                                 in1=tmps[tj][:], op=ALU.add))
                for tt in range(4):
                    i = tp * 4 + tt
                    P.dma("sp", X[i * 128:(i + 1) * 128, :], xts[tt][:], [("xt", tt)], [("X", i)])
            P.barrier()

    def attn_phase(l, src):
        SC = 128.0 ** -0.5
        with ExitStack() as st:
            UT = st.enter_context(sbt("UTa", [128, KC, 512], BF16))
            KT = st.enter_context(sbt("KT", [128, 4, 2560], BF16))
            VA = st.enter_context(sbt("VA", [128, 20, 512], BF16))
            cosT = st.enter_context(sbt("cosT", [128, 512], F32))
            sinT = st.enter_context(sbt("sinT", [128, 512], F32))
            maskb = st.enter_context(sbt("maskb_s", [128, 160], F32))
            permf = st.enter_context(sbt("permf", [128, 128], F32))
            permb = st.enter_context(sbt("permb", [128, 128], BF16))
            onesb = st.enter_context(sbt("onesb", [128, 128], BF16))
            qnw = st.enter_context(sbt("qnw", [128, 2], F32))
            knrow = st.enter_context(sbt("knrow", [128, 128], F32))
            sqb = st.enter_context(sbt("sqb", [128, 512], BF16))
            rt = st.enter_context(sbt("rt", [128, 512], F32))
            qn = st.enter_context(sbt("qn", [128, 512], BF16))
            t1 = st.enter_context(sbt("t1", [128, 512], F32))
            t2 = st.enter_context(sbt("t2", [128, 512], F32))
            QT0 = st.enter_context(sbt("QT0", [128, 512], BF16))
            QT1 = st.enter_context(sbt("QT1", [128, 512], BF16))
            PT0 = st.enter_context(sbt("PT0", [128, 256], BF16))
            PT1 = st.enter_context(sbt("PT1", [128, 256], BF16))
            PT2 = st.enter_context(sbt("PT2", [128, 256], BF16))
            rs = st.enter_context(sbt("rs", [128, 256], F32))
            AO0 = st.enter_context(sbt("AO0", [128, 256], BF16))
            AO1 = st.enter_context(sbt("AO1", [128, 256], BF16))
            kf = st.enter_context(sbt("kf", [128, 512], F32))
            vf = st.enter_context(sbt("vf", [128, 512], F32))
            ssk = st.enter_context(sbt("ssk", [128, 4], F32))
            ckf = st.enter_context(sbt("ckf", [128, 512], F32))
            ckb = st.enter_context(sbt("ckb", [128, 512], BF16))
            QTs = [QT0, QT1]
            PTs = [PT0, PT1, PT2]
            AOs = [AO0, AO1]
            P.dma("sp", maskb[:], maskb_d[:, :], [], ["maskb"])
            P.dma("sp", permf[:], perm_d[:, :], [], ["permf"])
            P.dma("sp", qnw[:, 0:1], attn_qn[:, :], [], ["qnw"])
            P.dma("sp", qnw[:, 1:2], attn_kn[:, :], ["qnw"], ["qnw"])
            P.dma("sp", knrow[:], attn_knrow.partition_broadcast(128), [], ["knrow"])
            P.op("dve", ["permf"], ["permb"], lambda E: E.tensor_copy(out=permb[:], in_=permf[:]))
            P.op("dve", [], ["onesb"], lambda E: E.memset(onesb[:], 1.0))
            P.op("dve", ["qnw"], ["qnw"], lambda E: E.tensor_scalar_mul(out=qnw[:, 0:1], in0=qnw[:, 0:1], scalar1=SC))
            def head_fm(w, wk, c0, tt, widx, dst, dstk):
                def mmh(E):
                    ins = None
                    for k in range(KC):
                        ins = E.matmul(ps[0][:, :], lhsT=w[:, k, c0:c0 + 128], rhs=UT[:, k, tt * 512:(tt + 1) * 512],
                                       start=(k == 0), stop=(k == KC - 1))
                    return ins
                P.op("pe", [wk] + utk, [("ps", 0)], mmh)
                P.op("act", [("ps", 0)], ["sqb"], lambda E: E.activation(out=sqb[:], in_=ps[0][:, :], func=AF.Square))
                P.op("pe", ["sqb", "onesb"], [("ps", 1)], lambda E: E.matmul(
                    ps[1][:, :], lhsT=onesb[:], rhs=sqb[:], start=True, stop=True))
                P.op("act", [("ps", 1)], ["rt"], lambda E: E.activation(
                    out=rt[:], in_=ps[1][:, :], func=AF.Ln, scale=1.0 / 128, bias=epsb[:, 0:1]))
                P.op("act", ["rt"], ["rt"], lambda E: E.activation(out=rt[:], in_=rt[:], func=AF.Exp, scale=-0.5))
                P.op("dve", [("ps", 0), "rt", "qnw"], ["qn"], lambda E: E.scalar_tensor_tensor(
                    out=qn[:], in0=ps[0][:, :], scalar=qnw[:, widx:widx + 1], in1=rt[:], op0=ALU.mult, op1=ALU.mult))
                P.op("pe", ["qn", "permb"], [("ps", 2)], lambda E: E.matmul(
                    ps[2][:, :], lhsT=permb[:], rhs=qn[:], start=True, stop=True))
                P.op("dve", [("ps", 2), "sinT"], ["t1"], lambda E: E.tensor_tensor(
                    out=t1[:], in0=ps[2][:, :], in1=sinT[:], op=ALU.mult))
                P.op("pool", ["qn", "cosT"], ["t2"], lambda E: E.tensor_tensor(
                    out=t2[:], in0=qn[:], in1=cosT[:], op=ALU.mult))
                P.op("dve", ["t1", "t2"], [dstk], lambda E: E.tensor_tensor(out=dst, in0=t1[:], in1=t2[:], op=ALU.add))

            def load_tables(tt):
                P.dma("sp", cosT[:], ropecos[:, tt * 512:(tt + 1) * 512], [], ["cosT"])
                P.dma("sp", sinT[:], ropesin[:, tt * 512:(tt + 1) * 512], [], ["sinT"])

            utk = [("UT", j) for j in range(4)]
            wkk, wkkk = W.take("attn_w_qkv", (slice(None), slice(2048, 2560)), KC, 512)
            wv, wvk = W.take("attn_w_qkv", (slice(None), slice(2560, 3072)), KC, 512)
            with sbt("xta", [128, D], F32) as xta, sbt("xtb", [128, D], F32) as xtb, \
                    sbt("xn", [128, D], BF16) as xn:
                xtl = [xta, xtb]
                for tt in range(4):
                    load_tables(tt)
                    for i4 in range(4):
                        i = tt * 4 + i4
                        norm_mod(l, 0, src, i, xtl[i % 2], ("xt", i % 2), i % 2, xn, UT, ("UT", i4), i4 * 128)
                    P.dma("sp", UTS[:, :, tt * 512:(tt + 1) * 512].rearrange("c p n -> p c n"), UT[:], utk, [("UTS", tt)])
                    for i4 in range(4):
                        i = tt * 4 + i4
                        pkb, pvb = ps[(i % 2) * 2], ps[(i % 2) * 2 + 1]
                        pkk, pvk = ("ps", (i % 2) * 2), ("ps", (i % 2) * 2 + 1)

                        def mmt(E, w, pb, i4=i4):
                            ins = None
                            for k in range(KC):
                                ins = E.matmul(pb[:, :], lhsT=UT[:, k, i4 * 128:(i4 + 1) * 128], rhs=w[:, k, :],
                                               start=(k == 0), stop=(k == KC - 1))
                            return ins
                        P.op("pe", [wkkk] + utk, [pkk], lambda E, pb=pkb, mmt=mmt: mmt(E, wkk, pb))
                        P.op("pe", [wvk] + utk, [pvk], lambda E, pb=pvb, mmt=mmt: mmt(E, wv, pb))
                        for hh in range(4):
                            P.op("act", [pkk], ["sqb", ("ssk", hh)], lambda E, pb=pkb, hh=hh: E.activation(
                                out=sqb[:, 0:128], in_=pb[:, hh * 128:(hh + 1) * 128], func=AF.Square,
                                accum_out=ssk[:, hh:hh + 1]))
                        sskk = [("ssk", hh) for hh in range(4)]
                        P.op("act", sskk, sskk, lambda E: E.activation(
                            out=ssk[:, :], in_=ssk[:, :], func=AF.Ln, scale=1.0 / 128, bias=epsb[:, 0:1]))
                        P.op("act", sskk, sskk, lambda E: E.activation(out=ssk[:, :], in_=ssk[:, :], func=AF.Exp, scale=-0.5))
                        for hh in range(4):
                            P.op("dve", [pkk, ("ssk", hh), "knrow"], ["kf"], lambda E, pb=pkb, hh=hh: E.scalar_tensor_tensor(
                                out=kf[:, hh * 128:(hh + 1) * 128], in0=pb[:, hh * 128:(hh + 1) * 128],
                                scalar=ssk[:, hh:hh + 1], in1=knrow[:], op0=ALU.mult, op1=ALU.mult))
                        P.dma("sp", kc_out[i * 128:(i + 1) * 128, :], kf[:], ["kf"], [("kco", i)])
                        P.op("act", [pvk], ["vf"], lambda E, pb=pvb: E.copy(out=vf[:], in_=pb[:, :]))
                        P.op("dve", [pvk], [("VA", 4 + i)], lambda E, pb=pvb, i=i: E.tensor_copy(out=VA[:, 4 + i, :], in_=pb[:, :]))
                        P.dma("sp", vc_out[i * 128:(i + 1) * 128, :], vf[:], ["vf"], [("vco", i)])
                    for hh in range(4):
                        head_fm(wkk, wkkk, hh * 128, 0, 1, KT[:, hh, 512 + tt * 512:512 + (tt + 1) * 512], ("KT", hh))
                P.barrier()
            for j in range(4):
                P.dma("pool", VA[:, j, :], cachev[j * 128:(j + 1) * 128, :], [], [("VA", j)])
                P.dma("sp", ckf[:], cachek[j * 128:(j + 1) * 128, :], [], ["ckf"])
                P.op("dve", ["ckf"], ["ckb"], lambda E: E.tensor_copy(out=ckb[:], in_=ckf[:]))
                pbb = ps[4].bitcast(BF16)

                def trc(E, pbb=pbb):
                    ins = None
                    for hh in range(4):
                        ins = E.transpose(out=pbb[:, hh * 128:(hh + 1) * 128], in_=ckb[:, hh * 128:(hh + 1) * 128],
                                          identity=ident[:])
                    return ins
                P.op("pe", ["ckb", "ident"], [("ps", 4)], trc)
                for hh in range(4):
                    P.op("dve", [("ps", 4)], [("KT", hh)], lambda E, hh=hh, j=j, pbb=pbb: E.tensor_copy(
                        out=KT[:, hh, j * 128:(j + 1) * 128], in_=pbb[:, hh * 128:(hh + 1) * 128]))

            nq = 0
            npt = 0
            nao = 0
            vak = [("VA", j) for j in range(20)]
            for tt in range(4):
                load_tables(tt)
                P.dma("sp", UT[:], UTS[:, :, tt * 512:(tt + 1) * 512].rearrange("c p n -> p c n"), [("UTS", tt)], utk)
                for qb in range(4):
                    wq, wqk = W.take("attn_w_qkv", (slice(None), slice(qb * 512, (qb + 1) * 512)), KC, 512)
                    for hq in range(4):
                        h = qb * 4 + hq
                        kv = h // 4
                        qj = nq % 2
                        nq += 1
                        head_fm(wq, wqk, hq * 128, 0, 0, QTs[qj][:], ("QT", qj))
                        for sub in range(2):
                            qt8 = tt * 2 + sub
                            SB = [3, 4, 7]

                            def emit_S(kt, kv=kv, qj=qj, sub=sub):
                                bi = SB[kt % 3]
                                P.op("pe", [("KT", kv), ("QT", qj)], [("ps", bi)], lambda E: E.matmul(
                                    ps[bi][:, 0:256], lhsT=KT[:, kv, kt * 128:(kt + 1) * 128],
                                    rhs=QTs[qj][:, sub * 256:(sub + 1) * 256], start=True, stop=True))

                            emit_S(0)
                            emit_S(1)
                            for kt in range(20):
                                if kt + 2 < 20:
                                    emit_S(kt + 2)
                                bi = SB[kt % 3]
                                sb = ps[bi]
                                sk = ("ps", bi)
                                pj = npt % 3
                                npt += 1
                                P.op("act", [sk, "maskb"], [("PT", pj)], lambda E, sb=sb, pj=pj, qt8=qt8, kt=kt: E.activation(
                                    out=PTs[pj][:], in_=sb[:, 0:256], func=AF.Exp,
                                    bias=maskb[:, qt8 * 20 + kt:qt8 * 20 + kt + 1]))
                                P.op("pe", [("PT", pj)] + vak, [("ps", 5), ("ps", 6)],
                                     lambda E, pj=pj, kt=kt, kv=kv: (
                                         E.matmul(ps[5][:, 0:256], lhsT=VA[:, kt, kv * 128:(kv + 1) * 128], rhs=PTs[pj][:],
                                                  start=(kt == 0), stop=(kt == 19)),
                                         E.matmul(ps[6][:, 0:256], lhsT=onesb[:], rhs=PTs[pj][:],
                                                  start=(kt == 0), stop=(kt == 19)))[1])
                            P.op("dve", [("ps", 6)], ["rs"], lambda E: E.reciprocal(out=rs[:], in_=ps[6][:, 0:256]))
                            aj = nao % 2
                            nao += 1
                            P.op("dve", [("ps", 5), "rs"], [("AO", aj)], lambda E, aj=aj: E.tensor_tensor(
                                out=AOs[aj][:], in0=ps[5][:, 0:256], in1=rs[:], op=ALU.mult))
                            P.dma("sp", YT[h, :, qt8 * 256:(qt8 + 1) * 256], AOs[aj][:], [("AO", aj)], [("YT", h, qt8)])
            P.barrier()


    def v3(ap):
        return ap.rearrange("p (r q) -> p r q", q=64)

    def ssd_phase(l, src):
        jj = ssd_js.index(l // 3)
        with ExitStack() as so:
            def sb(name, shape, dt, st=so):
                return st.enter_context(sbt(name, shape, dt))
            DTt = sb("DTt", [128, 16, 128], F32)
            At = sb("At", [128, 16, 128], F32)
            EX = sb("EX", [128, 16, 4, 64], F32)
            ETOT = sb("ETOT", [128, 16, 128], F32)
            cst = sb("ssdc", [128, 4, 128], F32)
            onesf = sb("onesf", [128, 128], F32)
            cont = sb("contt", [128, 1], F32)
            ncont = sb("ncont", [128, 1], F32)
            maskF, maskB, SLf, SLb = cst[:, 0, :], cst[:, 1, :], cst[:, 2, :], cst[:, 3, :]
            P.dma("sp", cst[:], ssd_consts[:, :, :], [], ["ssdc"])
            P.dma("sp", cont[:], contd[:, :], [], ["cont"])
            P.op("dve", [], ["onesf"], lambda E: E.memset(onesf[:], 1.0))
            P.op("dve", ["cont"], ["ncont"], lambda E: E.tensor_scalar_add(out=ncont[:], in0=cont[:], scalar1=-1.0))

            with ExitStack() as s1:
                UT = sb("UTs", [128, KC, NTOK], BF16, s1)
                cw = sb("cw", [128, 48, 3], F32, s1)
                ncw = sb("ncw", [128, 48, 3], F32, s1)
                cbb = sb("cbb", [128, 48], F32, s1)
                dtb = sb("dtb", [128, 128], F32, s1)
                arow = sb("arow", [128, 128], F32, s1)
                P.dma("sp", cw[:], ssd_cw[:, jj, :, :], [], ["cw"])
                P.dma("sp", cbb[:], ssd_cb[:, jj, :], [], ["cbb"])
                P.dma("sp", dtb[:], ssd_dtb[jj, :].partition_broadcast(128), [], ["dtb"])
                P.dma("sp", arow[:], ssd_alog[jj, :].partition_broadcast(128), [], ["arow"])
                P.op("act", ["arow"], ["arow"], lambda E: E.activation(out=arow[:], in_=arow[:], func=AF.Exp))
                P.op("dve", ["arow"], ["arow"], lambda E: E.tensor_scalar_mul(out=arow[:], in0=arow[:], scalar1=-1.0))
                P.op("dve", ["cw", "ncont"], ["ncw"], lambda E: E.tensor_scalar_mul(
                    out=ncw[:].rearrange("p a b -> p (a b)"), in0=cw[:].rearrange("p a b -> p (a b)"), scalar1=ncont[:, 0:1]))
                with ExitStack() as s0:
                    xta = sb("xta", [128, D], F32, s0)
                    xtb = sb("xtb", [128, D], F32, s0)
                    xn = sb("xn", [128, D], BF16, s0)
                    xtl = [xta, xtb]
                    for i in range(16):
                        norm_mod(l, 0, src, i, xtl[i % 2], ("xt", i % 2), i % 2, xn, UT, ("UT", i // 4), i * 128)
                    P.barrier()
                utk = [("UT", q) for q in range(4)]
                with ExitStack() as s2:
                    t_a = sb("t_a", [128, 128], F32, s2)
                    t_b = sb("t_b", [128, 128], F32, s2)
                    t_c = sb("t_c", [128, 128], F32, s2)
                    wdt, wdtk = W.take("ssd_w_in", (jj, slice(None), slice(10240, 10368)), KC, 128)
                    for i in range(16):
                        pb = ps[i % 2]
                        pk = ("ps", i % 2)

                        def mmdt(E, pb=pb, i=i):
                            ins = None
                            for k in range(KC):
                                ins = E.matmul(pb[:, 0:128], lhsT=UT[:, k, i * 128:(i + 1) * 128], rhs=wdt[:, k, :],
                                               start=(k == 0), stop=(k == KC - 1))
                            return ins
                        P.op("pe", [wdtk] + utk, [pk], mmdt)
                        P.op("dve", [pk, "dtb"], ["t_a"], lambda E, pb=pb: E.tensor_tensor(
                            out=t_a[:], in0=pb[:, 0:128], in1=dtb[:], op=ALU.add))
                        P.op("act", ["t_a"], ["t_b"], lambda E: E.activation(out=t_b[:], in_=t_a[:], func=AF.Abs))
                        P.op("act", ["t_b"], ["t_b"], lambda E: E.activation(out=t_b[:], in_=t_b[:], func=AF.Exp, scale=-1.0))
                        P.op("act", ["t_b"], ["t_b"], lambda E: E.activation(out=t_b[:], in_=t_b[:], func=AF.Ln, bias=1.0))
                        P.op("dve", ["t_a"], ["t_c"], lambda E: E.tensor_scalar_max(out=t_c[:], in0=t_a[:], scalar1=0.0))
                        P.op("dve", ["t_b", "t_c"], [("DT", i)], lambda E, i=i: E.tensor_tensor(
                            out=DTt[:, i, :], in0=t_b[:], in1=t_c[:], op=ALU.add))
                        P.op("dve", [("DT", i), "arow"], [("A", i)], lambda E, i=i: E.tensor_tensor(
                            out=At[:, i, :], in0=DTt[:, i, :], in1=arow[:], op=ALU.mult))
                        pc = ps[2 + i % 2]
                        pck = ("ps", 2 + i % 2)

                        def mmcs(E, pc=pc, i=i):
                            E.matmul(pc[:, 0:64], lhsT=maskF, rhs=At[:, i, 0:64], start=True, stop=True)
                            E.matmul(pc[:, 64:128], lhsT=SLf, rhs=At[:, i, 0:64], start=True, stop=True)
                            E.matmul(pc[:, 128:192], lhsT=maskB, rhs=At[:, i, 64:128], start=True, stop=True)
                            E.matmul(pc[:, 192:256], lhsT=SLb, rhs=At[:, i, 64:128], start=True, stop=True)
                            return E.matmul(pc[:, 256:384], lhsT=onesf[:], rhs=At[:, i, :], start=True, stop=True)
                        P.op("pe", [("A", i), "ssdc", "onesf"], [pck], mmcs)
                        P.op("act", [pck], [("EX", i)], lambda E, pc=pc, i=i: E.activation(
                            out=EX[:, i, :, :].rearrange("p a b -> p (a b)"), in_=pc[:, 0:256], func=AF.Exp))
                        P.op("act", [pck], [("ETOT", i)], lambda E, pc=pc, i=i: E.activation(
                            out=ETOT[:, i, :], in_=pc[:, 256:384], func=AF.Exp))
                with ExitStack() as s3:
                    zt0 = sb("zt0", [128, 512], BF16, s3)
                    zt1 = sb("zt1", [128, 512], BF16, s3)
                    zts = [zt0, zt1]
                    nz = 0
                    for zb in range(8):
                        wz, wzk = W.take("ssd_w_in", (jj, slice(None), slice(zb * 512, (zb + 1) * 512)), KC, 512)
                        for i in range(16):
                            pb = ps[nz % 2]
                            pk = ("ps", nz % 2)
                            zj = nz % 2
                            nz += 1

                            def mmz(E, pb=pb, i=i, wz=wz):
                                ins = None
                                for k in range(KC):
                                    ins = E.matmul(pb[:, :], lhsT=UT[:, k, i * 128:(i + 1) * 128], rhs=wz[:, k, :],
                                                   start=(k == 0), stop=(k == KC - 1))
                                return ins
                            P.op("pe", [wzk] + utk, [pk], mmz)
                            P.op("act", [pk], [("zt", zj)], lambda E, pb=pb, zj=zj: E.activation(
                                out=zts[zj][:], in_=pb[:, :], func=AF.Silu))
                            P.dma("sp", ZS[zb, i, :, :], zts[zj][:], [("zt", zj)], [("ZS", zb, i)])
                    raw0 = sb("raw0", [128, NTOK], F32, s3)
                    raw1 = raw0
                    acc = sb("acc", [128, NTOK], F32, s3)
                    xb0 = sb("xb0", [128, NTOK], BF16, s3)
                    xb1 = xb0
                    tr0 = sb("tr0", [128, 512], BF16, s3)
                    tr1 = sb("tr1", [128, 512], BF16, s3)
                    raws = [raw0, raw1]
                    xbs = [xb0, xb1]
                    trs = [tr0, tr1]
                    nft = 0
                    ntr = 0
                    for cb in range(12):
                        wx, wxk = W.take("ssd_w_in", (jj, slice(None), slice(4096 + cb * 512, 4096 + (cb + 1) * 512)), KC, 512)
                        for ft in range(4):
                            ct = cb * 4 + ft
                            rj = 0
                            nft += 1
                            raw = raws[rj]
                            xb = xbs[rj]
                            for tt in range(4):
                                pb = ps[tt % 2]
                                pk = ("ps", tt % 2)

                                def mmx(E, pb=pb, tt=tt, ft=ft, wx=wx):
                                    ins = None
                                    for k in range(KC):
                                        ins = E.matmul(pb[:, :], lhsT=wx[:, k, ft * 128:(ft + 1) * 128],
                                                       rhs=UT[:, k, tt * 512:(tt + 1) * 512], start=(k == 0), stop=(k == KC - 1))
                                    return ins
                                P.op("pe", [wxk] + utk, [pk], mmx)
                                P.op("act", [pk], [("raw", rj)], lambda E, pb=pb, tt=tt, raw=raw: E.copy(
                                    out=raw[:, tt * 512:(tt + 1) * 512], in_=pb[:, :]))
                            rk = ("raw", rj)
                            P.op("dve", [rk, "cw"], ["acc"], lambda E, raw=raw, ct=ct: E.tensor_scalar_mul(
                                out=acc[:], in0=raw[:], scalar1=cw[:, ct, 1:2]))
                            P.op("dve", [rk, "cw", "acc"], ["acc"], lambda E, raw=raw, ct=ct: E.scalar_tensor_tensor(
                                out=acc[:, 1:NTOK], in0=raw[:, 0:NTOK - 1], scalar=cw[:, ct, 0:1], in1=acc[:, 1:NTOK],
                                op0=ALU.mult, op1=ALU.add))
                            P.op("dve", [rk, "cw", "acc"], ["acc"], lambda E, raw=raw, ct=ct: E.scalar_tensor_tensor(
                                out=acc[:, 0:NTOK - 1], in0=raw[:, 1:NTOK], scalar=cw[:, ct, 2:3], in1=acc[:, 0:NTOK - 1],
                                op0=ALU.mult, op1=ALU.add))
                            accv = acc[:].rearrange("p (s t) -> p s t", t=256)
                            rawv = raw[:].rearrange("p (s t) -> p s t", t=256)
                            P.op("dve", [rk, "ncw", "acc"], ["acc"], lambda E, accv=accv, rawv=rawv, ct=ct: E.scalar_tensor_tensor(
                                out=accv[:, 1:8, 0], in0=rawv[:, 0:7, 255], scalar=ncw[:, ct, 0:1], in1=accv[:, 1:8, 0],
                                op0=ALU.mult, op1=ALU.add))
                            P.op("dve", [rk, "ncw", "acc"], ["acc"], lambda E, accv=accv, rawv=rawv, ct=ct: E.scalar_tensor_tensor(
                                out=accv[:, 0:7, 255], in0=rawv[:, 1:8, 0], scalar=ncw[:, ct, 2:3], in1=accv[:, 0:7, 255],
                                op0=ALU.mult, op1=ALU.add))
                            P.op("act", ["acc", "cbb"], [("xb", rj)], lambda E, xb=xb, ct=ct: E.activation(
                                out=xb[:], in_=acc[:], func=AF.Silu, bias=cbb[:, ct:ct + 1]))
                            xk = ("xb", rj)
                            if ct >= 40:
                                P.dma("sp", CTS[ct - 40, :, :], xb[:], [xk], [("CTS", ct)])
                                continue
                            if ct >= 32:
                                P.dma("sp", BTS[ct - 32, :, :], xb[:], [xk], [("BTS", ct)])
                            for i4 in range(4):
                                pbb = ps[4 + i4 % 2].bitcast(BF16)
                                pk = ("ps", 4 + i4 % 2)

                                def trx(E, pbb=pbb, i4=i4, xb=xb):
                                    ins = None
                                    for q in range(4):
                                        i = i4 * 4 + q
                                        ins = E.transpose(out=pbb[:, q * 128:(q + 1) * 128], in_=xb[:, i * 128:(i + 1) * 128],
                                                          identity=ident[:])
                                    return ins
                                P.op("pe", [xk, "ident"], [pk], trx)
                                tj = ntr % 2
                                ntr += 1
                                P.op("act", [pk], [("tr", tj)], lambda E, pbb=pbb, tj=tj: E.copy(out=trs[tj][:], in_=pbb[:, 0:512]))
                                if ct < 32:
                                    g, q4 = ct // 4, ct % 4
                                    dst = XS[g, i4 * 4:(i4 + 1) * 4, :, q4 * 128:(q4 + 1) * 128].rearrange("i p n -> p i n")
                                else:
                                    dst = BS[ct - 32, i4 * 4:(i4 + 1) * 4, :, :].rearrange("i p n -> p i n")
                                P.dma("sp", dst, trs[tj][:].rearrange("p (i n) -> p i n", n=128), [("tr", tj)], [("XSw", ct, i4)])
                P.barrier()

            if cfg.get("ssd_s2", True) is False:
                return
            with ExitStack() as s4:
                xg = sb("xg", [128, 16, 512], BF16, s4)
                Bg = sb("Bg", [128, 16, 128], BF16, s4)
                BT = sb("BT", [128, NTOK], BF16, s4)
                CT = sb("CT", [128, NTOK], BF16, s4)
                szs = [sb("sz0", [128, 512], BF16, s4), sb("sz1", [128, 512], BF16, s4)]
                Y = sb("Y", [128, 16, 512], F32, s4)
                hs = [sb("hf", [128, 512], F32, s4), sb("hb", [128, 512], F32, s4)]
                h16 = sb("h16", [128, 512], BF16, s4)
                xdt = [sb("xdtf", [128, 512], BF16, s4), sb("xdtb", [128, 512], BF16, s4)]
                xdd = [sb("xddf", [128, 512], BF16, s4), sb("xddb", [128, 512], BF16, s4)]
                xdd2 = [xdd[0], sb("xddf2", [128, 512], BF16, s4)]
                CBm = [sb("CBf", [128, 128], F32, s4), sb("CBb", [128, 128], F32, s4)]
                lh = [sb("lh0", [128, 128], F32, s4), sb("lh1", [128, 128], F32, s4),
                      sb("lh2", [128, 128], F32, s4), sb("lh3", [128, 128], F32, s4)]
                LT = sb("LT", [128, 1024], F32, s4)
                MT = [sb("MTf", [128, 8, 128], BF16, s4), sb("MTb", [128, 8, 128], BF16, s4)]
                tmpy = sb("tmpy", [128, 512], F32, s4)
                dx = sb("dx", [128, 512], F32, s4)
                G = sb("G", [128, 512], F32, s4)
                yn = sb("yn", [128, 512], BF16, s4)
                ynT = sb("ynT", [128, 512], BF16, s4)
                drow = sb("drow", [128, 64], F32, s4)
                nwrow = sb("nwrow", [128, 512], F32, s4)
                ssq = sb("ssq", [128, 2], F32, s4)
                h0t = sb("h0t", [128, 128], F32, s4)
                hot0_ = sb("hot0", [128, 512], F32, s4)
                hots = [hot0_, hot0_]
                nhot = [0]
                P.dma("sp", drow[:], ssd_dskip[jj, :].partition_broadcast(128), [], ["drow"])
                dsel = [(0, maskF, SLf), (1, maskB, SLb)]
                nlh = [0]
                for g in range(cfg.get("ssd_groups", 8)):
                    P.dma("sp", xg[:], XS[g, :, :, :].rearrange("i p n -> p i n"), [], ["xg"])
                    P.dma("sp", Bg[:], BS[g, :, :, :].rearrange("i p n -> p i n"), [], ["Bg"])
                    P.dma("sp", BT[:], BTS[g, :, :], [], ["BT"])
                    P.dma("sp", CT[:], CTS[g, :, :], [], ["CT"])
                    P.dma("sp", nwrow[:], ssd_nw[jj, g * 512:(g + 1) * 512].partition_broadcast(128), [], ["nwrow"])
                    for d_ in range(2):
                        for q in range(4):
                            P.dma("sp", h0t[:], ssd_h0[jj, d_, g * 512 + q * 128:g * 512 + (q + 1) * 128, :], [], ["h0t"])
                            P.op("pe", ["h0t", "ident_f"], [("ps", 7)], lambda E: E.transpose(
                                out=ps[7][:, 0:128], in_=h0t[:], identity=ident_f[:]))
                            P.op("dve", [("ps", 7)], [("h", d_)], lambda E, d_=d_, q=q: E.tensor_copy(
                                out=hs[d_][:, q * 128:(q + 1) * 128], in_=ps[7][:, 0:128]))

                    def emit_state(d_, seg):
                        def tr4(E):
                            ins = None
                            for q in range(4):
                                ins = E.transpose(out=ps[7][:, q * 128:(q + 1) * 128], in_=hs[d_][:, q * 128:(q + 1) * 128],
                                                  identity=ident_f[:])
                            return ins
                        P.op("pe", [("h", d_), "ident_f"], [("ps", 7)], tr4)
                        hj = 0
                        P.op("act", [("ps", 7)], [("hot", hj)], lambda E: E.copy(out=hots[hj][:], in_=ps[7][:, :]))
                        P.dma("sp", st_out[jj, seg, d_, g * 512:(g + 1) * 512, :].rearrange("(q p) n -> p q n", p=128),
                              hots[hj][:].rearrange("p (q n) -> p q n", n=128), [("hot", hj)], [("sto", seg, d_, g)])

                    def xprep(c, d_, eng, par=0, do_dd=True):
                        hsl = slice(d_ * 64 + g * 8, d_ * 64 + g * 8 + 8)
                        P.op(eng, ["xg", ("DT", c)], [("xdt", d_)], lambda E: E.tensor_tensor(
                            out=v3(xdt[d_][:]), in0=v3(xg[:, c, :]), in1=DTt[:, c, hsl].unsqueeze(2).to_broadcast([128, 8, 64]),
                            op=ALU.mult))
                        if not do_dd:
                            return
                        esl = EX[:, c, 1 if d_ == 0 else 3, g * 8:g * 8 + 8]
                        xd = xdd2[par] if d_ == 0 else xdd[1]
                        P.op(eng, [("xdt", d_), ("EX", c)], [("xdd", d_, par)], lambda E: E.tensor_tensor(
                            out=v3(xd[:]), in0=v3(xdt[d_][:]), in1=esl.unsqueeze(2).to_broadcast([128, 8, 64]), op=ALU.mult))

                    def state_step(c, d_, par=0):
                        hk = ("h", d_)
                        P.op("act", [hk], ["h16"], lambda E: E.copy(out=h16[:], in_=hs[d_][:]))
                        P.op("pe", ["CT", "h16"], [("ps", 6)], lambda E: E.matmul(
                            ps[6][:, :], lhsT=CT[:, c * 128:(c + 1) * 128], rhs=h16[:], start=True, stop=True))
                        esl = EX[:, c, 0 if d_ == 0 else 2, g * 8:g * 8 + 8]
                        P.op("dve", [("ps", 6), ("EX", c)], ["tmpy"], lambda E: E.tensor_tensor(
                            out=v3(tmpy[:]), in0=v3(ps[6][:, :]), in1=esl.unsqueeze(2).to_broadcast([128, 8, 64]), op=ALU.mult))
                        xd = xdd2[par] if d_ == 0 else xdd[1]
                        P.op("pe", ["Bg", ("xdd", d_, par)], [("ps", 7)], lambda E: E.matmul(
                            ps[7][:, :], lhsT=Bg[:, c, :], rhs=xd[:], start=True, stop=True))
                        tsl = ETOT[:, c, d_ * 64 + g * 8:d_ * 64 + g * 8 + 8]
                        P.op("dve", [hk, ("ETOT", c)], [hk], lambda E: E.tensor_tensor(
                            out=v3(hs[d_][:]), in0=v3(hs[d_][:]), in1=tsl.unsqueeze(2).to_broadcast([128, 8, 64]), op=ALU.mult))
                        P.op("dve", [hk, ("ps", 7)], [hk], lambda E: E.tensor_tensor(
                            out=hs[d_][:], in0=hs[d_][:], in1=ps[7][:, :], op=ALU.add))

                    def intra(c):
                        xprep(c, 0, "dve", c % 2)
                        xprep(c, 1, "pool", 0, False)
                        P.op("pe", ["BT", "CT"], [("ps", 0)], lambda E, c=c: E.matmul(
                            ps[0][:, 0:128], lhsT=BT[:, c * 128:(c + 1) * 128], rhs=CT[:, c * 128:(c + 1) * 128],
                            start=True, stop=True))
                        for d_, mk, SL in dsel:
                            P.op("dve", [("ps", 0), "ssdc"], [("CBm", d_)], lambda E, d_=d_, mk=mk: E.tensor_tensor(
                                out=CBm[d_][:], in0=ps[0][:, 0:128], in1=mk, op=ALU.mult))
                        for d_, mk, SL in dsel:
                            for half in range(2):
                                pb = ps[1 + d_ * 2 + half]
                                pk = ("ps", 1 + d_ * 2 + half)
                                for r4 in range(4):
                                    r = half * 4 + r4
                                    lj = nlh[0] % 4
                                    nlh[0] += 1
                                    col = d_ * 64 + g * 8 + r
                                    if lj % 2 == 0:
                                        P.op("act", [("A", c), "ssdc"], [("lh", lj)], lambda E, lj=lj, SL=SL, col=col, c=c: E.mul(
                                            out=lh[lj][:], in_=SL, mul=At[:, c, col:col + 1]))
                                    else:
                                        P.op("dve", [("A", c), "ssdc"], [("lh", lj)], lambda E, lj=lj, SL=SL, col=col, c=c: E.tensor_scalar_mul(
                                            out=lh[lj][:], in0=SL, scalar1=At[:, c, col:col + 1]))
                                    P.op("pe", [("lh", lj), "ssdc"], [pk], lambda E, pb=pb, r4=r4, lj=lj, mk=mk: E.matmul(
                                        pb[:, r4 * 128:(r4 + 1) * 128], lhsT=lh[lj][:], rhs=mk, start=True, stop=True))
                                P.op("act", [pk], [("LT", half)], lambda E, pb=pb, half=half: E.activation(
                                    out=LT[:, half * 512:(half + 1) * 512], in_=pb[:, :], func=AF.Exp))
                                P.op("dve", [("LT", half), ("CBm", d_)], [("MT", d_)], lambda E, d_=d_, half=half: E.tensor_tensor(
                                    out=MT[d_][:, half * 4:(half + 1) * 4, :],
                                    in0=LT[:, half * 512:(half + 1) * 512].rearrange("p (r t) -> p r t", t=128),
                                    in1=CBm[d_][:].unsqueeze(1).to_broadcast([128, 4, 128]), op=ALU.mult))

                        def mmy(E):
                            ins = None
                            for r in range(8):
                                E.matmul(ps[5][:, r * 64:(r + 1) * 64], lhsT=MT[0][:, r, :], rhs=xdt[0][:, r * 64:(r + 1) * 64],
                                         start=True, stop=False)
                                ins = E.matmul(ps[5][:, r * 64:(r + 1) * 64], lhsT=MT[1][:, r, :], rhs=xdt[1][:, r * 64:(r + 1) * 64],
                                               start=False, stop=True)
                            return ins
                        P.op("pe", [("MT", 0), ("MT", 1), ("xdt", 0), ("xdt", 1)], [("ps", 5)], mmy)
                        P.op("pool", ["xg", "drow"], ["dx"], lambda E, c=c: E.tensor_tensor(
                            out=v3(dx[:]), in0=v3(xg[:, c, :]), in1=drow[:, g * 8:g * 8 + 8].unsqueeze(2).to_broadcast([128, 8, 64]),
                            op=ALU.mult))
                        P.op("dve", [("ps", 5), "dx"], [("Y", c)], lambda E, c=c: E.tensor_tensor(
                            out=Y[:, c, :], in0=ps[5][:, :], in1=dx[:], op=ALU.add))

                    def rec_f(c):
                        if c > 0 and c % 2 == 0:
                            emit_state(0, c // 2 - 1)
                            P.op("dve", [("h", 0), "cont"], [("h", 0)], lambda E: E.tensor_scalar_mul(
                                out=hs[0][:], in0=hs[0][:], scalar1=cont[:, 0:1]))
                        state_step(c, 0, c % 2)
                        P.op("pool", ["tmpy", ("Y", c)], [("Y", c)], lambda E, c=c: E.tensor_tensor(
                            out=Y[:, c, :], in0=Y[:, c, :], in1=tmpy[:], op=ALU.add))

                    intra(0)
                    for c in range(16):
                        if c + 1 < 16:
                            intra(c + 1)
                        rec_f(c)
                    emit_state(0, 7)

                    def rec_b(c):
                        if c < 15 and c % 2 == 1:
                            emit_state(1, (c + 1) // 2)
                            P.op("dve", [("h", 1), "cont"], [("h", 1)], lambda E: E.tensor_scalar_mul(
                                out=hs[1][:], in0=hs[1][:], scalar1=cont[:, 0:1]))
                        xprep(c, 1, "pool", 0)
                        state_step(c, 1, 0)
                        P.op("pool", ["tmpy", ("Y", c)], [("Y", c)], lambda E, c=c: E.tensor_tensor(
                            out=Y[:, c, :], in0=Y[:, c, :], in1=tmpy[:], op=ALU.add))

                    def fin(c):
                        P.dma("sp", szs[c % 2][:], ZS[g, c, :, :], [], [("sz", c % 2)])
                        P.op("pool", [("Y", c), ("sz", c % 2)], ["G"], lambda E, c=c: E.tensor_tensor(
                            out=G[:], in0=Y[:, c, :], in1=szs[c % 2][:], op=ALU.mult))
                        P.op("act", ["G"], ["yn", "ssq"], lambda E: E.activation(
                            out=yn[:], in_=G[:], func=AF.Square, accum_out=ssq[:, 0:1]))
                        P.op("act", ["ssq"], ["ssq"], lambda E: E.activation(
                            out=ssq[:, 0:1], in_=ssq[:, 0:1], func=AF.Ln, scale=1.0 / 512, bias=epsb[:, 0:1]))
                        P.op("act", ["ssq"], ["ssq"], lambda E: E.activation(
                            out=ssq[:, 0:1], in_=ssq[:, 0:1], func=AF.Exp, scale=-0.5))
                        P.op("dve", ["G", "ssq", "nwrow"], ["yn"], lambda E: E.scalar_tensor_tensor(
                            out=yn[:], in0=G[:], scalar=ssq[:, 0:1], in1=nwrow[:], op0=ALU.mult, op1=ALU.mult))
                        pbb = ps[0].bitcast(BF16)

                        def try_(E, pbb=pbb):
                            ins = None
                            for q in range(4):
                                ins = E.transpose(out=pbb[:, q * 128:(q + 1) * 128], in_=yn[:, q * 128:(q + 1) * 128],
                                                  identity=ident[:])
                            return ins
                        P.op("pe", ["yn", "ident"], [("ps", 0)], try_)
                        P.op("act", [("ps", 0)], ["ynT"], lambda E, pbb=pbb: E.copy(out=ynT[:], in_=pbb[:, 0:512]))
                        P.dma("sp", YT[g * 4:(g + 1) * 4, :, c * 128:(c + 1) * 128].rearrange("q p n -> p q n"),
                              ynT[:].rearrange("p (q n) -> p q n", n=128), ["ynT"], [("YT", g, c)])

                    for c in range(15, -1, -1):
                        rec_b(c)
                        if c < 15:
                            fin(c + 1)
                    fin(0)
                    emit_state(1, 0)
                P.barrier()


    def mlstm_phase(l, src):
        with ExitStack() as so:
            def sb(name, shape, dt, st=so):
                return st.enter_context(sbt(name, shape, dt))
            GI = sb("GI", [128, 16, 16], F32)
            GF = sb("GF", [128, 16, 16], F32)
            BC = sb("BC", [128, 16, 16], F32)
            TOT = sb("TOT", [128, 16, 16], F32)
            cst = sb("mlc1", [128, 4, 128], F32)
            cst2 = sb("mlc2", [128, 4, 128], F32)
            onesf = sb("onesfm", [128, 128], F32)
            cont = sb("contm", [128, 1], F32)
            maskF, maskB, SLf, SLb = cst[:, 0, :], cst[:, 1, :], cst[:, 2, :], cst[:, 3, :]
            NEGf, NEGb, Self, Selb = cst2[:, 0, :], cst2[:, 1, :], cst2[:, 2, :], cst2[:, 3, :]
            P.dma("sp", cst[:], ssd_consts[:, :, :], [], ["mlc"])
            P.dma("sp", cst2[:], ml_consts[:, :, :], [], ["mlc"])
            P.dma("sp", cont[:], contd[:, :], [], ["contm"])
            P.op("dve", [], ["onesfm"], lambda E: E.memset(onesf[:], 1.0))
            with ExitStack() as s1:
                UT = sb("UTm", [128, KC, NTOK], BF16, s1)
                with ExitStack() as s0:
                    xta = sb("xta", [128, D], F32, s0)
                    xtb = sb("xtb", [128, D], F32, s0)
                    xn = sb("xn", [128, D], BF16, s0)
                    xtl = [xta, xtb]
                    for i in range(16):
                        norm_mod(l, 0, src, i, xtl[i % 2], ("xt", i % 2), i % 2, xn, UT, ("UT", i // 4), i * 128)
                    P.barrier()
                utk = [("UT", q) for q in range(4)]
                bgr = sb("bgr", [128, 32], F32, s1)
                t_a = sb("mt_a", [128, 32], F32, s1)
                t_b = sb("mt_b", [128, 16], F32, s1)
                t_c = sb("mt_c", [128, 16], F32, s1)
                ob0 = sb("ob0", [128, 512], BF16, s1)
                ob1 = sb("ob1", [128, 512], BF16, s1)
                obs = [ob0, ob1]
                fo0 = sb("fo0", [128, NTOK], BF16, s1)
                P.dma("sp", bgr[:], ml_bg.partition_broadcast(128), [], ["bgr"])
                wgt, wgtk = W.take("mlstm_w_in", (slice(None), slice(6144, 6176)), KC, 32)
                for i in range(16):
                    pb = ps[i % 2]
                    pk = ("ps", i % 2)

                    def mmg_(E, pb=pb, i=i):
                        ins = None
                        for k in range(KC):
                            ins = E.matmul(pb[:, 0:32], lhsT=UT[:, k, i * 128:(i + 1) * 128], rhs=wgt[:, k, :],
                                           start=(k == 0), stop=(k == KC - 1))
                        return ins
                    P.op("pe", [wgtk] + utk, [pk], mmg_)
                    P.op("dve", [pk, "bgr"], ["mt_a"], lambda E, pb=pb: E.tensor_tensor(
                        out=t_a[:], in0=pb[:, 0:32], in1=bgr[:], op=ALU.add))
                    t4 = t_a[:].rearrange("p (d g h) -> p d g h", d=2, g=2)
                    P.op("dve", ["mt_a"], [("GI", i)], lambda E, i=i, t4=t4: E.tensor_copy(
                        out=GI[:, i, :].rearrange("p (d h) -> p d h", d=2), in_=t4[:, :, 0, :]))
                    P.op("act", ["mt_a"], ["mt_b"], lambda E, t4=t4: E.activation(
                        out=t_b[:].rearrange("p (d h) -> p d h", d=2), in_=t4[:, :, 1, :], func=AF.Abs))
                    P.op("act", ["mt_b"], ["mt_b"], lambda E: E.activation(out=t_b[:], in_=t_b[:], func=AF.Exp, scale=-1.0))
                    P.op("act", ["mt_b"], ["mt_b"], lambda E: E.activation(out=t_b[:], in_=t_b[:], func=AF.Ln, bias=1.0))
                    P.op("dve", ["mt_a"], ["mt_c"], lambda E, t4=t4: E.tensor_scalar_min(
                        out=t_c[:].rearrange("p (d h) -> p d h", d=2), in0=t4[:, :, 1, :], scalar1=0.0))
                    P.op("dve", ["mt_b", "mt_c"], [("GF", i)], lambda E, i=i: E.tensor_tensor(
                        out=GF[:, i, :], in0=t_c[:], in1=t_b[:], op=ALU.subtract))
                    pc = ps[2 + i % 2]
                    pck = ("ps", 2 + i % 2)

                    def mmcs(E, pc=pc, i=i):
                        E.matmul(pc[:, 0:8], lhsT=maskF, rhs=GF[:, i, 0:8], start=True, stop=True)
                        E.matmul(pc[:, 8:16], lhsT=maskB, rhs=GF[:, i, 8:16], start=True, stop=True)
                        return E.matmul(pc[:, 16:32], lhsT=onesf[:], rhs=GF[:, i, :], start=True, stop=True)
                    P.op("pe", [("GF", i), "mlc", "onesfm"], [pck], mmcs)
                    P.op("dve", [pck], [("BC", i)], lambda E, pc=pc, i=i: E.tensor_copy(out=BC[:, i, :], in_=pc[:, 0:16]))
                    P.op("dve", [pck], [("TOT", i)], lambda E, pc=pc, i=i: E.tensor_copy(out=TOT[:, i, :], in_=pc[:, 16:32]))
                nob = 0
                for blk in range(4):
                    wq, wqk = W.take("mlstm_w_in", (slice(None), slice(blk * 512, (blk + 1) * 512)), KC, 512)
                    isq = blk < 2
                    for ft in range(4):
                        hh = (blk % 2) * 4 + ft
                        for tt in range(4):
                            pb = ps[tt % 2]
                            pk = ("ps", tt % 2)

                            def mmx(E, pb=pb, tt=tt, ft=ft, wq=wq):
                                ins = None
                                for k in range(KC):
                                    ins = E.matmul(pb[:, :], lhsT=wq[:, k, ft * 128:(ft + 1) * 128],
                                                   rhs=UT[:, k, tt * 512:(tt + 1) * 512], start=(k == 0), stop=(k == KC - 1))
                                return ins
                            P.op("pe", [wqk] + utk, [pk], mmx)
                            P.op("act", [pk], ["fo0"], lambda E, pb=pb, tt=tt, isq=isq: E.mul(
                                out=fo0[:, tt * 512:(tt + 1) * 512], in_=pb[:, :], mul=(128.0 ** -0.5) if isq else 1.0))
                        P.dma("sp", (QTS if isq else KTS)[hh, :, :], fo0[:], ["fo0"], [("QKS", blk, ft)])
                    if not isq:
                        for i in range(16):
                            pb = ps[2 + i % 2]
                            pk = ("ps", 2 + i % 2)
                            oj = nob % 2
                            nob += 1

                            def mmk(E, pb=pb, i=i, wq=wq):
                                ins = None
                                for k in range(KC):
                                    ins = E.matmul(pb[:, :], lhsT=UT[:, k, i * 128:(i + 1) * 128], rhs=wq[:, k, :],
                                                   start=(k == 0), stop=(k == KC - 1))
                                return ins
                            P.op("pe", [wqk] + utk, [pk], mmk)
                            P.op("dve", [pk], [("ob", oj)], lambda E, pb=pb, oj=oj: E.tensor_copy(out=obs[oj][:], in_=pb[:, :]))
                            h0_ = (blk % 2) * 4
                            P.dma("sp", KMS[h0_:h0_ + 4, i, :, :].rearrange("h p n -> p h n"),
                                  obs[oj][:].rearrange("p (h n) -> p h n", n=128), [("ob", oj)], [("KMS", blk, i)])
                for blk in range(8):
                    c0 = 2048 + blk * 512
                    wv_, wvk_ = W.take("mlstm_w_in", (slice(None), slice(c0, c0 + 512)), KC, 512)
                    isv = blk < 4
                    for i in range(16):
                        pb = ps[i % 2]
                        pk = ("ps", i % 2)
                        oj = nob % 2
                        nob += 1

                        def mmv(E, pb=pb, i=i, wv_=wv_):
                            ins = None
                            for k in range(KC):
                                ins = E.matmul(pb[:, :], lhsT=UT[:, k, i * 128:(i + 1) * 128], rhs=wv_[:, k, :],
                                               start=(k == 0), stop=(k == KC - 1))
                            return ins
                        P.op("pe", [wvk_] + utk, [pk], mmv)
                        if isv:
                            P.op("act", [pk], [("ob", oj)], lambda E, pb=pb, oj=oj: E.copy(out=obs[oj][:], in_=pb[:, :]))
                        else:
                            P.op("act", [pk], [("ob", oj)], lambda E, pb=pb, oj=oj: E.activation(
                                out=obs[oj][:], in_=pb[:, :], func=AF.Sigmoid))
                        h0_ = (blk % 4) * 2
                        P.dma("sp", (VS if isv else OGS)[h0_:h0_ + 2, i, :, :].rearrange("h p n -> p h n"),
                              obs[oj][:].rearrange("p (h n) -> p h n", n=256), [("ob", oj)], [("VOS", blk, i)])
                P.barrier()

            with ExitStack() as s4:
                QT = sb("QTm", [128, NTOK], BF16, s4)
                KTm = sb("KTm", [128, NTOK], BF16, s4)
                Km = sb("Km", [128, 16, 128], BF16, s4)
                V1 = sb("V1", [128, 16, 257], BF16, s4)
                Hs = sb("Hs", [128, 16, 256], F32, s4)
                CN = [sb("CNf", [128, 257], F32, s4), sb("CNb", [128, 257], F32, s4)]
                CN16 = sb("CN16", [128, 257], BF16, s4)
                Mst = sb("Mst", [128, 2], F32, s4)
                m0t = sb("m0t", [128, 16], F32, s4)
                mout = sb("mout", [1, 128], F32, s4)
                nout = sb("nout", [128, 128], F32, s4)
                nouts = sb("nouts", [128, 128], F32, s4)
                r1 = sb("r1", [128, 128], F32, s4)
                rD = sb("rD", [128, 128], F32, s4)
                Dmm = sb("Dmm", [128, 128], F32, s4)
                Wt = sb("Wt", [128, 128], BF16, s4)
                WTs = sb("WTs", [128, 128], BF16, s4)
                ST = sb("ST", [128, 128], BF16, s4)
                P1s = sb("P1s", [128, 257], F32, s4)
                R = sb("R", [128, 257], F32, s4)
                kw = sb("kw", [128, 128], BF16, s4)
                sm = sb("sm", [128, 16], F32, s4)
                ogs = [sb("og0", [128, 256], BF16, s4), sb("og1", [128, 256], BF16, s4)]
                hn = sb("hn", [128, 256], F32, s4)
                hg = sb("hg", [128, 256], BF16, s4)
                hgT = sb("hgT", [128, 256], BF16, s4)
                hnrow = sb("hnrow", [128, 256], F32, s4)
                P.dma("sp", m0t[:], ml_m0[:, :], [], ["m0t"])
                P.op("dve", [], ["V1"], lambda E: E.memset(V1[:, :, 256:257], 1.0))
                dsel = [(0, maskF, SLf, NEGf, Self), (1, maskB, SLb, NEGb, Selb)]
                MX, MI, MT_, NMT, INTER, EMT, TM, LW, MNEW, E2a, E2b, ADEN, RDN, SSQ = range(14)

                def col(j):
                    return sm[:, j:j + 1]

                for h in range(cfg.get("ml_heads", 8)):
                    P.dma("sp", QT[:], QTS[h, :, :], [], ["QTm"])
                    P.dma("sp", KTm[:], KTS[h, :, :], [], ["KTm"])
                    P.dma("sp", Km[:], KMS[h, :, :, :].rearrange("i p n -> p i n"), [], ["Km"])
                    P.dma("sp", V1[:, :, 0:256], VS[h, :, :, :].rearrange("i p n -> p i n"), ["V1"], ["V1"])
                    P.dma("sp", hnrow[:], ml_hn[h, :].partition_broadcast(128), [], ["hnrow"])
                    for d_ in range(2):
                        P.dma("sp", CN[d_][:, 0:256], ml_C0[d_, h, :, :], [], [("CN", d_)])
                        P.dma("sp", CN[d_][:, 256:257], ml_n0[d_, h, :].rearrange("(p o) -> p o", o=1), [("CN", d_)], [("CN", d_)])
                        P.op("dve", ["m0t"], [("M", d_)], lambda E, d_=d_: E.tensor_copy(
                            out=Mst[:, d_:d_ + 1], in_=m0t[:, d_ * 8 + h:d_ * 8 + h + 1]))

                    def emit_state(d_, seg):
                        P.dma("sp", mlC_out[seg, d_, h, :, :], CN[d_][:, 0:256], [("CN", d_)], [("mlCo", seg, d_, h)])
                        idx = (seg * 2 + d_) * 8 + h
                        P.op("dve", [("CN", d_)], ["nout"], lambda E: E.tensor_copy(
                            out=nout[:, idx:idx + 1], in_=CN[d_][:, 256:257]))
                        P.op("dve", [("M", d_)], ["mout"], lambda E: E.tensor_copy(
                            out=mout[0:1, idx:idx + 1], in_=Mst[0:1, d_:d_ + 1]))

                    def step(c, d_, first_dir):
                        _, mk, SL, NEG, Sel = dsel[d_]
                        gcol = d_ * 8 + h
                        fcol = GF[:, c, gcol:gcol + 1]
                        icol = GI[:, c, gcol:gcol + 1]
                        bcol = BC[:, c, gcol:gcol + 1]
                        tcol = TOT[:, c, gcol:gcol + 1]
                        mcol = Mst[:, d_:d_ + 1]
                        gk = [("GF", c), ("GI", c), ("BC", c), ("TOT", c)]
                        cs = slice(c * 128, (c + 1) * 128)
                        P.op("act", gk + ["mlc"], ["r1"], lambda E: E.mul(out=r1[:], in_=SL, mul=fcol))
                        P.op("dve", gk + ["r1", "ident_f"], ["rD"], lambda E: E.scalar_tensor_tensor(
                            out=rD[:], in0=ident_f[:], scalar=icol, in1=r1[:], op0=ALU.mult, op1=ALU.add))
                        P.op("pe", ["rD", "mlc"], [("ps", 0)], lambda E: E.matmul(
                            ps[0][:, 0:128], lhsT=mk, rhs=rD[:], start=True, stop=True))
                        P.op("dve", [("ps", 0), "mlc"], ["Dmm"], lambda E: E.tensor_tensor(
                            out=Dmm[:], in0=ps[0][:, 0:128], in1=NEG, op=ALU.add))
                        P.op("dve", ["Dmm"], [("sm", MX)], lambda E: E.reduce_max(
                            out=col(MX), in_=Dmm[:], axis=mybir.AxisListType.X))
                        P.op("dve", gk + [("M", d_)], [("sm", MI)], lambda E: E.tensor_tensor(
                            out=col(MI), in0=bcol, in1=mcol, op=ALU.add))
                        P.op("dve", [("sm", MX), ("sm", MI)], [("sm", MT_)], lambda E: E.tensor_tensor(
                            out=col(MT_), in0=col(MX), in1=col(MI), op=ALU.max))
                        P.op("dve", [("sm", MT_)], [("sm", NMT)], lambda E: E.tensor_scalar_mul(
                            out=col(NMT), in0=col(MT_), scalar1=-1.0))
                        P.op("act", [("sm", MI), ("sm", NMT)], [("sm", INTER)], lambda E: E.activation(
                            out=col(INTER), in_=col(MI), func=AF.Exp, bias=col(NMT)))
                        P.op("act", [("sm", NMT)], [("sm", EMT)], lambda E: E.activation(
                            out=col(EMT), in_=col(NMT), func=AF.Exp))
                        P.op("act", ["Dmm", ("sm", NMT)], ["Wt"], lambda E: E.activation(
                            out=Wt[:], in_=Dmm[:], func=AF.Exp, bias=col(NMT)))
                        pbb = ps[1].bitcast(BF16)
                        P.op("pe", ["Wt", "ident"], [("ps", 1)], lambda E: E.transpose(
                            out=pbb[:, 0:128], in_=Wt[:], identity=ident[:]))
                        P.op("act", [("ps", 1)], ["WTs"], lambda E: E.copy(out=WTs[:], in_=pbb[:, 0:128]))
                        P.op("pe", ["KTm", "QTm"], [("ps", 2)], lambda E: E.matmul(
                            ps[2][:, 0:128], lhsT=KTm[:, cs], rhs=QT[:, cs], start=True, stop=True))
                        P.op("dve", [("ps", 2), "WTs"], ["ST"], lambda E: E.tensor_tensor(
                            out=ST[:], in0=ps[2][:, 0:128], in1=WTs[:], op=ALU.mult))
                        P.op("act", [("CN", d_)], ["CN16"], lambda E: E.copy(out=CN16[:], in_=CN[d_][:]))
                        P.op("pe", ["ST", "V1"], [("ps", 3)], lambda E: E.matmul(
                            ps[3][:, 0:257], lhsT=ST[:], rhs=V1[:, c, :], start=True, stop=True))
                        P.op("pe", ["QTm", "CN16"], [("ps", 4)], lambda E: E.matmul(
                            ps[4][:, 0:257], lhsT=QT[:, cs], rhs=CN16[:], start=True, stop=True))
                        P.op("act", [("ps", 3)], ["P1s"], lambda E: E.copy(out=P1s[:], in_=ps[3][:, 0:257]))
                        P.op("dve", [("ps", 4), "P1s", ("sm", INTER)], ["R"], lambda E: E.scalar_tensor_tensor(
                            out=R[:], in0=ps[4][:, 0:257], scalar=col(INTER), in1=P1s[:], op0=ALU.mult, op1=ALU.add))
                        P.op("act", ["R"], [("sm", ADEN)], lambda E: E.activation(
                            out=col(ADEN), in_=R[:, 256:257], func=AF.Abs))
                        P.op("dve", [("sm", ADEN), ("sm", EMT)], [("sm", RDN)], lambda E: E.tensor_tensor(
                            out=col(RDN), in0=col(ADEN), in1=col(EMT), op=ALU.max))
                        P.op("dve", [("sm", RDN)], [("sm", RDN)], lambda E: E.reciprocal(out=col(RDN), in_=col(RDN)))
                        if first_dir:
                            P.op("dve", ["R", ("sm", RDN)], [("Hs", c)], lambda E: E.tensor_scalar_mul(
                                out=Hs[:, c, :], in0=R[:, 0:256], scalar1=col(RDN)))
                        else:
                            P.op("dve", ["R", ("sm", RDN), ("Hs", c)], [("Hs", c)], lambda E: E.scalar_tensor_tensor(
                                out=Hs[:, c, :], in0=R[:, 0:256], scalar=col(RDN), in1=Hs[:, c, :], op0=ALU.mult, op1=ALU.add))
                        P.op("dve", gk + [("M", d_)], [("sm", TM)], lambda E: E.tensor_tensor(
                            out=col(TM), in0=tcol, in1=mcol, op=ALU.add))
                        P.op("dve", gk, [("sm", LW)], lambda E: E.tensor_tensor(
                            out=col(LW), in0=tcol, in1=bcol, op=ALU.subtract))
                        P.op("dve", gk + [("sm", LW)], [("sm", LW)], lambda E: E.tensor_tensor(
                            out=col(LW), in0=col(LW), in1=icol, op=ALU.add))
                        P.op("pe", [("sm", MX), "mlc"], [("ps", 5)], lambda E: E.matmul(
                            ps[5][:, 0:1], lhsT=Sel, rhs=col(MX), start=True, stop=True))
                        P.op("dve", [("ps", 5), ("sm", TM)], [("sm", MNEW)], lambda E: E.tensor_tensor(
                            out=col(MNEW), in0=ps[5][:, 0:1], in1=col(TM), op=ALU.max))
                        P.op("dve", [("sm", TM), ("sm", LW), ("sm", MNEW)], [("sm", TM), ("sm", LW)], lambda E: E.tensor_scalar(
                            out=sm[:, TM:TM + 2], in0=sm[:, TM:TM + 2], scalar1=col(MNEW), scalar2=None, op0=ALU.subtract))
                        P.op("act", [("sm", TM), ("sm", LW)], [("sm", E2a), ("sm", E2b)], lambda E: E.activation(
                            out=sm[:, E2a:E2a + 2], in_=sm[:, TM:TM + 2], func=AF.Exp))
                        P.op("dve", ["Km", ("sm", E2b)], ["kw"], lambda E: E.tensor_scalar_mul(
                            out=kw[:], in0=Km[:, c, :], scalar1=col(E2b)))
                        P.op("pe", ["kw", "V1"], [("ps", 6)], lambda E: E.matmul(
                            ps[6][:, 0:257], lhsT=kw[:], rhs=V1[:, c, :], start=True, stop=True))
                        P.op("dve", [("CN", d_), ("sm", E2a), ("ps", 6)], [("CN", d_)], lambda E: E.scalar_tensor_tensor(
                            out=CN[d_][:], in0=CN[d_][:], scalar=col(E2a), in1=ps[6][:, 0:257], op0=ALU.mult, op1=ALU.add))
                        P.op("dve", [("sm", MNEW)], [("M", d_)], lambda E: E.tensor_copy(out=mcol, in_=col(MNEW)))

                    def seg_reset(d_):
                        P.op("dve", [("CN", d_), "contm"], [("CN", d_)], lambda E: E.tensor_scalar_mul(
                            out=CN[d_][:], in0=CN[d_][:], scalar1=cont[:, 0:1]))
                        P.op("dve", [("M", d_), "contm"], [("M", d_)], lambda E: E.tensor_scalar_mul(
                            out=Mst[:, d_:d_ + 1], in0=Mst[:, d_:d_ + 1], scalar1=cont[:, 0:1]))

                    for c in range(16):
                        if c > 0 and c % 2 == 0:
                            emit_state(0, c // 2 - 1)
                            seg_reset(0)
                        step(c, 0, True)
                    emit_state(0, 7)
                    for c in range(15, -1, -1):
                        if c < 15 and c % 2 == 1:
                            emit_state(1, (c + 1) // 2)
                            seg_reset(1)
                        P.dma("sp", ogs[c % 2][:], OGS[h, c, :, :], [], [("og", c % 2)])
                        step(c, 1, False)
                        P.op("act", [("Hs", c)], ["hn", ("sm", SSQ)], lambda E, c=c: E.activation(
                            out=hn[:], in_=Hs[:, c, :], func=AF.Square, accum_out=col(SSQ)))
                        P.op("act", [("sm", SSQ)], [("sm", SSQ)], lambda E: E.activation(
                            out=col(SSQ), in_=col(SSQ), func=AF.Ln, scale=1.0 / 256, bias=epsb[:, 0:1]))
                        P.op("act", [("sm", SSQ)], [("sm", SSQ)], lambda E: E.activation(
                            out=col(SSQ), in_=col(SSQ), func=AF.Exp, scale=-0.5))
                        P.op("dve", [("Hs", c), ("sm", SSQ), "hnrow"], ["hn"], lambda E, c=c: E.scalar_tensor_tensor(
                            out=hn[:], in0=Hs[:, c, :], scalar=col(SSQ), in1=hnrow[:], op0=ALU.mult, op1=ALU.mult))
                        P.op("pool", ["hn", ("og", c % 2)], ["hg"], lambda E, c=c: E.tensor_tensor(
                            out=hg[:], in0=hn[:], in1=ogs[c % 2][:], op=ALU.mult))
                        pbb = ps[7].bitcast(BF16)

                        def trh(E, pbb=pbb):
                            E.transpose(out=pbb[:, 0:128], in_=hg[:, 0:128], identity=ident[:])
                            return E.transpose(out=pbb[:, 128:256], in_=hg[:, 128:256], identity=ident[:])
                        P.op("pe", ["hg", "ident"], [("ps", 7)], trh)
                        P.op("act", [("ps", 7)], ["hgT"], lambda E, pbb=pbb: E.copy(out=hgT[:], in_=pbb[:, 0:256]))
                        P.dma("sp", YT[h * 2:h * 2 + 2, :, c * 128:(c + 1) * 128].rearrange("q p n -> p q n"),
                              hgT[:].rearrange("p (q n) -> p q n", n=128), ["hgT"], [("YT", h, c)])
                    emit_state(1, 0)
                P.op("pe", ["nout", "ident_f"], [("ps", 0)], lambda E: E.transpose(
                    out=ps[0][:, 0:128], in_=nout[:], identity=ident_f[:]))
                P.op("dve", [("ps", 0)], ["nouts"], lambda E: E.tensor_copy(out=nouts[:], in_=ps[0][:, 0:128]))
                P.dma("sp", mln_out.rearrange("s d h k -> (s d h) k"), nouts[:], ["nouts"], ["mlno"])
                P.dma("sp", mlm_out[:, :], mout[:], ["mout"], ["mlmo"])
                P.barrier()

    cur = xin
    for l in layers:
        if cfg.get("mixer", True):
            if l % 3 == 0:
                ssd_phase(l, cur)
                if cfg.get("ssd_out", True):
                    outproj_phase(l, cur, 32, "ssd_w_out", l // 3)
                cur = X
            if l % 3 == 2:
                mlstm_phase(l, cur)
                outproj_phase(l, cur, 16, "mlstm_w_out")
                cur = X
            if l % 3 == 1:
                attn_phase(l, cur)
                outproj_phase(l, cur, 16, "attn_w_out")
                cur = X
        if cfg.get("ffn", True):
            ffn_phase(l, cur)
            cur = X
    P.barrier()
    return nc, W.plan


def build(cfg):
    _, plan = build_program(cfg, None)
    nc, _ = build_program(cfg, plan)
    return nc


def _rope_tables():
    half, quarter = 64, 32
    t = np.arange(NTOK)
    pos_row = (t // 64).astype(np.float32)
    pos_col = (t % 64).astype(np.float32)
    inv_freq = (10000.0 ** (-np.arange(quarter, dtype=np.float32) / quarter)).astype(np.float32)
    cos = np.zeros((128, NTOK), np.float32)
    sin = np.zeros((128, NTOK), np.float32)
    for p in range(128):
        pos = pos_row if p < 64 else pos_col
        ang = (pos * inv_freq[p % 32]).astype(np.float32)
        cos[p] = np.cos(ang)
        sin[p] = np.sin(ang) * (-1.0 if (p % 64) < 32 else 1.0)
    return cos, sin


def _perm():
    pm = np.zeros((128, 128), np.float32)
    for m in range(128):
        sw = m + 32 if (m % 64) < 32 else m - 32
        pm[sw, m] = 1.0
    return pm


def _core_inputs(inp, core, layers):
    m = {}
    if core < 4:
        m["xin"] = np.ascontiguousarray(inp["x_sample"][core])
        cvec = inp["c"][core]
    else:
        j = core - 4
        m["xin"] = np.ascontiguousarray(inp["x_prompt"][8 * j:8 * j + 8].reshape(NTOK, D))
        cvec = inp["c_ctx"]
    m["cv"] = np.ascontiguousarray(cvec.reshape(KC, 128).T)
    js = sorted(set(l // 3 for l in layers if l % 3 == 0))
    if js:
        if core < 4:
            m["ssd_h0"] = np.ascontiguousarray(inp["state_ssd"][core][js].reshape(len(js), 2, 4096, 128))
            m["cont"] = np.ones((128, 1), np.float32)
        else:
            m["ssd_h0"] = np.zeros((len(js), 2, 4096, 128), np.float32)
            m["cont"] = np.zeros((128, 1), np.float32)
    if (2 in layers) and not js:
        m["cont"] = np.ones((128, 1), np.float32) if core < 4 else np.zeros((128, 1), np.float32)
    if 2 in layers:
        if core < 4:
            m["ml_C0"] = np.ascontiguousarray(inp["state_mlstm_C"][core, 0])
            m["ml_n0"] = np.ascontiguousarray(inp["state_mlstm_n"][core, 0])
            m["ml_m0"] = np.ascontiguousarray(np.broadcast_to(inp["state_mlstm_m"][core, 0].reshape(1, 16), (128, 16)))
        else:
            m["ml_C0"] = np.zeros((2, 8, 128, 256), np.float32)
            m["ml_n0"] = np.zeros((2, 8, 128), np.float32)
            m["ml_m0"] = np.zeros((128, 16), np.float32)
    if 1 in layers:
        mb = np.zeros((128, 160), np.float32)
        if core < 4:
            cos, sin = _rope_tables()
            m["cachek"] = np.ascontiguousarray(inp["cache_attn_k"][core, 0].reshape(512, 512))
            m["cachev"] = np.ascontiguousarray(inp["cache_attn_v"][core, 0].reshape(512, 512))
        else:
            cos = np.ones((128, NTOK), np.float32)
            sin = np.zeros((128, NTOK), np.float32)
            m["cachek"] = np.zeros((512, 512), np.float32)
            m["cachev"] = np.zeros((512, 512), np.float32)
            for q8 in range(8):
                for kt in range(20):
                    ok = kt >= 4 and (kt - 4) // 2 == q8
                    mb[:, q8 * 20 + kt] = 0.0 if ok else -30000.0
        m["ropecos"], m["ropesin"], m["maskb"] = cos, sin, mb
    return m


def _shared_inputs(inp, layers):
    L = list(layers)
    sh = {}
    sh["consts"] = np.eye(128, dtype=np.float32)
    sh["ada_w"] = inp["ada_w"][L] if len(L) < DEPTH else inp["ada_w"]
    sh["ada_b"] = inp["ada_b"][L]
    sh["adab_f"] = np.ascontiguousarray(inp["ada_b"].reshape(DEPTH, 96, 128).transpose(2, 0, 1))
    nw = np.stack([inp["norm_mix_w"], inp["norm_ffn_w"]], axis=1)
    sh["nw_f"] = np.ascontiguousarray(nw.reshape(DEPTH, 2, KC, 128).transpose(3, 0, 1, 2))
    for k in ("ffn_w_gate", "ffn_w_up", "ffn_w_down"):
        sh[k] = inp[k][L] if len(L) < DEPTH else inp[k]
    js = sorted(set(l // 3 for l in L if l % 3 == 0))
    if js:
        sh["ssd_w_in"] = inp["ssd_w_in"][js]
        sh["ssd_w_out"] = inp["ssd_w_out"][js]
        cw = inp["ssd_conv_w"][js]
        sh["ssd_cw"] = np.ascontiguousarray(cw.reshape(len(js), 3, 48, 128).transpose(3, 0, 2, 1))
        sh["ssd_cb"] = np.ascontiguousarray(inp["ssd_conv_b"][js].reshape(len(js), 48, 128).transpose(2, 0, 1))
        sh["ssd_dtb"] = np.ascontiguousarray(inp["ssd_dt_bias"][js].reshape(len(js), 128))
        sh["ssd_alog"] = np.ascontiguousarray(inp["ssd_a_log"][js].reshape(len(js), 128))
        sh["ssd_dskip"] = np.ascontiguousarray(inp["ssd_d"][js])
        sh["ssd_nw"] = np.ascontiguousarray(inp["ssd_norm_w"][js])
        idx = np.arange(128)
        mF = (idx[:, None] <= idx[None, :]).astype(np.float32)
        mB = (idx[:, None] >= idx[None, :]).astype(np.float32)
        slf = (idx[:, None] > idx[None, :]).astype(np.float32)
        slb = (idx[:, None] < idx[None, :]).astype(np.float32)
        sh["ssd_consts"] = np.ascontiguousarray(np.stack([mF, mB, slf, slb], axis=1))
    if 2 in L or js:
        idx = np.arange(128)
        mF = (idx[:, None] <= idx[None, :]).astype(np.float32)
        mB = (idx[:, None] >= idx[None, :]).astype(np.float32)
        slf = (idx[:, None] > idx[None, :]).astype(np.float32)
        slb = (idx[:, None] < idx[None, :]).astype(np.float32)
        sh["ssd_consts"] = np.ascontiguousarray(np.stack([mF, mB, slf, slb], axis=1))
    if 2 in L:
        sh["mlstm_w_in"] = inp["mlstm_w_in"][0]
        sh["mlstm_w_out"] = inp["mlstm_w_out"][0]
        sh["ml_bg"] = np.ascontiguousarray(inp["mlstm_b_gates"][0].reshape(32))
        sh["ml_hn"] = np.ascontiguousarray(inp["mlstm_head_norm"][0])
        idx = np.arange(128)
        negf = np.where(idx[None, :] > idx[:, None], -30000.0, 0.0).astype(np.float32)
        negb = np.where(idx[None, :] < idx[:, None], -30000.0, 0.0).astype(np.float32)
        self_ = np.zeros((128, 128), np.float32)
        self_[127, :] = 1.0
        selb = np.zeros((128, 128), np.float32)
        selb[0, :] = 1.0
        sh["ml_consts"] = np.ascontiguousarray(np.stack([negf, negb, self_, selb], axis=1))
    if 1 in L:
        sh["attn_w_qkv"] = inp["attn_w_qkv"][0]
        sh["attn_w_out"] = inp["attn_w_out"][0]
        sh["attn_qn"] = np.ascontiguousarray(inp["attn_q_norm"][0].reshape(128, 1))
        sh["attn_kn"] = np.ascontiguousarray(inp["attn_k_norm"][0].reshape(128, 1))
        sh["attn_knrow"] = np.ascontiguousarray(inp["attn_k_norm"][0])
        sh["perm"] = _perm()
    return sh


def run(inp, cfg, cores=None):
    inp = {k: np.asarray(v) for k, v in inp.items()}
    cores = list(range(8)) if cores is None else cores
    layers = cfg.get("layers", list(range(DEPTH)))
    nc = build(cfg)
    sh = _shared_inputs(inp, layers)
    in_maps = []
    for core in cores:
        m = dict(sh)
        m.update(_core_inputs(inp, core, layers))
        in_maps.append(m)
    res = run_bass_kernel_spmd(nc, in_maps, core_ids=list(range(len(cores))))
    return res


def kernel(**inputs):
    res = run(inputs, {})
    r = res.results
    y_sample = np.stack([r[c]["xout"] for c in range(4)], axis=0)
    y_prompt = np.concatenate([r[c]["xout"].reshape(8, 256, D) for c in range(4, 8)], axis=0)
    st = np.concatenate([r[c]["st_out"].reshape(2, 8, 2, 64, 64, 128).transpose(1, 0, 2, 3, 4, 5)
                         for c in range(4, 8)], axis=0)
    kc = np.concatenate([r[c]["kc_out"].reshape(8, 1, 256, 4, 128) for c in range(4, 8)], axis=0)
    vc = np.concatenate([r[c]["vc_out"].reshape(8, 1, 256, 4, 128) for c in range(4, 8)], axis=0)
    mC = np.concatenate([r[c]["mlC_out"].reshape(8, 1, 2, 8, 128, 256) for c in range(4, 8)], axis=0)
    mn = np.concatenate([r[c]["mln_out"].reshape(8, 1, 2, 8, 128) for c in range(4, 8)], axis=0)
    mm = np.concatenate([r[c]["mlm_out"].reshape(8, 1, 2, 8) for c in range(4, 8)], axis=0)
    return (np.ascontiguousarray(y_prompt), y_sample, np.ascontiguousarray(st), kc, vc, mC, mn, mm)
```
